# Optimizing a Trainium2 kernel written in Bass

```python
import math
import jax, jax.numpy as jnp
from jax import lax
import numpy as np

D_MODEL = 2048
BATCH = 4
SEQ = 2048
DEPTH = 1

CHUNK = 64
D_S5 = D_MODEL // 2
S5_GROUP = 16
S5_GROUPS = D_S5 // S5_GROUP
S5_STATE = 64
DN_HEADS = 8
DN_HEAD_DIM = 128
D_DN = DN_HEADS * DN_HEAD_DIM
CONV_K = 4
EPS = 1e-6
IN_SPLIT_SIZES = (D_S5, D_S5, D_DN, D_DN, D_DN, D_DN, DN_HEADS, DN_HEADS, D_MODEL, D_MODEL)
D_IN = 2 * D_S5 + 4 * D_DN + 2 * DN_HEADS + 2 * D_MODEL

kernel_name = "hybrid_s5_gated_deltanet_block"


def _f32(t):
    return t.astype(jnp.float32)


def rmsnorm(x, w):
    xf = _f32(x)
    return xf * lax.rsqrt(jnp.mean(xf * xf, axis=-1, keepdims=True) + EPS) * _f32(w)


def l2norm(t):
    return t * lax.rsqrt(jnp.sum(t * t, axis=-1, keepdims=True) + EPS)


def causal_depthwise_conv(x, w):
    c = x.shape[-1]
    return lax.conv_general_dilated(
        x, w[:, None, :], window_strides=(1,), padding=[(CONV_K - 1, 0)],
        dimension_numbers=("NWC", "WIO", "NWC"), feature_group_count=c)


def s5_mixer(u, z, lam_re, lam_im, log_step, b_re, b_im, c_re, c_im, d_skip, w_glu):
    bsz, l, _ = u.shape
    lam_re, lam_im = _f32(lam_re), _f32(lam_im)
    step = jnp.exp(_f32(log_step))[:, None]
    mag = jnp.exp(lam_re * step)
    abar_re = mag * jnp.cos(lam_im * step)
    abar_im = mag * jnp.sin(lam_im * step)
    den = lam_re * lam_re + lam_im * lam_im
    xr = abar_re - 1.0
    f_re = (xr * lam_re + abar_im * lam_im) / den
    f_im = (abar_im * lam_re - xr * lam_im) / den
    b_re, b_im = _f32(b_re), _f32(b_im)
    bb_re = f_re[..., None] * b_re - f_im[..., None] * b_im
    bb_im = f_re[..., None] * b_im + f_im[..., None] * b_re
    ug = u.reshape(bsz, l, S5_GROUPS, S5_GROUP)
    bu_re = jnp.einsum("blgc,gpc->blgp", ug, bb_re)
    bu_im = jnp.einsum("blgc,gpc->blgp", ug, bb_im)
    a_re = jnp.broadcast_to(abar_re, bu_re.shape)
    a_im = jnp.broadcast_to(abar_im, bu_im.shape)

    def combine(e1, e2):
        a1r, a1i, b1r, b1i = e1
        a2r, a2i, b2r, b2i = e2
        return (a2r * a1r - a2i * a1i,
                a2r * a1i + a2i * a1r,
                a2r * b1r - a2i * b1i + b2r,
                a2r * b1i + a2i * b1r + b2i)

    _, _, s_re, s_im = lax.associative_scan(combine, (a_re, a_im, bu_re, bu_im), axis=1)
    y = (jnp.einsum("blgp,gcp->blgc", s_re, _f32(c_re))
         - jnp.einsum("blgp,gcp->blgc", s_im, _f32(c_im)))
    y = y.reshape(bsz, l, D_S5) + _f32(d_skip) * u
    y = jax.nn.gelu(y)
    y = y * jax.nn.sigmoid(y @ _f32(w_glu))
    return y * jax.nn.silu(z)


def gated_delta_rule(q, k, v, g, beta):
    bsz, l, h, dk = q.shape
    dv = v.shape[-1]
    n = l // CHUNK

    def chunks(t):
        return t.reshape(bsz, n, CHUNK, h, -1).transpose(0, 3, 1, 2, 4)

    q = chunks(q) * (dk ** -0.5)
    k = chunks(k)
    v = chunks(v)
    g = g.reshape(bsz, n, CHUNK, h).transpose(0, 3, 1, 2)
    beta = beta.reshape(bsz, n, CHUNK, h).transpose(0, 3, 1, 2)
    gc = jnp.cumsum(g, axis=-1)
    causal = jnp.tril(jnp.ones((CHUNK, CHUNK), dtype=bool))
    strict = jnp.tril(jnp.ones((CHUNK, CHUNK), dtype=bool), -1)
    decay = jnp.exp(jnp.where(causal, gc[..., :, None] - gc[..., None, :], -jnp.inf))
    kk = jnp.einsum("bhncd,bhnsd->bhncs", k, k)
    a_mat = jnp.where(strict, beta[..., None] * kk * decay, 0.0)
    rhs = jnp.concatenate([v * beta[..., None], k * (beta * jnp.exp(gc))[..., None]], axis=-1)
    sol = lax.linalg.triangular_solve(a_mat, rhs, left_side=True, lower=True,
                                      unit_diagonal=True)
    u_c, w_c = sol[..., :dv], sol[..., dv:]
    qk = jnp.einsum("bhncd,bhnsd->bhncs", q, k) * decay
    q_dec = q * jnp.exp(gc)[..., None]
    k_dec = k * jnp.exp(gc[..., -1:] - gc)[..., None]
    g_last = jnp.exp(gc[..., -1])

    def step(state, inp):
        u_i, w_i, qk_i, qd_i, kd_i, gl_i = inp
        v_new = u_i - jnp.einsum("bhcd,bhde->bhce", w_i, state)
        o = (jnp.einsum("bhcd,bhde->bhce", qd_i, state)
             + jnp.einsum("bhcs,bhse->bhce", qk_i, v_new))
        state = state * gl_i[..., None, None] + jnp.einsum("bhcd,bhce->bhde", kd_i, v_new)
        return state, o

    xs = (jnp.moveaxis(u_c, 2, 0), jnp.moveaxis(w_c, 2, 0), jnp.moveaxis(qk, 2, 0),
          jnp.moveaxis(q_dec, 2, 0), jnp.moveaxis(k_dec, 2, 0), jnp.moveaxis(g_last, 2, 0))
    s0 = jnp.zeros((bsz, h, dk, dv), q.dtype)
    _, o = lax.scan(step, s0, xs)
    return o.transpose(1, 0, 3, 2, 4).reshape(bsz, l, h, dv)


def deltanet_mixer(q, k, v, z, beta_logit, a_logit, conv_w, a_log, dt_bias, norm_w):
    bsz, l, _ = q.shape
    qkv = jax.nn.silu(causal_depthwise_conv(jnp.concatenate([q, k, v], axis=-1), _f32(conv_w)))
    q, k, v = jnp.split(qkv, [D_DN, 2 * D_DN], axis=-1)
    q = l2norm(q.reshape(bsz, l, DN_HEADS, DN_HEAD_DIM))
    k = l2norm(k.reshape(bsz, l, DN_HEADS, DN_HEAD_DIM))
    v = v.reshape(bsz, l, DN_HEADS, DN_HEAD_DIM)
    beta = jax.nn.sigmoid(beta_logit)
    g = -jnp.exp(_f32(a_log)) * jax.nn.softplus(a_logit + _f32(dt_bias))
    o = gated_delta_rule(q, k, v, g, beta)
    o = rmsnorm(o, norm_w) * jax.nn.silu(z.reshape(bsz, l, DN_HEADS, DN_HEAD_DIM))
    return o.reshape(bsz, l, D_DN)


def setup_inputs(seed: int = 0) -> dict:
    key = jax.random.key(seed)
    ks = jax.random.split(key, 24)
    f = jnp.float32
    x = jax.random.normal(ks[0], (BATCH, SEQ, D_MODEL), f)
    ln_w = 1.0 + 0.01 * jax.random.normal(ks[1], (DEPTH, D_MODEL), f)
    w_in = jax.random.normal(ks[2], (DEPTH, D_MODEL, D_IN), f) * D_MODEL ** -0.5
    n_idx = jnp.arange(S5_STATE, dtype=f)
    s5_lam_re = -0.5 + 0.01 * jax.random.normal(ks[3], (DEPTH, S5_GROUPS, S5_STATE), f)
    s5_lam_im = math.pi * n_idx + 0.01 * jax.random.normal(ks[4], (DEPTH, S5_GROUPS, S5_STATE), f)
    s5_log_step = jax.random.uniform(ks[5], (DEPTH, S5_GROUPS), f, math.log(1e-3), math.log(1e-1))
    bsc = (2.0 * S5_GROUP) ** -0.5
    s5_b_re = jax.random.normal(ks[6], (DEPTH, S5_GROUPS, S5_STATE, S5_GROUP), f) * bsc
    s5_b_im = jax.random.normal(ks[7], (DEPTH, S5_GROUPS, S5_STATE, S5_GROUP), f) * bsc
    csc = (2.0 * S5_STATE) ** -0.5
    s5_c_re = jax.random.normal(ks[8], (DEPTH, S5_GROUPS, S5_GROUP, S5_STATE), f) * csc
    s5_c_im = jax.random.normal(ks[9], (DEPTH, S5_GROUPS, S5_GROUP, S5_STATE), f) * csc
    s5_d = jax.random.normal(ks[10], (DEPTH, D_S5), f)
    s5_w_glu = jax.random.normal(ks[11], (DEPTH, D_S5, D_S5), f) * D_S5 ** -0.5
    s5_w_up = jax.random.normal(ks[12], (DEPTH, D_S5, D_MODEL), f) * D_S5 ** -0.5
    dn_conv_w = jax.random.normal(ks[13], (DEPTH, CONV_K, 3 * D_DN), f) * CONV_K ** -0.5
    dn_a_log = jnp.log(jax.random.uniform(ks[14], (DEPTH, DN_HEADS), f, 1.0, 16.0))
    dt = jnp.exp(jax.random.uniform(ks[15], (DEPTH, DN_HEADS), f, math.log(1e-3), math.log(1e-1)))
    dn_dt_bias = dt + jnp.log(-jnp.expm1(-dt))
    dn_norm_w = 1.0 + 0.01 * jax.random.normal(ks[16], (DEPTH, DN_HEAD_DIM), f)
    dn_w_up = jax.random.normal(ks[17], (DEPTH, D_DN, D_MODEL), f) * D_DN ** -0.5
    w_out = jax.random.normal(ks[18], (DEPTH, D_MODEL, D_MODEL), f) * D_MODEL ** -0.5
    final_norm_w = 1.0 + 0.01 * jax.random.normal(ks[19], (D_MODEL,), f)
    return {"x": x, "ln_w": ln_w, "w_in": w_in, "s5_lam_re": s5_lam_re, "s5_lam_im": s5_lam_im,
            "s5_log_step": s5_log_step, "s5_b_re": s5_b_re, "s5_b_im": s5_b_im,
            "s5_c_re": s5_c_re, "s5_c_im": s5_c_im, "s5_d": s5_d, "s5_w_glu": s5_w_glu,
            "s5_w_up": s5_w_up, "dn_conv_w": dn_conv_w, "dn_a_log": dn_a_log,
            "dn_dt_bias": dn_dt_bias, "dn_norm_w": dn_norm_w, "dn_w_up": dn_w_up,
            "w_out": w_out, "final_norm_w": final_norm_w}


def reference(x, ln_w, w_in, s5_lam_re, s5_lam_im, s5_log_step, s5_b_re, s5_b_im,
              s5_c_re, s5_c_im, s5_d, s5_w_glu, s5_w_up, dn_conv_w, dn_a_log,
              dn_dt_bias, dn_norm_w, dn_w_up, w_out, final_norm_w):
    split_points = np.cumsum(np.array(IN_SPLIT_SIZES))[:-1].tolist()
    for layer in range(DEPTH):
        h = rmsnorm(x, ln_w[layer])
        proj = _f32(h @ _f32(w_in[layer]))
        (u_s, z_s, q, k, v, z_d, beta_l, a_l, gate_s, gate_d) = jnp.split(proj, split_points, axis=-1)
        y_s = s5_mixer(u_s, z_s, s5_lam_re[layer], s5_lam_im[layer], s5_log_step[layer],
                       s5_b_re[layer], s5_b_im[layer], s5_c_re[layer], s5_c_im[layer],
                       s5_d[layer], s5_w_glu[layer]) @ _f32(s5_w_up[layer])
        y_d = deltanet_mixer(q, k, v, z_d, beta_l, a_l, dn_conv_w[layer], dn_a_log[layer],
                             dn_dt_bias[layer], dn_norm_w[layer]) @ _f32(dn_w_up[layer])
        mixed = jax.nn.sigmoid(gate_s) * y_s + jax.nn.sigmoid(gate_d) * y_d
        x = x + (mixed @ _f32(w_out[layer])).astype(x.dtype)
    return rmsnorm(x, final_norm_w).astype(x.dtype)
```

```python
import contextlib
import numpy as np
import concourse.bass as bass
import concourse.mybir as mybir
from concourse.bass_utils import run_bass_kernel_spmd

F32 = mybir.dt.float32
BF16 = mybir.dt.bfloat16
AF = mybir.ActivationFunctionType
ALU = mybir.AluOpType

ENGS = ("pe", "act", "dve", "pool", "sp")
N_DMA_SEMS = 32


class Buf:
    __slots__ = ("name", "writers", "readers")

    def __init__(self, name):
        self.name = name
        self.writers = []
        self.readers = []


class Ins:
    __slots__ = ("eng", "fn", "deps", "is_dma", "sem", "val", "tag")

    def __init__(self, eng, fn, is_dma, tag=None):
        self.eng = eng
        self.fn = fn
        self.deps = []
        self.is_dma = is_dma
        self.sem = None
        self.val = None
        self.tag = tag


def _mk(method, *args, **kw):
    def fn(e):
        return getattr(e, method)(*args, **kw)
    return fn


class Prog:
    def __init__(self, nc):
        self.nc = nc
        self.q = {e: [] for e in ENGS}
        self.cnt = {e: 0 for e in ENGS}
        self.dma_rr = 0
        self.dma_rr_pool = 0
        self.dma_cnt = [0] * N_DMA_SEMS
        self.dma_last = [None] * N_DMA_SEMS
        self.out_dmas = []

    def mark(self, name):
        if not hasattr(self, 'marks'):
            self.marks = []
        self.marks.append((name, dict(self.cnt)))

    def _add(self, ins, reads, writes):
        deps = []
        for b in reads:
            deps.extend(b.writers)
        for b in writes:
            deps.extend(b.writers)
            deps.extend(b.readers)
        seen = set()
        for d in deps:
            if d is ins or id(d) in seen:
                continue
            seen.add(id(d))
            if d.eng == "pe" and ins.eng == "pe" and not d.is_dma and not ins.is_dma:
                continue
            ins.deps.append(d)
        for b in reads:
            b.readers.append(ins)
        for b in writes:
            b.writers = [ins]
            b.readers = []
        self.q[ins.eng].append(ins)
        return ins

    def x(self, eng, method, *args, reads=(), writes=(), **kw):
        ins = Ins(eng, _mk(method, *args, **kw), False, method)
        self.cnt[eng] += 1
        ins.val = self.cnt[eng]
        return self._add(ins, list(reads), list(writes))

    def d(self, eng, out, in_, reads=(), writes=(), is_output=False, **kw):
        ins = Ins(eng, _mk("dma_start", out=out, in_=in_, **kw), True, "dma")
        half = N_DMA_SEMS // 2
        if eng == "pool":
            k = half + (self.dma_rr_pool % half)
            self.dma_rr_pool += 1
        else:
            k = self.dma_rr % half
            self.dma_rr += 1
        ins.sem = k
        self.dma_cnt[k] += 16
        ins.val = self.dma_cnt[k]
        prev = self.dma_last[k]
        self.dma_last[k] = ins
        self._add(ins, list(reads), list(writes))
        if prev is not None and all(dd is not prev for dd in ins.deps):
            ins.deps.append(prev)
        if is_output:
            self.out_dmas.append(ins)
        return ins

    def emit(self):
        nc = self.nc
        with contextlib.ExitStack() as st:
            csem = {e: st.enter_context(nc.semaphore(f"c_{e}")) for e in ("pe", "act", "dve", "pool")}
            dsem = [st.enter_context(nc.semaphore(f"d_{i}")) for i in range(N_DMA_SEMS)]
            block = st.enter_context(nc.Block())
            final_waits = list(self.out_dmas)

            def run(eng_name, handle):
                seen_c = {}
                seen_d = {}
                for ins in self.q[eng_name]:
                    for d in ins.deps:
                        if d.is_dma:
                            if seen_d.get(d.sem, 0) >= d.val:
                                continue
                            seen_d[d.sem] = d.val
                            handle.wait_ge(dsem[d.sem], d.val)
                        else:
                            if seen_c.get(d.eng, 0) >= d.val:
                                continue
                            seen_c[d.eng] = d.val
                            handle.wait_ge(csem[d.eng], d.val)
                    bi = ins.fn(handle)
                    if ins.is_dma:
                        bi.then_inc(dsem[ins.sem], 16)
                    else:
                        bi.then_inc(csem[ins.eng], 1)
                if eng_name == "sp":
                    for d in final_waits:
                        if seen_d.get(d.sem, 0) >= d.val:
                            continue
                        seen_d[d.sem] = d.val
                        handle.wait_ge(dsem[d.sem], d.val)

            @block.tensor
            def _(e):
                run("pe", e)

            @block.scalar
            def _(e):
                run("act", e)

            @block.vector
            def _(e):
                run("dve", e)

            @block.gpsimd
            def _(e):
                run("pool", e)

            @block.sync
            def _(e):
                run("sp", e)


class Arena:
    def __init__(self, nc, nbytes, name="arena"):
        self.t = nc.alloc_sbuf_tensor(name, [128, nbytes // 2], BF16).ap()
        self.nbytes = nbytes
        self.top = 0
        self.top2 = nbytes
        self.hist = []
        self.peak = 0

    def _mkbuf(self, name, s, e):
        b = Buf(name)
        keep = []
        for (s0, e0, b0) in self.hist:
            if s0 < e and s < e0:
                b.readers.extend(b0.readers)
                b.readers.extend(b0.writers)
            keep.append((s0, e0, b0))
        self.hist = keep
        self.hist.append((s, e, b))
        return b

    def alloc(self, name, shape, dt, nbufs=1, top=False):
        esz = 4 if dt == F32 else 2
        free = int(np.prod(shape[1:]))
        nb = (free * esz + 31) // 32 * 32
        if top:
            e = self.top2
            s = e - nb
            assert s >= self.top, f"arena overflow (top) allocating {name}"
            self.top2 = s
        else:
            s = self.top
            e = s + nb
            assert e <= self.top2, f"arena overflow allocating {name}: {e} > {self.top2}"
            self.top = e
        self.peak = max(self.peak, self.top + (self.nbytes - self.top2))
        ap = self.t[0:shape[0], s // 2:(s + free * esz) // 2]
        if dt == F32:
            ap = ap.bitcast(F32)
        if len(shape) > 2:
            names = "abcdefg"[:len(shape) - 1]
            pat = "p (" + " ".join(names) + ") -> p " + " ".join(names)
            ap = ap.rearrange(pat, **{n: int(v) for n, v in zip(names, shape[1:])})
        if nbufs == 1:
            return ap, self._mkbuf(name, s, e)
        return ap, [self._mkbuf(f"{name}{i}", s, e) for i in range(nbufs)]

    def release_top(self):
        self.top2 = self.nbytes

    def mark(self):
        return self.top

    def release(self, mark):
        self.top = mark


class Ring:
    def __init__(self, items):
        self.items = items
        self.i = 0

    def next(self):
        it = self.items[self.i % len(self.items)]
        self.i += 1
        return it


D = 2048
DIN = 10256
KT = 16
NH = 8
C_US, C_ZS, C_Q, C_K, C_V, C_ZD, C_BETA, C_A, C_GS, C_GD = 0, 1024, 2048, 3072, 4096, 5120, 6144, 6152, 6160, 8208
EPS = 1e-6
PI = float(np.pi)


def build(LT=2048, LO=1024, debug=(), stop_after=None, nheads=NH):
    assert LT % 512 == 0 and LO % 512 == 0 and LO <= LT
    NT = LT // 128
    NTO = LO // 128
    T0 = LT - LO
    NB = LT // 512
    NBO = LO // 512
    NCH = LT // 64
    NP = LT // 128
    NC8 = LT // 8
    NC8O = LO // 8

    nc = bass.Bass("TRN2", target_bir_lowering=False)
    P = Prog(nc)

    def din(name, shape, dt=F32):
        return nc.dram_tensor(name, list(shape), dt, kind="ExternalInput").ap()

    x_in = din("x", [LT, D])
    w_in_t = din("w_in_t", [81, 128, KT, 128])
    lnw_in = din("lnw", [128, KT])
    fnw_in = din("fnw", [1, D])
    convw_in = din("convw", [128, 24, 4])
    dnp_in = din("dnp", [128, 2])
    dnnw_in = din("dnnw", [128, 1])
    lam_in = din("lam", [128, 3, 32])
    sb_in = din("s5b", [128, 2, 32, 16])
    sc_in = din("s5c", [128, 2, 32, 16])
    sd_in = din("s5d", [128, 64])
    wglu_t = din("w_glu_t", [8, 128, 8, 128])
    wups_t = din("w_ups_t", [16, 128, 8, 128])
    wupd_t = din("w_upd_t", [16, 128, 8, 128])
    wout_t = din("w_out_t", [4, 128, KT, 512])
    out_ap = nc.dram_tensor("out", [LO, D], F32, kind="ExternalOutput").ap()
    U_scr = nc.dram_tensor("U_scr", [1024, 8, NC8], BF16).ap()
    Y_scr = nc.dram_tensor("Y_scr", [1024, 8, NC8O], BF16).ap()
    dbg = {}

    def dump(name, ap, buf, shape, dt=F32):
        if name not in debug:
            return
        o = nc.dram_tensor("dbg_" + name, list(shape), dt, kind="ExternalOutput").ap()
        bufs = buf if isinstance(buf, list) else [buf]
        P.d("sp", o, ap, reads=bufs, is_output=True)
        dbg[name] = o

    A = Arena(nc, 143 * 1024, "arena")
    H = Arena(nc, 64 * 1024, "harena")

    def psum_ring(prefix, nbanks, per_bank, dt=F32):
        items = []
        for i in range(nbanks):
            width = 512 if dt == F32 else 1024
            t = nc.alloc_psum_tensor(f"{prefix}{i}", [128, width], dt).ap()
            w = width // per_bank
            for j in range(per_bank):
                items.append((t[:, j * w:(j + 1) * w], Buf(f"{prefix}{i}_{j}")))
        return Ring(items)

    BIG = psum_ring("psb", 2, 1)
    _SM = psum_ring("pss", 6, 1)

    BIG8 = Ring(list(BIG.items) + list(_SM.items))
    cur = {"big": BIG8}

    cur["small"] = _SM

    class _SmallF:
        @staticmethod
        def next():
            t, b = cur["small"].next()
            return t[:, 0:128], b

    class _SmallB:
        @staticmethod
        def next():
            t, b = cur["small"].next()
            return t.bitcast(BF16)[:, 0:128], b
    SMALL = _SmallF
    TRB = _SmallB

    ident_f, B_identf = A.alloc("ident_f", [128, 128], F32)
    ident_b, B_identb = A.alloc("ident_b", [128, 128], BF16)
    ones_b, B_onesb = A.alloc("ones_b", [128, 128], BF16)
    ones_f, B_onesf = A.alloc("ones_f", [128, 128], F32)
    SLm, B_SL = A.alloc("SL", [128, 128], F32)
    CUm, B_CU = A.alloc("CU", [64, 64], F32)
    lnw, B_lnw = A.alloc("lnw", [128, KT], F32)
    convw, B_convw = A.alloc("convw", [128, 24, 4], F32)
    dnnw, B_dnnw = A.alloc("dnnw", [128, 1], F32)
    dnp, B_dnp = A.alloc("dnp", [128, 2], F32)
    epsc, B_eps = A.alloc("epsc", [128, 1], F32)
    onec, B_one = A.alloc("onec", [128, 1], F32)

    P.x("pool", "memset", ident_f, 0.0, writes=[B_identf])
    P.x("pool", "affine_select", out=ident_f, in_=ident_f, pattern=[[-1, 128]], compare_op=ALU.not_equal,
        fill=1.0, base=0, channel_multiplier=1, reads=[B_identf], writes=[B_identf])
    P.x("pool", "tensor_copy", out=ident_b, in_=ident_f, reads=[B_identf], writes=[B_identb])
    P.x("pool", "memset", ones_b, 1.0, writes=[B_onesb])
    P.x("pool", "memset", ones_f, 1.0, writes=[B_onesf])
    P.x("pool", "memset", epsc, EPS, writes=[B_eps])
    P.x("pool", "memset", onec, 1.0, writes=[B_one])
    P.x("pool", "memset", SLm, 1.0, writes=[B_SL])
    P.x("pool", "affine_select", out=SLm, in_=SLm, pattern=[[-1, 128]], compare_op=ALU.is_gt,
        fill=0.0, base=0, channel_multiplier=1, reads=[B_SL], writes=[B_SL])
    P.x("pool", "memset", SLm[64:128, 0:64], 0.0, reads=[B_SL], writes=[B_SL])
    P.x("pool", "memset", CUm, 1.0, writes=[B_CU])
    P.x("pool", "affine_select", out=CUm, in_=CUm, pattern=[[1, 64]], compare_op=ALU.is_ge,
        fill=0.0, base=0, channel_multiplier=-1, reads=[B_CU], writes=[B_CU])
    P.d("sp", lnw, lnw_in, writes=[B_lnw])
    P.d("sp", convw, convw_in, writes=[B_convw])
    P.d("sp", dnnw, dnnw_in, writes=[B_dnnw])
    P.d("sp", dnp, dnp_in, writes=[B_dnp])

    hTo, B_hTo = H.alloc("hTo", [128, KT, LO], BF16, nbufs=NTO)
    mH = H.mark()
    if T0 > 0:
        hTp, B_hTp = H.alloc("hTp", [128, KT, T0], BF16, nbufs=NT - NTO)
    else:
        hTp, B_hTp = None, []
    B_hT = list(B_hTp) + list(B_hTo)

    def hcols(kt, t0, n):
        if t0 >= T0:
            return hTo[:, kt, t0 - T0:t0 - T0 + n]
        assert t0 + n <= T0
        return hTp[:, kt, t0:t0 + n]

    def hcols8(k0, t0, n):
        if t0 >= T0:
            return hTo[:, k0:k0 + 8, t0 - T0:t0 - T0 + n]
        return hTp[:, k0:k0 + 8, t0:t0 + n]

    wslots = [A.alloc(f"wslot{i}", [128, KT, 128], BF16) for i in range(4)]
    WS = Ring(wslots)

    def load_w(src_tile_ap, kt=KT):
        slot, bslot = WS.next()
        P.d("pool", slot[:, 0:kt, :], src_tile_ap, writes=[bslot])
        return slot, bslot

    T_US, T_ZS, T_Q, T_K, T_V, T_ZD, T_BA, T_GS, T_GD = 0, 8, 16, 24, 32, 40, 48, 49, 65

    def proj_fm_g(slot, bslot, kts, rhs_fn, rhs_bufs_fn, blocks, consume, m=128, ring=None, G=None):
        blocks = list(blocks)
        rg = ring if ring is not None else cur["big"]
        if G is None:
            G = 4 if rg is BIG8 else 2
        for g0 in range(0, len(blocks), G):
            grp = blocks[g0:g0 + G]
            pss = [rg.next() for _ in grp]
            for kt in range(kts):
                for b, (ps, B_ps) in zip(grp, pss):
                    P.x("pe", "matmul", ps[0:m, :], lhsT=slot[:, kt, 0:m], rhs=rhs_fn(kt, b), start=(kt == 0), stop=(kt == kts - 1),
                        reads=[bslot] + rhs_bufs_fn(b), writes=[B_ps])
            for b, (ps, B_ps) in zip(grp, pss):
                consume(b, ps, B_ps)
            yield "P"

    def proj_fm(*a, **kw):
        for _ in proj_fm_g(*a, **kw):
            pass

    def h_rhs(kt, b):
        return hcols(kt, b * 512, 512)

    def h_bufs(b):
        return B_hT[b * 4:(b + 1) * 4]

    G_scr = nc.dram_tensor("G_scr", [128, 64 * 2 * 128], BF16).ap()
    Mi_scr = nc.dram_tensor("Mi_scr", [128, 64 * 128], BF16).ap()
    Mr_scr = nc.dram_tensor("Mr_scr", [128, 32 * 2 * 128], BF16).ap()
    A8_scr = nc.dram_tensor("A8_scr", [128, 128], F32).ap()
    B_Gscr, B_Miscr, B_Mrscr, B_A8scr = Buf("Gscr"), Buf("Miscr"), Buf("Mrscr"), Buf("A8scr")

    def s5_tables_gen():
        Gpad, B_Gpad = A.alloc("Gpad", [128, 64, 2, 128], BF16, top=True)
        Mintra, B_Mintra = A.alloc("Mintra", [128, 64, 128], BF16, top=True)
        Minter, B_Minter = A.alloc("Minter", [128, 32, 2, 128], BF16, top=True)
        A8, B_A8 = A.alloc("A8", [128, 2, 2, 32], F32, top=True)
        A8P, B_A8P = A.alloc("A8P", [128, 8, 2, 32], F32, top=True)
        A64, B_A64 = A.alloc("A64", [128, 2, 2, 32], F32, top=True)
        _wmark[0] = A.top2
        lam, B_lam = A.alloc("lam", [128, 3, 32], F32, top=True)
        sbt, B_sbt = A.alloc("sbt", [128, 2, 32, 16], F32, top=True)
        sct, B_sct = A.alloc("sct", [128, 2, 32, 16], F32, top=True)
        sdt, B_sdt = A.alloc("sdt", [128, 64], F32, top=True)
        MK, B_MK = A.alloc("MK", [128, 128], F32, top=True)
        POW, B_POW = A.alloc("POW", [128, 9, 2, 32], F32, top=True)
        NEG, B_NEG = A.alloc("NEG", [128, 8, 2, 32], F32, top=True)
        bbt, B_bbt = A.alloc("bbt", [128, 2, 32, 16], F32, top=True)
        sm = {}
        for nm in ("step", "x1", "mag", "th", "kk", "ths", "s", "c", "ar", "ai", "den", "xr", "fre", "fim", "t1", "t2", "t3", "t4", "n1", "n2", "n3", "n4", "ivr", "ivi"):
            sm[nm] = A.alloc("sm_" + nm, [128, 32], F32, top=True)
        P.d("sp", lam, lam_in, writes=[B_lam])
        P.d("sp", sbt, sb_in, writes=[B_sbt])
        P.d("sp", sct, sc_in, writes=[B_sct])
        P.d("sp", sdt, sd_in, writes=[B_sdt])
        P.x("pool", "memset", MK, 1.0, writes=[B_MK])
        P.x("pool", "affine_select", out=MK.rearrange("p (t c) -> p t c", c=16), in_=MK.rearrange("p (t c) -> p t c", c=16),
            pattern=[[16, 8], [0, 16]], compare_op=ALU.is_ge, fill=0.0, base=15, channel_multiplier=-1,
            reads=[B_MK], writes=[B_MK])
        P.x("pool", "memset", Gpad, 0.0, writes=[B_Gpad])

        def V(nm):
            return sm[nm][0]

        def BV(nm):
            return sm[nm][1]

        def tt(out, ob, a, ab, b, bb_, op, eng="dve"):
            P.x(eng, "tensor_tensor", out=out, in0=a, in1=b, op=op, reads=list(ab) + list(bb_), writes=list(ob))

        lre, lim, lst = lam[:, 0, :], lam[:, 1, :], lam[:, 2, :]
        P.x("act", "activation", out=V("step"), in_=lst, func=AF.Exp, reads=[B_lam], writes=[BV("step")])
        tt(V("x1"), [BV("x1")], lre, [B_lam], V("step"), [BV("step")], ALU.mult)
        P.x("act", "activation", out=V("mag"), in_=V("x1"), func=AF.Exp, reads=[BV("x1")], writes=[BV("mag")])
        tt(V("th"), [BV("th")], lim, [B_lam], V("step"), [BV("step")], ALU.mult)

        def sin_of(dst, shift):
            P.x("dve", "tensor_scalar", out=V("ths"), in0=V("th"), scalar1=shift, scalar2=None, op0=ALU.add,
                reads=[BV("th")], writes=[BV("ths")])
            P.x("dve", "tensor_scalar", out=V("kk"), in0=V("ths"), scalar1=PI, scalar2=None, op0=ALU.is_ge,
                reads=[BV("ths")], writes=[BV("kk")])
            for mth in range(1, 6):
                P.x("dve", "scalar_tensor_tensor", out=V("kk"), in0=V("ths"), scalar=(2 * mth + 1) * PI, in1=V("kk"),
                    op0=ALU.is_ge, op1=ALU.add, reads=[BV("ths"), BV("kk")], writes=[BV("kk")])
            P.x("dve", "scalar_tensor_tensor", out=V("ths"), in0=V("kk"), scalar=-2.0 * PI, in1=V("ths"),
                op0=ALU.mult, op1=ALU.add, reads=[BV("ths"), BV("kk")], writes=[BV("ths")])
            P.x("act", "activation", out=V(dst), in_=V("ths"), func=AF.Sin, reads=[BV("ths")], writes=[BV(dst)])
        sin_of("s", 0.0)
        yield
        sin_of("c", PI / 2)
        yield
        tt(V("ar"), [BV("ar")], V("mag"), [BV("mag")], V("c"), [BV("c")], ALU.mult)
        tt(V("ai"), [BV("ai")], V("mag"), [BV("mag")], V("s"), [BV("s")], ALU.mult)
        tt(V("t1"), [BV("t1")], lre, [B_lam], lre, [B_lam], ALU.mult)
        tt(V("t2"), [BV("t2")], lim, [B_lam], lim, [B_lam], ALU.mult)
        tt(V("den"), [BV("den")], V("t1"), [BV("t1")], V("t2"), [BV("t2")], ALU.add)
        P.x("dve", "reciprocal", out=V("den"), in_=V("den"), reads=[BV("den")], writes=[BV("den")])
        P.x("dve", "tensor_scalar", out=V("xr"), in0=V("ar"), scalar1=-1.0, scalar2=None, op0=ALU.add, reads=[BV("ar")], writes=[BV("xr")])
        tt(V("t1"), [BV("t1")], V("xr"), [BV("xr")], lre, [B_lam], ALU.mult)
        tt(V("t2"), [BV("t2")], V("ai"), [BV("ai")], lim, [B_lam], ALU.mult)
        tt(V("t1"), [BV("t1")], V("t1"), [BV("t1")], V("t2"), [BV("t2")], ALU.add)
        tt(V("fre"), [BV("fre")], V("t1"), [BV("t1")], V("den"), [BV("den")], ALU.mult)
        tt(V("t3"), [BV("t3")], V("ai"), [BV("ai")], lre, [B_lam], ALU.mult)
        tt(V("t4"), [BV("t4")], V("xr"), [BV("xr")], lim, [B_lam], ALU.mult)
        tt(V("t3"), [BV("t3")], V("t3"), [BV("t3")], V("t4"), [BV("t4")], ALU.subtract)
        tt(V("fim"), [BV("fim")], V("t3"), [BV("t3")], V("den"), [BV("den")], ALU.mult)
        tt(V("t1"), [BV("t1")], V("mag"), [BV("mag")], V("mag"), [BV("mag")], ALU.mult)
        P.x("dve", "reciprocal", out=V("t1"), in_=V("t1"), reads=[BV("t1")], writes=[BV("t1")])
        tt(V("ivr"), [BV("ivr")], V("ar"), [BV("ar")], V("t1"), [BV("t1")], ALU.mult)
        tt(V("t2"), [BV("t2")], V("ai"), [BV("ai")], V("t1"), [BV("t1")], ALU.mult)
        P.x("dve", "tensor_scalar", out=V("ivi"), in0=V("t2"), scalar1=-1.0, scalar2=None, op0=ALU.mult, reads=[BV("t2")], writes=[BV("ivi")])

        def cmul_small(dst, dbuf, j, src, sbuf, jm, mr, mrb, mi, mib, eng="dve", tp="t"):
            pr, pi = src[:, jm, 0, :], src[:, jm, 1, :]
            n1, n2, n3, n4 = tp + "1", tp + "2", tp + "3", tp + "4"
            tt(V(n1), [BV(n1)], pr, [sbuf], mr, [mrb], ALU.mult, eng=eng)
            tt(V(n2), [BV(n2)], pi, [sbuf], mi, [mib], ALU.mult, eng=eng)
            tt(dst[:, j, 0, :], [dbuf], V(n1), [BV(n1), dbuf], V(n2), [BV(n2)], ALU.subtract, eng=eng)
            tt(V(n3), [BV(n3)], pr, [sbuf], mi, [mib], ALU.mult, eng=eng)
            tt(V(n4), [BV(n4)], pi, [sbuf], mr, [mrb], ALU.mult, eng=eng)
            tt(dst[:, j, 1, :], [dbuf], V(n3), [BV(n3), dbuf], V(n4), [BV(n4)], ALU.add, eng=eng)

        P.x("pool", "memset", POW[:, 0, 0, :], 1.0, writes=[B_POW])
        P.x("pool", "memset", POW[:, 0, 1, :], 0.0, reads=[B_POW], writes=[B_POW])
        P.x("pool", "memset", NEG[:, 0, 0, :], 1.0, writes=[B_NEG])
        P.x("pool", "memset", NEG[:, 0, 1, :], 0.0, reads=[B_NEG], writes=[B_NEG])
        for j in range(1, 9):
            cmul_small(POW, B_POW, j, POW, B_POW, j - 1, V("ar"), BV("ar"), V("ai"), BV("ai"))
            yield
        for j in range(1, 8):
            cmul_small(NEG, B_NEG, j, NEG, B_NEG, j - 1, V("ivr"), BV("ivr"), V("ivi"), BV("ivi"), eng="pool", tp="n")
            yield
        for r in range(2):
            P.x("pool", "tensor_copy", out=A8[:, 0, r, :], in_=POW[:, 8, 0, :], reads=[B_POW, B_A8], writes=[B_A8])
            P.x("pool", "tensor_copy", out=A8[:, 1, r, :], in_=POW[:, 8, 1, :], reads=[B_POW, B_A8], writes=[B_A8])
        for r in range(2):
            P.x("pool", "tensor_copy", out=A8P[:, 0, r, :], in_=POW[:, 8, r, :], reads=[B_POW, B_A8P], writes=[B_A8P])
        for k in range(1, 8):
            cmul_small(A8P, B_A8P, k, A8P, B_A8P, k - 1, POW[:, 8, 0, :], B_POW, POW[:, 8, 1, :], B_POW)
        for r in range(2):
            P.x("pool", "tensor_copy", out=A64[:, 0, r, :], in_=A8P[:, 7, 0, :], reads=[B_A8P, B_A64], writes=[B_A64])
            P.x("pool", "tensor_copy", out=A64[:, 1, r, :], in_=A8P[:, 7, 1, :], reads=[B_A8P, B_A64], writes=[B_A64])
        yield
        fre_b = V("fre").unsqueeze(2).to_broadcast([128, 32, 16])
        fim_b = V("fim").unsqueeze(2).to_broadcast([128, 32, 16])
        big1, B_big1 = A.alloc("big1", [128, 32, 16], F32, top=True)
        big2, B_big2 = A.alloc("big2", [128, 32, 16], F32, top=True)
        tt(big1, [B_big1], sbt[:, 0], [B_sbt], fre_b, [BV("fre")], ALU.mult)
        tt(big2, [B_big2], sbt[:, 1], [B_sbt], fim_b, [BV("fim")], ALU.mult)
        tt(bbt[:, 0], [B_bbt], big1, [B_big1], big2, [B_big2], ALU.subtract)
        tt(big1, [B_big1], sbt[:, 1], [B_sbt], fre_b, [BV("fre")], ALU.mult)
        tt(big2, [B_big2], sbt[:, 0], [B_sbt], fim_b, [BV("fim")], ALU.mult)
        tt(bbt[:, 1], [B_bbt], big1, [B_big1, B_bbt], big2, [B_big2], ALU.add)

        GB = 2
        Pt, B_Pt = A.alloc("Pt", [128, 2, GB, 8, 16], F32, top=True)
        GTt, B_GTt = A.alloc("GTt", [128, 2, GB, 8, 16], F32, top=True)
        Qt, B_Qt = A.alloc("Qt", [128, 2, GB, 9, 16], F32, top=True)
        w1, B_w1 = A.alloc("w1", [128, GB, 9, 16], F32, top=True)
        w2, B_w2 = A.alloc("w2", [128, GB, 9, 16], F32, top=True)
        mtmp, B_mtmp = A.alloc("mtmp", [128, 128], F32, top=True)
        q1, B_q1 = A.alloc("q1", [128, GB, 9, 16], F32, top=True)
        q2, B_q2 = A.alloc("q2", [128, GB, 9, 16], F32, top=True)
        qz, B_qz = A.alloc("qz", [128, GB, 9, 16], F32, top=True)
        P.x("pool", "memset", qz, 0.0, writes=[B_qz])
        for gb in range(32 // GB):
            g0 = gb * GB
            gs = slice(g0, g0 + GB)
            sh = [128, GB, 8, 16]
            nr = NEG[:, :, 0, gs].rearrange("p t g -> p g t").unsqueeze(3).to_broadcast(sh)
            ni = NEG[:, :, 1, gs].rearrange("p t g -> p g t").unsqueeze(3).to_broadcast(sh)
            br = bbt[:, 0, gs, :].unsqueeze(2).to_broadcast(sh)
            bi = bbt[:, 1, gs, :].unsqueeze(2).to_broadcast(sh)
            a1, a2 = w1[:, :, 0:8, :], w2[:, :, 0:8, :]
            tt(a1, [B_w1], nr, [B_NEG], br, [B_bbt], ALU.mult)
            tt(a2, [B_w2], ni, [B_NEG], bi, [B_bbt], ALU.mult)
            tt(Pt[:, 0], [B_Pt], a1, [B_w1], a2, [B_w2], ALU.subtract)
            tt(a1, [B_w1], nr, [B_NEG], bi, [B_bbt], ALU.mult)
            tt(a2, [B_w2], ni, [B_NEG], br, [B_bbt], ALU.mult)
            tt(Pt[:, 1], [B_Pt], a1, [B_w1, B_Pt], a2, [B_w2], ALU.add)
            yield
            p7r = POW[:, 7, 0, gs].unsqueeze(2).unsqueeze(3).to_broadcast(sh)
            p7i = POW[:, 7, 1, gs].unsqueeze(2).unsqueeze(3).to_broadcast(sh)
            tt(a1, [B_w1], Pt[:, 0], [B_Pt], p7r, [B_POW], ALU.mult)
            tt(a2, [B_w2], Pt[:, 1], [B_Pt], p7i, [B_POW], ALU.mult)
            tt(GTt[:, 0], [B_GTt], a1, [B_w1], a2, [B_w2], ALU.subtract)
            tt(a1, [B_w1], Pt[:, 0], [B_Pt], p7i, [B_POW], ALU.mult)
            tt(a2, [B_w2], Pt[:, 1], [B_Pt], p7r, [B_POW], ALU.mult)
            tt(GTt[:, 1], [B_GTt], a1, [B_w1, B_GTt], a2, [B_w2], ALU.add)
            yield
            shq = [128, GB, 9, 16]
            pr = POW[:, :, 0, gs].rearrange("p j g -> p g j").unsqueeze(3).to_broadcast(shq)
            pi_ = POW[:, :, 1, gs].rearrange("p j g -> p g j").unsqueeze(3).to_broadcast(shq)
            cr = sct[:, 0, gs, :].unsqueeze(2).to_broadcast(shq)
            ci = sct[:, 1, gs, :].unsqueeze(2).to_broadcast(shq)
            tt(q1, [B_q1], cr, [B_sct], pr, [B_POW], ALU.mult, eng="pool")
            tt(q2, [B_q2], ci, [B_sct], pi_, [B_POW], ALU.mult, eng="pool")
            tt(Qt[:, 0], [B_Qt], q1, [B_q1], q2, [B_q2], ALU.subtract, eng="pool")
            tt(q1, [B_q1], cr, [B_sct], pi_, [B_POW], ALU.mult, eng="pool")
            tt(q2, [B_q2], ci, [B_sct], pr, [B_POW], ALU.mult, eng="pool")
            tt(q1, [B_q1], q1, [B_q1], q2, [B_q2], ALU.add, eng="pool")
            tt(Qt[:, 1], [B_Qt], qz, [B_qz], q1, [B_q1, B_Qt], ALU.subtract, eng="pool")
            for ri in range(2):
                P.x("pool", "tensor_copy", out=Minter[:, gs, ri, :].rearrange("p g (t c) -> p g t c", c=16),
                    in_=Qt[:, ri, :, 1:9, :], reads=[B_Qt, B_Minter], writes=[B_Minter])
            yield
            for gl in range(GB):
                for half in range(2):
                    yield
                    g = half * 32 + g0 + gl
                    base = half * 64
                    rows = slice(base, base + 64)
                    ps, B_ps = SMALL.next()
                    P.x("pe", "matmul", ps, lhsT=Pt[rows, 0, gl].rearrange("p t c -> p (t c)"),
                        rhs=Qt[rows, 0, gl, 0:8, :].rearrange("p t c -> p (t c)"), start=True, stop=False,
                        reads=[B_Pt, B_Qt], writes=[B_ps])
                    P.x("pe", "matmul", ps, lhsT=Pt[rows, 1, gl].rearrange("p t c -> p (t c)"),
                        rhs=Qt[rows, 1, gl, 0:8, :].rearrange("p t c -> p (t c)"), start=False, stop=True,
                        reads=[B_Pt, B_Qt], writes=[B_ps])
                    P.x("dve", "tensor_tensor", out=mtmp, in0=ps, in1=MK, op=ALU.mult, reads=[B_ps, B_MK], writes=[B_mtmp])
                    P.x("dve", "scalar_tensor_tensor", out=Mintra[:, g, :], in0=ident_f, scalar=sdt[:, g:g + 1], in1=mtmp,
                        op0=ALU.mult, op1=ALU.add, reads=[B_identf, B_sdt, B_mtmp, B_Mintra], writes=[B_Mintra])
                    for ri in range(2):
                        ps2, B_ps2 = SMALL.next()
                        P.x("pe", "transpose", ps2[:, 0:64], GTt[rows, ri, gl].rearrange("p t c -> p (t c)"),
                            ident_f[rows, base:base + 64], reads=[B_GTt, B_identf], writes=[B_ps2])
                        P.x("act", "activation", out=Gpad[:, g, ri, base:base + 64], in_=ps2[:, 0:64], func=AF.Copy,
                            reads=[B_ps2, B_Gpad], writes=[B_Gpad])

        A.top2 = _wmark[0]
        _tres.update(Gpad=(Gpad, B_Gpad), Mintra=(Mintra, B_Mintra), Minter=(Minter, B_Minter), A8=(A8, B_A8), A8P=(A8P, B_A8P), A64=(A64, B_A64))

    _wmark = [None]
    _tres = {}
    _tg = [s5_tables_gen()]

    def tick(n=1):
        for _ in range(n):
            if _tg[0] is None:
                return
            try:
                next(_tg[0])
            except StopIteration:
                _tg[0] = None
                return

    def drain_tables():
        while _tg[0] is not None:
            tick()

    m1 = A.mark()
    xts = [A.alloc(f"xt{i}", [128, D], F32) for i in range(2)]
    xss = [A.alloc(f"xs{i}", [128, D], BF16) for i in range(2)]
    ss, B_ss = A.alloc("ss", [128, NT], F32, nbufs=NT)
    rstd, B_rstd = A.alloc("rstd", [128, NT], F32, nbufs=NT)
    for tt in range(NT):
        xt, B_xt = xts[tt % 2]
        xs, B_xs = xss[tt % 2]
        P.d("sp", xt, x_in[tt * 128:(tt + 1) * 128, :], writes=[B_xt])
        P.x("act", "activation", out=xs, in_=xt, func=AF.Square, accum_out=ss[:, tt:tt + 1],
            reads=[B_xt], writes=[B_xs, B_ss[tt]])
        P.x("act", "activation", out=rstd[:, tt:tt + 1], in_=ss[:, tt:tt + 1], func=AF.Sqrt,
            bias=epsc[:, 0:1], scale=1.0 / D, reads=[B_ss[tt], B_eps], writes=[B_rstd[tt]])
        P.x("dve", "reciprocal", out=rstd[:, tt:tt + 1], in_=rstd[:, tt:tt + 1], reads=[B_rstd[tt]], writes=[B_rstd[tt]])
        P.x("act", "activation", out=xs, in_=xt, func=AF.Copy, scale=rstd[:, tt:tt + 1],
            reads=[B_xt, B_rstd[tt]], writes=[B_xs])
        for half in range(2):
            psf, B_psf = cur["big"].next()
            psb = psf.bitcast(BF16)
            for k in range(8):
                kt = half * 8 + k
                P.x("pe", "transpose", psb[:, k * 128:(k + 1) * 128], xs[:, kt * 128:(kt + 1) * 128], ident_b,
                    reads=[B_xs, B_identb], writes=[B_psf])
            P.x("dve", "tensor_tensor", out=hcols8(half * 8, tt * 128, 128),
                in0=psb.rearrange("p (k t) -> p k t", k=8),
                in1=lnw[:, half * 8:half * 8 + 8].unsqueeze(2).to_broadcast([128, 8, 128]), op=ALU.mult,
                reads=[B_psf, B_lnw], writes=[B_hT[tt]])
    A.release(m1)
    P.mark("stage1")
    dump("hTo", hTo, B_hTo, [128, KT, LO], BF16)
    if stop_after == "stage1":
        P.emit()
        return nc, dbg

    m1 = A.mark()
    usts = [A.alloc(f"ust{i}", [128, 8, NC8], BF16) for i in range(2)]
    B_U = [Buf(f"U{j}") for j in range(8)]
    for j in range(8):
        slot, bslot = load_w(w_in_t[T_US + j])
        ust, B_ust = usts[j % 2]

        def cons_u(b, ps, B_ps, ust=ust, B_ust=B_ust):
            P.x("act", "activation", out=ust[:, :, b * 64:(b + 1) * 64].rearrange("p t c -> p c t"),
                in_=ps.rearrange("p (c t) -> p c t", t=8), func=AF.Copy, reads=[B_ps], writes=[B_ust])
        proj_fm(slot, bslot, KT, h_rhs, h_bufs, range(NB), cons_u)
        P.d("sp", U_scr[j * 128:(j + 1) * 128], ust, reads=[B_ust], writes=[B_U[j]])
    A.release(m1)
    P.mark("u")
    if stop_after == "u":
        P.emit()
        return nc, dbg

    dnoT, B_dno = A.alloc("dnoT", [128, NH, LO], BF16, nbufs=NH)
    m_dn0 = A.mark()
    GC, B_GC = A.alloc("GC", [128, LT], F32)
    PT, B_PT = A.alloc("PT", [128, NP, 4, 8], F32)
    CT, B_CT = A.alloc("CT", [64, NCH, 3, 8], F32)
    GLB, B_GLB = A.alloc("GLB", [128, NCH * 8], F32)
    m0 = A.mark()
    G1, B_G1 = A.alloc("G1", [128, LT], F32)
    RM, B_RM = A.alloc("RM", [128, LT], F32)
    PTin, B_PTin = A.alloc("PTin", [128, LT], F32)
    CTin, B_CTin = A.alloc("CTin", [128, LT], F32)
    Dg, B_Dg = A.alloc("Dg", [128, NCH, 8], F32)
    negA, B_negA = A.alloc("negA", [128, 1], F32)

    P.x("pool", "memset", PTin, 0.0, writes=[B_PTin])
    P.x("pool", "memset", CTin, 0.0, writes=[B_CTin])
    P.x("pool", "memset", G1, 0.0, writes=[B_G1])
    P.x("pool", "memset", RM, 1.0, writes=[B_RM])
    P.x("pool", "memset", RM.rearrange("p (c t) -> p c t", t=64)[:, :, 0:1], 0.0, reads=[B_RM], writes=[B_RM])
    P.x("act", "activation", out=negA, in_=dnp[:, 0:1], func=AF.Exp, reads=[B_dnp], writes=[B_negA])
    P.x("dve", "tensor_scalar", out=negA, in0=negA, scalar1=-1.0, scalar2=None, op0=ALU.mult, reads=[B_negA], writes=[B_negA])

    wba, B_wba = load_w(w_in_t[T_BA])

    def cons_ba(b, ps, B_ps):
        sl = slice(b * 512, (b + 1) * 512)
        P.x("act", "activation", out=PTin[0:8, sl], in_=ps[0:8, :], func=AF.Sigmoid, reads=[B_ps], writes=[B_PTin])
        for (ra, rb) in ((32, 40), (64, 104)):
            P.x("act", "activation", out=G1[ra:rb, sl], in_=ps[ra:rb, :], func=AF.Exp, bias=dnp[ra:rb, 1:2], scale=1.0,
                reads=[B_ps, B_dnp], writes=[B_G1])
    proj_fm(wba, B_wba, KT, h_rhs, h_bufs, range(NB), cons_ba, m=104)
    for (ra, rb) in ((32, 40), (64, 104)):
        P.x("act", "activation", out=G1[ra:rb, :], in_=G1[ra:rb, :], func=AF.Ln, bias=onec[ra:rb, 0:1], scale=1.0,
            reads=[B_G1, B_one], writes=[B_G1])
        P.x("dve", "tensor_scalar", out=G1[ra:rb, :], in0=G1[ra:rb, :], scalar1=negA[ra:rb, 0:1], scalar2=None, op0=ALU.mult,
            reads=[B_G1, B_negA], writes=[B_G1])
        P.x("dve", "tensor_tensor_scan", out=GC[ra:rb, :], data0=RM[ra:rb, :], data1=G1[ra:rb, :], initial=0.0,
            op0=ALU.mult, op1=ALU.add, reads=[B_RM, B_G1], writes=[B_GC])
    P.x("pool", "tensor_copy", out=PTin[32:40, :], in_=GC[32:40, :], reads=[B_GC, B_PTin], writes=[B_PTin])
    P.x("act", "activation", out=PTin[64:72, :], in_=GC[64:72, :], func=AF.Exp, reads=[B_GC, B_PTin], writes=[B_PTin])
    P.x("dve", "tensor_scalar", out=CTin[32:40, :], in0=GC[32:40, :], scalar1=-1.0, scalar2=None, op0=ALU.mult,
        reads=[B_GC, B_CTin], writes=[B_CTin])
    P.x("act", "activation", out=CTin[64:72, :], in_=GC[64:72, :], func=AF.Exp, reads=[B_GC, B_CTin], writes=[B_CTin])
    gc3 = GC[96:104, :].rearrange("p (c t) -> p c t", t=64)
    P.x("dve", "tensor_tensor", out=CTin[96:104, :].rearrange("p (c t) -> p c t", t=64),
        in0=gc3[:, :, 63:64].to_broadcast([8, NCH, 64]), in1=gc3, op=ALU.subtract,
        reads=[B_GC, B_CTin], writes=[B_CTin])
    P.x("act", "activation", out=CTin[96:104, :], in_=CTin[96:104, :], func=AF.Exp, reads=[B_CTin], writes=[B_CTin])
    P.x("pool", "tensor_copy", out=PTin[96:104, :], in_=CTin[96:104, :], reads=[B_CTin, B_PTin], writes=[B_PTin])
    eg3 = PTin[64:72, :].rearrange("p (c t) -> p c t", t=64)
    P.x("dve", "tensor_tensor", out=Dg[64:72], in0=eg3[:, :, 63:64].to_broadcast([8, NCH, 8]),
        in1=ident_f[64:72, 64:72].unsqueeze(1).to_broadcast([8, NCH, 8]), op=ALU.mult,
        reads=[B_PTin, B_identf], writes=[B_Dg])
    psg, B_psg = BIG.next()
    P.x("pe", "matmul", psg[:, 0:NCH * 8], lhsT=ones_f[64:72, :], rhs=Dg[64:72].rearrange("p c h -> p (c h)"),
        start=True, stop=True, reads=[B_onesf, B_Dg], writes=[B_psg])
    P.x("act", "activation", out=GLB, in_=psg[:, 0:NCH * 8], func=AF.Copy, reads=[B_psg], writes=[B_GLB])
    for i in range(NP):
        ps, B_ps = SMALL.next()
        P.x("pe", "transpose", ps[:, 0:104], PTin[0:104, i * 128:(i + 1) * 128], ident_f[0:104, 0:104],
            reads=[B_PTin, B_identf], writes=[B_ps])
        pv = ps[:, 0:128].rearrange("p (k c) -> p k c", c=32)[:, :, 0:8]
        if i % 2:
            P.x("dve", "tensor_copy", out=PT[:, i], in_=pv, reads=[B_ps], writes=[B_PT])
        else:
            P.x("act", "activation", out=PT[:, i], in_=pv, func=AF.Copy, reads=[B_ps], writes=[B_PT])
    for ch in range(NCH):
        ps, B_ps = SMALL.next()
        P.x("pe", "transpose", ps[0:64, 0:104], CTin[0:104, ch * 64:(ch + 1) * 64], ident_f[0:104, 0:104],
            reads=[B_CTin, B_identf], writes=[B_ps])
        cv = ps[0:64, 32:128].rearrange("p (k c) -> p k c", c=32)[:, :, 0:8]
        if ch % 2:
            P.x("dve", "tensor_copy", out=CT[:, ch], in_=cv, reads=[B_ps], writes=[B_CT])
        else:
            P.x("act", "activation", out=CT[:, ch], in_=cv, func=AF.Copy, reads=[B_ps], writes=[B_CT])
    A.release(m0)
    P.mark("dn0")
    dump("PT", PT, B_PT, [128, NP, 4, 8])
    dump("CT", CT, B_CT, [64, NCH, 3, 8])
    dump("GLB", GLB, B_GLB, [128, NCH * 8])
    if stop_after == "dn0":
        P.emit()
        return nc, dbg

    HG = 2
    cur["big"] = BIG
    m_dn = A.mark()
    pre, B_pre = A.alloc("pre", [128, 3 + LT], F32)
    P.x("pool", "memset", pre[:, 0:3], 0.0, writes=[B_pre])
    acc, B_acc = A.alloc("acc", [128, LT], F32)
    MnSL, B_MnSL = A.alloc("MnSL", [128, 128], F32)
    MnCU, B_MnCU = A.alloc("MnCU", [64, 64], F32)
    P.x("dve", "tensor_scalar", out=MnSL, in0=SLm, scalar1=-1.0, scalar2=30000.0, op0=ALU.add, op1=ALU.mult,
        reads=[B_SL], writes=[B_MnSL])
    P.x("dve", "tensor_scalar", out=MnCU, in0=CUm, scalar1=-1.0, scalar2=30000.0, op0=ALU.add, op1=ALU.mult,
        reads=[B_CU], writes=[B_MnCU])
    MnCUp, B_MnCUp = A.alloc("MnCUp", [128, 128], F32)
    P.x("pool", "memset", MnCUp, 1.0, writes=[B_MnCUp])
    P.x("pool", "affine_select", out=MnCUp, in_=MnCUp, pattern=[[1, 128]], compare_op=ALU.is_ge,
        fill=0.0, base=0, channel_multiplier=-1, reads=[B_MnCUp], writes=[B_MnCUp])
    P.x("pool", "memset", MnCUp[0:64, 64:128], 0.0, reads=[B_MnCUp], writes=[B_MnCUp])
    P.x("dve", "tensor_scalar", out=MnCUp, in0=MnCUp, scalar1=-1.0, scalar2=30000.0, op0=ALU.add, op1=ALU.mult,
        reads=[B_MnCUp], writes=[B_MnCUp])
    sq, B_sq = pre[:, 3:3 + LT // 2].bitcast(BF16), B_pre
    RN = Ring([A.alloc(f"rn{i}", [128, 512], F32) for i in range(1)])

    def ring(name, n, shape, dt):
        return Ring([A.alloc(f"{name}{i}", shape, dt) for i in range(n)])

    slots = []
    for s_ in range(HG):
        d_ = {}
        for nm in ("qT", "kT", "vT"):
            d_[nm] = A.alloc(f"{nm}{s_}", [128, LT], BF16)
        d_["szd"] = A.alloc(f"szd{s_}", [128, LO], BF16)
        d_["S_f"] = A.alloc(f"S_f{s_}", [128, 128], F32)
        d_["S_b"] = A.alloc(f"S_b{s_}", [128, 128], BF16)
        d_["E"] = ring(f"E{s_}_", 3, [128, 128], F32)
        d_["A"] = ring(f"Am{s_}_", 23, [128, 128], BF16)
        d_["P"] = ring(f"Pm{s_}_", 10, [128, 128], BF16)
        d_["bv"] = ring(f"bv{s_}_", 3, [128, 128], BF16)
        d_["kbg"] = ring(f"kbg{s_}_", 3, [128, 128], BF16)
        d_["wT"] = ring(f"wT{s_}_", 3, [128, 128], BF16)
        d_["TT"] = ring(f"TT{s_}_", 3, [128, 128], BF16)
        d_["u"] = ring(f"u{s_}_", 3, [128, 128], F32)
        d_["kd"] = ring(f"kd{s_}_", 3, [128, 128], BF16)
        d_["ET"] = ring(f"ET{s_}_", 3, [128, 128], F32)
        d_["qk"] = ring(f"qk{s_}_", 3, [128, 128], BF16)
        d_["vn"] = ring(f"vn{s_}_", 2, [128, 128], BF16)
        d_["wTz"] = ring(f"wTz{s_}_", 3, [128, 128], BF16)
        for (wz_, B_wz_) in d_["wTz"].items:
            P.x("pool", "memset", wz_, 0.0, writes=[B_wz_])
        d_["o1"] = ring(f"o1{s_}_", 2, [64, 128], F32)
        d_["o"] = ring(f"o{s_}_", 2, [64, 128], F32)
        d_["on"] = ring(f"on{s_}_", 2, [64, 128], BF16)
        d_["st"] = ring(f"st{s_}_", 4, [64, 2], F32)
        d_["ojunk"] = A.alloc(f"ojunk{s_}", [64, 128], BF16)
        slots.append(d_)

    QSCALE = 128.0 ** -0.5

    def head_proj(h, sl_):
        qT, B_qT = sl_["qT"]
        kT, B_kT = sl_["kT"]
        vT, B_vT = sl_["vT"]
        szd, B_szd = sl_["szd"]
        for idx, (cbase, dst, B_dst) in enumerate(((T_Q, qT, B_qT), (T_K, kT, B_kT), (T_V, vT, B_vT))):
            slot, bslot = load_w(w_in_t[cbase + h])

            def cons_pre(b, ps, B_ps):
                P.x("act", "activation", out=pre[:, 3 + b * 512:3 + (b + 1) * 512], in_=ps, func=AF.Copy,
                    reads=[B_ps], writes=[B_pre])
            yield from proj_fm_g(slot, bslot, KT, h_rhs, h_bufs, range(NB), cons_pre, ring=BIG8, G=4)
            tile = idx * 8 + h
            P.x("dve", "tensor_scalar", out=acc, in0=pre[:, 0:LT], scalar1=convw[:, tile, 0:1], scalar2=None, op0=ALU.mult,
                reads=[B_pre, B_convw], writes=[B_acc])
            for j in range(1, 4):
                P.x("dve", "scalar_tensor_tensor", out=acc, in0=pre[:, j:j + LT], scalar=convw[:, tile, j:j + 1], in1=acc,
                    op0=ALU.mult, op1=ALU.add, reads=[B_pre, B_convw, B_acc], writes=[B_acc])
            yield "P"
            if idx == 2:
                P.x("act", "activation", out=vT, in_=acc, func=AF.Silu, reads=[B_acc], writes=[B_vT])
            else:
                P.x("act", "activation", out=acc, in_=acc, func=AF.Silu, reads=[B_acc], writes=[B_acc])
                P.x("act", "activation", out=sq, in_=acc, func=AF.Square, reads=[B_acc], writes=[B_sq])
                for b in range(NB):
                    sl = slice(b * 512, (b + 1) * 512)
                    ps, B_ps = BIG.next()
                    P.x("pe", "matmul", ps, lhsT=ones_b, rhs=sq[:, sl], start=True, stop=True,
                        reads=[B_onesb, B_sq], writes=[B_ps])
                    rn, B_rn = RN.next()
                    P.x("act", "activation", out=rn, in_=ps, func=AF.Sqrt, bias=epsc[:, 0:1], scale=1.0,
                        reads=[B_ps, B_eps], writes=[B_rn])
                    P.x("dve", "reciprocal", out=rn, in_=rn, reads=[B_rn], writes=[B_rn])
                    P.x("dve", "scalar_tensor_tensor", out=dst[:, sl], in0=acc[:, sl], scalar=(QSCALE if idx == 0 else 1.0),
                        in1=rn, op0=ALU.mult, op1=ALU.mult, reads=[B_acc, B_rn], writes=[B_dst])
        slot, bslot = load_w(w_in_t[T_ZD + h])

        def cons_zd(b, ps, B_ps):
            bo = b - (NB - NBO)
            P.x("act", "activation", out=szd[:, bo * 512:(bo + 1) * 512], in_=ps, func=AF.Silu, reads=[B_ps], writes=[B_szd])
        yield from proj_fm_g(slot, bslot, KT, h_rhs, h_bufs, range(NB - NBO, NB), cons_zd, ring=BIG8, G=4)

    def intra_gen(h, i, sl_):
        qT, B_qT = sl_["qT"]
        kT, B_kT = sl_["kT"]
        vT, B_vT = sl_["vT"]
        tok = slice(i * 128, (i + 1) * 128)
        psD, B_psD = SMALL.next()
        P.x("pe", "matmul", psD, lhsT=ident_f[32:40, 32 + h:33 + h].to_broadcast([8, 128]), rhs=GC[32:40, tok], start=True, stop=True,
            reads=[B_identf, B_GC], writes=[B_psD])
        E, B_E = sl_["E"].next()
        gcp = PT[:, i, 1, h:h + 1]
        P.x("dve", "scalar_tensor_tensor", out=E, in0=psD, scalar=gcp, in1=MnSL, op0=ALU.subtract, op1=ALU.subtract,
            reads=[B_psD, B_PT, B_MnSL], writes=[B_E])
        ETp, B_ETp = sl_["ET"].next()
        P.x("dve", "scalar_tensor_tensor", out=ETp, in0=psD, scalar=gcp, in1=MnCUp, op0=ALU.subtract, op1=ALU.add,
            reads=[B_psD, B_PT, B_MnCUp], writes=[B_ETp])
        P.x("act", "activation", out=E, in_=E, func=AF.Exp, scale=-1.0, reads=[B_E], writes=[B_E])
        P.x("act", "activation", out=ETp, in_=ETp, func=AF.Exp, reads=[B_ETp], writes=[B_ETp])
        yield
        pskk, B_pskk = SMALL.next()
        P.x("pe", "matmul", pskk, lhsT=kT[:, tok], rhs=kT[:, tok], start=True, stop=True, reads=[B_kT], writes=[B_pskk])
        Am, B_Am = sl_["A"].next()
        P.x("dve", "scalar_tensor_tensor", out=Am, in0=pskk, scalar=PT[:, i, 0, h:h + 1], in1=E, op0=ALU.mult, op1=ALU.mult,
            reads=[B_pskk, B_PT, B_E], writes=[B_Am])
        yield
        psB, B_psB = TRB.next()
        P.x("pe", "transpose", psB, Am, ident_b, reads=[B_Am, B_identb], writes=[B_psB])
        Bm, B_Bm = sl_["A"].next()
        P.x("act", "activation", out=Bm, in_=psB, func=AF.Copy, reads=[B_psB], writes=[B_Bm])
        P0, B_P0 = sl_["P"].next()
        P.x("pool", "tensor_tensor", out=P0, in0=ident_b, in1=Bm, op=ALU.subtract, reads=[B_identb, B_Bm], writes=[B_P0])
        yield
        psv, B_psv = TRB.next()
        P.x("pe", "transpose", psv, vT[:, tok], ident_b, reads=[B_vT, B_identb], writes=[B_psv])
        bv, B_bv = sl_["bv"].next()
        P.x("act", "activation", out=bv, in_=psv, func=AF.Copy, scale=PT[:, i, 0, h:h + 1], reads=[B_psv, B_PT], writes=[B_bv])
        psk, B_psk = TRB.next()
        P.x("pe", "transpose", psk, kT[:, tok], ident_b, reads=[B_kT, B_identb], writes=[B_psk])
        kbg, B_kbg = sl_["kbg"].next()
        P.x("dve", "tensor_scalar", out=kbg, in0=psk, scalar1=PT[:, i, 0, h:h + 1], scalar2=PT[:, i, 2, h:h + 1],
            op0=ALU.mult, op1=ALU.mult, reads=[B_psk, B_PT], writes=[B_kbg])
        kdp, B_kdp = sl_["kd"].next()
        P.x("dve", "tensor_scalar", out=kdp, in0=psk, scalar1=PT[:, i, 3, h:h + 1], scalar2=None, op0=ALU.mult,
            reads=[B_psk, B_PT], writes=[B_kdp])
        yield
        pskq, B_pskq = SMALL.next()
        P.x("pe", "matmul", pskq, lhsT=kT[:, tok], rhs=qT[:, tok], start=True, stop=True, reads=[B_kT, B_qT], writes=[B_pskq])
        qkp, B_qkp = sl_["qk"].next()
        P.x("dve", "tensor_tensor", out=qkp, in0=pskq, in1=ETp, op=ALU.mult, reads=[B_pskq, B_ETp], writes=[B_qkp])
        yield
        chunks = []
        for xh in range(2):
            chunks.append(dict(ch=2 * i + xh, xh=xh, R=slice(64 * xh, 64 * xh + 64),
                               ctok=slice(i * 128 + 64 * xh, i * 128 + 64 * xh + 64)))
        Ac, B_Ac, Bc, B_Bc, Pc, B_Pc = Am, B_Am, Bm, B_Bm, P0, B_P0
        for lvl in range(5):
            psA, B_psA = SMALL.next()
            P.x("pe", "matmul", psA, lhsT=Bc, rhs=Ac, start=True, stop=True, reads=[B_Bc, B_Ac], writes=[B_psA])
            A2, B_A2 = sl_["A"].next()
            P.x("dve", "tensor_copy", out=A2, in_=psA, reads=[B_psA], writes=[B_A2])
            if lvl < 4:
                psB2, B_psB2 = SMALL.next()
                P.x("pe", "matmul", psB2, lhsT=Ac, rhs=Bc, start=True, stop=True, reads=[B_Bc, B_Ac], writes=[B_psB2])
                B2, B_B2 = sl_["A"].next()
                P.x("act", "activation", out=B2, in_=psB2, func=AF.Copy, reads=[B_psB2], writes=[B_B2])
            else:
                B2, B_B2 = None, None
            yield
            psP, B_psP = SMALL.next()
            P.x("pe", "matmul", psP, lhsT=A2, rhs=Pc, start=True, stop=True, reads=[B_A2, B_Pc], writes=[B_psP])
            if lvl < 4:
                Pn, B_Pn = sl_["P"].next()
            else:
                Pn, B_Pn = sl_["TT"].next()
            P.x("dve", "tensor_tensor", out=Pn, in0=Pc, in1=psP, op=ALU.add, reads=[B_Pc, B_psP], writes=[B_Pn])
            Ac, B_Ac, Bc, B_Bc, Pc, B_Pc = A2, B_A2, B2, B_B2, Pn, B_Pn
            yield
        TT, B_TT = Pc, B_Pc
        psw, B_psw = SMALL.next()
        P.x("pe", "matmul", psw, lhsT=kbg, rhs=TT, start=True, stop=True, reads=[B_kbg, B_TT], writes=[B_psw])
        wT, B_wT = sl_["wT"].next()
        P.x("act", "activation", out=wT, in_=psw, func=AF.Copy, reads=[B_psw], writes=[B_wT])
        wTz, B_wTz = sl_["wTz"].next()
        P.x("act", "activation", out=wTz[:, 64:128], in_=psw[:, 64:128], func=AF.Copy, reads=[B_psw, B_wTz], writes=[B_wTz])
        yield
        psu, B_psu = SMALL.next()
        P.x("pe", "matmul", psu, lhsT=TT, rhs=bv, start=True, stop=True, reads=[B_TT, B_bv], writes=[B_psu])
        u_sb, B_u = sl_["u"].next()
        P.x("act", "activation", out=u_sb, in_=psu, func=AF.Copy, reads=[B_psu], writes=[B_u])
        yield
        return dict(wT=wT, B_wT=B_wT, wTz=wTz, B_wTz=B_wTz, u=u_sb, B_u=B_u, kd=kdp, B_kd=B_kdp, qk=qkp, B_qk=B_qkp, chunks=chunks)

    def recur_gen(h, i, r, sl_):
        qT, B_qT = sl_["qT"]
        szd, B_szd = sl_["szd"]
        S_f, B_Sf = sl_["S_f"]
        S_b, B_Sb = sl_["S_b"]
        ojunk, B_ojunk = sl_["ojunk"]
        vn, B_vn = sl_["vn"].next()
        for c in r["chunks"]:
            ch, R, ctok, xh = c["ch"], c["R"], c["ctok"], c["xh"]
            own = ctok.start >= T0
            psws, B_psws = SMALL.next()
            if xh == 0:
                P.x("pe", "matmul", psws[0:64, :], lhsT=r["wT"][:, 0:64], rhs=S_b, start=True, stop=True,
                    reads=[r["B_wT"], B_Sb], writes=[B_psws])
            else:
                P.x("pe", "matmul", psws, lhsT=r["wTz"], rhs=S_b, start=True, stop=True,
                    reads=[r["B_wTz"], B_Sb], writes=[B_psws])
            P.x("dve", "tensor_tensor", out=vn[R, :], in0=r["u"][R, :], in1=psws[R, :], op=ALU.subtract,
                reads=[r["B_u"], B_psws, B_vn], writes=[B_vn])
            if own:
                pso1, B_pso1 = SMALL.next()
                P.x("pe", "matmul", pso1[0:64, :], lhsT=qT[:, ctok], rhs=S_b, start=True, stop=True,
                    reads=[B_qT, B_Sb], writes=[B_pso1])
                o1, B_o1 = sl_["o1"].next()
                P.x("act", "activation", out=o1, in_=pso1[0:64, :], func=AF.Copy, scale=CT[:, ch, 1, h:h + 1],
                    reads=[B_pso1, B_CT], writes=[B_o1])
            yield
            psdS, B_psdS = SMALL.next()
            P.x("pe", "matmul", psdS, lhsT=r["kd"][R, :], rhs=vn[R, :], start=True, stop=True, reads=[r["B_kd"], B_vn], writes=[B_psdS])
            gl = GLB[:, ch * 8 + h:ch * 8 + h + 1]
            P.x("dve", "scalar_tensor_tensor", out=S_b, in0=S_f, scalar=gl, in1=psdS,
                op0=ALU.mult, op1=ALU.add, reads=[B_Sf, B_GLB, B_psdS, B_Sb], writes=[B_Sb])
            P.x("dve", "scalar_tensor_tensor", out=S_f, in0=S_f, scalar=gl, in1=psdS,
                op0=ALU.mult, op1=ALU.add, reads=[B_Sf, B_GLB, B_psdS], writes=[B_Sf])
            yield
            if own:
                pso2, B_pso2 = SMALL.next()
                P.x("pe", "matmul", pso2[0:64, :], lhsT=r["qk"][R, R], rhs=vn[R, :], start=True, stop=True,
                    reads=[r["B_qk"], B_vn], writes=[B_pso2])
                o, B_o = sl_["o"].next()
                P.x("dve", "tensor_tensor", out=o, in0=o1, in1=pso2[0:64, :], op=ALU.add, reads=[B_o1, B_pso2], writes=[B_o])
                st_, B_st = sl_["st"].next()
                P.x("act", "activation", out=ojunk, in_=o, func=AF.Square, accum_out=st_[:, 0:1],
                    reads=[B_o], writes=[B_ojunk, B_st])
                P.x("act", "activation", out=st_[:, 1:2], in_=st_[:, 0:1], func=AF.Sqrt, bias=epsc[0:64, 0:1], scale=1.0 / 128,
                    reads=[B_st, B_eps], writes=[B_st])
                P.x("dve", "reciprocal", out=st_[:, 1:2], in_=st_[:, 1:2], reads=[B_st], writes=[B_st])
                on, B_on = sl_["on"].next()
                P.x("act", "activation", out=on, in_=o, func=AF.Copy, scale=st_[:, 1:2], reads=[B_o, B_st], writes=[B_on])
                yield
                psoT, B_psoT = TRB.next()
                P.x("pe", "transpose", psoT[:, 0:64], on, ident_b[0:64, 0:64], reads=[B_on, B_identb], writes=[B_psoT])
                t0o = ctok.start - T0
                P.x("dve", "scalar_tensor_tensor", out=dnoT[:, h, t0o:t0o + 64], in0=psoT[:, 0:64], scalar=dnnw[:, 0:1],
                    in1=szd[:, t0o:t0o + 64], op0=ALU.mult, op1=ALU.mult,
                    reads=[B_psoT, B_dnnw, B_szd], writes=[B_dno[h]])
                yield

    def interleave(g1, g2):
        res = None
        act = [g for g in (g1, g2) if g is not None]
        while act:
            for g in list(act):
                try:
                    next(g)
                except StopIteration as e:
                    if g is g1:
                        res = e.value
                    act.remove(g)
            yield
        return res

    def head_gen(h, sl_):
        S_f, B_Sf = sl_["S_f"]
        S_b, B_Sb = sl_["S_b"]
        P.x("pool", "memset", S_f, 0.0, writes=[B_Sf])
        P.x("pool", "memset", S_b, 0.0, writes=[B_Sb])
        results = {}
        intras = {}
        next_intra = 0
        next_recur = 0
        recur_g = None
        recur_done = 0
        while recur_done < NP:
            while len(intras) < 2 and next_intra < NP and next_intra <= recur_done + 2:
                intras[next_intra] = intra_gen(h, next_intra, sl_)
                next_intra += 1
            if recur_g is None and next_recur in results:
                recur_g = recur_gen(h, next_recur, results.pop(next_recur), sl_)
                next_recur += 1
            for j in list(intras):
                try:
                    next(intras[j])
                except StopIteration as e:
                    results[j] = e.value
                    del intras[j]
            if recur_g is not None:
                try:
                    next(recur_g)
                except StopIteration:
                    recur_g = None
                    recur_done += 1
            yield

    for hg in range(0, nheads, HG):
        hs = list(range(hg, min(hg + HG, nheads)))
        for k, h in enumerate(hs):
            for _ in head_proj(h, slots[k]):
                pass
        if hg == 0 and hs:
            dump("qT", slots[len(hs) - 1]["qT"][0], slots[len(hs) - 1]["qT"][1], [128, LT], BF16)
            dump("kT", slots[len(hs) - 1]["kT"][0], slots[len(hs) - 1]["kT"][1], [128, LT], BF16)
            dump("vT", slots[len(hs) - 1]["vT"][0], slots[len(hs) - 1]["vT"][1], [128, LT], BF16)
        P.mark(f"dn_g{hg}_proj")
        gens = [head_gen(h, slots[k]) for k, h in enumerate(hs)]
        cur["small"] = BIG8
        while gens:
            for g in list(gens):
                try:
                    next(g)
                except StopIteration:
                    gens.remove(g)
        cur["small"] = _SM
    A.release(m_dn0)
    H.release(mH)
    cur["big"] = BIG8
    P.mark("dn")
    dump("dnoT", dnoT[:, 0:nheads], B_dno, [128, nheads, LO], BF16)
    if stop_after == "dn":
        P.emit()
        return nc, dbg

    drain_tables()
    m_s5w = A.mark()
    Gpad, B_Gpad = _tres["Gpad"]
    Mintra, B_Mintra = _tres["Mintra"]
    Minter, B_Minter = _tres["Minter"]
    A8, B_A8 = _tres["A8"]
    A8P, B_A8P = _tres["A8P"]
    A64, B_A64 = _tres["A64"]
    Sst, B_Sst = A.alloc("Sst", [128, 2, 32], F32)
    y2T, B_y2T = H.alloc("y2T", [128, 8, LO], BF16, nbufs=8)
    P.x("pool", "memset", Sst, 0.0, writes=[B_Sst])
    P.mark("s5tab")
    m_blk = A.mark()
    UC = Ring([A.alloc(f"ucol{i}", [128, 64, 64], BF16) for i in range(1)])
    YC = Ring([A.alloc(f"ycol{i}", [128, 64, 64], BF16) for i in range(1)])
    Lt, B_Lt = A.alloc("Lt", [128, 2, 32, 64], F32)
    B_Lre, B_Lim = Buf("Lre"), Buf("Lim")
    ct1, B_ct1 = A.alloc("ct1", [128, 32, 4, 8], F32)
    ct2, B_ct2 = A.alloc("ct2", [128, 32, 4, 8], F32)
    mH2 = H.mark()
    hist, B_hist = H.alloc("hist", [128, 2, 32, 64], BF16)
    sc1, B_sc1 = H.alloc("sc1", [128, 2, 32], F32)
    sc2, B_sc2 = H.alloc("sc2", [128, 2, 32], F32)
    xm1, B_xm1 = H.alloc("xm1", [128, 2, 32, 8], F32)
    xm2, B_xm2 = H.alloc("xm2", [128, 2, 32, 8], F32)
    Cs, B_Cs = H.alloc("Cs", [128, 2, 32, 9], F32)
    Uv = U_scr.rearrange("(g ci) t c -> t ci g c", ci=16)
    Yv = Y_scr.rearrange("(g co) t c -> t co g c", co=16)
    B_Y = Buf("Yscr")
    AR2 = A8[:, 0]
    AI2 = A8[:, 1]
    for b in range(NB):
        own = b >= NB - NBO
        bo = b - (NB - NBO)
        ucol, B_uc = UC.next()
        for tau in range(8):
            P.d("sp", ucol[16 * tau:16 * tau + 16, :, :], Uv[tau][:, :, b * 64:(b + 1) * 64], reads=B_U, writes=[B_uc])
        for q4 in range(8):
            ps, B_ps = _SM.next()
            for k in range(4):
                gp = q4 * 4 + k
                for ri in range(2):
                    o_ = ps[:, (k * 2 + ri) * 64:(k * 2 + ri + 1) * 64]
                    P.x("pe", "matmul", o_, lhsT=Gpad[:, gp, ri, :], rhs=ucol[:, gp, :], start=True, stop=False,
                        reads=[B_Gpad, B_uc], writes=[B_ps])
                    P.x("pe", "matmul", o_, lhsT=Gpad[:, 32 + gp, ri, :], rhs=ucol[:, 32 + gp, :], start=False, stop=True,
                        reads=[B_Gpad, B_uc], writes=[B_ps])
            o_l = Lt[:, :, q4 * 4:q4 * 4 + 4, :].rearrange("p r g c -> p g r c")
            i_l = ps.rearrange("p (g r c) -> p g r c", g=4, r=2)
            if q4 % 2:
                P.x("act", "activation", out=o_l, in_=i_l, func=AF.Copy, reads=[B_ps, B_Lt, B_Lre, B_Lim], writes=[B_Lt, B_Lre, B_Lim])
            else:
                P.x("dve", "tensor_copy", out=o_l, in_=i_l, reads=[B_ps, B_Lt, B_Lre, B_Lim], writes=[B_Lt, B_Lre, B_Lim])
        L5 = Lt.rearrange("p r g (s k) -> p r g s k", k=8)
        LB = [B_Lt, B_Lre, B_Lim]
        ARb = A8[:, 0].unsqueeze(3).to_broadcast([128, 2, 32, 8])
        AIb = A8[:, 1].unsqueeze(3).to_broadcast([128, 2, 32, 8])
        for k in range(1, 8):
            xp = L5[:, :, :, :, k - 1]
            P.x("dve", "tensor_tensor", out=xm1, in0=ARb, in1=xp, op=ALU.mult, reads=[B_A8] + LB, writes=[B_xm1])
            P.x("dve", "tensor_tensor", out=xm2, in0=AIb, in1=xp, op=ALU.mult, reads=[B_A8] + LB, writes=[B_xm2])
            P.x("dve", "tensor_tensor", out=xm1, in0=xm1, in1=L5[:, :, :, :, k], op=ALU.add, reads=[B_xm1] + LB, writes=[B_xm1])
            P.x("dve", "tensor_tensor", out=L5[:, 0, :, :, k], in0=xm1[:, 0], in1=xm2[:, 1], op=ALU.subtract,
                reads=[B_xm1, B_xm2] + LB, writes=LB)
            P.x("dve", "tensor_tensor", out=L5[:, 1, :, :, k], in0=xm1[:, 1], in1=xm2[:, 0], op=ALU.add,
                reads=[B_xm1, B_xm2] + LB, writes=LB)
        P.x("dve", "tensor_copy", out=Cs[:, :, :, 0], in_=Sst, reads=[B_Sst, B_Cs], writes=[B_Cs])
        for sg_ in range(8):
            cp = Cs[:, :, :, sg_]
            P.x("dve", "tensor_tensor", out=sc1, in0=A64[:, 0], in1=cp, op=ALU.mult, reads=[B_A64, B_Cs], writes=[B_sc1])
            P.x("dve", "tensor_tensor", out=sc2, in0=A64[:, 1], in1=cp, op=ALU.mult, reads=[B_A64, B_Cs], writes=[B_sc2])
            P.x("dve", "tensor_tensor", out=sc1, in0=sc1, in1=L5[:, :, :, sg_, 7], op=ALU.add, reads=[B_sc1] + LB, writes=[B_sc1])
            P.x("dve", "tensor_tensor", out=Cs[:, 0, :, sg_ + 1], in0=sc1[:, 0, :], in1=sc2[:, 1, :], op=ALU.subtract,
                reads=[B_sc1, B_sc2, B_Cs], writes=[B_Cs])
            P.x("dve", "tensor_tensor", out=Cs[:, 1, :, sg_ + 1], in0=sc1[:, 1, :], in1=sc2[:, 0, :], op=ALU.add,
                reads=[B_sc1, B_sc2, B_Cs], writes=[B_Cs])
        if own:
            sh4 = [128, 32, 4, 8]
            Wr = A8P[:, :, 0, :].rearrange("p k g -> p g k").unsqueeze(2).to_broadcast(sh4)
            Wi = A8P[:, :, 1, :].rearrange("p k g -> p g k").unsqueeze(2).to_broadcast(sh4)
            for s0 in (0, 4):
                Cr = Cs[:, 0, :, s0:s0 + 4].unsqueeze(3).to_broadcast(sh4)
                Ci = Cs[:, 1, :, s0:s0 + 4].unsqueeze(3).to_broadcast(sh4)
                Lre, Lim = L5[:, 0, :, s0:s0 + 4, :], L5[:, 1, :, s0:s0 + 4, :]
                P.x("dve", "tensor_tensor", out=ct1, in0=Wr, in1=Cr, op=ALU.mult, reads=[B_A8P, B_Cs, B_ct1], writes=[B_ct1])
                P.x("dve", "tensor_tensor", out=Lre, in0=Lre, in1=ct1, op=ALU.add, reads=[B_ct1, B_Lre, B_Lt], writes=[B_Lre])
                P.x("dve", "tensor_tensor", out=ct1, in0=Wi, in1=Ci, op=ALU.mult, reads=[B_A8P, B_Cs, B_ct1], writes=[B_ct1])
                P.x("dve", "tensor_tensor", out=Lre, in0=Lre, in1=ct1, op=ALU.subtract, reads=[B_ct1, B_Lre], writes=[B_Lre])
                P.x("pool", "tensor_tensor", out=ct2, in0=Wr, in1=Ci, op=ALU.mult, reads=[B_A8P, B_Cs, B_ct2], writes=[B_ct2])
                P.x("pool", "tensor_tensor", out=Lim, in0=Lim, in1=ct2, op=ALU.add, reads=[B_ct2, B_Lim, B_Lt], writes=[B_Lim])
                P.x("pool", "tensor_tensor", out=ct2, in0=Wi, in1=Cr, op=ALU.mult, reads=[B_A8P, B_Cs, B_ct2], writes=[B_ct2])
                P.x("pool", "tensor_tensor", out=Lim, in0=Lim, in1=ct2, op=ALU.add, reads=[B_ct2, B_Lim], writes=[B_Lim])
            P.x("act", "activation", out=hist[:, :, :, 1:64], in_=Lt[:, :, :, 0:63], func=AF.Copy,
                reads=[B_Lre, B_Lim, B_Lt, B_hist], writes=[B_hist])
            P.x("pool", "tensor_copy", out=hist[:, :, :, 0], in_=Sst, reads=[B_Sst, B_hist], writes=[B_hist])
        P.x("dve", "tensor_copy", out=Sst, in_=Cs[:, :, :, 8], reads=[B_Cs, B_Sst, B_hist], writes=[B_Sst])
        if not own:
            continue
        ycol, B_yc = YC.next()
        for q8 in range(8):
            ps, B_ps = _SM.next()
            for k in range(8):
                g = q8 * 8 + k
                half, gp = g // 32, g % 32
                rows = slice(half * 64, half * 64 + 64)
                o_ = ps[:, k * 64:(k + 1) * 64]
                P.x("pe", "matmul", o_, lhsT=Mintra[:, g, :], rhs=ucol[:, g, :], start=True, stop=False,
                    reads=[B_Mintra, B_uc], writes=[B_ps])
                P.x("pe", "matmul", o_, lhsT=Minter[rows, gp, 0, :], rhs=hist[rows, 0, gp, :], start=False, stop=False,
                    reads=[B_Minter, B_hist], writes=[B_ps])
                P.x("pe", "matmul", o_, lhsT=Minter[rows, gp, 1, :], rhs=hist[rows, 1, gp, :], start=False, stop=True,
                    reads=[B_Minter, B_hist], writes=[B_ps])
            P.x("act", "activation", out=ycol[:, q8 * 8:q8 * 8 + 8, :], in_=ps.rearrange("p (g c) -> p g c", g=8),
                func=AF.Gelu_apprx_tanh, reads=[B_ps, B_yc], writes=[B_yc])
        for t in range(8):
            P.d("sp", Yv[t][:, :, bo * 64:(bo + 1) * 64], ycol[16 * t:16 * t + 16, :, :], reads=[B_yc], writes=[B_Y])
    A.release(m_s5w)
    A.release_top()
    H.release(mH2)
    for j in range(8):
        P.d("sp", y2T[:, j, :], Y_scr[j * 128:(j + 1) * 128].rearrange("p t c -> p (t c)"), reads=[B_Y], writes=[B_y2T[j]])
    P.mark("s5blk")
    dump("y2T", y2T, B_y2T, [128, 8, LO], BF16)
    if stop_after == "s5":
        P.emit()
        return nc, dbg

    y4T, B_y4 = A.alloc("y4T", [128, 8, LO], BF16, nbufs=8)
    m_glu = A.mark()
    y3s = [A.alloc(f"y3_{i}", [128, LO], BF16) for i in range(2)]
    SG = Ring([A.alloc(f"sg{i}", [128, 512], BF16) for i in range(2)])
    SZ = Ring([A.alloc(f"sz{i}", [128, 512], BF16) for i in range(2)])

    def y2_rhs(kt, pb):
        return y2T[:, kt, pb * 512:(pb + 1) * 512]

    def y2_bufs(pb):
        return list(B_y2T)

    for j in range(8):
        y3, B_y3 = y3s[j % 2]
        slot, bslot = load_w(wglu_t[j], kt=8)

        def cons_glu(pb, ps, B_ps, j=j, y3=y3, B_y3=B_y3):
            sg, B_sg = SG.next()
            P.x("act", "activation", out=sg, in_=ps, func=AF.Sigmoid, reads=[B_ps], writes=[B_sg])
            P.x("dve", "tensor_tensor", out=y3[:, pb * 512:(pb + 1) * 512], in0=y2T[:, j, pb * 512:(pb + 1) * 512], in1=sg,
                op=ALU.mult, reads=[B_y2T[j], B_sg], writes=[B_y3])
        proj_fm(slot, bslot, 8, y2_rhs, y2_bufs, range(NBO), cons_glu)
        slot2, bslot2 = load_w(w_in_t[T_ZS + j])

        def cons_zs(b, ps, B_ps, j=j, y3=y3, B_y3=B_y3):
            bo = b - (NB - NBO)
            sz, B_sz = SZ.next()
            P.x("act", "activation", out=sz, in_=ps, func=AF.Silu, reads=[B_ps], writes=[B_sz])
            P.x("dve", "tensor_tensor", out=y4T[:, j, bo * 512:(bo + 1) * 512].rearrange("p (c t) -> p c t", t=8),
                in0=y3.rearrange("p (t c) -> p c t", t=8)[:, bo * 64:(bo + 1) * 64, :],
                in1=sz.rearrange("p (c t) -> p c t", t=8), op=ALU.mult,
                reads=[B_y3, B_sz], writes=[B_y4[j]])
        proj_fm(slot2, bslot2, KT, h_rhs, h_bufs, range(NB - NBO, NB), cons_zs)
    A.release(m_glu)
    P.mark("glu")
    dump("y4T", y4T, B_y4, [128, 8, LO], BF16)
    if stop_after == "glu":
        P.emit()
        return nc, dbg

    mixT, B_mix = A.alloc("mixT", [128, KT, LO], BF16, nbufs=NTO)
    m_mix = A.mark()
    SGS = Ring([A.alloc(f"sgs{i}", [128, 512], BF16) for i in range(4)])
    SGD = Ring([A.alloc(f"sgd{i}", [128, 512], BF16) for i in range(4)])
    M1 = Ring([A.alloc(f"m1_{i}", [128, 512], F32) for i in range(4)])
    M2 = Ring([A.alloc(f"m2_{i}", [128, 512], F32) for i in range(3)])

    def y4_rhs(kt, bo):
        return y4T[:, kt, bo * 512:(bo + 1) * 512]

    def dn_rhs(kt, bo):
        return dnoT[:, kt, bo * 512:(bo + 1) * 512]

    for j in range(KT):
        s_gs = load_w(w_in_t[T_GS + j])
        s_gd = load_w(w_in_t[T_GD + j])
        s_us = load_w(wups_t[j], kt=8)
        st_ = {bo: {} for bo in range(NBO)}

        def cons_gs(b_, ps, B_ps, st_=st_):
            sg, B_sg = SGS.next()
            P.x("act", "activation", out=sg, in_=ps, func=AF.Sigmoid, reads=[B_ps], writes=[B_sg])
            st_[b_ - (NB - NBO)]["gs"] = (sg, B_sg)
        proj_fm(s_gs[0], s_gs[1], KT, h_rhs, h_bufs, range(NB - NBO, NB), cons_gs)
        s_ud = load_w(wupd_t[j], kt=8)

        def cons_gd(b_, ps, B_ps, st_=st_):
            sg, B_sg = SGD.next()
            P.x("act", "activation", out=sg, in_=ps, func=AF.Sigmoid, reads=[B_ps], writes=[B_sg])
            st_[b_ - (NB - NBO)]["gd"] = (sg, B_sg)
        proj_fm(s_gd[0], s_gd[1], KT, h_rhs, h_bufs, range(NB - NBO, NB), cons_gd)

        def cons_us(bo_, ps, B_ps, st_=st_):
            m1_, B_m1 = M1.next()
            sg, B_sg = st_[bo_]["gs"]
            P.x("dve", "tensor_tensor", out=m1_, in0=ps, in1=sg, op=ALU.mult, reads=[B_ps, B_sg], writes=[B_m1])
            st_[bo_]["m1"] = (m1_, B_m1)
        proj_fm(s_us[0], s_us[1], 8, y4_rhs, lambda bo_: list(B_y4), range(NBO), cons_us)

        def cons_ud(bo_, ps, B_ps, j=j, st_=st_):
            m2_, B_m2 = M2.next()
            sg, B_sg = st_[bo_]["gd"]
            P.x("dve", "tensor_tensor", out=m2_, in0=ps, in1=sg, op=ALU.mult, reads=[B_ps, B_sg], writes=[B_m2])
            m1_, B_m1 = st_[bo_]["m1"]
            P.x("pool", "tensor_tensor", out=mixT[:, j, bo_ * 512:(bo_ + 1) * 512], in0=m1_, in1=m2_, op=ALU.add,
                reads=[B_m1, B_m2], writes=B_mix[bo_ * 4:(bo_ + 1) * 4])
        proj_fm(s_ud[0], s_ud[1], 8, dn_rhs, lambda bo_: list(B_dno), range(NBO), cons_ud)
    A.release(m_mix)
    H.release(0)
    P.mark("mix")

    rT, B_r = H.alloc("rT", [128, NTO, D], F32, nbufs=NTO)
    fnw, B_fnw = A.alloc("fnw", [128, D], F32)
    P.d("sp", fnw, fnw_in.to_broadcast([128, D]),
        writes=[B_fnw])
    WO = Ring([A.alloc(f"wo{i}", [128, KT, 512], BF16) for i in range(2)])
    XO = Ring([A.alloc(f"xo{i}", [128, 512], F32) for i in range(3)])
    st2, B_st2 = A.alloc("st2", [128, NTO, 2], F32, nbufs=NTO)
    fjunk, B_fjunk = A.alloc("fjunk", [128, D], BF16)
    for cb in range(4):
        wo, B_wo = WO.next()
        P.d("pool", wo, wout_t[cb], writes=[B_wo])
        for tt_ in range(NTO):
            xo, B_xo = XO.next()
            P.d("sp", xo, x_in[T0 + tt_ * 128:T0 + (tt_ + 1) * 128, cb * 512:(cb + 1) * 512], writes=[B_xo])
            ps, B_ps = cur["big"].next()
            for kt in range(KT):
                P.x("pe", "matmul", ps, lhsT=mixT[:, kt, tt_ * 128:(tt_ + 1) * 128], rhs=wo[:, kt, :],
                    start=(kt == 0), stop=(kt == KT - 1), reads=[B_mix[tt_], B_wo], writes=[B_ps])
            P.x("dve", "tensor_tensor", out=rT[:, tt_, cb * 512:(cb + 1) * 512], in0=ps, in1=xo, op=ALU.add,
                reads=[B_ps, B_xo], writes=[B_r[tt_]])
    for tt_ in range(NTO):
        P.x("act", "activation", out=fjunk, in_=rT[:, tt_, :], func=AF.Square, accum_out=st2[:, tt_, 0:1],
            reads=[B_r[tt_]], writes=[B_fjunk, B_st2[tt_]])
        P.x("act", "activation", out=st2[:, tt_, 1:2], in_=st2[:, tt_, 0:1], func=AF.Sqrt, bias=epsc[:, 0:1], scale=1.0 / D,
            reads=[B_st2[tt_], B_eps], writes=[B_st2[tt_]])
        P.x("dve", "reciprocal", out=st2[:, tt_, 1:2], in_=st2[:, tt_, 1:2], reads=[B_st2[tt_]], writes=[B_st2[tt_]])
        P.x("dve", "scalar_tensor_tensor", out=rT[:, tt_, :], in0=rT[:, tt_, :], scalar=st2[:, tt_, 1:2], in1=fnw,
            op0=ALU.mult, op1=ALU.mult, reads=[B_r[tt_], B_st2[tt_], B_fnw], writes=[B_r[tt_]])
        P.d("sp", out_ap[tt_ * 128:(tt_ + 1) * 128, :], rT[:, tt_, :], reads=[B_r[tt_]], is_output=True)
    P.mark("out")
    P.emit()
    build.last_marks = P.marks
    return nc, dbg


_NC_CACHE = {}


def kernel(**inputs):
    x = np.asarray(inputs["x"], dtype=np.float32)
    Bsz, L, Dm = x.shape
    LO = L // 2
    key = (L,)
    if key not in _NC_CACHE:
        _NC_CACHE[key] = build(LT=L, LO=LO)[0]
    nc = _NC_CACHE[key]
    in_maps = []
    wmap = make_in_map(inputs, np.zeros((1, 1), np.float32))
    wmap.pop("x")
    for c in range(8):
        b, p = c // 2, c % 2
        if p == 0:
            xloc = np.concatenate([np.zeros((L - LO, Dm), np.float32), x[b, :LO]], axis=0)
        else:
            xloc = x[b]
        in_maps.append(make_in_map(inputs, xloc, wmap))
    res = run_bass_kernel_spmd(nc, in_maps, core_ids=list(range(8)))
    out = np.empty((Bsz, L, Dm), np.float32)
    for c in range(8):
        b, p = c // 2, c % 2
        out[b, p * LO:(p + 1) * LO] = np.asarray(res.results[c]["out"], dtype=np.float32)
    return out


def make_in_map(inp, xloc, wmap=None):
    if wmap is not None:
        m = dict(wmap)
        m["x"] = np.ascontiguousarray(xloc, dtype=np.float32)
        return m
    f = np.float32
    g = lambda k: np.asarray(inp[k], dtype=f)
    m = {}
    m["x"] = np.ascontiguousarray(xloc, dtype=f)
    w_in = g("w_in")[0]

    def tile_of(w, col0, ncols=128, kt=16):
        return w[:, col0:col0 + ncols].reshape(kt, 128, ncols).transpose(1, 0, 2)
    col0s = ([0 + 128 * j for j in range(8)] + [1024 + 128 * j for j in range(8)] + [2048 + 128 * h for h in range(8)]
             + [3072 + 128 * h for h in range(8)] + [4096 + 128 * h for h in range(8)] + [5120 + 128 * h for h in range(8)]
             + [None] + [6160 + 128 * j for j in range(16)] + [8208 + 128 * j for j in range(16)])
    wt = np.zeros((81, 128, 16, 128), f)
    for ti, c0 in enumerate(col0s):
        if c0 is None:
            wt[ti, :, :, 0:8] = tile_of(w_in, 6144, 8)
            for r0 in (32, 64, 96):
                wt[ti, :, :, r0:r0 + 8] = tile_of(w_in, 6152, 8)
        else:
            wt[ti] = tile_of(w_in, c0)
    m["w_in_t"] = wt
    m["lnw"] = np.ascontiguousarray(g("ln_w")[0].reshape(16, 128).T)
    m["fnw"] = np.ascontiguousarray(g("final_norm_w")[None, :])
    m["convw"] = np.ascontiguousarray(g("dn_conv_w")[0].reshape(4, 24, 128).transpose(2, 1, 0))
    dnp = np.zeros((128, 2), f)
    for r0 in (32, 64, 96):
        dnp[r0:r0 + 8, 0] = g("dn_a_log")[0]
        dnp[r0:r0 + 8, 1] = g("dn_dt_bias")[0]
    m["dnp"] = dnp
    m["dnnw"] = np.ascontiguousarray(g("dn_norm_w")[0][:, None])
    lam = np.zeros((128, 3, 32), f)
    lre, lim, lst = g("s5_lam_re")[0], g("s5_lam_im")[0], g("s5_log_step")[0]
    for half in range(2):
        gs = slice(half * 32, half * 32 + 32)
        lam[half * 64:(half + 1) * 64, 0, :] = lre[gs].T
        lam[half * 64:(half + 1) * 64, 1, :] = lim[gs].T
        lam[half * 64:(half + 1) * 64, 2, :] = np.broadcast_to(lst[gs][None, :], (64, 32))
    m["lam"] = lam
    sb = np.zeros((128, 2, 32, 16), f)
    sc = np.zeros((128, 2, 32, 16), f)
    for half in range(2):
        gs = slice(half * 32, half * 32 + 32)
        ps = slice(half * 64, half * 64 + 64)
        sb[ps, 0] = g("s5_b_re")[0][gs].transpose(1, 0, 2)
        sb[ps, 1] = g("s5_b_im")[0][gs].transpose(1, 0, 2)
        sc[ps, 0] = g("s5_c_re")[0][gs].transpose(2, 0, 1)
        sc[ps, 1] = g("s5_c_im")[0][gs].transpose(2, 0, 1)
    m["s5b"] = sb
    m["s5c"] = sc
    d = g("s5_d")[0].reshape(64, 16)
    m["s5d"] = np.ascontiguousarray(np.tile(d.T, (8, 1)))
    m["w_glu_t"] = np.ascontiguousarray(np.stack([tile_of(g("s5_w_glu")[0], 128 * j, 128, 8) for j in range(8)]))
    m["w_ups_t"] = np.ascontiguousarray(np.stack([tile_of(g("s5_w_up")[0], 128 * j, 128, 8) for j in range(16)]))
    m["w_upd_t"] = np.ascontiguousarray(np.stack([tile_of(g("dn_w_up")[0], 128 * j, 128, 8) for j in range(16)]))
    m["w_out_t"] = np.ascontiguousarray(np.stack([tile_of(g("w_out")[0], 512 * cb, 512, 16) for cb in range(4)]))
    return m
```

```python
import contextlib
import numpy as np
import concourse.bass as bass
import concourse.mybir as mybir
from concourse.bass_utils import run_bass_kernel_spmd

F32 = mybir.dt.float32
BF16 = mybir.dt.bfloat16
AF = mybir.ActivationFunctionType
ALU = mybir.AluOpType

ENGS = ("pe", "act", "dve", "pool", "sp")
N_DMA_SEMS = 32


class Buf:
    __slots__ = ("name", "writers", "readers")

    def __init__(self, name):
        self.name = name
        self.writers = []
        self.readers = []


class Ins:
    __slots__ = ("eng", "fn", "deps", "is_dma", "sem", "val", "tag")

    def __init__(self, eng, fn, is_dma, tag=None):
        self.eng = eng
        self.fn = fn
        self.deps = []
        self.is_dma = is_dma
        self.sem = None
        self.val = None
        self.tag = tag


def _mk(method, *args, **kw):
    def fn(e):
        return getattr(e, method)(*args, **kw)
    return fn


class Prog:
    def __init__(self, nc):
        self.nc = nc
        self.q = {e: [] for e in ENGS}
        self.cnt = {e: 0 for e in ENGS}
        self.dma_rr = 0
        self.dma_rr_pool = 0
        self.dma_cnt = [0] * N_DMA_SEMS
        self.dma_last = [None] * N_DMA_SEMS
        self.out_dmas = []

    def mark(self, name):
        if not hasattr(self, 'marks'):
            self.marks = []
        self.marks.append((name, dict(self.cnt)))

    def _add(self, ins, reads, writes):
        deps = []
        for b in reads:
            deps.extend(b.writers)
        for b in writes:
            deps.extend(b.writers)
            deps.extend(b.readers)
        seen = set()
        for d in deps:
            if d is ins or id(d) in seen:
                continue
            seen.add(id(d))
            if d.eng == "pe" and ins.eng == "pe" and not d.is_dma and not ins.is_dma:
                continue
            ins.deps.append(d)
        for b in reads:
            b.readers.append(ins)
        for b in writes:
            b.writers = [ins]
            b.readers = []
        self.q[ins.eng].append(ins)
        return ins

    def x(self, eng, method, *args, reads=(), writes=(), **kw):
        ins = Ins(eng, _mk(method, *args, **kw), False, method)
        self.cnt[eng] += 1
        ins.val = self.cnt[eng]
        return self._add(ins, list(reads), list(writes))

    def d(self, eng, out, in_, reads=(), writes=(), is_output=False, **kw):
        ins = Ins(eng, _mk("dma_start", out=out, in_=in_, **kw), True, "dma")
        half = N_DMA_SEMS // 2
        if eng == "pool":
            k = half + (self.dma_rr_pool % half)
            self.dma_rr_pool += 1
        else:
            k = self.dma_rr % half
            self.dma_rr += 1
        ins.sem = k
        self.dma_cnt[k] += 16
        ins.val = self.dma_cnt[k]
        prev = self.dma_last[k]
        self.dma_last[k] = ins
        self._add(ins, list(reads), list(writes))
        if prev is not None and all(dd is not prev for dd in ins.deps):
            ins.deps.append(prev)
        if is_output:
            self.out_dmas.append(ins)
        return ins

    def emit(self):
        nc = self.nc
        with contextlib.ExitStack() as st:
            csem = {e: st.enter_context(nc.semaphore(f"c_{e}")) for e in ("pe", "act", "dve", "pool")}
            dsem = [st.enter_context(nc.semaphore(f"d_{i}")) for i in range(N_DMA_SEMS)]
            block = st.enter_context(nc.Block())
            final_waits = list(self.out_dmas)

            def run(eng_name, handle):
                seen_c = {}
                seen_d = {}
                for ins in self.q[eng_name]:
                    for d in ins.deps:
                        if d.is_dma:
                            if seen_d.get(d.sem, 0) >= d.val:
                                continue
                            seen_d[d.sem] = d.val
                            handle.wait_ge(dsem[d.sem], d.val)
                        else:
                            if seen_c.get(d.eng, 0) >= d.val:
                                continue
                            seen_c[d.eng] = d.val
                            handle.wait_ge(csem[d.eng], d.val)
                    bi = ins.fn(handle)
                    if ins.is_dma:
                        bi.then_inc(dsem[ins.sem], 16)
                    else:
                        bi.then_inc(csem[ins.eng], 1)
                if eng_name == "sp":
                    for d in final_waits:
                        if seen_d.get(d.sem, 0) >= d.val:
                            continue
                        seen_d[d.sem] = d.val
                        handle.wait_ge(dsem[d.sem], d.val)

            @block.tensor
            def _(e):
                run("pe", e)

            @block.scalar
            def _(e):
                run("act", e)

            @block.vector
            def _(e):
                run("dve", e)

            @block.gpsimd
            def _(e):
                run("pool", e)

            @block.sync
            def _(e):
                run("sp", e)


class Arena:
    def __init__(self, nc, nbytes, name="arena"):
        self.t = nc.alloc_sbuf_tensor(name, [128, nbytes // 2], BF16).ap()
        self.nbytes = nbytes
        self.top = 0
        self.top2 = nbytes
        self.hist = []
        self.peak = 0

    def _mkbuf(self, name, s, e):
        b = Buf(name)
        keep = []
        for (s0, e0, b0) in self.hist:
            if s0 < e and s < e0:
                b.readers.extend(b0.readers)
                b.readers.extend(b0.writers)
            keep.append((s0, e0, b0))
        self.hist = keep
        self.hist.append((s, e, b))
        return b

    def alloc(self, name, shape, dt, nbufs=1, top=False):
        esz = 4 if dt == F32 else 2
        free = int(np.prod(shape[1:]))
        nb = (free * esz + 31) // 32 * 32
        if top:
            e = self.top2
            s = e - nb
            assert s >= self.top, f"arena overflow (top) allocating {name}"
            self.top2 = s
        else:
            s = self.top
            e = s + nb
            assert e <= self.top2, f"arena overflow allocating {name}: {e} > {self.top2}"
            self.top = e
        self.peak = max(self.peak, self.top + (self.nbytes - self.top2))
        ap = self.t[0:shape[0], s // 2:(s + free * esz) // 2]
        if dt == F32:
            ap = ap.bitcast(F32)
        if len(shape) > 2:
            names = "abcdefg"[:len(shape) - 1]
            pat = "p (" + " ".join(names) + ") -> p " + " ".join(names)
            ap = ap.rearrange(pat, **{n: int(v) for n, v in zip(names, shape[1:])})
        if nbufs == 1:
            return ap, self._mkbuf(name, s, e)
        return ap, [self._mkbuf(f"{name}{i}", s, e) for i in range(nbufs)]

    def release_top(self):
        self.top2 = self.nbytes

    def mark(self):
        return self.top

    def release(self, mark):
        self.top = mark


class Ring:
    def __init__(self, items):
        self.items = items
        self.i = 0

    def next(self):
        it = self.items[self.i % len(self.items)]
        self.i += 1
        return it


D = 2048
DIN = 10256
KT = 16
NH = 8
C_US, C_ZS, C_Q, C_K, C_V, C_ZD, C_BETA, C_A, C_GS, C_GD = 0, 1024, 2048, 3072, 4096, 5120, 6144, 6152, 6160, 8208
EPS = 1e-6
PI = float(np.pi)


def build(LT=2048, LO=1024, debug=(), stop_after=None, nheads=NH):
    assert LT % 512 == 0 and LO % 512 == 0 and LO <= LT
    NT = LT // 128
    NTO = LO // 128
    T0 = LT - LO
    NB = LT // 512
    NBO = LO // 512
    NCH = LT // 64
    NP = LT // 128
    NC8 = LT // 8
    NC8O = LO // 8

    nc = bass.Bass("TRN2", target_bir_lowering=False)
    P = Prog(nc)

    def din(name, shape, dt=F32):
        return nc.dram_tensor(name, list(shape), dt, kind="ExternalInput").ap()

    x_in = din("x", [LT, D])
    w_in_t = din("w_in_t", [81, 128, KT, 128])
    lnw_in = din("lnw", [128, KT])
    fnw_in = din("fnw", [1, D])
    convw_in = din("convw", [128, 24, 4])
    dnp_in = din("dnp", [128, 2])
    dnnw_in = din("dnnw", [128, 1])
    lam_in = din("lam", [128, 3, 32])
    sb_in = din("s5b", [128, 2, 32, 16])
    sc_in = din("s5c", [128, 2, 32, 16])
    sd_in = din("s5d", [128, 64])
    wglu_t = din("w_glu_t", [8, 128, 8, 128])
    wups_t = din("w_ups_t", [16, 128, 8, 128])
    wupd_t = din("w_upd_t", [16, 128, 8, 128])
    wout_t = din("w_out_t", [4, 128, KT, 512])
    out_ap = nc.dram_tensor("out", [LO, D], F32, kind="ExternalOutput").ap()
    U_scr = nc.dram_tensor("U_scr", [1024, 8, NC8], BF16).ap()
    Y_scr = nc.dram_tensor("Y_scr", [1024, 8, NC8O], BF16).ap()
    dbg = {}

    def dump(name, ap, buf, shape, dt=F32):
        if name not in debug:
            return
        o = nc.dram_tensor("dbg_" + name, list(shape), dt, kind="ExternalOutput").ap()
        bufs = buf if isinstance(buf, list) else [buf]
        P.d("sp", o, ap, reads=bufs, is_output=True)
        dbg[name] = o

    A = Arena(nc, 143 * 1024, "arena")
    H = Arena(nc, 64 * 1024, "harena")

    def psum_ring(prefix, nbanks, per_bank, dt=F32):
        items = []
        for i in range(nbanks):
            width = 512 if dt == F32 else 1024
            t = nc.alloc_psum_tensor(f"{prefix}{i}", [128, width], dt).ap()
            w = width // per_bank
            for j in range(per_bank):
                items.append((t[:, j * w:(j + 1) * w], Buf(f"{prefix}{i}_{j}")))
        return Ring(items)

    BIG = psum_ring("psb", 2, 1)
    _SM = psum_ring("pss", 6, 1)

    BIG8 = Ring(list(BIG.items) + list(_SM.items))
    cur = {"big": BIG8}

    cur["small"] = _SM

    class _SmallF:
        @staticmethod
        def next():
            t, b = cur["small"].next()
            return t[:, 0:128], b

    class _SmallB:
        @staticmethod
        def next():
            t, b = cur["small"].next()
            return t.bitcast(BF16)[:, 0:128], b
    SMALL = _SmallF
    TRB = _SmallB

    ident_f, B_identf = A.alloc("ident_f", [128, 128], F32)
    ident_b, B_identb = A.alloc("ident_b", [128, 128], BF16)
    ones_b, B_onesb = A.alloc("ones_b", [128, 128], BF16)
    ones_f, B_onesf = A.alloc("ones_f", [128, 128], F32)
    SLm, B_SL = A.alloc("SL", [128, 128], F32)
    CUm, B_CU = A.alloc("CU", [64, 64], F32)
    lnw, B_lnw = A.alloc("lnw", [128, KT], F32)
    convw, B_convw = A.alloc("convw", [128, 24, 4], F32)
    dnnw, B_dnnw = A.alloc("dnnw", [128, 1], F32)
    dnp, B_dnp = A.alloc("dnp", [128, 2], F32)
    epsc, B_eps = A.alloc("epsc", [128, 1], F32)
    onec, B_one = A.alloc("onec", [128, 1], F32)

    P.x("pool", "memset", ident_f, 0.0, writes=[B_identf])
    P.x("pool", "affine_select", out=ident_f, in_=ident_f, pattern=[[-1, 128]], compare_op=ALU.not_equal,
        fill=1.0, base=0, channel_multiplier=1, reads=[B_identf], writes=[B_identf])
    P.x("pool", "tensor_copy", out=ident_b, in_=ident_f, reads=[B_identf], writes=[B_identb])
    P.x("pool", "memset", ones_b, 1.0, writes=[B_onesb])
    P.x("pool", "memset", ones_f, 1.0, writes=[B_onesf])
    P.x("pool", "memset", epsc, EPS, writes=[B_eps])
    P.x("pool", "memset", onec, 1.0, writes=[B_one])
    P.x("pool", "memset", SLm, 1.0, writes=[B_SL])
    P.x("pool", "affine_select", out=SLm, in_=SLm, pattern=[[-1, 128]], compare_op=ALU.is_gt,
        fill=0.0, base=0, channel_multiplier=1, reads=[B_SL], writes=[B_SL])
    P.x("pool", "memset", SLm[64:128, 0:64], 0.0, reads=[B_SL], writes=[B_SL])
    P.x("pool", "memset", CUm, 1.0, writes=[B_CU])
    P.x("pool", "affine_select", out=CUm, in_=CUm, pattern=[[1, 64]], compare_op=ALU.is_ge,
        fill=0.0, base=0, channel_multiplier=-1, reads=[B_CU], writes=[B_CU])
    P.d("sp", lnw, lnw_in, writes=[B_lnw])
    P.d("sp", convw, convw_in, writes=[B_convw])
    P.d("sp", dnnw, dnnw_in, writes=[B_dnnw])
    P.d("sp", dnp, dnp_in, writes=[B_dnp])

    hTo, B_hTo = H.alloc("hTo", [128, KT, LO], BF16, nbufs=NTO)
    mH = H.mark()
    if T0 > 0:
        hTp, B_hTp = H.alloc("hTp", [128, KT, T0], BF16, nbufs=NT - NTO)
    else:
        hTp, B_hTp = None, []
    B_hT = list(B_hTp) + list(B_hTo)

    def hcols(kt, t0, n):
        if t0 >= T0:
            return hTo[:, kt, t0 - T0:t0 - T0 + n]
        assert t0 + n <= T0
        return hTp[:, kt, t0:t0 + n]

    def hcols8(k0, t0, n):
        if t0 >= T0:
            return hTo[:, k0:k0 + 8, t0 - T0:t0 - T0 + n]
        return hTp[:, k0:k0 + 8, t0:t0 + n]

    wslots = [A.alloc(f"wslot{i}", [128, KT, 128], BF16) for i in range(3)]
    WS = Ring(wslots)

    wsr = {"ring": WS}

    def load_w(src_tile_ap, kt=KT):
        slot, bslot = wsr["ring"].next()
        P.d("pool", slot[:, 0:kt, :], src_tile_ap, writes=[bslot])
        return slot, bslot

    T_US, T_ZS, T_Q, T_K, T_V, T_ZD, T_BA, T_GS, T_GD = 0, 8, 16, 24, 32, 40, 48, 49, 65

    class WPipe:
        def __init__(self, tiles, depth=1):
            self.tiles = list(tiles)
            self.depth = depth
            self.q = []
            self.p = 0
            self.started = False

        def start(self):
            if not self.started:
                self.started = True
                self._fill(self.depth)

        def _fill(self, n):
            while len(self.q) < n and self.p < len(self.tiles):
                ap_, kt_ = self.tiles[self.p]
                self.q.append(load_w(ap_, kt=kt_))
                self.p += 1

        def take(self):
            self.start()
            self._fill(self.depth + 1)
            return self.q.pop(0)


    def proj_fm_g(slot, bslot, kts, rhs_fn, rhs_bufs_fn, blocks, consume, m=128, ring=None, G=None):
        blocks = list(blocks)
        rg = ring if ring is not None else cur["big"]
        if G is None:
            G = 4 if rg is BIG8 else 2
        for g0 in range(0, len(blocks), G):
            grp = blocks[g0:g0 + G]
            pss = [rg.next() for _ in grp]
            for kt in range(kts):
                for b, (ps, B_ps) in zip(grp, pss):
                    P.x("pe", "matmul", ps[0:m, :], lhsT=slot[:, kt, 0:m], rhs=rhs_fn(kt, b), start=(kt == 0), stop=(kt == kts - 1),
                        reads=[bslot] + rhs_bufs_fn(b), writes=[B_ps])
            for b, (ps, B_ps) in zip(grp, pss):
                consume(b, ps, B_ps)
            yield "P"

    def proj_fm(*a, **kw):
        for _ in proj_fm_g(*a, **kw):
            pass

    def h_rhs(kt, b):
        return hcols(kt, b * 512, 512)

    def h_bufs(b):
        return B_hT[b * 4:(b + 1) * 4]

    G_scr = nc.dram_tensor("G_scr", [128, 64 * 2 * 128], BF16).ap()
    Mi_scr = nc.dram_tensor("Mi_scr", [128, 64 * 128], BF16).ap()
    Mr_scr = nc.dram_tensor("Mr_scr", [128, 32 * 2 * 128], BF16).ap()
    A8_scr = nc.dram_tensor("A8_scr", [128, 128], F32).ap()
    B_Gscr, B_Miscr, B_Mrscr, B_A8scr = Buf("Gscr"), Buf("Miscr"), Buf("Mrscr"), Buf("A8scr")

    def s5_tables_gen():
        Gpad, B_Gpad = A.alloc("Gpad", [128, 64, 2, 128], BF16, top=True)
        Mintra, B_Mintra = A.alloc("Mintra", [128, 64, 128], BF16, top=True)
        Minter, B_Minter = A.alloc("Minter", [128, 32, 2, 128], BF16, top=True)
        A8, B_A8 = A.alloc("A8", [128, 2, 2, 32], F32, top=True)
        A8P, B_A8P = A.alloc("A8P", [128, 8, 2, 32], F32, top=True)
        A64, B_A64 = A.alloc("A64", [128, 2, 2, 32], F32, top=True)
        _wmark[0] = A.top2
        lam, B_lam = A.alloc("lam", [128, 3, 32], F32, top=True)
        sbt, B_sbt = A.alloc("sbt", [128, 2, 32, 16], F32, top=True)
        sct, B_sct = A.alloc("sct", [128, 2, 32, 16], F32, top=True)
        sdt, B_sdt = A.alloc("sdt", [128, 64], F32, top=True)
        MK, B_MK = A.alloc("MK", [128, 128], F32, top=True)
        POW, B_POW = A.alloc("POW", [128, 9, 2, 32], F32, top=True)
        NEG, B_NEG = A.alloc("NEG", [128, 8, 2, 32], F32, top=True)
        bbt, B_bbt = A.alloc("bbt", [128, 2, 32, 16], F32, top=True)
        sm = {}
        for nm in ("step", "x1", "mag", "th", "kk", "ths", "s", "c", "ar", "ai", "den", "xr", "fre", "fim", "t1", "t2", "t3", "t4", "n1", "n2", "n3", "n4", "ivr", "ivi"):
            sm[nm] = A.alloc("sm_" + nm, [128, 32], F32, top=True)
        P.d("sp", lam, lam_in, writes=[B_lam])
        P.d("sp", sbt, sb_in, writes=[B_sbt])
        P.d("sp", sct, sc_in, writes=[B_sct])
        P.d("sp", sdt, sd_in, writes=[B_sdt])
        P.x("pool", "memset", MK, 1.0, writes=[B_MK])
        P.x("pool", "affine_select", out=MK.rearrange("p (t c) -> p t c", c=16), in_=MK.rearrange("p (t c) -> p t c", c=16),
            pattern=[[16, 8], [0, 16]], compare_op=ALU.is_ge, fill=0.0, base=15, channel_multiplier=-1,
            reads=[B_MK], writes=[B_MK])
        P.x("pool", "memset", Gpad, 0.0, writes=[B_Gpad])

        def V(nm):
            return sm[nm][0]

        def BV(nm):
            return sm[nm][1]

        def tt(out, ob, a, ab, b, bb_, op, eng="dve"):
            P.x(eng, "tensor_tensor", out=out, in0=a, in1=b, op=op, reads=list(ab) + list(bb_), writes=list(ob))

        lre, lim, lst = lam[:, 0, :], lam[:, 1, :], lam[:, 2, :]
        P.x("act", "activation", out=V("step"), in_=lst, func=AF.Exp, reads=[B_lam], writes=[BV("step")])
        tt(V("x1"), [BV("x1")], lre, [B_lam], V("step"), [BV("step")], ALU.mult)
        P.x("act", "activation", out=V("mag"), in_=V("x1"), func=AF.Exp, reads=[BV("x1")], writes=[BV("mag")])
        tt(V("th"), [BV("th")], lim, [B_lam], V("step"), [BV("step")], ALU.mult)

        def sin_of(dst, shift):
            P.x("dve", "tensor_scalar", out=V("ths"), in0=V("th"), scalar1=shift, scalar2=None, op0=ALU.add,
                reads=[BV("th")], writes=[BV("ths")])
            P.x("dve", "tensor_scalar", out=V("kk"), in0=V("ths"), scalar1=PI, scalar2=None, op0=ALU.is_ge,
                reads=[BV("ths")], writes=[BV("kk")])
            for mth in range(1, 6):
                P.x("dve", "scalar_tensor_tensor", out=V("kk"), in0=V("ths"), scalar=(2 * mth + 1) * PI, in1=V("kk"),
                    op0=ALU.is_ge, op1=ALU.add, reads=[BV("ths"), BV("kk")], writes=[BV("kk")])
            P.x("dve", "scalar_tensor_tensor", out=V("ths"), in0=V("kk"), scalar=-2.0 * PI, in1=V("ths"),
                op0=ALU.mult, op1=ALU.add, reads=[BV("ths"), BV("kk")], writes=[BV("ths")])
            P.x("act", "activation", out=V(dst), in_=V("ths"), func=AF.Sin, reads=[BV("ths")], writes=[BV(dst)])
        sin_of("s", 0.0)
        yield
        sin_of("c", PI / 2)
        yield
        tt(V("ar"), [BV("ar")], V("mag"), [BV("mag")], V("c"), [BV("c")], ALU.mult)
        tt(V("ai"), [BV("ai")], V("mag"), [BV("mag")], V("s"), [BV("s")], ALU.mult)
        tt(V("t1"), [BV("t1")], lre, [B_lam], lre, [B_lam], ALU.mult)
        tt(V("t2"), [BV("t2")], lim, [B_lam], lim, [B_lam], ALU.mult)
        tt(V("den"), [BV("den")], V("t1"), [BV("t1")], V("t2"), [BV("t2")], ALU.add)
        P.x("dve", "reciprocal", out=V("den"), in_=V("den"), reads=[BV("den")], writes=[BV("den")])
        P.x("dve", "tensor_scalar", out=V("xr"), in0=V("ar"), scalar1=-1.0, scalar2=None, op0=ALU.add, reads=[BV("ar")], writes=[BV("xr")])
        tt(V("t1"), [BV("t1")], V("xr"), [BV("xr")], lre, [B_lam], ALU.mult)
        tt(V("t2"), [BV("t2")], V("ai"), [BV("ai")], lim, [B_lam], ALU.mult)
        tt(V("t1"), [BV("t1")], V("t1"), [BV("t1")], V("t2"), [BV("t2")], ALU.add)
        tt(V("fre"), [BV("fre")], V("t1"), [BV("t1")], V("den"), [BV("den")], ALU.mult)
        tt(V("t3"), [BV("t3")], V("ai"), [BV("ai")], lre, [B_lam], ALU.mult)
        tt(V("t4"), [BV("t4")], V("xr"), [BV("xr")], lim, [B_lam], ALU.mult)
        tt(V("t3"), [BV("t3")], V("t3"), [BV("t3")], V("t4"), [BV("t4")], ALU.subtract)
        tt(V("fim"), [BV("fim")], V("t3"), [BV("t3")], V("den"), [BV("den")], ALU.mult)
        tt(V("t1"), [BV("t1")], V("mag"), [BV("mag")], V("mag"), [BV("mag")], ALU.mult)
        P.x("dve", "reciprocal", out=V("t1"), in_=V("t1"), reads=[BV("t1")], writes=[BV("t1")])
        tt(V("ivr"), [BV("ivr")], V("ar"), [BV("ar")], V("t1"), [BV("t1")], ALU.mult)
        tt(V("t2"), [BV("t2")], V("ai"), [BV("ai")], V("t1"), [BV("t1")], ALU.mult)
        P.x("dve", "tensor_scalar", out=V("ivi"), in0=V("t2"), scalar1=-1.0, scalar2=None, op0=ALU.mult, reads=[BV("t2")], writes=[BV("ivi")])

        def cmul_small(dst, dbuf, j, src, sbuf, jm, mr, mrb, mi, mib, eng="dve", tp="t"):
            pr, pi = src[:, jm, 0, :], src[:, jm, 1, :]
            n1, n2, n3, n4 = tp + "1", tp + "2", tp + "3", tp + "4"
            tt(V(n1), [BV(n1)], pr, [sbuf], mr, [mrb], ALU.mult, eng=eng)
            tt(V(n2), [BV(n2)], pi, [sbuf], mi, [mib], ALU.mult, eng=eng)
            tt(dst[:, j, 0, :], [dbuf], V(n1), [BV(n1), dbuf], V(n2), [BV(n2)], ALU.subtract, eng=eng)
            tt(V(n3), [BV(n3)], pr, [sbuf], mi, [mib], ALU.mult, eng=eng)
            tt(V(n4), [BV(n4)], pi, [sbuf], mr, [mrb], ALU.mult, eng=eng)
            tt(dst[:, j, 1, :], [dbuf], V(n3), [BV(n3), dbuf], V(n4), [BV(n4)], ALU.add, eng=eng)

        P.x("pool", "memset", POW[:, 0, 0, :], 1.0, writes=[B_POW])
        P.x("pool", "memset", POW[:, 0, 1, :], 0.0, reads=[B_POW], writes=[B_POW])
        P.x("pool", "memset", NEG[:, 0, 0, :], 1.0, writes=[B_NEG])
        P.x("pool", "memset", NEG[:, 0, 1, :], 0.0, reads=[B_NEG], writes=[B_NEG])
        for j in range(1, 9):
            cmul_small(POW, B_POW, j, POW, B_POW, j - 1, V("ar"), BV("ar"), V("ai"), BV("ai"))
            yield
        for j in range(1, 8):
            cmul_small(NEG, B_NEG, j, NEG, B_NEG, j - 1, V("ivr"), BV("ivr"), V("ivi"), BV("ivi"), eng="pool", tp="n")
            yield
        for r in range(2):
            P.x("pool", "tensor_copy", out=A8[:, 0, r, :], in_=POW[:, 8, 0, :], reads=[B_POW, B_A8], writes=[B_A8])
            P.x("pool", "tensor_copy", out=A8[:, 1, r, :], in_=POW[:, 8, 1, :], reads=[B_POW, B_A8], writes=[B_A8])
        for r in range(2):
            P.x("pool", "tensor_copy", out=A8P[:, 0, r, :], in_=POW[:, 8, r, :], reads=[B_POW, B_A8P], writes=[B_A8P])
        for k in range(1, 8):
            cmul_small(A8P, B_A8P, k, A8P, B_A8P, k - 1, POW[:, 8, 0, :], B_POW, POW[:, 8, 1, :], B_POW)
        for r in range(2):
            P.x("pool", "tensor_copy", out=A64[:, 0, r, :], in_=A8P[:, 7, 0, :], reads=[B_A8P, B_A64], writes=[B_A64])
            P.x("pool", "tensor_copy", out=A64[:, 1, r, :], in_=A8P[:, 7, 1, :], reads=[B_A8P, B_A64], writes=[B_A64])
        yield
        fre_b = V("fre").unsqueeze(2).to_broadcast([128, 32, 16])
        fim_b = V("fim").unsqueeze(2).to_broadcast([128, 32, 16])
        big1, B_big1 = A.alloc("big1", [128, 32, 16], F32, top=True)
        big2, B_big2 = A.alloc("big2", [128, 32, 16], F32, top=True)
        tt(big1, [B_big1], sbt[:, 0], [B_sbt], fre_b, [BV("fre")], ALU.mult)
        tt(big2, [B_big2], sbt[:, 1], [B_sbt], fim_b, [BV("fim")], ALU.mult)
        tt(bbt[:, 0], [B_bbt], big1, [B_big1], big2, [B_big2], ALU.subtract)
        tt(big1, [B_big1], sbt[:, 1], [B_sbt], fre_b, [BV("fre")], ALU.mult)
        tt(big2, [B_big2], sbt[:, 0], [B_sbt], fim_b, [BV("fim")], ALU.mult)
        tt(bbt[:, 1], [B_bbt], big1, [B_big1, B_bbt], big2, [B_big2], ALU.add)

        GB = 2
        Pt, B_Pt = A.alloc("Pt", [128, 2, GB, 8, 16], F32, top=True)
        GTt, B_GTt = A.alloc("GTt", [128, 2, GB, 8, 16], F32, top=True)
        Qt, B_Qt = A.alloc("Qt", [128, 2, GB, 9, 16], F32, top=True)
        w1, B_w1 = A.alloc("w1", [128, GB, 9, 16], F32, top=True)
        w2, B_w2 = A.alloc("w2", [128, GB, 9, 16], F32, top=True)
        mtmp, B_mtmp = A.alloc("mtmp", [128, 128], F32, top=True)
        q1, B_q1 = A.alloc("q1", [128, GB, 9, 16], F32, top=True)
        q2, B_q2 = A.alloc("q2", [128, GB, 9, 16], F32, top=True)
        qz, B_qz = A.alloc("qz", [128, GB, 9, 16], F32, top=True)
        P.x("pool", "memset", qz, 0.0, writes=[B_qz])
        for gb in range(32 // GB):
            g0 = gb * GB
            gs = slice(g0, g0 + GB)
            sh = [128, GB, 8, 16]
            nr = NEG[:, :, 0, gs].rearrange("p t g -> p g t").unsqueeze(3).to_broadcast(sh)
            ni = NEG[:, :, 1, gs].rearrange("p t g -> p g t").unsqueeze(3).to_broadcast(sh)
            br = bbt[:, 0, gs, :].unsqueeze(2).to_broadcast(sh)
            bi = bbt[:, 1, gs, :].unsqueeze(2).to_broadcast(sh)
            a1, a2 = w1[:, :, 0:8, :], w2[:, :, 0:8, :]
            tt(a1, [B_w1], nr, [B_NEG], br, [B_bbt], ALU.mult)
            tt(a2, [B_w2], ni, [B_NEG], bi, [B_bbt], ALU.mult)
            tt(Pt[:, 0], [B_Pt], a1, [B_w1], a2, [B_w2], ALU.subtract)
            tt(a1, [B_w1], nr, [B_NEG], bi, [B_bbt], ALU.mult)
            tt(a2, [B_w2], ni, [B_NEG], br, [B_bbt], ALU.mult)
            tt(Pt[:, 1], [B_Pt], a1, [B_w1, B_Pt], a2, [B_w2], ALU.add)
            yield
            p7r = POW[:, 7, 0, gs].unsqueeze(2).unsqueeze(3).to_broadcast(sh)
            p7i = POW[:, 7, 1, gs].unsqueeze(2).unsqueeze(3).to_broadcast(sh)
            tt(a1, [B_w1], Pt[:, 0], [B_Pt], p7r, [B_POW], ALU.mult)
            tt(a2, [B_w2], Pt[:, 1], [B_Pt], p7i, [B_POW], ALU.mult)
            tt(GTt[:, 0], [B_GTt], a1, [B_w1], a2, [B_w2], ALU.subtract)
            tt(a1, [B_w1], Pt[:, 0], [B_Pt], p7i, [B_POW], ALU.mult)
            tt(a2, [B_w2], Pt[:, 1], [B_Pt], p7r, [B_POW], ALU.mult)
            tt(GTt[:, 1], [B_GTt], a1, [B_w1, B_GTt], a2, [B_w2], ALU.add)
            yield
            shq = [128, GB, 9, 16]
            pr = POW[:, :, 0, gs].rearrange("p j g -> p g j").unsqueeze(3).to_broadcast(shq)
            pi_ = POW[:, :, 1, gs].rearrange("p j g -> p g j").unsqueeze(3).to_broadcast(shq)
            cr = sct[:, 0, gs, :].unsqueeze(2).to_broadcast(shq)
            ci = sct[:, 1, gs, :].unsqueeze(2).to_broadcast(shq)
            tt(q1, [B_q1], cr, [B_sct], pr, [B_POW], ALU.mult, eng="pool")
            tt(q2, [B_q2], ci, [B_sct], pi_, [B_POW], ALU.mult, eng="pool")
            tt(Qt[:, 0], [B_Qt], q1, [B_q1], q2, [B_q2], ALU.subtract, eng="pool")
            tt(q1, [B_q1], cr, [B_sct], pi_, [B_POW], ALU.mult, eng="pool")
            tt(q2, [B_q2], ci, [B_sct], pr, [B_POW], ALU.mult, eng="pool")
            tt(q1, [B_q1], q1, [B_q1], q2, [B_q2], ALU.add, eng="pool")
            tt(Qt[:, 1], [B_Qt], qz, [B_qz], q1, [B_q1, B_Qt], ALU.subtract, eng="pool")
            for ri in range(2):
                P.x("pool", "tensor_copy", out=Minter[:, gs, ri, :].rearrange("p g (t c) -> p g t c", c=16),
                    in_=Qt[:, ri, :, 1:9, :], reads=[B_Qt, B_Minter], writes=[B_Minter])
            yield
            for gl in range(GB):
                for half in range(2):
                    yield
                    g = half * 32 + g0 + gl
                    base = half * 64
                    rows = slice(base, base + 64)
                    ps, B_ps = SMALL.next()
                    P.x("pe", "matmul", ps, lhsT=Pt[rows, 0, gl].rearrange("p t c -> p (t c)"),
                        rhs=Qt[rows, 0, gl, 0:8, :].rearrange("p t c -> p (t c)"), start=True, stop=False,
                        reads=[B_Pt, B_Qt], writes=[B_ps])
                    P.x("pe", "matmul", ps, lhsT=Pt[rows, 1, gl].rearrange("p t c -> p (t c)"),
                        rhs=Qt[rows, 1, gl, 0:8, :].rearrange("p t c -> p (t c)"), start=False, stop=True,
                        reads=[B_Pt, B_Qt], writes=[B_ps])
                    P.x("dve", "tensor_tensor", out=mtmp, in0=ps, in1=MK, op=ALU.mult, reads=[B_ps, B_MK], writes=[B_mtmp])
                    P.x("dve", "scalar_tensor_tensor", out=Mintra[:, g, :], in0=ident_f, scalar=sdt[:, g:g + 1], in1=mtmp,
                        op0=ALU.mult, op1=ALU.add, reads=[B_identf, B_sdt, B_mtmp, B_Mintra], writes=[B_Mintra])
                    for ri in range(2):
                        ps2, B_ps2 = SMALL.next()
                        P.x("pe", "transpose", ps2[:, 0:64], GTt[rows, ri, gl].rearrange("p t c -> p (t c)"),
                            ident_f[rows, base:base + 64], reads=[B_GTt, B_identf], writes=[B_ps2])
                        P.x("act", "activation", out=Gpad[:, g, ri, base:base + 64], in_=ps2[:, 0:64], func=AF.Copy,
                            reads=[B_ps2, B_Gpad], writes=[B_Gpad])

        A.top2 = _wmark[0]
        _tres.update(Gpad=(Gpad, B_Gpad), Mintra=(Mintra, B_Mintra), Minter=(Minter, B_Minter), A8=(A8, B_A8), A8P=(A8P, B_A8P), A64=(A64, B_A64))

    _wmark = [None]
    _tres = {}
    _tg = [s5_tables_gen()]

    def tick(n=1):
        for _ in range(n):
            if _tg[0] is None:
                return
            try:
                next(_tg[0])
            except StopIteration:
                _tg[0] = None
                return

    def drain_tables():
        while _tg[0] is not None:
            tick()

    m1 = A.mark()
    xts = [A.alloc(f"xt{i}", [128, D], F32) for i in range(2)]
    xss = [A.alloc(f"xs{i}", [128, D], BF16) for i in range(2)]
    ss, B_ss = A.alloc("ss", [128, NT], F32, nbufs=NT)
    rstd, B_rstd = A.alloc("rstd", [128, NT], F32, nbufs=NT)
    for tt in range(NT):
        xt, B_xt = xts[tt % 2]
        xs, B_xs = xss[tt % 2]
        P.d("sp", xt, x_in[tt * 128:(tt + 1) * 128, :], writes=[B_xt])
        P.x("act", "activation", out=xs, in_=xt, func=AF.Square, accum_out=ss[:, tt:tt + 1],
            reads=[B_xt], writes=[B_xs, B_ss[tt]])
        P.x("act", "activation", out=rstd[:, tt:tt + 1], in_=ss[:, tt:tt + 1], func=AF.Sqrt,
            bias=epsc[:, 0:1], scale=1.0 / D, reads=[B_ss[tt], B_eps], writes=[B_rstd[tt]])
        P.x("dve", "reciprocal", out=rstd[:, tt:tt + 1], in_=rstd[:, tt:tt + 1], reads=[B_rstd[tt]], writes=[B_rstd[tt]])
        P.x("act", "activation", out=xs, in_=xt, func=AF.Copy, scale=rstd[:, tt:tt + 1],
            reads=[B_xt, B_rstd[tt]], writes=[B_xs])
        for half in range(2):
            psf, B_psf = cur["big"].next()
            psb = psf.bitcast(BF16)
            for k in range(8):
                kt = half * 8 + k
                P.x("pe", "transpose", psb[:, k * 128:(k + 1) * 128], xs[:, kt * 128:(kt + 1) * 128], ident_b,
                    reads=[B_xs, B_identb], writes=[B_psf])
            P.x("dve", "tensor_tensor", out=hcols8(half * 8, tt * 128, 128),
                in0=psb.rearrange("p (k t) -> p k t", k=8),
                in1=lnw[:, half * 8:half * 8 + 8].unsqueeze(2).to_broadcast([128, 8, 128]), op=ALU.mult,
                reads=[B_psf, B_lnw], writes=[B_hT[tt]])
    A.release(m1)
    P.mark("stage1")
    dump("hTo", hTo, B_hTo, [128, KT, LO], BF16)
    if stop_after == "stage1":
        P.emit()
        return nc, dbg

    m1 = A.mark()
    usts = [A.alloc(f"ust{i}", [128, 8, NC8], BF16) for i in range(2)]
    u_pipe = WPipe([(w_in_t[T_US + j], KT) for j in range(8)], depth=1)
    B_U = [Buf(f"U{j}") for j in range(8)]
    for j in range(8):
        slot, bslot = u_pipe.take()
        ust, B_ust = usts[j % 2]

        def cons_u(b, ps, B_ps, ust=ust, B_ust=B_ust):
            P.x("act", "activation", out=ust[:, :, b * 64:(b + 1) * 64].rearrange("p t c -> p c t"),
                in_=ps.rearrange("p (c t) -> p c t", t=8), func=AF.Copy, reads=[B_ps], writes=[B_ust])
        proj_fm(slot, bslot, KT, h_rhs, h_bufs, range(NB), cons_u)
        P.d("sp", U_scr[j * 128:(j + 1) * 128], ust, reads=[B_ust], writes=[B_U[j]])
    A.release(m1)
    P.mark("u")
    if stop_after == "u":
        P.emit()
        return nc, dbg

    dnoT, B_dno = A.alloc("dnoT", [128, NH, LO], BF16, nbufs=NH)
    m_dn0 = A.mark()
    GC, B_GC = A.alloc("GC", [128, LT], F32)
    PT, B_PT = A.alloc("PT", [128, NP, 4, 8], F32)
    CT, B_CT = A.alloc("CT", [64, NCH, 3, 8], F32)
    GLB, B_GLB = A.alloc("GLB", [128, NCH * 8], F32)
    SEL, B_SEL = A.alloc("SEL", [128, 8, 128], F32)
    m0 = A.mark()
    G1, B_G1 = A.alloc("G1", [128, LT], F32)
    RM, B_RM = A.alloc("RM", [128, LT], F32)
    PTin, B_PTin = A.alloc("PTin", [128, LT], F32)
    CTin, B_CTin = A.alloc("CTin", [128, LT], F32)
    Dg, B_Dg = A.alloc("Dg", [128, NCH, 8], F32)
    negA, B_negA = A.alloc("negA", [128, 1], F32)

    P.x("pool", "memset", PTin, 0.0, writes=[B_PTin])
    P.x("pool", "memset", CTin, 0.0, writes=[B_CTin])
    P.x("pool", "memset", G1, 0.0, writes=[B_G1])
    P.x("pool", "memset", RM, 1.0, writes=[B_RM])
    P.x("pool", "memset", RM.rearrange("p (c t) -> p c t", t=64)[:, :, 0:1], 0.0, reads=[B_RM], writes=[B_RM])
    P.x("pool", "tensor_copy", out=SEL[32:40], in_=ident_f[32:40, 32:40].unsqueeze(2).to_broadcast([8, 8, 128]),
        reads=[B_identf], writes=[B_SEL])
    P.x("act", "activation", out=negA, in_=dnp[:, 0:1], func=AF.Exp, reads=[B_dnp], writes=[B_negA])
    P.x("dve", "tensor_scalar", out=negA, in0=negA, scalar1=-1.0, scalar2=None, op0=ALU.mult, reads=[B_negA], writes=[B_negA])

    wba, B_wba = load_w(w_in_t[T_BA])

    def cons_ba(b, ps, B_ps):
        sl = slice(b * 512, (b + 1) * 512)
        P.x("act", "activation", out=PTin[0:8, sl], in_=ps[0:8, :], func=AF.Sigmoid, reads=[B_ps], writes=[B_PTin])
        for (ra, rb) in ((32, 40), (64, 104)):
            P.x("act", "activation", out=G1[ra:rb, sl], in_=ps[ra:rb, :], func=AF.Exp, bias=dnp[ra:rb, 1:2], scale=1.0,
                reads=[B_ps, B_dnp], writes=[B_G1])
    proj_fm(wba, B_wba, KT, h_rhs, h_bufs, range(NB), cons_ba, m=104)
    for (ra, rb) in ((32, 40), (64, 104)):
        P.x("act", "activation", out=G1[ra:rb, :], in_=G1[ra:rb, :], func=AF.Ln, bias=onec[ra:rb, 0:1], scale=1.0,
            reads=[B_G1, B_one], writes=[B_G1])
        P.x("dve", "tensor_scalar", out=G1[ra:rb, :], in0=G1[ra:rb, :], scalar1=negA[ra:rb, 0:1], scalar2=None, op0=ALU.mult,
            reads=[B_G1, B_negA], writes=[B_G1])
        P.x("dve", "tensor_tensor_scan", out=GC[ra:rb, :], data0=RM[ra:rb, :], data1=G1[ra:rb, :], initial=0.0,
            op0=ALU.mult, op1=ALU.add, reads=[B_RM, B_G1], writes=[B_GC])
    P.x("pool", "tensor_copy", out=PTin[32:40, :], in_=GC[32:40, :], reads=[B_GC, B_PTin], writes=[B_PTin])
    P.x("act", "activation", out=PTin[64:72, :], in_=GC[64:72, :], func=AF.Exp, reads=[B_GC, B_PTin], writes=[B_PTin])
    P.x("dve", "tensor_scalar", out=CTin[32:40, :], in0=GC[32:40, :], scalar1=-1.0, scalar2=None, op0=ALU.mult,
        reads=[B_GC, B_CTin], writes=[B_CTin])
    P.x("act", "activation", out=CTin[64:72, :], in_=GC[64:72, :], func=AF.Exp, reads=[B_GC, B_CTin], writes=[B_CTin])
    gc3 = GC[96:104, :].rearrange("p (c t) -> p c t", t=64)
    P.x("dve", "tensor_tensor", out=CTin[96:104, :].rearrange("p (c t) -> p c t", t=64),
        in0=gc3[:, :, 63:64].to_broadcast([8, NCH, 64]), in1=gc3, op=ALU.subtract,
        reads=[B_GC, B_CTin], writes=[B_CTin])
    P.x("act", "activation", out=CTin[96:104, :], in_=CTin[96:104, :], func=AF.Exp, reads=[B_CTin], writes=[B_CTin])
    P.x("pool", "tensor_copy", out=PTin[96:104, :], in_=CTin[96:104, :], reads=[B_CTin, B_PTin], writes=[B_PTin])
    eg3 = PTin[64:72, :].rearrange("p (c t) -> p c t", t=64)
    P.x("dve", "tensor_tensor", out=Dg[64:72], in0=eg3[:, :, 63:64].to_broadcast([8, NCH, 8]),
        in1=ident_f[64:72, 64:72].unsqueeze(1).to_broadcast([8, NCH, 8]), op=ALU.mult,
        reads=[B_PTin, B_identf], writes=[B_Dg])
    psg, B_psg = BIG.next()
    P.x("pe", "matmul", psg[:, 0:NCH * 8], lhsT=ones_f[64:72, :], rhs=Dg[64:72].rearrange("p c h -> p (c h)"),
        start=True, stop=True, reads=[B_onesf, B_Dg], writes=[B_psg])
    P.x("act", "activation", out=GLB, in_=psg[:, 0:NCH * 8], func=AF.Copy, reads=[B_psg], writes=[B_GLB])
    for i in range(NP):
        ps, B_ps = SMALL.next()
        P.x("pe", "transpose", ps[:, 0:104], PTin[0:104, i * 128:(i + 1) * 128], ident_f[0:104, 0:104],
            reads=[B_PTin, B_identf], writes=[B_ps])
        pv = ps[:, 0:128].rearrange("p (k c) -> p k c", c=32)[:, :, 0:8]
        if i % 2:
            P.x("dve", "tensor_copy", out=PT[:, i], in_=pv, reads=[B_ps], writes=[B_PT])
        else:
            P.x("act", "activation", out=PT[:, i], in_=pv, func=AF.Copy, reads=[B_ps], writes=[B_PT])
    for ch in range(NCH):
        ps, B_ps = SMALL.next()
        P.x("pe", "transpose", ps[0:64, 0:104], CTin[0:104, ch * 64:(ch + 1) * 64], ident_f[0:104, 0:104],
            reads=[B_CTin, B_identf], writes=[B_ps])
        cv = ps[0:64, 32:128].rearrange("p (k c) -> p k c", c=32)[:, :, 0:8]
        if ch % 2:
            P.x("dve", "tensor_copy", out=CT[:, ch], in_=cv, reads=[B_ps], writes=[B_CT])
        else:
            P.x("act", "activation", out=CT[:, ch], in_=cv, func=AF.Copy, reads=[B_ps], writes=[B_CT])
    A.release(m0)
    P.mark("dn0")
    dump("PT", PT, B_PT, [128, NP, 4, 8])
    dump("CT", CT, B_CT, [64, NCH, 3, 8])
    dump("GLB", GLB, B_GLB, [128, NCH * 8])
    if stop_after == "dn0":
        P.emit()
        return nc, dbg

    HG = 2
    cur["big"] = BIG
    m_dn = A.mark()
    pre, B_pre = A.alloc("pre", [128, 3 + LT], F32)
    P.x("pool", "memset", pre[:, 0:3], 0.0, writes=[B_pre])
    acc, B_acc = A.alloc("acc", [128, LT], F32)
    MnSL, B_MnSL = A.alloc("MnSL", [128, 128], F32)
    MnCU, B_MnCU = A.alloc("MnCU", [64, 64], F32)
    P.x("dve", "tensor_scalar", out=MnSL, in0=SLm, scalar1=-1.0, scalar2=30000.0, op0=ALU.add, op1=ALU.mult,
        reads=[B_SL], writes=[B_MnSL])
    P.x("dve", "tensor_scalar", out=MnCU, in0=CUm, scalar1=-1.0, scalar2=30000.0, op0=ALU.add, op1=ALU.mult,
        reads=[B_CU], writes=[B_MnCU])
    MnCUp, B_MnCUp = A.alloc("MnCUp", [128, 128], F32)
    P.x("pool", "memset", MnCUp, 1.0, writes=[B_MnCUp])
    P.x("pool", "affine_select", out=MnCUp, in_=MnCUp, pattern=[[1, 128]], compare_op=ALU.is_ge,
        fill=0.0, base=0, channel_multiplier=-1, reads=[B_MnCUp], writes=[B_MnCUp])
    P.x("pool", "memset", MnCUp[0:64, 64:128], 0.0, reads=[B_MnCUp], writes=[B_MnCUp])
    P.x("dve", "tensor_scalar", out=MnCUp, in0=MnCUp, scalar1=-1.0, scalar2=30000.0, op0=ALU.add, op1=ALU.mult,
        reads=[B_MnCUp], writes=[B_MnCUp])
    sq, B_sq = pre[:, 3:3 + LT // 2].bitcast(BF16), B_pre
    RN = Ring([A.alloc(f"rn{i}", [128, 512], F32) for i in range(1)])

    def ring(name, n, shape, dt):
        return Ring([A.alloc(f"{name}{i}", shape, dt) for i in range(n)])

    slots = []
    for s_ in range(HG):
        d_ = {}
        for nm in ("qT", "kT", "vT"):
            d_[nm] = A.alloc(f"{nm}{s_}", [128, LT], BF16)
        d_["szd"] = A.alloc(f"szd{s_}", [128, LO], BF16)
        d_["S_f"] = A.alloc(f"S_f{s_}", [128, 128], F32)
        d_["S_b"] = A.alloc(f"S_b{s_}", [128, 128], BF16)
        d_["E"] = ring(f"E{s_}_", 3, [128, 128], F32)
        d_["A"] = ring(f"Am{s_}_", 23, [128, 128], BF16)
        d_["P"] = ring(f"Pm{s_}_", 10, [128, 128], BF16)
        d_["bv"] = ring(f"bv{s_}_", 3, [128, 128], BF16)
        d_["kbg"] = ring(f"kbg{s_}_", 3, [128, 128], BF16)
        d_["wT"] = ring(f"wT{s_}_", 3, [128, 128], BF16)
        d_["TT"] = ring(f"TT{s_}_", 3, [128, 128], BF16)
        d_["u"] = ring(f"u{s_}_", 3, [128, 128], F32)
        d_["kd"] = ring(f"kd{s_}_", 3, [128, 128], BF16)
        d_["ET"] = ring(f"ET{s_}_", 3, [128, 128], F32)
        d_["qk"] = ring(f"qk{s_}_", 3, [128, 128], BF16)
        d_["vn"] = ring(f"vn{s_}_", 2, [128, 128], BF16)
        d_["wTz"] = ring(f"wTz{s_}_", 3, [128, 128], BF16)
        for (wz_, B_wz_) in d_["wTz"].items:
            P.x("pool", "memset", wz_, 0.0, writes=[B_wz_])
        d_["o1"] = ring(f"o1{s_}_", 2, [64, 128], F32)
        d_["o"] = ring(f"o{s_}_", 2, [64, 128], F32)
        d_["on"] = ring(f"on{s_}_", 2, [64, 128], BF16)
        d_["st"] = ring(f"st{s_}_", 4, [64, 2], F32)
        d_["ojunk"] = A.alloc(f"ojunk{s_}", [64, 128], BF16)
        slots.append(d_)

    QSCALE = 128.0 ** -0.5

    def head_proj(h, sl_):
        qT, B_qT = sl_["qT"]
        kT, B_kT = sl_["kT"]
        vT, B_vT = sl_["vT"]
        szd, B_szd = sl_["szd"]
        for idx, (cbase, dst, B_dst) in enumerate(((T_Q, qT, B_qT), (T_K, kT, B_kT), (T_V, vT, B_vT))):
            slot, bslot = dn_pipe.take()

            def cons_pre(b, ps, B_ps):
                P.x("act", "activation", out=pre[:, 3 + b * 512:3 + (b + 1) * 512], in_=ps, func=AF.Copy,
                    reads=[B_ps], writes=[B_pre])
            yield from proj_fm_g(slot, bslot, KT, h_rhs, h_bufs, range(NB), cons_pre, ring=BIG8, G=4)
            tile = idx * 8 + h
            P.x("dve", "tensor_scalar", out=acc, in0=pre[:, 0:LT], scalar1=convw[:, tile, 0:1], scalar2=None, op0=ALU.mult,
                reads=[B_pre, B_convw], writes=[B_acc])
            for j in range(1, 4):
                P.x("dve", "scalar_tensor_tensor", out=acc, in0=pre[:, j:j + LT], scalar=convw[:, tile, j:j + 1], in1=acc,
                    op0=ALU.mult, op1=ALU.add, reads=[B_pre, B_convw, B_acc], writes=[B_acc])
            yield "P"
            if idx == 2:
                P.x("act", "activation", out=vT, in_=acc, func=AF.Silu, reads=[B_acc], writes=[B_vT])
            else:
                P.x("act", "activation", out=acc, in_=acc, func=AF.Silu, reads=[B_acc], writes=[B_acc])
                P.x("act", "activation", out=sq, in_=acc, func=AF.Square, reads=[B_acc], writes=[B_sq])
                for b in range(NB):
                    sl = slice(b * 512, (b + 1) * 512)
                    ps, B_ps = BIG.next()
                    P.x("pe", "matmul", ps, lhsT=ones_b, rhs=sq[:, sl], start=True, stop=True,
                        reads=[B_onesb, B_sq], writes=[B_ps])
                    rn, B_rn = RN.next()
                    P.x("act", "activation", out=rn, in_=ps, func=AF.Sqrt, bias=epsc[:, 0:1], scale=1.0,
                        reads=[B_ps, B_eps], writes=[B_rn])
                    P.x("dve", "reciprocal", out=rn, in_=rn, reads=[B_rn], writes=[B_rn])
                    P.x("dve", "scalar_tensor_tensor", out=dst[:, sl], in0=acc[:, sl], scalar=(QSCALE if idx == 0 else 1.0),
                        in1=rn, op0=ALU.mult, op1=ALU.mult, reads=[B_acc, B_rn], writes=[B_dst])
        slot, bslot = dn_pipe.take()

        def cons_zd(b, ps, B_ps):
            bo = b - (NB - NBO)
            P.x("act", "activation", out=szd[:, bo * 512:(bo + 1) * 512], in_=ps, func=AF.Silu, reads=[B_ps], writes=[B_szd])
        yield from proj_fm_g(slot, bslot, KT, h_rhs, h_bufs, range(NB - NBO, NB), cons_zd, ring=BIG8, G=4)

    def intra_gen(h, i, sl_):
        qT, B_qT = sl_["qT"]
        kT, B_kT = sl_["kT"]
        vT, B_vT = sl_["vT"]
        tok = slice(i * 128, (i + 1) * 128)
        psD, B_psD = SMALL.next()
        P.x("pe", "matmul", psD, lhsT=SEL[32:40, h, :], rhs=GC[32:40, tok], start=True, stop=True,
            reads=[B_SEL, B_GC], writes=[B_psD])
        E, B_E = sl_["E"].next()
        gcp = PT[:, i, 1, h:h + 1]
        P.x("dve", "scalar_tensor_tensor", out=E, in0=psD, scalar=gcp, in1=MnSL, op0=ALU.subtract, op1=ALU.subtract,
            reads=[B_psD, B_PT, B_MnSL], writes=[B_E])
        ETp, B_ETp = sl_["ET"].next()
        P.x("dve", "scalar_tensor_tensor", out=ETp, in0=psD, scalar=gcp, in1=MnCUp, op0=ALU.subtract, op1=ALU.add,
            reads=[B_psD, B_PT, B_MnCUp], writes=[B_ETp])
        P.x("act", "activation", out=E, in_=E, func=AF.Exp, scale=-1.0, reads=[B_E], writes=[B_E])
        P.x("act", "activation", out=ETp, in_=ETp, func=AF.Exp, reads=[B_ETp], writes=[B_ETp])
        yield
        pskk, B_pskk = SMALL.next()
        P.x("pe", "matmul", pskk, lhsT=kT[:, tok], rhs=kT[:, tok], start=True, stop=True, reads=[B_kT], writes=[B_pskk])
        Am, B_Am = sl_["A"].next()
        P.x("dve", "scalar_tensor_tensor", out=Am, in0=pskk, scalar=PT[:, i, 0, h:h + 1], in1=E, op0=ALU.mult, op1=ALU.mult,
            reads=[B_pskk, B_PT, B_E], writes=[B_Am])
        yield
        psB, B_psB = TRB.next()
        P.x("pe", "transpose", psB, Am, ident_b, reads=[B_Am, B_identb], writes=[B_psB])
        Bm, B_Bm = sl_["A"].next()
        P.x("act", "activation", out=Bm, in_=psB, func=AF.Copy, reads=[B_psB], writes=[B_Bm])
        P0, B_P0 = sl_["P"].next()
        P.x("pool", "tensor_tensor", out=P0, in0=ident_b, in1=Bm, op=ALU.subtract, reads=[B_identb, B_Bm], writes=[B_P0])
        yield
        psv, B_psv = TRB.next()
        P.x("pe", "transpose", psv, vT[:, tok], ident_b, reads=[B_vT, B_identb], writes=[B_psv])
        bv, B_bv = sl_["bv"].next()
        P.x("act", "activation", out=bv, in_=psv, func=AF.Copy, scale=PT[:, i, 0, h:h + 1], reads=[B_psv, B_PT], writes=[B_bv])
        psk, B_psk = TRB.next()
        P.x("pe", "transpose", psk, kT[:, tok], ident_b, reads=[B_kT, B_identb], writes=[B_psk])
        kbg, B_kbg = sl_["kbg"].next()
        P.x("dve", "tensor_scalar", out=kbg, in0=psk, scalar1=PT[:, i, 0, h:h + 1], scalar2=PT[:, i, 2, h:h + 1],
            op0=ALU.mult, op1=ALU.mult, reads=[B_psk, B_PT], writes=[B_kbg])
        kdp, B_kdp = sl_["kd"].next()
        P.x("dve", "tensor_scalar", out=kdp, in0=psk, scalar1=PT[:, i, 3, h:h + 1], scalar2=None, op0=ALU.mult,
            reads=[B_psk, B_PT], writes=[B_kdp])
        yield
        pskq, B_pskq = SMALL.next()
        P.x("pe", "matmul", pskq, lhsT=kT[:, tok], rhs=qT[:, tok], start=True, stop=True, reads=[B_kT, B_qT], writes=[B_pskq])
        qkp, B_qkp = sl_["qk"].next()
        P.x("dve", "tensor_tensor", out=qkp, in0=pskq, in1=ETp, op=ALU.mult, reads=[B_pskq, B_ETp], writes=[B_qkp])
        yield
        chunks = []
        for xh in range(2):
            chunks.append(dict(ch=2 * i + xh, xh=xh, R=slice(64 * xh, 64 * xh + 64),
                               ctok=slice(i * 128 + 64 * xh, i * 128 + 64 * xh + 64)))
        Ac, B_Ac, Bc, B_Bc, Pc, B_Pc = Am, B_Am, Bm, B_Bm, P0, B_P0
        for lvl in range(5):
            psA, B_psA = SMALL.next()
            P.x("pe", "matmul", psA, lhsT=Bc, rhs=Ac, start=True, stop=True, reads=[B_Bc, B_Ac], writes=[B_psA])
            A2, B_A2 = sl_["A"].next()
            P.x("dve", "tensor_copy", out=A2, in_=psA, reads=[B_psA], writes=[B_A2])
            if lvl < 4:
                psB2, B_psB2 = SMALL.next()
                P.x("pe", "matmul", psB2, lhsT=Ac, rhs=Bc, start=True, stop=True, reads=[B_Bc, B_Ac], writes=[B_psB2])
                B2, B_B2 = sl_["A"].next()
                P.x("act", "activation", out=B2, in_=psB2, func=AF.Copy, reads=[B_psB2], writes=[B_B2])
            else:
                B2, B_B2 = None, None
            yield
            psP, B_psP = SMALL.next()
            P.x("pe", "matmul", psP, lhsT=A2, rhs=Pc, start=True, stop=True, reads=[B_A2, B_Pc], writes=[B_psP])
            if lvl < 4:
                Pn, B_Pn = sl_["P"].next()
            else:
                Pn, B_Pn = sl_["TT"].next()
            P.x("dve", "tensor_tensor", out=Pn, in0=Pc, in1=psP, op=ALU.add, reads=[B_Pc, B_psP], writes=[B_Pn])
            Ac, B_Ac, Bc, B_Bc, Pc, B_Pc = A2, B_A2, B2, B_B2, Pn, B_Pn
            yield
        TT, B_TT = Pc, B_Pc
        psw, B_psw = SMALL.next()
        P.x("pe", "matmul", psw, lhsT=kbg, rhs=TT, start=True, stop=True, reads=[B_kbg, B_TT], writes=[B_psw])
        wT, B_wT = sl_["wT"].next()
        P.x("act", "activation", out=wT, in_=psw, func=AF.Copy, reads=[B_psw], writes=[B_wT])
        wTz, B_wTz = sl_["wTz"].next()
        P.x("act", "activation", out=wTz[:, 64:128], in_=psw[:, 64:128], func=AF.Copy, reads=[B_psw, B_wTz], writes=[B_wTz])
        yield
        psu, B_psu = SMALL.next()
        P.x("pe", "matmul", psu, lhsT=TT, rhs=bv, start=True, stop=True, reads=[B_TT, B_bv], writes=[B_psu])
        u_sb, B_u = sl_["u"].next()
        P.x("act", "activation", out=u_sb, in_=psu, func=AF.Copy, reads=[B_psu], writes=[B_u])
        yield
        return dict(wT=wT, B_wT=B_wT, wTz=wTz, B_wTz=B_wTz, u=u_sb, B_u=B_u, kd=kdp, B_kd=B_kdp, qk=qkp, B_qk=B_qkp, chunks=chunks)

    def recur_gen(h, i, r, sl_):
        qT, B_qT = sl_["qT"]
        szd, B_szd = sl_["szd"]
        S_f, B_Sf = sl_["S_f"]
        S_b, B_Sb = sl_["S_b"]
        ojunk, B_ojunk = sl_["ojunk"]
        vn, B_vn = sl_["vn"].next()
        for c in r["chunks"]:
            ch, R, ctok, xh = c["ch"], c["R"], c["ctok"], c["xh"]
            own = ctok.start >= T0
            psws, B_psws = SMALL.next()
            if xh == 0:
                P.x("pe", "matmul", psws[0:64, :], lhsT=r["wT"][:, 0:64], rhs=S_b, start=True, stop=True,
                    reads=[r["B_wT"], B_Sb], writes=[B_psws])
            else:
                P.x("pe", "matmul", psws, lhsT=r["wTz"], rhs=S_b, start=True, stop=True,
                    reads=[r["B_wTz"], B_Sb], writes=[B_psws])
            P.x("dve", "tensor_tensor", out=vn[R, :], in0=r["u"][R, :], in1=psws[R, :], op=ALU.subtract,
                reads=[r["B_u"], B_psws, B_vn], writes=[B_vn])
            if own:
                pso1, B_pso1 = SMALL.next()
                P.x("pe", "matmul", pso1[0:64, :], lhsT=qT[:, ctok], rhs=S_b, start=True, stop=True,
                    reads=[B_qT, B_Sb], writes=[B_pso1])
                o1, B_o1 = sl_["o1"].next()
                P.x("act", "activation", out=o1, in_=pso1[0:64, :], func=AF.Copy, scale=CT[:, ch, 1, h:h + 1],
                    reads=[B_pso1, B_CT], writes=[B_o1])
            yield
            psdS, B_psdS = SMALL.next()
            P.x("pe", "matmul", psdS, lhsT=r["kd"][R, :], rhs=vn[R, :], start=True, stop=True, reads=[r["B_kd"], B_vn], writes=[B_psdS])
            gl = GLB[:, ch * 8 + h:ch * 8 + h + 1]
            P.x("dve", "scalar_tensor_tensor", out=S_b, in0=S_f, scalar=gl, in1=psdS,
                op0=ALU.mult, op1=ALU.add, reads=[B_Sf, B_GLB, B_psdS, B_Sb], writes=[B_Sb])
            P.x("dve", "scalar_tensor_tensor", out=S_f, in0=S_f, scalar=gl, in1=psdS,
                op0=ALU.mult, op1=ALU.add, reads=[B_Sf, B_GLB, B_psdS], writes=[B_Sf])
            yield
            if own:
                pso2, B_pso2 = SMALL.next()
                P.x("pe", "matmul", pso2[0:64, :], lhsT=r["qk"][R, R], rhs=vn[R, :], start=True, stop=True,
                    reads=[r["B_qk"], B_vn], writes=[B_pso2])
                o, B_o = sl_["o"].next()
                P.x("dve", "tensor_tensor", out=o, in0=o1, in1=pso2[0:64, :], op=ALU.add, reads=[B_o1, B_pso2], writes=[B_o])
                st_, B_st = sl_["st"].next()
                P.x("act", "activation", out=ojunk, in_=o, func=AF.Square, accum_out=st_[:, 0:1],
                    reads=[B_o], writes=[B_ojunk, B_st])
                P.x("act", "activation", out=st_[:, 1:2], in_=st_[:, 0:1], func=AF.Sqrt, bias=epsc[0:64, 0:1], scale=1.0 / 128,
                    reads=[B_st, B_eps], writes=[B_st])
                P.x("dve", "reciprocal", out=st_[:, 1:2], in_=st_[:, 1:2], reads=[B_st], writes=[B_st])
                on, B_on = sl_["on"].next()
                P.x("act", "activation", out=on, in_=o, func=AF.Copy, scale=st_[:, 1:2], reads=[B_o, B_st], writes=[B_on])
                yield
                psoT, B_psoT = TRB.next()
                P.x("pe", "transpose", psoT[:, 0:64], on, ident_b[0:64, 0:64], reads=[B_on, B_identb], writes=[B_psoT])
                t0o = ctok.start - T0
                P.x("dve", "scalar_tensor_tensor", out=dnoT[:, h, t0o:t0o + 64], in0=psoT[:, 0:64], scalar=dnnw[:, 0:1],
                    in1=szd[:, t0o:t0o + 64], op0=ALU.mult, op1=ALU.mult,
                    reads=[B_psoT, B_dnnw, B_szd], writes=[B_dno[h]])
                yield

    def interleave(g1, g2):
        res = None
        act = [g for g in (g1, g2) if g is not None]
        while act:
            for g in list(act):
                try:
                    next(g)
                except StopIteration as e:
                    if g is g1:
                        res = e.value
                    act.remove(g)
            yield
        return res

    def head_gen(h, sl_):
        S_f, B_Sf = sl_["S_f"]
        S_b, B_Sb = sl_["S_b"]
        P.x("pool", "memset", S_f, 0.0, writes=[B_Sf])
        P.x("pool", "memset", S_b, 0.0, writes=[B_Sb])
        results = {}
        intras = {}
        next_intra = 0
        next_recur = 0
        recur_g = None
        recur_done = 0
        while recur_done < NP:
            while len(intras) < 2 and next_intra < NP and next_intra <= recur_done + 2:
                intras[next_intra] = intra_gen(h, next_intra, sl_)
                next_intra += 1
            if recur_g is None and next_recur in results:
                recur_g = recur_gen(h, next_recur, results.pop(next_recur), sl_)
                next_recur += 1
            for j in list(intras):
                try:
                    next(intras[j])
                except StopIteration as e:
                    results[j] = e.value
                    del intras[j]
            if recur_g is not None:
                try:
                    next(recur_g)
                except StopIteration:
                    recur_g = None
                    recur_done += 1
            yield

    dn_tiles = []
    for hg in range(0, nheads, HG):
        for h in range(hg, min(hg + HG, nheads)):
            dn_tiles += [(w_in_t[T_Q + h], KT), (w_in_t[T_K + h], KT), (w_in_t[T_V + h], KT), (w_in_t[T_ZD + h], KT)]
    dn_pipe = WPipe(dn_tiles, depth=1)
    dn_pipe.start()
    for hg in range(0, nheads, HG):
        hs = list(range(hg, min(hg + HG, nheads)))
        for k, h in enumerate(hs):
            for _ in head_proj(h, slots[k]):
                pass
        if hg == 0 and hs:
            dump("qT", slots[len(hs) - 1]["qT"][0], slots[len(hs) - 1]["qT"][1], [128, LT], BF16)
            dump("kT", slots[len(hs) - 1]["kT"][0], slots[len(hs) - 1]["kT"][1], [128, LT], BF16)
            dump("vT", slots[len(hs) - 1]["vT"][0], slots[len(hs) - 1]["vT"][1], [128, LT], BF16)
        P.mark(f"dn_g{hg}_proj")
        gens = [head_gen(h, slots[k]) for k, h in enumerate(hs)]
        cur["small"] = BIG8
        while gens:
            for g in list(gens):
                try:
                    next(g)
                except StopIteration:
                    gens.remove(g)
        cur["small"] = _SM
    A.release(m_dn0)
    H.release(mH)
    cur["big"] = BIG8
    P.mark("dn")
    dump("dnoT", dnoT[:, 0:nheads], B_dno, [128, nheads, LO], BF16)
    if stop_after == "dn":
        P.emit()
        return nc, dbg

    drain_tables()
    m_s5w = A.mark()
    Gpad, B_Gpad = _tres["Gpad"]
    Mintra, B_Mintra = _tres["Mintra"]
    Minter, B_Minter = _tres["Minter"]
    A8, B_A8 = _tres["A8"]
    A8P, B_A8P = _tres["A8P"]
    A64, B_A64 = _tres["A64"]
    Sst, B_Sst = A.alloc("Sst", [128, 2, 32], F32)
    y2T, B_y2T = H.alloc("y2T", [128, 8, LO], BF16, nbufs=8)
    P.x("pool", "memset", Sst, 0.0, writes=[B_Sst])
    P.mark("s5tab")
    m_blk = A.mark()
    UC = Ring([A.alloc(f"ucol{i}", [128, 64, 64], BF16) for i in range(1)])
    YC = Ring([A.alloc(f"ycol{i}", [128, 64, 64], BF16) for i in range(1)])
    Lt, B_Lt = A.alloc("Lt", [128, 2, 32, 64], F32)
    B_Lre, B_Lim = Buf("Lre"), Buf("Lim")
    ct1, B_ct1 = A.alloc("ct1", [128, 32, 4, 8], F32)
    ct2, B_ct2 = A.alloc("ct2", [128, 32, 4, 8], F32)
    mH2 = H.mark()
    hist, B_hist = H.alloc("hist", [128, 2, 32, 64], BF16)
    sc1, B_sc1 = H.alloc("sc1", [128, 2, 32], F32)
    sc2, B_sc2 = H.alloc("sc2", [128, 2, 32], F32)
    xm1, B_xm1 = H.alloc("xm1", [128, 2, 32, 8], F32)
    xm2, B_xm2 = H.alloc("xm2", [128, 2, 32, 8], F32)
    Cs, B_Cs = H.alloc("Cs", [128, 2, 32, 9], F32)
    Uv = U_scr.rearrange("(g ci) t c -> t ci g c", ci=16)
    Yv = Y_scr.rearrange("(g co) t c -> t co g c", co=16)
    B_Y = Buf("Yscr")
    AR2 = A8[:, 0]
    AI2 = A8[:, 1]
    for b in range(NB):
        own = b >= NB - NBO
        bo = b - (NB - NBO)
        ucol, B_uc = UC.next()
        for tau in range(8):
            P.d("sp", ucol[16 * tau:16 * tau + 16, :, :], Uv[tau][:, :, b * 64:(b + 1) * 64], reads=B_U, writes=[B_uc])
        for q4 in range(8):
            ps, B_ps = _SM.next()
            for k in range(4):
                gp = q4 * 4 + k
                for ri in range(2):
                    o_ = ps[:, (k * 2 + ri) * 64:(k * 2 + ri + 1) * 64]
                    P.x("pe", "matmul", o_, lhsT=Gpad[:, gp, ri, :], rhs=ucol[:, gp, :], start=True, stop=False,
                        reads=[B_Gpad, B_uc], writes=[B_ps])
                    P.x("pe", "matmul", o_, lhsT=Gpad[:, 32 + gp, ri, :], rhs=ucol[:, 32 + gp, :], start=False, stop=True,
                        reads=[B_Gpad, B_uc], writes=[B_ps])
            o_l = Lt[:, :, q4 * 4:q4 * 4 + 4, :].rearrange("p r g c -> p g r c")
            i_l = ps.rearrange("p (g r c) -> p g r c", g=4, r=2)
            if q4 % 2:
                P.x("act", "activation", out=o_l, in_=i_l, func=AF.Copy, reads=[B_ps, B_Lt, B_Lre, B_Lim], writes=[B_Lt, B_Lre, B_Lim])
            else:
                P.x("dve", "tensor_copy", out=o_l, in_=i_l, reads=[B_ps, B_Lt, B_Lre, B_Lim], writes=[B_Lt, B_Lre, B_Lim])
        L5 = Lt.rearrange("p r g (s k) -> p r g s k", k=8)
        LB = [B_Lt, B_Lre, B_Lim]
        ARb = A8[:, 0].unsqueeze(3).to_broadcast([128, 2, 32, 8])
        AIb = A8[:, 1].unsqueeze(3).to_broadcast([128, 2, 32, 8])
        for k in range(1, 8):
            xp = L5[:, :, :, :, k - 1]
            P.x("dve", "tensor_tensor", out=xm1, in0=ARb, in1=xp, op=ALU.mult, reads=[B_A8] + LB, writes=[B_xm1])
            P.x("dve", "tensor_tensor", out=xm2, in0=AIb, in1=xp, op=ALU.mult, reads=[B_A8] + LB, writes=[B_xm2])
            P.x("dve", "tensor_tensor", out=xm1, in0=xm1, in1=L5[:, :, :, :, k], op=ALU.add, reads=[B_xm1] + LB, writes=[B_xm1])
            P.x("dve", "tensor_tensor", out=L5[:, 0, :, :, k], in0=xm1[:, 0], in1=xm2[:, 1], op=ALU.subtract,
                reads=[B_xm1, B_xm2] + LB, writes=LB)
            P.x("dve", "tensor_tensor", out=L5[:, 1, :, :, k], in0=xm1[:, 1], in1=xm2[:, 0], op=ALU.add,
                reads=[B_xm1, B_xm2] + LB, writes=LB)
        P.x("dve", "tensor_copy", out=Cs[:, :, :, 0], in_=Sst, reads=[B_Sst, B_Cs], writes=[B_Cs])
        for sg_ in range(8):
            cp = Cs[:, :, :, sg_]
            P.x("dve", "tensor_tensor", out=sc1, in0=A64[:, 0], in1=cp, op=ALU.mult, reads=[B_A64, B_Cs], writes=[B_sc1])
            P.x("dve", "tensor_tensor", out=sc2, in0=A64[:, 1], in1=cp, op=ALU.mult, reads=[B_A64, B_Cs], writes=[B_sc2])
            P.x("dve", "tensor_tensor", out=sc1, in0=sc1, in1=L5[:, :, :, sg_, 7], op=ALU.add, reads=[B_sc1] + LB, writes=[B_sc1])
            P.x("dve", "tensor_tensor", out=Cs[:, 0, :, sg_ + 1], in0=sc1[:, 0, :], in1=sc2[:, 1, :], op=ALU.subtract,
                reads=[B_sc1, B_sc2, B_Cs], writes=[B_Cs])
            P.x("dve", "tensor_tensor", out=Cs[:, 1, :, sg_ + 1], in0=sc1[:, 1, :], in1=sc2[:, 0, :], op=ALU.add,
                reads=[B_sc1, B_sc2, B_Cs], writes=[B_Cs])
        if own:
            sh4 = [128, 32, 4, 8]
            Wr = A8P[:, :, 0, :].rearrange("p k g -> p g k").unsqueeze(2).to_broadcast(sh4)
            Wi = A8P[:, :, 1, :].rearrange("p k g -> p g k").unsqueeze(2).to_broadcast(sh4)
            for s0 in (0, 4):
                Cr = Cs[:, 0, :, s0:s0 + 4].unsqueeze(3).to_broadcast(sh4)
                Ci = Cs[:, 1, :, s0:s0 + 4].unsqueeze(3).to_broadcast(sh4)
                Lre, Lim = L5[:, 0, :, s0:s0 + 4, :], L5[:, 1, :, s0:s0 + 4, :]
                P.x("dve", "tensor_tensor", out=ct1, in0=Wr, in1=Cr, op=ALU.mult, reads=[B_A8P, B_Cs, B_ct1], writes=[B_ct1])
                P.x("dve", "tensor_tensor", out=Lre, in0=Lre, in1=ct1, op=ALU.add, reads=[B_ct1, B_Lre, B_Lt], writes=[B_Lre])
                P.x("dve", "tensor_tensor", out=ct1, in0=Wi, in1=Ci, op=ALU.mult, reads=[B_A8P, B_Cs, B_ct1], writes=[B_ct1])
                P.x("dve", "tensor_tensor", out=Lre, in0=Lre, in1=ct1, op=ALU.subtract, reads=[B_ct1, B_Lre], writes=[B_Lre])
                P.x("pool", "tensor_tensor", out=ct2, in0=Wr, in1=Ci, op=ALU.mult, reads=[B_A8P, B_Cs, B_ct2], writes=[B_ct2])
                P.x("pool", "tensor_tensor", out=Lim, in0=Lim, in1=ct2, op=ALU.add, reads=[B_ct2, B_Lim, B_Lt], writes=[B_Lim])
                P.x("pool", "tensor_tensor", out=ct2, in0=Wi, in1=Cr, op=ALU.mult, reads=[B_A8P, B_Cs, B_ct2], writes=[B_ct2])
                P.x("pool", "tensor_tensor", out=Lim, in0=Lim, in1=ct2, op=ALU.add, reads=[B_ct2, B_Lim], writes=[B_Lim])
            P.x("act", "activation", out=hist[:, :, :, 1:64], in_=Lt[:, :, :, 0:63], func=AF.Copy,
                reads=[B_Lre, B_Lim, B_Lt, B_hist], writes=[B_hist])
            P.x("pool", "tensor_copy", out=hist[:, :, :, 0], in_=Sst, reads=[B_Sst, B_hist], writes=[B_hist])
        P.x("dve", "tensor_copy", out=Sst, in_=Cs[:, :, :, 8], reads=[B_Cs, B_Sst, B_hist], writes=[B_Sst])
        if not own:
            continue
        ycol, B_yc = YC.next()
        for q8 in range(8):
            ps, B_ps = _SM.next()
            for k in range(8):
                g = q8 * 8 + k
                half, gp = g // 32, g % 32
                rows = slice(half * 64, half * 64 + 64)
                o_ = ps[:, k * 64:(k + 1) * 64]
                P.x("pe", "matmul", o_, lhsT=Mintra[:, g, :], rhs=ucol[:, g, :], start=True, stop=False,
                    reads=[B_Mintra, B_uc], writes=[B_ps])
                P.x("pe", "matmul", o_, lhsT=Minter[rows, gp, 0, :], rhs=hist[rows, 0, gp, :], start=False, stop=False,
                    reads=[B_Minter, B_hist], writes=[B_ps])
                P.x("pe", "matmul", o_, lhsT=Minter[rows, gp, 1, :], rhs=hist[rows, 1, gp, :], start=False, stop=True,
                    reads=[B_Minter, B_hist], writes=[B_ps])
            P.x("act", "activation", out=ycol[:, q8 * 8:q8 * 8 + 8, :], in_=ps.rearrange("p (g c) -> p g c", g=8),
                func=AF.Gelu_apprx_tanh, reads=[B_ps, B_yc], writes=[B_yc])
        for t in range(8):
            P.d("sp", Yv[t][:, :, bo * 64:(bo + 1) * 64], ycol[16 * t:16 * t + 16, :, :], reads=[B_yc], writes=[B_Y])
    A.release(m_s5w)
    A.release_top()
    H.release(mH2)
    for j in range(8):
        P.d("sp", y2T[:, j, :], Y_scr[j * 128:(j + 1) * 128].rearrange("p t c -> p (t c)"), reads=[B_Y], writes=[B_y2T[j]])
    P.mark("s5blk")
    dump("y2T", y2T, B_y2T, [128, 8, LO], BF16)
    if stop_after == "s5":
        P.emit()
        return nc, dbg

    y4T, B_y4 = A.alloc("y4T", [128, 8, LO], BF16, nbufs=8)
    m_glu = A.mark()
    y3s = [A.alloc(f"y3_{i}", [128, LO], BF16) for i in range(2)]
    SG = Ring([A.alloc(f"sg{i}", [128, 512], BF16) for i in range(2)])
    SZ = Ring([A.alloc(f"sz{i}", [128, 512], BF16) for i in range(2)])

    def y2_rhs(kt, pb):
        return y2T[:, kt, pb * 512:(pb + 1) * 512]

    def y2_bufs(pb):
        return list(B_y2T)

    glu_tiles = []
    for j in range(8):
        glu_tiles += [(wglu_t[j], 8), (w_in_t[T_ZS + j], KT)]
    glu_pipe = WPipe(glu_tiles, depth=1)
    for j in range(8):
        y3, B_y3 = y3s[j % 2]
        slot, bslot = glu_pipe.take()

        def cons_glu(pb, ps, B_ps, j=j, y3=y3, B_y3=B_y3):
            sg, B_sg = SG.next()
            P.x("act", "activation", out=sg, in_=ps, func=AF.Sigmoid, reads=[B_ps], writes=[B_sg])
            P.x("dve", "tensor_tensor", out=y3[:, pb * 512:(pb + 1) * 512], in0=y2T[:, j, pb * 512:(pb + 1) * 512], in1=sg,
                op=ALU.mult, reads=[B_y2T[j], B_sg], writes=[B_y3])
        proj_fm(slot, bslot, 8, y2_rhs, y2_bufs, range(NBO), cons_glu)
        slot2, bslot2 = glu_pipe.take()

        def cons_zs(b, ps, B_ps, j=j, y3=y3, B_y3=B_y3):
            bo = b - (NB - NBO)
            sz, B_sz = SZ.next()
            P.x("act", "activation", out=sz, in_=ps, func=AF.Silu, reads=[B_ps], writes=[B_sz])
            P.x("dve", "tensor_tensor", out=y4T[:, j, bo * 512:(bo + 1) * 512].rearrange("p (c t) -> p c t", t=8),
                in0=y3.rearrange("p (t c) -> p c t", t=8)[:, bo * 64:(bo + 1) * 64, :],
                in1=sz.rearrange("p (c t) -> p c t", t=8), op=ALU.mult,
                reads=[B_y3, B_sz], writes=[B_y4[j]])
        proj_fm(slot2, bslot2, KT, h_rhs, h_bufs, range(NB - NBO, NB), cons_zs)
    A.release(m_glu)
    P.mark("glu")
    dump("y4T", y4T, B_y4, [128, 8, LO], BF16)
    if stop_after == "glu":
        P.emit()
        return nc, dbg

    mixT, B_mix = A.alloc("mixT", [128, KT, LO], BF16, nbufs=NTO)
    m_mix = A.mark()
    SGS = Ring([A.alloc(f"sgs{i}", [128, 512], BF16) for i in range(4)])
    SGD = Ring([A.alloc(f"sgd{i}", [128, 512], BF16) for i in range(4)])
    M1 = Ring([A.alloc(f"m1_{i}", [128, 512], F32) for i in range(4)])
    M2 = Ring([A.alloc(f"m2_{i}", [128, 512], F32) for i in range(3)])

    def y4_rhs(kt, bo):
        return y4T[:, kt, bo * 512:(bo + 1) * 512]

    def dn_rhs(kt, bo):
        return dnoT[:, kt, bo * 512:(bo + 1) * 512]

    mix_tiles = []
    for j in range(KT):
        mix_tiles += [(w_in_t[T_GS + j], KT), (w_in_t[T_GD + j], KT), (wups_t[j], 8), (wupd_t[j], 8)]
    wsr["ring"] = Ring(list(wslots) + [A.alloc(f"wslotx{i}", [128, KT, 128], BF16) for i in range(2)])
    mix_pipe = WPipe(mix_tiles, depth=2)
    for j in range(KT):
        s_gs = mix_pipe.take()
        st_ = {bo: {} for bo in range(NBO)}

        def cons_gs(b_, ps, B_ps, st_=st_):
            sg, B_sg = SGS.next()
            P.x("act", "activation", out=sg, in_=ps, func=AF.Sigmoid, reads=[B_ps], writes=[B_sg])
            st_[b_ - (NB - NBO)]["gs"] = (sg, B_sg)
        proj_fm(s_gs[0], s_gs[1], KT, h_rhs, h_bufs, range(NB - NBO, NB), cons_gs)

        s_gd = mix_pipe.take()

        def cons_gd(b_, ps, B_ps, st_=st_):
            sg, B_sg = SGD.next()
            P.x("act", "activation", out=sg, in_=ps, func=AF.Sigmoid, reads=[B_ps], writes=[B_sg])
            st_[b_ - (NB - NBO)]["gd"] = (sg, B_sg)
        proj_fm(s_gd[0], s_gd[1], KT, h_rhs, h_bufs, range(NB - NBO, NB), cons_gd)

        s_us = mix_pipe.take()

        def cons_us(bo_, ps, B_ps, st_=st_):
            m1_, B_m1 = M1.next()
            sg, B_sg = st_[bo_]["gs"]
            P.x("dve", "tensor_tensor", out=m1_, in0=ps, in1=sg, op=ALU.mult, reads=[B_ps, B_sg], writes=[B_m1])
            st_[bo_]["m1"] = (m1_, B_m1)
        proj_fm(s_us[0], s_us[1], 8, y4_rhs, lambda bo_: list(B_y4), range(NBO), cons_us)

        s_ud = mix_pipe.take()

        def cons_ud(bo_, ps, B_ps, j=j, st_=st_):
            m2_, B_m2 = M2.next()
            sg, B_sg = st_[bo_]["gd"]
            P.x("dve", "tensor_tensor", out=m2_, in0=ps, in1=sg, op=ALU.mult, reads=[B_ps, B_sg], writes=[B_m2])
            m1_, B_m1 = st_[bo_]["m1"]
            P.x("pool", "tensor_tensor", out=mixT[:, j, bo_ * 512:(bo_ + 1) * 512], in0=m1_, in1=m2_, op=ALU.add,
                reads=[B_m1, B_m2], writes=B_mix[bo_ * 4:(bo_ + 1) * 4])
        proj_fm(s_ud[0], s_ud[1], 8, dn_rhs, lambda bo_: list(B_dno), range(NBO), cons_ud)
    wsr["ring"] = WS
    A.release(m_mix)
    H.release(0)
    P.mark("mix")

    rT, B_r = H.alloc("rT", [128, NTO, D], F32, nbufs=NTO)
    fnw, B_fnw = A.alloc("fnw", [128, D], F32)
    P.d("sp", fnw, fnw_in.to_broadcast([128, D]),
        writes=[B_fnw])
    WO = Ring([A.alloc(f"wo{i}", [128, KT, 512], BF16) for i in range(2)])
    XO = Ring([A.alloc(f"xo{i}", [128, 512], F32) for i in range(3)])
    st2, B_st2 = A.alloc("st2", [128, NTO, 2], F32, nbufs=NTO)
    fjunk, B_fjunk = A.alloc("fjunk", [128, D], BF16)
    for cb in range(4):
        wo, B_wo = WO.next()
        P.d("pool", wo, wout_t[cb], writes=[B_wo])
        for tt_ in range(NTO):
            xo, B_xo = XO.next()
            P.d("sp", xo, x_in[T0 + tt_ * 128:T0 + (tt_ + 1) * 128, cb * 512:(cb + 1) * 512], writes=[B_xo])
            ps, B_ps = cur["big"].next()
            for kt in range(KT):
                P.x("pe", "matmul", ps, lhsT=mixT[:, kt, tt_ * 128:(tt_ + 1) * 128], rhs=wo[:, kt, :],
                    start=(kt == 0), stop=(kt == KT - 1), reads=[B_mix[tt_], B_wo], writes=[B_ps])
            P.x("dve", "tensor_tensor", out=rT[:, tt_, cb * 512:(cb + 1) * 512], in0=ps, in1=xo, op=ALU.add,
                reads=[B_ps, B_xo], writes=[B_r[tt_]])
    for tt_ in range(NTO):
        P.x("act", "activation", out=fjunk, in_=rT[:, tt_, :], func=AF.Square, accum_out=st2[:, tt_, 0:1],
            reads=[B_r[tt_]], writes=[B_fjunk, B_st2[tt_]])
        P.x("act", "activation", out=st2[:, tt_, 1:2], in_=st2[:, tt_, 0:1], func=AF.Sqrt, bias=epsc[:, 0:1], scale=1.0 / D,
            reads=[B_st2[tt_], B_eps], writes=[B_st2[tt_]])
        P.x("dve", "reciprocal", out=st2[:, tt_, 1:2], in_=st2[:, tt_, 1:2], reads=[B_st2[tt_]], writes=[B_st2[tt_]])
        P.x("dve", "scalar_tensor_tensor", out=rT[:, tt_, :], in0=rT[:, tt_, :], scalar=st2[:, tt_, 1:2], in1=fnw,
            op0=ALU.mult, op1=ALU.mult, reads=[B_r[tt_], B_st2[tt_], B_fnw], writes=[B_r[tt_]])
        P.d("sp", out_ap[tt_ * 128:(tt_ + 1) * 128, :], rT[:, tt_, :], reads=[B_r[tt_]], is_output=True)
    P.mark("out")
    P.emit()
    build.last_marks = P.marks
    return nc, dbg


_NC_CACHE = {}


def kernel(**inputs):
    x = np.asarray(inputs["x"], dtype=np.float32)
    Bsz, L, Dm = x.shape
    LO = L // 2
    key = (L,)
    if key not in _NC_CACHE:
        _NC_CACHE[key] = build(LT=L, LO=LO)[0]
    nc = _NC_CACHE[key]
    in_maps = []
    wmap = make_in_map(inputs, np.zeros((1, 1), np.float32))
    wmap.pop("x")
    for c in range(8):
        b, p = c // 2, c % 2
        if p == 0:
            xloc = np.concatenate([np.zeros((L - LO, Dm), np.float32), x[b, :LO]], axis=0)
        else:
            xloc = x[b]
        in_maps.append(make_in_map(inputs, xloc, wmap))
    res = run_bass_kernel_spmd(nc, in_maps, core_ids=list(range(8)))
    out = np.empty((Bsz, L, Dm), np.float32)
    for c in range(8):
        b, p = c // 2, c % 2
        out[b, p * LO:(p + 1) * LO] = np.asarray(res.results[c]["out"], dtype=np.float32)
    return out


def make_in_map(inp, xloc, wmap=None):
    if wmap is not None:
        m = dict(wmap)
        m["x"] = np.ascontiguousarray(xloc, dtype=np.float32)
        return m
    f = np.float32
    g = lambda k: np.asarray(inp[k], dtype=f)
    m = {}
    m["x"] = np.ascontiguousarray(xloc, dtype=f)
    w_in = g("w_in")[0]

    def tile_of(w, col0, ncols=128, kt=16):
        return w[:, col0:col0 + ncols].reshape(kt, 128, ncols).transpose(1, 0, 2)
    col0s = ([0 + 128 * j for j in range(8)] + [1024 + 128 * j for j in range(8)] + [2048 + 128 * h for h in range(8)]
             + [3072 + 128 * h for h in range(8)] + [4096 + 128 * h for h in range(8)] + [5120 + 128 * h for h in range(8)]
             + [None] + [6160 + 128 * j for j in range(16)] + [8208 + 128 * j for j in range(16)])
    wt = np.zeros((81, 128, 16, 128), f)
    for ti, c0 in enumerate(col0s):
        if c0 is None:
            wt[ti, :, :, 0:8] = tile_of(w_in, 6144, 8)
            for r0 in (32, 64, 96):
                wt[ti, :, :, r0:r0 + 8] = tile_of(w_in, 6152, 8)
        else:
            wt[ti] = tile_of(w_in, c0)
    m["w_in_t"] = wt
    m["lnw"] = np.ascontiguousarray(g("ln_w")[0].reshape(16, 128).T)
    m["fnw"] = np.ascontiguousarray(g("final_norm_w")[None, :])
    m["convw"] = np.ascontiguousarray(g("dn_conv_w")[0].reshape(4, 24, 128).transpose(2, 1, 0))
    dnp = np.zeros((128, 2), f)
    for r0 in (32, 64, 96):
        dnp[r0:r0 + 8, 0] = g("dn_a_log")[0]
        dnp[r0:r0 + 8, 1] = g("dn_dt_bias")[0]
    m["dnp"] = dnp
    m["dnnw"] = np.ascontiguousarray(g("dn_norm_w")[0][:, None])
    lam = np.zeros((128, 3, 32), f)
    lre, lim, lst = g("s5_lam_re")[0], g("s5_lam_im")[0], g("s5_log_step")[0]
    for half in range(2):
        gs = slice(half * 32, half * 32 + 32)
        lam[half * 64:(half + 1) * 64, 0, :] = lre[gs].T
        lam[half * 64:(half + 1) * 64, 1, :] = lim[gs].T
        lam[half * 64:(half + 1) * 64, 2, :] = np.broadcast_to(lst[gs][None, :], (64, 32))
    m["lam"] = lam
    sb = np.zeros((128, 2, 32, 16), f)
    sc = np.zeros((128, 2, 32, 16), f)
    for half in range(2):
        gs = slice(half * 32, half * 32 + 32)
        ps = slice(half * 64, half * 64 + 64)
        sb[ps, 0] = g("s5_b_re")[0][gs].transpose(1, 0, 2)
        sb[ps, 1] = g("s5_b_im")[0][gs].transpose(1, 0, 2)
        sc[ps, 0] = g("s5_c_re")[0][gs].transpose(2, 0, 1)
        sc[ps, 1] = g("s5_c_im")[0][gs].transpose(2, 0, 1)
    m["s5b"] = sb
    m["s5c"] = sc
    d = g("s5_d")[0].reshape(64, 16)
    m["s5d"] = np.ascontiguousarray(np.tile(d.T, (8, 1)))
    m["w_glu_t"] = np.ascontiguousarray(np.stack([tile_of(g("s5_w_glu")[0], 128 * j, 128, 8) for j in range(8)]))
    m["w_ups_t"] = np.ascontiguousarray(np.stack([tile_of(g("s5_w_up")[0], 128 * j, 128, 8) for j in range(16)]))
    m["w_upd_t"] = np.ascontiguousarray(np.stack([tile_of(g("dn_w_up")[0], 128 * j, 128, 8) for j in range(16)]))
    m["w_out_t"] = np.ascontiguousarray(np.stack([tile_of(g("w_out")[0], 512 * cb, 512, 16) for cb in range(4)]))
    return m
```

```python
import contextlib
import numpy as np
import concourse.bass as bass
import concourse.mybir as mybir
from concourse.bass_utils import run_bass_kernel_spmd

F32 = mybir.dt.float32
BF16 = mybir.dt.bfloat16
AF = mybir.ActivationFunctionType
ALU = mybir.AluOpType

ENGS = ("pe", "act", "dve", "pool", "sp")
N_DMA_SEMS = 32


class Buf:
    __slots__ = ("name", "writers", "readers")

    def __init__(self, name):
        self.name = name
        self.writers = []
        self.readers = []


class Ins:
    __slots__ = ("eng", "fn", "deps", "is_dma", "sem", "val", "tag")

    def __init__(self, eng, fn, is_dma, tag=None):
        self.eng = eng
        self.fn = fn
        self.deps = []
        self.is_dma = is_dma
        self.sem = None
        self.val = None
        self.tag = tag


def _mk(method, *args, **kw):
    def fn(e):
        return getattr(e, method)(*args, **kw)
    return fn


class Prog:
    def __init__(self, nc):
        self.nc = nc
        self.q = {e: [] for e in ENGS}
        self.cnt = {e: 0 for e in ENGS}
        self.dma_rr = 0
        self.dma_rr_pool = 0
        self.dma_cnt = [0] * N_DMA_SEMS
        self.dma_last = [None] * N_DMA_SEMS
        self.out_dmas = []

    def mark(self, name):
        if not hasattr(self, 'marks'):
            self.marks = []
        self.marks.append((name, dict(self.cnt)))

    def _add(self, ins, reads, writes):
        deps = []
        for b in reads:
            deps.extend(b.writers)
        for b in writes:
            deps.extend(b.writers)
            deps.extend(b.readers)
        seen = set()
        for d in deps:
            if d is ins or id(d) in seen:
                continue
            seen.add(id(d))
            if d.eng == "pe" and ins.eng == "pe" and not d.is_dma and not ins.is_dma:
                continue
            ins.deps.append(d)
        for b in reads:
            b.readers.append(ins)
        for b in writes:
            b.writers = [ins]
            b.readers = []
        self.q[ins.eng].append(ins)
        return ins

    def x(self, eng, method, *args, reads=(), writes=(), **kw):
        ins = Ins(eng, _mk(method, *args, **kw), False, method)
        self.cnt[eng] += 1
        ins.val = self.cnt[eng]
        return self._add(ins, list(reads), list(writes))

    def d(self, eng, out, in_, reads=(), writes=(), is_output=False, **kw):
        ins = Ins(eng, _mk("dma_start", out=out, in_=in_, **kw), True, "dma")
        half = N_DMA_SEMS // 2
        if eng == "pool":
            k = half + (self.dma_rr_pool % half)
            self.dma_rr_pool += 1
        else:
            k = self.dma_rr % half
            self.dma_rr += 1
        ins.sem = k
        self.dma_cnt[k] += 16
        ins.val = self.dma_cnt[k]
        prev = self.dma_last[k]
        self.dma_last[k] = ins
        self._add(ins, list(reads), list(writes))
        if prev is not None and all(dd is not prev for dd in ins.deps):
            ins.deps.append(prev)
        if is_output:
            self.out_dmas.append(ins)
        return ins

    def emit(self):
        nc = self.nc
        with contextlib.ExitStack() as st:
            csem = {e: st.enter_context(nc.semaphore(f"c_{e}")) for e in ("pe", "act", "dve", "pool")}
            dsem = [st.enter_context(nc.semaphore(f"d_{i}")) for i in range(N_DMA_SEMS)]
            block = st.enter_context(nc.Block())
            final_waits = list(self.out_dmas)

            def run(eng_name, handle):
                seen_c = {}
                seen_d = {}
                for ins in self.q[eng_name]:
                    for d in ins.deps:
                        if d.is_dma:
                            if seen_d.get(d.sem, 0) >= d.val:
                                continue
                            seen_d[d.sem] = d.val
                            handle.wait_ge(dsem[d.sem], d.val)
                        else:
                            if seen_c.get(d.eng, 0) >= d.val:
                                continue
                            seen_c[d.eng] = d.val
                            handle.wait_ge(csem[d.eng], d.val)
                    bi = ins.fn(handle)
                    if ins.is_dma:
                        bi.then_inc(dsem[ins.sem], 16)
                    else:
                        bi.then_inc(csem[ins.eng], 1)
                if eng_name == "sp":
                    for d in final_waits:
                        if seen_d.get(d.sem, 0) >= d.val:
                            continue
                        seen_d[d.sem] = d.val
                        handle.wait_ge(dsem[d.sem], d.val)

            @block.tensor
            def _(e):
                run("pe", e)

            @block.scalar
            def _(e):
                run("act", e)

            @block.vector
            def _(e):
                run("dve", e)

            @block.gpsimd
            def _(e):
                run("pool", e)

            @block.sync
            def _(e):
                run("sp", e)


class Arena:
    def __init__(self, nc, nbytes, name="arena"):
        self.t = nc.alloc_sbuf_tensor(name, [128, nbytes // 2], BF16).ap()
        self.nbytes = nbytes
        self.top = 0
        self.top2 = nbytes
        self.hist = []
        self.peak = 0

    def _mkbuf(self, name, s, e):
        b = Buf(name)
        keep = []
        for (s0, e0, b0) in self.hist:
            if s0 < e and s < e0:
                b.readers.extend(b0.readers)
                b.readers.extend(b0.writers)
            keep.append((s0, e0, b0))
        self.hist = keep
        self.hist.append((s, e, b))
        return b

    def alloc(self, name, shape, dt, nbufs=1, top=False):
        esz = 4 if dt == F32 else 2
        free = int(np.prod(shape[1:]))
        nb = (free * esz + 31) // 32 * 32
        if top:
            e = self.top2
            s = e - nb
            assert s >= self.top, f"arena overflow (top) allocating {name}"
            self.top2 = s
        else:
            s = self.top
            e = s + nb
            assert e <= self.top2, f"arena overflow allocating {name}: {e} > {self.top2}"
            self.top = e
        self.peak = max(self.peak, self.top + (self.nbytes - self.top2))
        ap = self.t[0:shape[0], s // 2:(s + free * esz) // 2]
        if dt == F32:
            ap = ap.bitcast(F32)
        if len(shape) > 2:
            names = "abcdefg"[:len(shape) - 1]
            pat = "p (" + " ".join(names) + ") -> p " + " ".join(names)
            ap = ap.rearrange(pat, **{n: int(v) for n, v in zip(names, shape[1:])})
        if nbufs == 1:
            return ap, self._mkbuf(name, s, e)
        return ap, [self._mkbuf(f"{name}{i}", s, e) for i in range(nbufs)]

    def release_top(self):
        self.top2 = self.nbytes

    def mark(self):
        return self.top

    def release(self, mark):
        self.top = mark


class Ring:
    def __init__(self, items):
        self.items = items
        self.i = 0

    def next(self):
        it = self.items[self.i % len(self.items)]
        self.i += 1
        return it


D = 2048
DIN = 10256
KT = 16
NH = 8
C_US, C_ZS, C_Q, C_K, C_V, C_ZD, C_BETA, C_A, C_GS, C_GD = 0, 1024, 2048, 3072, 4096, 5120, 6144, 6152, 6160, 8208
EPS = 1e-6
PI = float(np.pi)


def build(LT=2048, LO=1024, debug=(), stop_after=None, nheads=NH):
    assert LT % 512 == 0 and LO % 512 == 0 and LO <= LT
    NT = LT // 128
    NTO = LO // 128
    T0 = LT - LO
    NB = LT // 512
    NBO = LO // 512
    NCH = LT // 64
    NP = LT // 128
    NC8 = LT // 8
    NC8O = LO // 8

    nc = bass.Bass("TRN2", target_bir_lowering=False)
    P = Prog(nc)

    def din(name, shape, dt=F32):
        return nc.dram_tensor(name, list(shape), dt, kind="ExternalInput").ap()

    x_in = din("x", [LT, D])
    w_in_t = din("w_in_t", [81, 128, KT, 128])
    lnw_in = din("lnw", [128, KT])
    fnw_in = din("fnw", [1, D])
    convw_in = din("convw", [128, 24, 4])
    dnp_in = din("dnp", [128, 2])
    dnnw_in = din("dnnw", [128, 1])
    lam_in = din("lam", [128, 3, 32])
    sb_in = din("s5b", [128, 2, 32, 16])
    sc_in = din("s5c", [128, 2, 32, 16])
    sd_in = din("s5d", [128, 64])
    wglu_t = din("w_glu_t", [8, 128, 8, 128])
    wups_t = din("w_ups_t", [16, 128, 8, 128])
    wupd_t = din("w_upd_t", [16, 128, 8, 128])
    wout_t = din("w_out_t", [4, 128, KT, 512])
    out_ap = nc.dram_tensor("out", [LO, D], F32, kind="ExternalOutput").ap()
    U_scr = nc.dram_tensor("U_scr", [1024, 8, NC8], BF16).ap()
    Y_scr = nc.dram_tensor("Y_scr", [1024, 8, NC8O], BF16).ap()
    dbg = {}

    def dump(name, ap, buf, shape, dt=F32):
        if name not in debug:
            return
        o = nc.dram_tensor("dbg_" + name, list(shape), dt, kind="ExternalOutput").ap()
        bufs = buf if isinstance(buf, list) else [buf]
        P.d("sp", o, ap, reads=bufs, is_output=True)
        dbg[name] = o

    A = Arena(nc, 143 * 1024, "arena")
    H = Arena(nc, 64 * 1024, "harena")

    def psum_ring(prefix, nbanks, per_bank, dt=F32):
        items = []
        for i in range(nbanks):
            width = 512 if dt == F32 else 1024
            t = nc.alloc_psum_tensor(f"{prefix}{i}", [128, width], dt).ap()
            w = width // per_bank
            for j in range(per_bank):
                items.append((t[:, j * w:(j + 1) * w], Buf(f"{prefix}{i}_{j}")))
        return Ring(items)

    BIG = psum_ring("psb", 2, 1)
    _SM = psum_ring("pss", 6, 1)

    BIG8 = Ring(list(BIG.items) + list(_SM.items))
    cur = {"big": BIG8}

    cur["small"] = _SM

    class _SmallF:
        @staticmethod
        def next():
            t, b = cur["small"].next()
            return t[:, 0:128], b

    class _SmallB:
        @staticmethod
        def next():
            t, b = cur["small"].next()
            return t.bitcast(BF16)[:, 0:128], b
    SMALL = _SmallF
    TRB = _SmallB

    ident_f, B_identf = A.alloc("ident_f", [128, 128], F32)
    ident_b, B_identb = A.alloc("ident_b", [128, 128], BF16)
    ones_b, B_onesb = A.alloc("ones_b", [128, 128], BF16)
    ones_f, B_onesf = A.alloc("ones_f", [128, 128], F32)
    SLm, B_SL = A.alloc("SL", [128, 128], F32)
    CUm, B_CU = A.alloc("CU", [64, 64], F32)
    lnw, B_lnw = A.alloc("lnw", [128, KT], F32)
    convw, B_convw = A.alloc("convw", [128, 24, 4], F32)
    dnnw, B_dnnw = A.alloc("dnnw", [128, 1], F32)
    dnp, B_dnp = A.alloc("dnp", [128, 2], F32)
    epsc, B_eps = A.alloc("epsc", [128, 1], F32)
    onec, B_one = A.alloc("onec", [128, 1], F32)

    P.x("pool", "memset", ident_f, 0.0, writes=[B_identf])
    P.x("pool", "affine_select", out=ident_f, in_=ident_f, pattern=[[-1, 128]], compare_op=ALU.not_equal,
        fill=1.0, base=0, channel_multiplier=1, reads=[B_identf], writes=[B_identf])
    P.x("pool", "tensor_copy", out=ident_b, in_=ident_f, reads=[B_identf], writes=[B_identb])
    P.x("pool", "memset", ones_b, 1.0, writes=[B_onesb])
    P.x("pool", "memset", ones_f, 1.0, writes=[B_onesf])
    P.x("pool", "memset", epsc, EPS, writes=[B_eps])
    P.x("pool", "memset", onec, 1.0, writes=[B_one])
    P.x("pool", "memset", SLm, 1.0, writes=[B_SL])
    P.x("pool", "affine_select", out=SLm, in_=SLm, pattern=[[-1, 128]], compare_op=ALU.is_gt,
        fill=0.0, base=0, channel_multiplier=1, reads=[B_SL], writes=[B_SL])
    P.x("pool", "memset", SLm[64:128, 0:64], 0.0, reads=[B_SL], writes=[B_SL])
    P.x("pool", "memset", CUm, 1.0, writes=[B_CU])
    P.x("pool", "affine_select", out=CUm, in_=CUm, pattern=[[1, 64]], compare_op=ALU.is_ge,
        fill=0.0, base=0, channel_multiplier=-1, reads=[B_CU], writes=[B_CU])
    P.d("sp", lnw, lnw_in, writes=[B_lnw])
    P.d("sp", convw, convw_in, writes=[B_convw])
    P.d("sp", dnnw, dnnw_in, writes=[B_dnnw])
    P.d("sp", dnp, dnp_in, writes=[B_dnp])

    hTo, B_hTo = H.alloc("hTo", [128, KT, LO], BF16, nbufs=NTO)
    mH = H.mark()
    if T0 > 0:
        hTp, B_hTp = H.alloc("hTp", [128, KT, T0], BF16, nbufs=NT - NTO)
    else:
        hTp, B_hTp = None, []
    B_hT = list(B_hTp) + list(B_hTo)

    def hcols(kt, t0, n):
        if t0 >= T0:
            return hTo[:, kt, t0 - T0:t0 - T0 + n]
        assert t0 + n <= T0
        return hTp[:, kt, t0:t0 + n]

    def hcols8(k0, t0, n):
        if t0 >= T0:
            return hTo[:, k0:k0 + 8, t0 - T0:t0 - T0 + n]
        return hTp[:, k0:k0 + 8, t0:t0 + n]

    wslots = [A.alloc(f"wslot{i}", [128, KT, 128], BF16) for i in range(3)]
    WS = Ring(wslots)

    wsr = {"ring": WS}

    def load_w(src_tile_ap, kt=KT):
        slot, bslot = wsr["ring"].next()
        P.d("pool", slot[:, 0:kt, :], src_tile_ap, writes=[bslot])
        return slot, bslot

    T_US, T_ZS, T_Q, T_K, T_V, T_ZD, T_BA, T_GS, T_GD = 0, 8, 16, 24, 32, 40, 48, 49, 65

    class WPipe:
        def __init__(self, tiles, depth=1):
            self.tiles = list(tiles)
            self.depth = depth
            self.q = []
            self.p = 0
            self.started = False

        def start(self):
            if not self.started:
                self.started = True
                self._fill(self.depth)

        def _fill(self, n):
            while len(self.q) < n and self.p < len(self.tiles):
                ap_, kt_ = self.tiles[self.p]
                self.q.append(load_w(ap_, kt=kt_))
                self.p += 1

        def take(self):
            self.start()
            self._fill(self.depth + 1)
            return self.q.pop(0)


    def proj_fm_g(slot, bslot, kts, rhs_fn, rhs_bufs_fn, blocks, consume, m=128, ring=None, G=None):
        blocks = list(blocks)
        rg = ring if ring is not None else cur["big"]
        if G is None:
            G = 4 if rg is BIG8 else 2
        for g0 in range(0, len(blocks), G):
            grp = blocks[g0:g0 + G]
            pss = [rg.next() for _ in grp]
            for kt in range(kts):
                for b, (ps, B_ps) in zip(grp, pss):
                    P.x("pe", "matmul", ps[0:m, :], lhsT=slot[:, kt, 0:m], rhs=rhs_fn(kt, b), start=(kt == 0), stop=(kt == kts - 1),
                        reads=[bslot] + rhs_bufs_fn(b), writes=[B_ps])
            for b, (ps, B_ps) in zip(grp, pss):
                consume(b, ps, B_ps)
            yield "P"

    def proj_fm(*a, **kw):
        for _ in proj_fm_g(*a, **kw):
            pass

    def h_rhs(kt, b):
        return hcols(kt, b * 512, 512)

    def h_bufs(b):
        return B_hT[b * 4:(b + 1) * 4]

    G_scr = nc.dram_tensor("G_scr", [128, 64 * 2 * 128], BF16).ap()
    Mi_scr = nc.dram_tensor("Mi_scr", [128, 64 * 128], BF16).ap()
    Mr_scr = nc.dram_tensor("Mr_scr", [128, 32 * 2 * 128], BF16).ap()
    A8_scr = nc.dram_tensor("A8_scr", [128, 128], F32).ap()
    B_Gscr, B_Miscr, B_Mrscr, B_A8scr = Buf("Gscr"), Buf("Miscr"), Buf("Mrscr"), Buf("A8scr")

    def s5_tables_gen():
        Gpad, B_Gpad = A.alloc("Gpad", [128, 64, 2, 128], BF16, top=True)
        Mintra, B_Mintra = A.alloc("Mintra", [128, 64, 128], BF16, top=True)
        Minter, B_Minter = A.alloc("Minter", [128, 32, 2, 128], BF16, top=True)
        A8, B_A8 = A.alloc("A8", [128, 2, 2, 32], F32, top=True)
        A8P, B_A8P = A.alloc("A8P", [128, 8, 2, 32], F32, top=True)
        A64, B_A64 = A.alloc("A64", [128, 2, 2, 32], F32, top=True)
        _wmark[0] = A.top2
        lam, B_lam = A.alloc("lam", [128, 3, 32], F32, top=True)
        sbt, B_sbt = A.alloc("sbt", [128, 2, 32, 16], F32, top=True)
        sct, B_sct = A.alloc("sct", [128, 2, 32, 16], F32, top=True)
        sdt, B_sdt = A.alloc("sdt", [128, 64], F32, top=True)
        MK, B_MK = A.alloc("MK", [128, 128], F32, top=True)
        POW, B_POW = A.alloc("POW", [128, 9, 2, 32], F32, top=True)
        NEG, B_NEG = A.alloc("NEG", [128, 8, 2, 32], F32, top=True)
        bbt, B_bbt = A.alloc("bbt", [128, 2, 32, 16], F32, top=True)
        sm = {}
        for nm in ("step", "x1", "mag", "th", "kk", "ths", "s", "c", "ar", "ai", "den", "xr", "fre", "fim", "t1", "t2", "t3", "t4", "n1", "n2", "n3", "n4", "ivr", "ivi"):
            sm[nm] = A.alloc("sm_" + nm, [128, 32], F32, top=True)
        P.d("sp", lam, lam_in, writes=[B_lam])
        P.d("sp", sbt, sb_in, writes=[B_sbt])
        P.d("sp", sct, sc_in, writes=[B_sct])
        P.d("sp", sdt, sd_in, writes=[B_sdt])
        P.x("pool", "memset", MK, 1.0, writes=[B_MK])
        P.x("pool", "affine_select", out=MK.rearrange("p (t c) -> p t c", c=16), in_=MK.rearrange("p (t c) -> p t c", c=16),
            pattern=[[16, 8], [0, 16]], compare_op=ALU.is_ge, fill=0.0, base=15, channel_multiplier=-1,
            reads=[B_MK], writes=[B_MK])
        P.x("pool", "memset", Gpad, 0.0, writes=[B_Gpad])

        def V(nm):
            return sm[nm][0]

        def BV(nm):
            return sm[nm][1]

        def tt(out, ob, a, ab, b, bb_, op, eng="dve"):
            P.x(eng, "tensor_tensor", out=out, in0=a, in1=b, op=op, reads=list(ab) + list(bb_), writes=list(ob))

        lre, lim, lst = lam[:, 0, :], lam[:, 1, :], lam[:, 2, :]
        P.x("act", "activation", out=V("step"), in_=lst, func=AF.Exp, reads=[B_lam], writes=[BV("step")])
        tt(V("x1"), [BV("x1")], lre, [B_lam], V("step"), [BV("step")], ALU.mult)
        P.x("act", "activation", out=V("mag"), in_=V("x1"), func=AF.Exp, reads=[BV("x1")], writes=[BV("mag")])
        tt(V("th"), [BV("th")], lim, [B_lam], V("step"), [BV("step")], ALU.mult)

        def sin_of(dst, shift):
            P.x("dve", "tensor_scalar", out=V("ths"), in0=V("th"), scalar1=shift, scalar2=None, op0=ALU.add,
                reads=[BV("th")], writes=[BV("ths")])
            P.x("dve", "tensor_scalar", out=V("kk"), in0=V("ths"), scalar1=PI, scalar2=None, op0=ALU.is_ge,
                reads=[BV("ths")], writes=[BV("kk")])
            for mth in range(1, 6):
                P.x("dve", "scalar_tensor_tensor", out=V("kk"), in0=V("ths"), scalar=(2 * mth + 1) * PI, in1=V("kk"),
                    op0=ALU.is_ge, op1=ALU.add, reads=[BV("ths"), BV("kk")], writes=[BV("kk")])
            P.x("dve", "scalar_tensor_tensor", out=V("ths"), in0=V("kk"), scalar=-2.0 * PI, in1=V("ths"),
                op0=ALU.mult, op1=ALU.add, reads=[BV("ths"), BV("kk")], writes=[BV("ths")])
            P.x("act", "activation", out=V(dst), in_=V("ths"), func=AF.Sin, reads=[BV("ths")], writes=[BV(dst)])
        sin_of("s", 0.0)
        yield
        sin_of("c", PI / 2)
        yield
        tt(V("ar"), [BV("ar")], V("mag"), [BV("mag")], V("c"), [BV("c")], ALU.mult)
        tt(V("ai"), [BV("ai")], V("mag"), [BV("mag")], V("s"), [BV("s")], ALU.mult)
        tt(V("t1"), [BV("t1")], lre, [B_lam], lre, [B_lam], ALU.mult)
        tt(V("t2"), [BV("t2")], lim, [B_lam], lim, [B_lam], ALU.mult)
        tt(V("den"), [BV("den")], V("t1"), [BV("t1")], V("t2"), [BV("t2")], ALU.add)
        P.x("dve", "reciprocal", out=V("den"), in_=V("den"), reads=[BV("den")], writes=[BV("den")])
        P.x("dve", "tensor_scalar", out=V("xr"), in0=V("ar"), scalar1=-1.0, scalar2=None, op0=ALU.add, reads=[BV("ar")], writes=[BV("xr")])
        tt(V("t1"), [BV("t1")], V("xr"), [BV("xr")], lre, [B_lam], ALU.mult)
        tt(V("t2"), [BV("t2")], V("ai"), [BV("ai")], lim, [B_lam], ALU.mult)
        tt(V("t1"), [BV("t1")], V("t1"), [BV("t1")], V("t2"), [BV("t2")], ALU.add)
        tt(V("fre"), [BV("fre")], V("t1"), [BV("t1")], V("den"), [BV("den")], ALU.mult)
        tt(V("t3"), [BV("t3")], V("ai"), [BV("ai")], lre, [B_lam], ALU.mult)
        tt(V("t4"), [BV("t4")], V("xr"), [BV("xr")], lim, [B_lam], ALU.mult)
        tt(V("t3"), [BV("t3")], V("t3"), [BV("t3")], V("t4"), [BV("t4")], ALU.subtract)
        tt(V("fim"), [BV("fim")], V("t3"), [BV("t3")], V("den"), [BV("den")], ALU.mult)
        tt(V("t1"), [BV("t1")], V("mag"), [BV("mag")], V("mag"), [BV("mag")], ALU.mult)
        P.x("dve", "reciprocal", out=V("t1"), in_=V("t1"), reads=[BV("t1")], writes=[BV("t1")])
        tt(V("ivr"), [BV("ivr")], V("ar"), [BV("ar")], V("t1"), [BV("t1")], ALU.mult)
        tt(V("t2"), [BV("t2")], V("ai"), [BV("ai")], V("t1"), [BV("t1")], ALU.mult)
        P.x("dve", "tensor_scalar", out=V("ivi"), in0=V("t2"), scalar1=-1.0, scalar2=None, op0=ALU.mult, reads=[BV("t2")], writes=[BV("ivi")])

        def cmul_small(dst, dbuf, j, src, sbuf, jm, mr, mrb, mi, mib, eng="dve", tp="t"):
            pr, pi = src[:, jm, 0, :], src[:, jm, 1, :]
            n1, n2, n3, n4 = tp + "1", tp + "2", tp + "3", tp + "4"
            tt(V(n1), [BV(n1)], pr, [sbuf], mr, [mrb], ALU.mult, eng=eng)
            tt(V(n2), [BV(n2)], pi, [sbuf], mi, [mib], ALU.mult, eng=eng)
            tt(dst[:, j, 0, :], [dbuf], V(n1), [BV(n1), dbuf], V(n2), [BV(n2)], ALU.subtract, eng=eng)
            tt(V(n3), [BV(n3)], pr, [sbuf], mi, [mib], ALU.mult, eng=eng)
            tt(V(n4), [BV(n4)], pi, [sbuf], mr, [mrb], ALU.mult, eng=eng)
            tt(dst[:, j, 1, :], [dbuf], V(n3), [BV(n3), dbuf], V(n4), [BV(n4)], ALU.add, eng=eng)

        P.x("pool", "memset", POW[:, 0, 0, :], 1.0, writes=[B_POW])
        P.x("pool", "memset", POW[:, 0, 1, :], 0.0, reads=[B_POW], writes=[B_POW])
        P.x("pool", "memset", NEG[:, 0, 0, :], 1.0, writes=[B_NEG])
        P.x("pool", "memset", NEG[:, 0, 1, :], 0.0, reads=[B_NEG], writes=[B_NEG])
        for j in range(1, 9):
            cmul_small(POW, B_POW, j, POW, B_POW, j - 1, V("ar"), BV("ar"), V("ai"), BV("ai"))
            yield
        for j in range(1, 8):
            cmul_small(NEG, B_NEG, j, NEG, B_NEG, j - 1, V("ivr"), BV("ivr"), V("ivi"), BV("ivi"), eng="pool", tp="n")
            yield
        for r in range(2):
            P.x("pool", "tensor_copy", out=A8[:, 0, r, :], in_=POW[:, 8, 0, :], reads=[B_POW, B_A8], writes=[B_A8])
            P.x("pool", "tensor_copy", out=A8[:, 1, r, :], in_=POW[:, 8, 1, :], reads=[B_POW, B_A8], writes=[B_A8])
        for r in range(2):
            P.x("pool", "tensor_copy", out=A8P[:, 0, r, :], in_=POW[:, 8, r, :], reads=[B_POW, B_A8P], writes=[B_A8P])
        for k in range(1, 8):
            cmul_small(A8P, B_A8P, k, A8P, B_A8P, k - 1, POW[:, 8, 0, :], B_POW, POW[:, 8, 1, :], B_POW)
        for r in range(2):
            P.x("pool", "tensor_copy", out=A64[:, 0, r, :], in_=A8P[:, 7, 0, :], reads=[B_A8P, B_A64], writes=[B_A64])
            P.x("pool", "tensor_copy", out=A64[:, 1, r, :], in_=A8P[:, 7, 1, :], reads=[B_A8P, B_A64], writes=[B_A64])
        yield
        fre_b = V("fre").unsqueeze(2).to_broadcast([128, 32, 16])
        fim_b = V("fim").unsqueeze(2).to_broadcast([128, 32, 16])
        big1, B_big1 = A.alloc("big1", [128, 32, 16], F32, top=True)
        big2, B_big2 = A.alloc("big2", [128, 32, 16], F32, top=True)
        tt(big1, [B_big1], sbt[:, 0], [B_sbt], fre_b, [BV("fre")], ALU.mult)
        tt(big2, [B_big2], sbt[:, 1], [B_sbt], fim_b, [BV("fim")], ALU.mult)
        tt(bbt[:, 0], [B_bbt], big1, [B_big1], big2, [B_big2], ALU.subtract)
        tt(big1, [B_big1], sbt[:, 1], [B_sbt], fre_b, [BV("fre")], ALU.mult)
        tt(big2, [B_big2], sbt[:, 0], [B_sbt], fim_b, [BV("fim")], ALU.mult)
        tt(bbt[:, 1], [B_bbt], big1, [B_big1, B_bbt], big2, [B_big2], ALU.add)

        GB = 2
        Pt, B_Pt = A.alloc("Pt", [128, 2, GB, 8, 16], F32, top=True)
        GTt, B_GTt = A.alloc("GTt", [128, 2, GB, 8, 16], F32, top=True)
        Qt, B_Qt = A.alloc("Qt", [128, 2, GB, 9, 16], F32, top=True)
        w1, B_w1 = A.alloc("w1", [128, GB, 9, 16], F32, top=True)
        w2, B_w2 = A.alloc("w2", [128, GB, 9, 16], F32, top=True)
        mtmp, B_mtmp = A.alloc("mtmp", [128, 128], F32, top=True)
        q1, B_q1 = A.alloc("q1", [128, GB, 9, 16], F32, top=True)
        q2, B_q2 = A.alloc("q2", [128, GB, 9, 16], F32, top=True)
        qz, B_qz = A.alloc("qz", [128, GB, 9, 16], F32, top=True)
        P.x("pool", "memset", qz, 0.0, writes=[B_qz])
        for gb in range(32 // GB):
            g0 = gb * GB
            gs = slice(g0, g0 + GB)
            sh = [128, GB, 8, 16]
            nr = NEG[:, :, 0, gs].rearrange("p t g -> p g t").unsqueeze(3).to_broadcast(sh)
            ni = NEG[:, :, 1, gs].rearrange("p t g -> p g t").unsqueeze(3).to_broadcast(sh)
            br = bbt[:, 0, gs, :].unsqueeze(2).to_broadcast(sh)
            bi = bbt[:, 1, gs, :].unsqueeze(2).to_broadcast(sh)
            a1, a2 = w1[:, :, 0:8, :], w2[:, :, 0:8, :]
            tt(a1, [B_w1], nr, [B_NEG], br, [B_bbt], ALU.mult)
            tt(a2, [B_w2], ni, [B_NEG], bi, [B_bbt], ALU.mult)
            tt(Pt[:, 0], [B_Pt], a1, [B_w1], a2, [B_w2], ALU.subtract)
            tt(a1, [B_w1], nr, [B_NEG], bi, [B_bbt], ALU.mult)
            tt(a2, [B_w2], ni, [B_NEG], br, [B_bbt], ALU.mult)
            tt(Pt[:, 1], [B_Pt], a1, [B_w1, B_Pt], a2, [B_w2], ALU.add)
            yield
            p7r = POW[:, 7, 0, gs].unsqueeze(2).unsqueeze(3).to_broadcast(sh)
            p7i = POW[:, 7, 1, gs].unsqueeze(2).unsqueeze(3).to_broadcast(sh)
            tt(a1, [B_w1], Pt[:, 0], [B_Pt], p7r, [B_POW], ALU.mult)
            tt(a2, [B_w2], Pt[:, 1], [B_Pt], p7i, [B_POW], ALU.mult)
            tt(GTt[:, 0], [B_GTt], a1, [B_w1], a2, [B_w2], ALU.subtract)
            tt(a1, [B_w1], Pt[:, 0], [B_Pt], p7i, [B_POW], ALU.mult)
            tt(a2, [B_w2], Pt[:, 1], [B_Pt], p7r, [B_POW], ALU.mult)
            tt(GTt[:, 1], [B_GTt], a1, [B_w1, B_GTt], a2, [B_w2], ALU.add)
            yield
            shq = [128, GB, 9, 16]
            pr = POW[:, :, 0, gs].rearrange("p j g -> p g j").unsqueeze(3).to_broadcast(shq)
            pi_ = POW[:, :, 1, gs].rearrange("p j g -> p g j").unsqueeze(3).to_broadcast(shq)
            cr = sct[:, 0, gs, :].unsqueeze(2).to_broadcast(shq)
            ci = sct[:, 1, gs, :].unsqueeze(2).to_broadcast(shq)
            tt(q1, [B_q1], cr, [B_sct], pr, [B_POW], ALU.mult, eng="pool")
            tt(q2, [B_q2], ci, [B_sct], pi_, [B_POW], ALU.mult, eng="pool")
            tt(Qt[:, 0], [B_Qt], q1, [B_q1], q2, [B_q2], ALU.subtract, eng="pool")
            tt(q1, [B_q1], cr, [B_sct], pi_, [B_POW], ALU.mult, eng="pool")
            tt(q2, [B_q2], ci, [B_sct], pr, [B_POW], ALU.mult, eng="pool")
            tt(q1, [B_q1], q1, [B_q1], q2, [B_q2], ALU.add, eng="pool")
            tt(Qt[:, 1], [B_Qt], qz, [B_qz], q1, [B_q1, B_Qt], ALU.subtract, eng="pool")
            for ri in range(2):
                P.x("pool", "tensor_copy", out=Minter[:, gs, ri, :].rearrange("p g (t c) -> p g t c", c=16),
                    in_=Qt[:, ri, :, 1:9, :], reads=[B_Qt, B_Minter], writes=[B_Minter])
            yield
            for gl in range(GB):
                for half in range(2):
                    yield
                    g = half * 32 + g0 + gl
                    base = half * 64
                    rows = slice(base, base + 64)
                    ps, B_ps = SMALL.next()
                    P.x("pe", "matmul", ps, lhsT=Pt[rows, 0, gl].rearrange("p t c -> p (t c)"),
                        rhs=Qt[rows, 0, gl, 0:8, :].rearrange("p t c -> p (t c)"), start=True, stop=False,
                        reads=[B_Pt, B_Qt], writes=[B_ps])
                    P.x("pe", "matmul", ps, lhsT=Pt[rows, 1, gl].rearrange("p t c -> p (t c)"),
                        rhs=Qt[rows, 1, gl, 0:8, :].rearrange("p t c -> p (t c)"), start=False, stop=True,
                        reads=[B_Pt, B_Qt], writes=[B_ps])
                    P.x("dve", "tensor_tensor", out=mtmp, in0=ps, in1=MK, op=ALU.mult, reads=[B_ps, B_MK], writes=[B_mtmp])
                    P.x("dve", "scalar_tensor_tensor", out=Mintra[:, g, :], in0=ident_f, scalar=sdt[:, g:g + 1], in1=mtmp,
                        op0=ALU.mult, op1=ALU.add, reads=[B_identf, B_sdt, B_mtmp, B_Mintra], writes=[B_Mintra])
                    for ri in range(2):
                        ps2, B_ps2 = SMALL.next()
                        P.x("pe", "transpose", ps2[:, 0:64], GTt[rows, ri, gl].rearrange("p t c -> p (t c)"),
                            ident_f[rows, base:base + 64], reads=[B_GTt, B_identf], writes=[B_ps2])
                        P.x("act", "activation", out=Gpad[:, g, ri, base:base + 64], in_=ps2[:, 0:64], func=AF.Copy,
                            reads=[B_ps2, B_Gpad], writes=[B_Gpad])

        A.top2 = _wmark[0]
        _tres.update(Gpad=(Gpad, B_Gpad), Mintra=(Mintra, B_Mintra), Minter=(Minter, B_Minter), A8=(A8, B_A8), A8P=(A8P, B_A8P), A64=(A64, B_A64))

    _wmark = [None]
    _tres = {}
    _tg = [s5_tables_gen()]

    def tick(n=1):
        for _ in range(n):
            if _tg[0] is None:
                return
            try:
                next(_tg[0])
            except StopIteration:
                _tg[0] = None
                return

    def drain_tables():
        while _tg[0] is not None:
            tick()

    m1 = A.mark()
    xts = [A.alloc(f"xt{i}", [128, D], F32) for i in range(2)]
    xss = [A.alloc(f"xs{i}", [128, D], BF16) for i in range(2)]
    ss, B_ss = A.alloc("ss", [128, NT], F32, nbufs=NT)
    rstd, B_rstd = A.alloc("rstd", [128, NT], F32, nbufs=NT)
    for tt in range(NT):
        xt, B_xt = xts[tt % 2]
        xs, B_xs = xss[tt % 2]
        P.d("sp", xt, x_in[tt * 128:(tt + 1) * 128, :], writes=[B_xt])
        P.x("act", "activation", out=xs, in_=xt, func=AF.Square, accum_out=ss[:, tt:tt + 1],
            reads=[B_xt], writes=[B_xs, B_ss[tt]])
        P.x("act", "activation", out=rstd[:, tt:tt + 1], in_=ss[:, tt:tt + 1], func=AF.Sqrt,
            bias=epsc[:, 0:1], scale=1.0 / D, reads=[B_ss[tt], B_eps], writes=[B_rstd[tt]])
        P.x("dve", "reciprocal", out=rstd[:, tt:tt + 1], in_=rstd[:, tt:tt + 1], reads=[B_rstd[tt]], writes=[B_rstd[tt]])
        P.x("act", "activation", out=xs, in_=xt, func=AF.Copy, scale=rstd[:, tt:tt + 1],
            reads=[B_xt, B_rstd[tt]], writes=[B_xs])
        for half in range(2):
            psf, B_psf = cur["big"].next()
            psb = psf.bitcast(BF16)
            for k in range(8):
                kt = half * 8 + k
                P.x("pe", "transpose", psb[:, k * 128:(k + 1) * 128], xs[:, kt * 128:(kt + 1) * 128], ident_b,
                    reads=[B_xs, B_identb], writes=[B_psf])
            P.x("dve", "tensor_tensor", out=hcols8(half * 8, tt * 128, 128),
                in0=psb.rearrange("p (k t) -> p k t", k=8),
                in1=lnw[:, half * 8:half * 8 + 8].unsqueeze(2).to_broadcast([128, 8, 128]), op=ALU.mult,
                reads=[B_psf, B_lnw], writes=[B_hT[tt]])
    A.release(m1)
    P.mark("stage1")
    dump("hTo", hTo, B_hTo, [128, KT, LO], BF16)
    if stop_after == "stage1":
        P.emit()
        return nc, dbg

    m1 = A.mark()
    usts = [A.alloc(f"ust{i}", [128, 8, NC8], BF16) for i in range(2)]
    u_pipe = WPipe([(w_in_t[T_US + j], KT) for j in range(8)], depth=1)
    B_U = [Buf(f"U{j}") for j in range(8)]
    for j in range(8):
        slot, bslot = u_pipe.take()
        ust, B_ust = usts[j % 2]

        def cons_u(b, ps, B_ps, ust=ust, B_ust=B_ust):
            P.x("act", "activation", out=ust[:, :, b * 64:(b + 1) * 64].rearrange("p t c -> p c t"),
                in_=ps.rearrange("p (c t) -> p c t", t=8), func=AF.Copy, reads=[B_ps], writes=[B_ust])
        proj_fm(slot, bslot, KT, h_rhs, h_bufs, range(NB), cons_u)
        P.d("sp", U_scr[j * 128:(j + 1) * 128], ust, reads=[B_ust], writes=[B_U[j]])
    A.release(m1)
    P.mark("u")
    if stop_after == "u":
        P.emit()
        return nc, dbg

    dnoT, B_dno = A.alloc("dnoT", [128, NH, LO], BF16, nbufs=NH)
    m_dn0 = A.mark()
    GC, B_GC = A.alloc("GC", [128, LT], F32)
    PT, B_PT = A.alloc("PT", [128, NP, 4, 8], F32)
    CT, B_CT = A.alloc("CT", [64, NCH, 3, 8], F32)
    GLB, B_GLB = A.alloc("GLB", [128, NCH * 8], F32)
    SEL, B_SEL = A.alloc("SEL", [128, 8, 128], F32)
    m0 = A.mark()
    G1, B_G1 = A.alloc("G1", [128, LT], F32)
    RM, B_RM = A.alloc("RM", [128, LT], F32)
    PTin, B_PTin = A.alloc("PTin", [128, LT], F32)
    CTin, B_CTin = A.alloc("CTin", [128, LT], F32)
    Dg, B_Dg = A.alloc("Dg", [128, NCH, 8], F32)
    negA, B_negA = A.alloc("negA", [128, 1], F32)

    P.x("pool", "memset", PTin, 0.0, writes=[B_PTin])
    P.x("pool", "memset", CTin, 0.0, writes=[B_CTin])
    P.x("pool", "memset", G1, 0.0, writes=[B_G1])
    P.x("pool", "memset", RM, 1.0, writes=[B_RM])
    P.x("pool", "memset", RM.rearrange("p (c t) -> p c t", t=64)[:, :, 0:1], 0.0, reads=[B_RM], writes=[B_RM])
    P.x("pool", "tensor_copy", out=SEL[32:40], in_=ident_f[32:40, 32:40].unsqueeze(2).to_broadcast([8, 8, 128]),
        reads=[B_identf], writes=[B_SEL])
    P.x("act", "activation", out=negA, in_=dnp[:, 0:1], func=AF.Exp, reads=[B_dnp], writes=[B_negA])
    P.x("dve", "tensor_scalar", out=negA, in0=negA, scalar1=-1.0, scalar2=None, op0=ALU.mult, reads=[B_negA], writes=[B_negA])

    wba, B_wba = load_w(w_in_t[T_BA])

    def cons_ba(b, ps, B_ps):
        sl = slice(b * 512, (b + 1) * 512)
        P.x("act", "activation", out=PTin[0:8, sl], in_=ps[0:8, :], func=AF.Sigmoid, reads=[B_ps], writes=[B_PTin])
        for (ra, rb) in ((32, 40), (64, 104)):
            P.x("act", "activation", out=G1[ra:rb, sl], in_=ps[ra:rb, :], func=AF.Exp, bias=dnp[ra:rb, 1:2], scale=1.0,
                reads=[B_ps, B_dnp], writes=[B_G1])
    proj_fm(wba, B_wba, KT, h_rhs, h_bufs, range(NB), cons_ba, m=104)
    for (ra, rb) in ((32, 40), (64, 104)):
        P.x("act", "activation", out=G1[ra:rb, :], in_=G1[ra:rb, :], func=AF.Ln, bias=onec[ra:rb, 0:1], scale=1.0,
            reads=[B_G1, B_one], writes=[B_G1])
        P.x("dve", "tensor_scalar", out=G1[ra:rb, :], in0=G1[ra:rb, :], scalar1=negA[ra:rb, 0:1], scalar2=None, op0=ALU.mult,
            reads=[B_G1, B_negA], writes=[B_G1])
        P.x("dve", "tensor_tensor_scan", out=GC[ra:rb, :], data0=RM[ra:rb, :], data1=G1[ra:rb, :], initial=0.0,
            op0=ALU.mult, op1=ALU.add, reads=[B_RM, B_G1], writes=[B_GC])
    P.x("pool", "tensor_copy", out=PTin[32:40, :], in_=GC[32:40, :], reads=[B_GC, B_PTin], writes=[B_PTin])
    P.x("act", "activation", out=PTin[64:72, :], in_=GC[64:72, :], func=AF.Exp, reads=[B_GC, B_PTin], writes=[B_PTin])
    P.x("dve", "tensor_scalar", out=CTin[32:40, :], in0=GC[32:40, :], scalar1=-1.0, scalar2=None, op0=ALU.mult,
        reads=[B_GC, B_CTin], writes=[B_CTin])
    P.x("act", "activation", out=CTin[64:72, :], in_=GC[64:72, :], func=AF.Exp, reads=[B_GC, B_CTin], writes=[B_CTin])
    gc3 = GC[96:104, :].rearrange("p (c t) -> p c t", t=64)
    P.x("dve", "tensor_tensor", out=CTin[96:104, :].rearrange("p (c t) -> p c t", t=64),
        in0=gc3[:, :, 63:64].to_broadcast([8, NCH, 64]), in1=gc3, op=ALU.subtract,
        reads=[B_GC, B_CTin], writes=[B_CTin])
    P.x("act", "activation", out=CTin[96:104, :], in_=CTin[96:104, :], func=AF.Exp, reads=[B_CTin], writes=[B_CTin])
    P.x("pool", "tensor_copy", out=PTin[96:104, :], in_=CTin[96:104, :], reads=[B_CTin, B_PTin], writes=[B_PTin])
    eg3 = PTin[64:72, :].rearrange("p (c t) -> p c t", t=64)
    P.x("dve", "tensor_tensor", out=Dg[64:72], in0=eg3[:, :, 63:64].to_broadcast([8, NCH, 8]),
        in1=ident_f[64:72, 64:72].unsqueeze(1).to_broadcast([8, NCH, 8]), op=ALU.mult,
        reads=[B_PTin, B_identf], writes=[B_Dg])
    psg, B_psg = BIG.next()
    P.x("pe", "matmul", psg[:, 0:NCH * 8], lhsT=ones_f[64:72, :], rhs=Dg[64:72].rearrange("p c h -> p (c h)"),
        start=True, stop=True, reads=[B_onesf, B_Dg], writes=[B_psg])
    P.x("act", "activation", out=GLB, in_=psg[:, 0:NCH * 8], func=AF.Copy, reads=[B_psg], writes=[B_GLB])
    for i in range(NP):
        ps, B_ps = SMALL.next()
        P.x("pe", "transpose", ps[:, 0:104], PTin[0:104, i * 128:(i + 1) * 128], ident_f[0:104, 0:104],
            reads=[B_PTin, B_identf], writes=[B_ps])
        pv = ps[:, 0:128].rearrange("p (k c) -> p k c", c=32)[:, :, 0:8]
        if i % 2:
            P.x("dve", "tensor_copy", out=PT[:, i], in_=pv, reads=[B_ps], writes=[B_PT])
        else:
            P.x("act", "activation", out=PT[:, i], in_=pv, func=AF.Copy, reads=[B_ps], writes=[B_PT])
    for ch in range(NCH):
        ps, B_ps = SMALL.next()
        P.x("pe", "transpose", ps[0:64, 0:104], CTin[0:104, ch * 64:(ch + 1) * 64], ident_f[0:104, 0:104],
            reads=[B_CTin, B_identf], writes=[B_ps])
        cv = ps[0:64, 32:128].rearrange("p (k c) -> p k c", c=32)[:, :, 0:8]
        if ch % 2:
            P.x("dve", "tensor_copy", out=CT[:, ch], in_=cv, reads=[B_ps], writes=[B_CT])
        else:
            P.x("act", "activation", out=CT[:, ch], in_=cv, func=AF.Copy, reads=[B_ps], writes=[B_CT])
    A.release(m0)
    P.mark("dn0")
    dump("PT", PT, B_PT, [128, NP, 4, 8])
    dump("CT", CT, B_CT, [64, NCH, 3, 8])
    dump("GLB", GLB, B_GLB, [128, NCH * 8])
    if stop_after == "dn0":
        P.emit()
        return nc, dbg

    HG = 2
    cur["big"] = BIG
    m_dn = A.mark()
    pre, B_pre = A.alloc("pre", [128, 3 + LT], F32)
    P.x("pool", "memset", pre[:, 0:3], 0.0, writes=[B_pre])
    acc, B_acc = A.alloc("acc", [128, LT], F32)
    MnSL, B_MnSL = A.alloc("MnSL", [128, 128], F32)
    MnCU, B_MnCU = A.alloc("MnCU", [64, 64], F32)
    P.x("dve", "tensor_scalar", out=MnSL, in0=SLm, scalar1=-1.0, scalar2=30000.0, op0=ALU.add, op1=ALU.mult,
        reads=[B_SL], writes=[B_MnSL])
    P.x("dve", "tensor_scalar", out=MnCU, in0=CUm, scalar1=-1.0, scalar2=30000.0, op0=ALU.add, op1=ALU.mult,
        reads=[B_CU], writes=[B_MnCU])
    MnCUp, B_MnCUp = A.alloc("MnCUp", [128, 128], F32)
    P.x("pool", "memset", MnCUp, 1.0, writes=[B_MnCUp])
    P.x("pool", "affine_select", out=MnCUp, in_=MnCUp, pattern=[[1, 128]], compare_op=ALU.is_ge,
        fill=0.0, base=0, channel_multiplier=-1, reads=[B_MnCUp], writes=[B_MnCUp])
    P.x("pool", "memset", MnCUp[0:64, 64:128], 0.0, reads=[B_MnCUp], writes=[B_MnCUp])
    P.x("dve", "tensor_scalar", out=MnCUp, in0=MnCUp, scalar1=-1.0, scalar2=30000.0, op0=ALU.add, op1=ALU.mult,
        reads=[B_MnCUp], writes=[B_MnCUp])
    sq, B_sq = A.alloc("sq", [128, LT], BF16)
    RN = Ring([A.alloc(f"rn{i}", [128, 512], F32) for i in range(1)])

    def ring(name, n, shape, dt):
        return Ring([A.alloc(f"{name}{i}", shape, dt) for i in range(n)])

    slots = []
    for s_ in range(HG):
        d_ = {}
        for nm in ("qT", "kT", "vT"):
            d_[nm] = A.alloc(f"{nm}{s_}", [128, LT], BF16)
        d_["szd"] = A.alloc(f"szd{s_}", [128, LO], BF16)
        d_["S_f"] = A.alloc(f"S_f{s_}", [128, 128], F32)
        d_["S_b"] = A.alloc(f"S_b{s_}", [128, 128], BF16)
        d_["E"] = ring(f"E{s_}_", 2, [128, 128], F32)
        d_["A"] = ring(f"Am{s_}_", 23, [128, 128], BF16)
        d_["P"] = ring(f"Pm{s_}_", 10, [128, 128], BF16)
        d_["bv"] = ring(f"bv{s_}_", 3, [128, 128], BF16)
        d_["kbg"] = ring(f"kbg{s_}_", 3, [128, 128], BF16)
        d_["wT"] = ring(f"wT{s_}_", 3, [128, 128], BF16)
        d_["TT"] = ring(f"TT{s_}_", 3, [128, 128], BF16)
        d_["u"] = ring(f"u{s_}_", 3, [128, 128], F32)
        d_["kd"] = ring(f"kd{s_}_", 3, [128, 128], BF16)
        d_["ET"] = ring(f"ET{s_}_", 2, [128, 128], F32)
        d_["qk"] = ring(f"qk{s_}_", 3, [128, 128], BF16)
        d_["vn"] = ring(f"vn{s_}_", 2, [128, 128], BF16)
        d_["wTz"] = ring(f"wTz{s_}_", 3, [128, 128], BF16)
        for (wz_, B_wz_) in d_["wTz"].items:
            P.x("pool", "memset", wz_, 0.0, writes=[B_wz_])
        d_["o1"] = ring(f"o1{s_}_", 2, [64, 128], F32)
        d_["o"] = ring(f"o{s_}_", 2, [64, 128], F32)
        d_["on"] = ring(f"on{s_}_", 2, [64, 128], BF16)
        d_["st"] = ring(f"st{s_}_", 4, [64, 2], F32)
        d_["ojunk"] = A.alloc(f"ojunk{s_}", [64, 128], BF16)
        slots.append(d_)

    QSCALE = 128.0 ** -0.5

    def head_proj(h, sl_):
        qT, B_qT = sl_["qT"]
        kT, B_kT = sl_["kT"]
        vT, B_vT = sl_["vT"]
        szd, B_szd = sl_["szd"]
        deferred = []

        def l2norm(idx, dst, B_dst):
            for b in range(NB):
                sl = slice(b * 512, (b + 1) * 512)
                ps, B_ps = BIG.next()
                P.x("pe", "matmul", ps, lhsT=ones_b, rhs=sq[:, sl], start=True, stop=True,
                    reads=[B_onesb, B_sq], writes=[B_ps])
                rn, B_rn = RN.next()
                P.x("act", "activation", out=rn, in_=ps, func=AF.Sqrt, bias=epsc[:, 0:1], scale=1.0,
                    reads=[B_ps, B_eps], writes=[B_rn])
                P.x("dve", "reciprocal", out=rn, in_=rn, reads=[B_rn], writes=[B_rn])
                P.x("dve", "scalar_tensor_tensor", out=dst[:, sl], in0=acc[:, sl], scalar=(QSCALE if idx == 0 else 1.0),
                    in1=rn, op0=ALU.mult, op1=ALU.mult, reads=[B_acc, B_rn], writes=[B_dst])

        for idx, (cbase, dst, B_dst) in enumerate(((T_Q, qT, B_qT), (T_K, kT, B_kT), (T_V, vT, B_vT))):
            slot, bslot = dn_pipe.take()

            def cons_pre(b, ps, B_ps):
                P.x("act", "activation", out=pre[:, 3 + b * 512:3 + (b + 1) * 512], in_=ps, func=AF.Copy,
                    reads=[B_ps], writes=[B_pre])
            yield from proj_fm_g(slot, bslot, KT, h_rhs, h_bufs, range(NB), cons_pre, ring=BIG8, G=4)
            while deferred:
                deferred.pop(0)()
            tile = idx * 8 + h
            P.x("dve", "tensor_scalar", out=acc, in0=pre[:, 0:LT], scalar1=convw[:, tile, 0:1], scalar2=None, op0=ALU.mult,
                reads=[B_pre, B_convw], writes=[B_acc])
            for j in range(1, 4):
                P.x("dve", "scalar_tensor_tensor", out=acc, in0=pre[:, j:j + LT], scalar=convw[:, tile, j:j + 1], in1=acc,
                    op0=ALU.mult, op1=ALU.add, reads=[B_pre, B_convw, B_acc], writes=[B_acc])
            yield "P"
            if idx == 2:
                P.x("act", "activation", out=vT, in_=acc, func=AF.Silu, reads=[B_acc], writes=[B_vT])
            else:
                P.x("act", "activation", out=acc, in_=acc, func=AF.Silu, reads=[B_acc], writes=[B_acc])
                P.x("act", "activation", out=sq, in_=acc, func=AF.Square, reads=[B_acc], writes=[B_sq])
                deferred.append(lambda idx=idx, dst=dst, B_dst=B_dst: l2norm(idx, dst, B_dst))
        slot, bslot = dn_pipe.take()

        def cons_zd(b, ps, B_ps):
            bo = b - (NB - NBO)
            P.x("act", "activation", out=szd[:, bo * 512:(bo + 1) * 512], in_=ps, func=AF.Silu, reads=[B_ps], writes=[B_szd])
        yield from proj_fm_g(slot, bslot, KT, h_rhs, h_bufs, range(NB - NBO, NB), cons_zd, ring=BIG8, G=4)
        while deferred:
            deferred.pop(0)()

    def intra_gen(h, i, sl_):
        qT, B_qT = sl_["qT"]
        kT, B_kT = sl_["kT"]
        vT, B_vT = sl_["vT"]
        tok = slice(i * 128, (i + 1) * 128)
        psD, B_psD = SMALL.next()
        P.x("pe", "matmul", psD, lhsT=SEL[32:40, h, :], rhs=GC[32:40, tok], start=True, stop=True,
            reads=[B_SEL, B_GC], writes=[B_psD])
        E, B_E = sl_["E"].next()
        gcp = PT[:, i, 1, h:h + 1]
        P.x("dve", "scalar_tensor_tensor", out=E, in0=psD, scalar=gcp, in1=MnSL, op0=ALU.subtract, op1=ALU.subtract,
            reads=[B_psD, B_PT, B_MnSL], writes=[B_E])
        ETp, B_ETp = sl_["ET"].next()
        P.x("dve", "scalar_tensor_tensor", out=ETp, in0=psD, scalar=gcp, in1=MnCUp, op0=ALU.subtract, op1=ALU.add,
            reads=[B_psD, B_PT, B_MnCUp], writes=[B_ETp])
        P.x("act", "activation", out=E, in_=E, func=AF.Exp, scale=-1.0, reads=[B_E], writes=[B_E])
        P.x("act", "activation", out=ETp, in_=ETp, func=AF.Exp, reads=[B_ETp], writes=[B_ETp])
        yield
        pskk, B_pskk = SMALL.next()
        P.x("pe", "matmul", pskk, lhsT=kT[:, tok], rhs=kT[:, tok], start=True, stop=True, reads=[B_kT], writes=[B_pskk])
        Am, B_Am = sl_["A"].next()
        P.x("dve", "scalar_tensor_tensor", out=Am, in0=pskk, scalar=PT[:, i, 0, h:h + 1], in1=E, op0=ALU.mult, op1=ALU.mult,
            reads=[B_pskk, B_PT, B_E], writes=[B_Am])
        yield
        psB, B_psB = TRB.next()
        P.x("pe", "transpose", psB, Am, ident_b, reads=[B_Am, B_identb], writes=[B_psB])
        Bm, B_Bm = sl_["A"].next()
        P.x("act", "activation", out=Bm, in_=psB, func=AF.Copy, reads=[B_psB], writes=[B_Bm])
        P0, B_P0 = sl_["P"].next()
        P.x("pool", "tensor_tensor", out=P0, in0=ident_b, in1=Bm, op=ALU.subtract, reads=[B_identb, B_Bm], writes=[B_P0])
        yield
        psv, B_psv = TRB.next()
        P.x("pe", "transpose", psv, vT[:, tok], ident_b, reads=[B_vT, B_identb], writes=[B_psv])
        bv, B_bv = sl_["bv"].next()
        P.x("act", "activation", out=bv, in_=psv, func=AF.Copy, scale=PT[:, i, 0, h:h + 1], reads=[B_psv, B_PT], writes=[B_bv])
        psk, B_psk = TRB.next()
        P.x("pe", "transpose", psk, kT[:, tok], ident_b, reads=[B_kT, B_identb], writes=[B_psk])
        kbg, B_kbg = sl_["kbg"].next()
        P.x("dve", "tensor_scalar", out=kbg, in0=psk, scalar1=PT[:, i, 0, h:h + 1], scalar2=PT[:, i, 2, h:h + 1],
            op0=ALU.mult, op1=ALU.mult, reads=[B_psk, B_PT], writes=[B_kbg])
        kdp, B_kdp = sl_["kd"].next()
        P.x("dve", "tensor_scalar", out=kdp, in0=psk, scalar1=PT[:, i, 3, h:h + 1], scalar2=None, op0=ALU.mult,
            reads=[B_psk, B_PT], writes=[B_kdp])
        yield
        pskq, B_pskq = SMALL.next()
        P.x("pe", "matmul", pskq, lhsT=kT[:, tok], rhs=qT[:, tok], start=True, stop=True, reads=[B_kT, B_qT], writes=[B_pskq])
        qkp, B_qkp = sl_["qk"].next()
        P.x("dve", "tensor_tensor", out=qkp, in0=pskq, in1=ETp, op=ALU.mult, reads=[B_pskq, B_ETp], writes=[B_qkp])
        yield
        chunks = []
        for xh in range(2):
            chunks.append(dict(ch=2 * i + xh, xh=xh, R=slice(64 * xh, 64 * xh + 64),
                               ctok=slice(i * 128 + 64 * xh, i * 128 + 64 * xh + 64)))
        Ac, B_Ac, Bc, B_Bc, Pc, B_Pc = Am, B_Am, Bm, B_Bm, P0, B_P0
        for lvl in range(5):
            psA, B_psA = SMALL.next()
            P.x("pe", "matmul", psA, lhsT=Bc, rhs=Ac, start=True, stop=True, reads=[B_Bc, B_Ac], writes=[B_psA])
            A2, B_A2 = sl_["A"].next()
            P.x("dve", "tensor_copy", out=A2, in_=psA, reads=[B_psA], writes=[B_A2])
            if lvl < 4:
                psB2, B_psB2 = SMALL.next()
                P.x("pe", "matmul", psB2, lhsT=Ac, rhs=Bc, start=True, stop=True, reads=[B_Bc, B_Ac], writes=[B_psB2])
                B2, B_B2 = sl_["A"].next()
                P.x("act", "activation", out=B2, in_=psB2, func=AF.Copy, reads=[B_psB2], writes=[B_B2])
            else:
                B2, B_B2 = None, None
            yield
            psP, B_psP = SMALL.next()
            P.x("pe", "matmul", psP, lhsT=A2, rhs=Pc, start=True, stop=True, reads=[B_A2, B_Pc], writes=[B_psP])
            if lvl < 4:
                Pn, B_Pn = sl_["P"].next()
            else:
                Pn, B_Pn = sl_["TT"].next()
            P.x("dve", "tensor_tensor", out=Pn, in0=Pc, in1=psP, op=ALU.add, reads=[B_Pc, B_psP], writes=[B_Pn])
            Ac, B_Ac, Bc, B_Bc, Pc, B_Pc = A2, B_A2, B2, B_B2, Pn, B_Pn
            yield
        TT, B_TT = Pc, B_Pc
        psw, B_psw = SMALL.next()
        P.x("pe", "matmul", psw, lhsT=kbg, rhs=TT, start=True, stop=True, reads=[B_kbg, B_TT], writes=[B_psw])
        wT, B_wT = sl_["wT"].next()
        P.x("act", "activation", out=wT, in_=psw, func=AF.Copy, reads=[B_psw], writes=[B_wT])
        wTz, B_wTz = sl_["wTz"].next()
        P.x("act", "activation", out=wTz[:, 64:128], in_=psw[:, 64:128], func=AF.Copy, reads=[B_psw, B_wTz], writes=[B_wTz])
        yield
        psu, B_psu = SMALL.next()
        P.x("pe", "matmul", psu, lhsT=TT, rhs=bv, start=True, stop=True, reads=[B_TT, B_bv], writes=[B_psu])
        u_sb, B_u = sl_["u"].next()
        P.x("act", "activation", out=u_sb, in_=psu, func=AF.Copy, reads=[B_psu], writes=[B_u])
        yield
        return dict(wT=wT, B_wT=B_wT, wTz=wTz, B_wTz=B_wTz, u=u_sb, B_u=B_u, kd=kdp, B_kd=B_kdp, qk=qkp, B_qk=B_qkp, chunks=chunks)

    def recur_gen(h, i, r, sl_):
        qT, B_qT = sl_["qT"]
        szd, B_szd = sl_["szd"]
        S_f, B_Sf = sl_["S_f"]
        S_b, B_Sb = sl_["S_b"]
        ojunk, B_ojunk = sl_["ojunk"]
        vn, B_vn = sl_["vn"].next()
        for c in r["chunks"]:
            ch, R, ctok, xh = c["ch"], c["R"], c["ctok"], c["xh"]
            own = ctok.start >= T0
            psws, B_psws = SMALL.next()
            if xh == 0:
                P.x("pe", "matmul", psws[0:64, :], lhsT=r["wT"][:, 0:64], rhs=S_b, start=True, stop=True,
                    reads=[r["B_wT"], B_Sb], writes=[B_psws])
            else:
                P.x("pe", "matmul", psws, lhsT=r["wTz"], rhs=S_b, start=True, stop=True,
                    reads=[r["B_wTz"], B_Sb], writes=[B_psws])
            P.x("dve", "tensor_tensor", out=vn[R, :], in0=r["u"][R, :], in1=psws[R, :], op=ALU.subtract,
                reads=[r["B_u"], B_psws, B_vn], writes=[B_vn])
            if own:
                pso1, B_pso1 = SMALL.next()
                P.x("pe", "matmul", pso1[0:64, :], lhsT=qT[:, ctok], rhs=S_b, start=True, stop=True,
                    reads=[B_qT, B_Sb], writes=[B_pso1])
                o1, B_o1 = sl_["o1"].next()
                P.x("act", "activation", out=o1, in_=pso1[0:64, :], func=AF.Copy, scale=CT[:, ch, 1, h:h + 1],
                    reads=[B_pso1, B_CT], writes=[B_o1])
            yield
            psdS, B_psdS = SMALL.next()
            P.x("pe", "matmul", psdS, lhsT=r["kd"][R, :], rhs=vn[R, :], start=True, stop=True, reads=[r["B_kd"], B_vn], writes=[B_psdS])
            gl = GLB[:, ch * 8 + h:ch * 8 + h + 1]
            P.x("dve", "scalar_tensor_tensor", out=S_b, in0=S_f, scalar=gl, in1=psdS,
                op0=ALU.mult, op1=ALU.add, reads=[B_Sf, B_GLB, B_psdS, B_Sb], writes=[B_Sb])
            P.x("dve", "scalar_tensor_tensor", out=S_f, in0=S_f, scalar=gl, in1=psdS,
                op0=ALU.mult, op1=ALU.add, reads=[B_Sf, B_GLB, B_psdS], writes=[B_Sf])
            yield
            if own:
                pso2, B_pso2 = SMALL.next()
                P.x("pe", "matmul", pso2[0:64, :], lhsT=r["qk"][R, R], rhs=vn[R, :], start=True, stop=True,
                    reads=[r["B_qk"], B_vn], writes=[B_pso2])
                o, B_o = sl_["o"].next()
                P.x("dve", "tensor_tensor", out=o, in0=o1, in1=pso2[0:64, :], op=ALU.add, reads=[B_o1, B_pso2], writes=[B_o])
                st_, B_st = sl_["st"].next()
                P.x("act", "activation", out=ojunk, in_=o, func=AF.Square, accum_out=st_[:, 0:1],
                    reads=[B_o], writes=[B_ojunk, B_st])
                P.x("act", "activation", out=st_[:, 1:2], in_=st_[:, 0:1], func=AF.Sqrt, bias=epsc[0:64, 0:1], scale=1.0 / 128,
                    reads=[B_st, B_eps], writes=[B_st])
                P.x("dve", "reciprocal", out=st_[:, 1:2], in_=st_[:, 1:2], reads=[B_st], writes=[B_st])
                on, B_on = sl_["on"].next()
                P.x("act", "activation", out=on, in_=o, func=AF.Copy, scale=st_[:, 1:2], reads=[B_o, B_st], writes=[B_on])
                yield
                psoT, B_psoT = TRB.next()
                P.x("pe", "transpose", psoT[:, 0:64], on, ident_b[0:64, 0:64], reads=[B_on, B_identb], writes=[B_psoT])
                t0o = ctok.start - T0
                P.x("dve", "scalar_tensor_tensor", out=dnoT[:, h, t0o:t0o + 64], in0=psoT[:, 0:64], scalar=dnnw[:, 0:1],
                    in1=szd[:, t0o:t0o + 64], op0=ALU.mult, op1=ALU.mult,
                    reads=[B_psoT, B_dnnw, B_szd], writes=[B_dno[h]])
                yield

    def interleave(g1, g2):
        res = None
        act = [g for g in (g1, g2) if g is not None]
        while act:
            for g in list(act):
                try:
                    next(g)
                except StopIteration as e:
                    if g is g1:
                        res = e.value
                    act.remove(g)
            yield
        return res

    def head_gen(h, sl_):
        S_f, B_Sf = sl_["S_f"]
        S_b, B_Sb = sl_["S_b"]
        P.x("pool", "memset", S_f, 0.0, writes=[B_Sf])
        P.x("pool", "memset", S_b, 0.0, writes=[B_Sb])
        results = {}
        intras = {}
        next_intra = 0
        next_recur = 0
        recur_g = None
        recur_done = 0
        while recur_done < NP:
            while len(intras) < 2 and next_intra < NP and next_intra <= recur_done + 2:
                intras[next_intra] = intra_gen(h, next_intra, sl_)
                next_intra += 1
            if recur_g is None and next_recur in results:
                recur_g = recur_gen(h, next_recur, results.pop(next_recur), sl_)
                next_recur += 1
            for j in list(intras):
                try:
                    next(intras[j])
                except StopIteration as e:
                    results[j] = e.value
                    del intras[j]
            if recur_g is not None:
                try:
                    next(recur_g)
                except StopIteration:
                    recur_g = None
                    recur_done += 1
            yield

    dn_tiles = []
    for hg in range(0, nheads, HG):
        for h in range(hg, min(hg + HG, nheads)):
            dn_tiles += [(w_in_t[T_Q + h], KT), (w_in_t[T_K + h], KT), (w_in_t[T_V + h], KT), (w_in_t[T_ZD + h], KT)]
    dn_pipe = WPipe(dn_tiles, depth=1)
    dn_pipe.start()
    for hg in range(0, nheads, HG):
        hs = list(range(hg, min(hg + HG, nheads)))
        for k, h in enumerate(hs):
            for _ in head_proj(h, slots[k]):
                pass
        if hg == 0 and hs:
            dump("qT", slots[len(hs) - 1]["qT"][0], slots[len(hs) - 1]["qT"][1], [128, LT], BF16)
            dump("kT", slots[len(hs) - 1]["kT"][0], slots[len(hs) - 1]["kT"][1], [128, LT], BF16)
            dump("vT", slots[len(hs) - 1]["vT"][0], slots[len(hs) - 1]["vT"][1], [128, LT], BF16)
        P.mark(f"dn_g{hg}_proj")
        gens = [head_gen(h, slots[k]) for k, h in enumerate(hs)]
        cur["small"] = BIG8
        while gens:
            for g in list(gens):
                try:
                    next(g)
                except StopIteration:
                    gens.remove(g)
        cur["small"] = _SM
    A.release(m_dn0)
    H.release(mH)
    cur["big"] = BIG8
    P.mark("dn")
    dump("dnoT", dnoT[:, 0:nheads], B_dno, [128, nheads, LO], BF16)
    if stop_after == "dn":
        P.emit()
        return nc, dbg

    drain_tables()
    m_s5w = A.mark()
    Gpad, B_Gpad = _tres["Gpad"]
    Mintra, B_Mintra = _tres["Mintra"]
    Minter, B_Minter = _tres["Minter"]
    A8, B_A8 = _tres["A8"]
    A8P, B_A8P = _tres["A8P"]
    A64, B_A64 = _tres["A64"]
    Sst, B_Sst = A.alloc("Sst", [128, 2, 32], F32)
    y2T, B_y2T = H.alloc("y2T", [128, 8, LO], BF16, nbufs=8)
    P.x("pool", "memset", Sst, 0.0, writes=[B_Sst])
    P.mark("s5tab")
    m_blk = A.mark()
    UC = Ring([A.alloc(f"ucol{i}", [128, 64, 64], BF16) for i in range(1)])
    YC = Ring([A.alloc(f"ycol{i}", [128, 64, 64], BF16) for i in range(1)])
    Lt, B_Lt = A.alloc("Lt", [128, 2, 32, 64], F32)
    B_Lre, B_Lim = Buf("Lre"), Buf("Lim")
    ct1, B_ct1 = A.alloc("ct1", [128, 32, 4, 8], F32)
    ct2, B_ct2 = A.alloc("ct2", [128, 32, 4, 8], F32)
    mH2 = H.mark()
    hist, B_hist = H.alloc("hist", [128, 2, 32, 64], BF16)
    sc1, B_sc1 = H.alloc("sc1", [128, 2, 32], F32)
    sc2, B_sc2 = H.alloc("sc2", [128, 2, 32], F32)
    xm1, B_xm1 = H.alloc("xm1", [128, 2, 32, 8], F32)
    xm2, B_xm2 = H.alloc("xm2", [128, 2, 32, 8], F32)
    Cs, B_Cs = H.alloc("Cs", [128, 2, 32, 9], F32)
    Uv = U_scr.rearrange("(g ci) t c -> t ci g c", ci=16)
    Yv = Y_scr.rearrange("(g co) t c -> t co g c", co=16)
    B_Y = Buf("Yscr")
    AR2 = A8[:, 0]
    AI2 = A8[:, 1]
    for b in range(NB):
        own = b >= NB - NBO
        bo = b - (NB - NBO)
        ucol, B_uc = UC.next()
        for tau in range(8):
            P.d("sp", ucol[16 * tau:16 * tau + 16, :, :], Uv[tau][:, :, b * 64:(b + 1) * 64], reads=B_U, writes=[B_uc])
        for q4 in range(8):
            ps, B_ps = _SM.next()
            for k in range(4):
                gp = q4 * 4 + k
                for ri in range(2):
                    o_ = ps[:, (k * 2 + ri) * 64:(k * 2 + ri + 1) * 64]
                    P.x("pe", "matmul", o_, lhsT=Gpad[:, gp, ri, :], rhs=ucol[:, gp, :], start=True, stop=False,
                        reads=[B_Gpad, B_uc], writes=[B_ps])
                    P.x("pe", "matmul", o_, lhsT=Gpad[:, 32 + gp, ri, :], rhs=ucol[:, 32 + gp, :], start=False, stop=True,
                        reads=[B_Gpad, B_uc], writes=[B_ps])
            o_l = Lt[:, :, q4 * 4:q4 * 4 + 4, :].rearrange("p r g c -> p g r c")
            i_l = ps.rearrange("p (g r c) -> p g r c", g=4, r=2)
            if q4 % 2:
                P.x("act", "activation", out=o_l, in_=i_l, func=AF.Copy, reads=[B_ps, B_Lt, B_Lre, B_Lim], writes=[B_Lt, B_Lre, B_Lim])
            else:
                P.x("dve", "tensor_copy", out=o_l, in_=i_l, reads=[B_ps, B_Lt, B_Lre, B_Lim], writes=[B_Lt, B_Lre, B_Lim])
        L5 = Lt.rearrange("p r g (s k) -> p r g s k", k=8)
        LB = [B_Lt, B_Lre, B_Lim]
        ARb = A8[:, 0].unsqueeze(3).to_broadcast([128, 2, 32, 8])
        AIb = A8[:, 1].unsqueeze(3).to_broadcast([128, 2, 32, 8])
        for k in range(1, 8):
            xp = L5[:, :, :, :, k - 1]
            P.x("dve", "tensor_tensor", out=xm1, in0=ARb, in1=xp, op=ALU.mult, reads=[B_A8] + LB, writes=[B_xm1])
            P.x("dve", "tensor_tensor", out=xm2, in0=AIb, in1=xp, op=ALU.mult, reads=[B_A8] + LB, writes=[B_xm2])
            P.x("dve", "tensor_tensor", out=xm1, in0=xm1, in1=L5[:, :, :, :, k], op=ALU.add, reads=[B_xm1] + LB, writes=[B_xm1])
            P.x("dve", "tensor_tensor", out=L5[:, 0, :, :, k], in0=xm1[:, 0], in1=xm2[:, 1], op=ALU.subtract,
                reads=[B_xm1, B_xm2] + LB, writes=LB)
            P.x("dve", "tensor_tensor", out=L5[:, 1, :, :, k], in0=xm1[:, 1], in1=xm2[:, 0], op=ALU.add,
                reads=[B_xm1, B_xm2] + LB, writes=LB)
        P.x("dve", "tensor_copy", out=Cs[:, :, :, 0], in_=Sst, reads=[B_Sst, B_Cs], writes=[B_Cs])
        for sg_ in range(8):
            cp = Cs[:, :, :, sg_]
            P.x("dve", "tensor_tensor", out=sc1, in0=A64[:, 0], in1=cp, op=ALU.mult, reads=[B_A64, B_Cs], writes=[B_sc1])
            P.x("dve", "tensor_tensor", out=sc2, in0=A64[:, 1], in1=cp, op=ALU.mult, reads=[B_A64, B_Cs], writes=[B_sc2])
            P.x("dve", "tensor_tensor", out=sc1, in0=sc1, in1=L5[:, :, :, sg_, 7], op=ALU.add, reads=[B_sc1] + LB, writes=[B_sc1])
            P.x("dve", "tensor_tensor", out=Cs[:, 0, :, sg_ + 1], in0=sc1[:, 0, :], in1=sc2[:, 1, :], op=ALU.subtract,
                reads=[B_sc1, B_sc2, B_Cs], writes=[B_Cs])
            P.x("dve", "tensor_tensor", out=Cs[:, 1, :, sg_ + 1], in0=sc1[:, 1, :], in1=sc2[:, 0, :], op=ALU.add,
                reads=[B_sc1, B_sc2, B_Cs], writes=[B_Cs])
        if own:
            sh4 = [128, 32, 4, 8]
            Wr = A8P[:, :, 0, :].rearrange("p k g -> p g k").unsqueeze(2).to_broadcast(sh4)
            Wi = A8P[:, :, 1, :].rearrange("p k g -> p g k").unsqueeze(2).to_broadcast(sh4)
            for s0 in (0, 4):
                Cr = Cs[:, 0, :, s0:s0 + 4].unsqueeze(3).to_broadcast(sh4)
                Ci = Cs[:, 1, :, s0:s0 + 4].unsqueeze(3).to_broadcast(sh4)
                Lre, Lim = L5[:, 0, :, s0:s0 + 4, :], L5[:, 1, :, s0:s0 + 4, :]
                P.x("dve", "tensor_tensor", out=ct1, in0=Wr, in1=Cr, op=ALU.mult, reads=[B_A8P, B_Cs, B_ct1], writes=[B_ct1])
                P.x("dve", "tensor_tensor", out=Lre, in0=Lre, in1=ct1, op=ALU.add, reads=[B_ct1, B_Lre, B_Lt], writes=[B_Lre])
                P.x("dve", "tensor_tensor", out=ct1, in0=Wi, in1=Ci, op=ALU.mult, reads=[B_A8P, B_Cs, B_ct1], writes=[B_ct1])
                P.x("dve", "tensor_tensor", out=Lre, in0=Lre, in1=ct1, op=ALU.subtract, reads=[B_ct1, B_Lre], writes=[B_Lre])
                P.x("pool", "tensor_tensor", out=ct2, in0=Wr, in1=Ci, op=ALU.mult, reads=[B_A8P, B_Cs, B_ct2], writes=[B_ct2])
                P.x("pool", "tensor_tensor", out=Lim, in0=Lim, in1=ct2, op=ALU.add, reads=[B_ct2, B_Lim, B_Lt], writes=[B_Lim])
                P.x("pool", "tensor_tensor", out=ct2, in0=Wi, in1=Cr, op=ALU.mult, reads=[B_A8P, B_Cs, B_ct2], writes=[B_ct2])
                P.x("pool", "tensor_tensor", out=Lim, in0=Lim, in1=ct2, op=ALU.add, reads=[B_ct2, B_Lim], writes=[B_Lim])
            P.x("act", "activation", out=hist[:, :, :, 1:64], in_=Lt[:, :, :, 0:63], func=AF.Copy,
                reads=[B_Lre, B_Lim, B_Lt, B_hist], writes=[B_hist])
            P.x("pool", "tensor_copy", out=hist[:, :, :, 0], in_=Sst, reads=[B_Sst, B_hist], writes=[B_hist])
        P.x("dve", "tensor_copy", out=Sst, in_=Cs[:, :, :, 8], reads=[B_Cs, B_Sst, B_hist], writes=[B_Sst])
        if not own:
            continue
        ycol, B_yc = YC.next()
        for q8 in range(8):
            ps, B_ps = _SM.next()
            for k in range(8):
                g = q8 * 8 + k
                half, gp = g // 32, g % 32
                rows = slice(half * 64, half * 64 + 64)
                o_ = ps[:, k * 64:(k + 1) * 64]
                P.x("pe", "matmul", o_, lhsT=Mintra[:, g, :], rhs=ucol[:, g, :], start=True, stop=False,
                    reads=[B_Mintra, B_uc], writes=[B_ps])
                P.x("pe", "matmul", o_, lhsT=Minter[rows, gp, 0, :], rhs=hist[rows, 0, gp, :], start=False, stop=False,
                    reads=[B_Minter, B_hist], writes=[B_ps])
                P.x("pe", "matmul", o_, lhsT=Minter[rows, gp, 1, :], rhs=hist[rows, 1, gp, :], start=False, stop=True,
                    reads=[B_Minter, B_hist], writes=[B_ps])
            P.x("act", "activation", out=ycol[:, q8 * 8:q8 * 8 + 8, :], in_=ps.rearrange("p (g c) -> p g c", g=8),
                func=AF.Gelu_apprx_tanh, reads=[B_ps, B_yc], writes=[B_yc])
        for t in range(8):
            P.d("sp", Yv[t][:, :, bo * 64:(bo + 1) * 64], ycol[16 * t:16 * t + 16, :, :], reads=[B_yc], writes=[B_Y])
    A.release(m_s5w)
    A.release_top()
    H.release(mH2)
    for j in range(8):
        P.d("sp", y2T[:, j, :], Y_scr[j * 128:(j + 1) * 128].rearrange("p t c -> p (t c)"), reads=[B_Y], writes=[B_y2T[j]])
    P.mark("s5blk")
    dump("y2T", y2T, B_y2T, [128, 8, LO], BF16)
    if stop_after == "s5":
        P.emit()
        return nc, dbg

    y4T, B_y4 = A.alloc("y4T", [128, 8, LO], BF16, nbufs=8)
    m_glu = A.mark()
    y3s = [A.alloc(f"y3_{i}", [128, LO], BF16) for i in range(2)]
    SG = Ring([A.alloc(f"sg{i}", [128, 512], BF16) for i in range(2)])
    SZ = Ring([A.alloc(f"sz{i}", [128, 512], BF16) for i in range(2)])

    def y2_rhs(kt, pb):
        return y2T[:, kt, pb * 512:(pb + 1) * 512]

    def y2_bufs(pb):
        return list(B_y2T)

    glu_tiles = []
    for j in range(8):
        glu_tiles += [(wglu_t[j], 8), (w_in_t[T_ZS + j], KT)]
    glu_pipe = WPipe(glu_tiles, depth=1)
    for j in range(8):
        y3, B_y3 = y3s[j % 2]
        slot, bslot = glu_pipe.take()

        def cons_glu(pb, ps, B_ps, j=j, y3=y3, B_y3=B_y3):
            sg, B_sg = SG.next()
            P.x("act", "activation", out=sg, in_=ps, func=AF.Sigmoid, reads=[B_ps], writes=[B_sg])
            P.x("dve", "tensor_tensor", out=y3[:, pb * 512:(pb + 1) * 512], in0=y2T[:, j, pb * 512:(pb + 1) * 512], in1=sg,
                op=ALU.mult, reads=[B_y2T[j], B_sg], writes=[B_y3])
        proj_fm(slot, bslot, 8, y2_rhs, y2_bufs, range(NBO), cons_glu)
        slot2, bslot2 = glu_pipe.take()

        def cons_zs(b, ps, B_ps, j=j, y3=y3, B_y3=B_y3):
            bo = b - (NB - NBO)
            sz, B_sz = SZ.next()
            P.x("act", "activation", out=sz, in_=ps, func=AF.Silu, reads=[B_ps], writes=[B_sz])
            P.x("dve", "tensor_tensor", out=y4T[:, j, bo * 512:(bo + 1) * 512].rearrange("p (c t) -> p c t", t=8),
                in0=y3.rearrange("p (t c) -> p c t", t=8)[:, bo * 64:(bo + 1) * 64, :],
                in1=sz.rearrange("p (c t) -> p c t", t=8), op=ALU.mult,
                reads=[B_y3, B_sz], writes=[B_y4[j]])
        proj_fm(slot2, bslot2, KT, h_rhs, h_bufs, range(NB - NBO, NB), cons_zs)
    A.release(m_glu)
    P.mark("glu")
    dump("y4T", y4T, B_y4, [128, 8, LO], BF16)
    if stop_after == "glu":
        P.emit()
        return nc, dbg

    mixT, B_mix = A.alloc("mixT", [128, KT, LO], BF16, nbufs=NTO)
    m_mix = A.mark()
    SGS = Ring([A.alloc(f"sgs{i}", [128, 512], BF16) for i in range(4)])
    SGD = Ring([A.alloc(f"sgd{i}", [128, 512], BF16) for i in range(4)])
    M1 = Ring([A.alloc(f"m1_{i}", [128, 512], F32) for i in range(4)])
    M2 = Ring([A.alloc(f"m2_{i}", [128, 512], F32) for i in range(3)])

    def y4_rhs(kt, bo):
        return y4T[:, kt, bo * 512:(bo + 1) * 512]

    def dn_rhs(kt, bo):
        return dnoT[:, kt, bo * 512:(bo + 1) * 512]

    mix_tiles = []
    for j in range(KT):
        mix_tiles += [(w_in_t[T_GS + j], KT), (w_in_t[T_GD + j], KT), (wups_t[j], 8), (wupd_t[j], 8)]
    wsr["ring"] = Ring(list(wslots) + [A.alloc(f"wslotx{i}", [128, KT, 128], BF16) for i in range(2)])
    mix_pipe = WPipe(mix_tiles, depth=2)
    for j in range(KT):
        s_gs = mix_pipe.take()
        st_ = {bo: {} for bo in range(NBO)}

        def cons_gs(b_, ps, B_ps, st_=st_):
            sg, B_sg = SGS.next()
            P.x("act", "activation", out=sg, in_=ps, func=AF.Sigmoid, reads=[B_ps], writes=[B_sg])
            st_[b_ - (NB - NBO)]["gs"] = (sg, B_sg)
        proj_fm(s_gs[0], s_gs[1], KT, h_rhs, h_bufs, range(NB - NBO, NB), cons_gs)

        s_gd = mix_pipe.take()

        def cons_gd(b_, ps, B_ps, st_=st_):
            sg, B_sg = SGD.next()
            P.x("act", "activation", out=sg, in_=ps, func=AF.Sigmoid, reads=[B_ps], writes=[B_sg])
            st_[b_ - (NB - NBO)]["gd"] = (sg, B_sg)
        proj_fm(s_gd[0], s_gd[1], KT, h_rhs, h_bufs, range(NB - NBO, NB), cons_gd)

        s_us = mix_pipe.take()

        def cons_us(bo_, ps, B_ps, st_=st_):
            m1_, B_m1 = M1.next()
            sg, B_sg = st_[bo_]["gs"]
            P.x("dve", "tensor_tensor", out=m1_, in0=ps, in1=sg, op=ALU.mult, reads=[B_ps, B_sg], writes=[B_m1])
            st_[bo_]["m1"] = (m1_, B_m1)
        proj_fm(s_us[0], s_us[1], 8, y4_rhs, lambda bo_: list(B_y4), range(NBO), cons_us)

        s_ud = mix_pipe.take()

        def cons_ud(bo_, ps, B_ps, j=j, st_=st_):
            m2_, B_m2 = M2.next()
            sg, B_sg = st_[bo_]["gd"]
            P.x("dve", "tensor_tensor", out=m2_, in0=ps, in1=sg, op=ALU.mult, reads=[B_ps, B_sg], writes=[B_m2])
            m1_, B_m1 = st_[bo_]["m1"]
            P.x("pool", "tensor_tensor", out=mixT[:, j, bo_ * 512:(bo_ + 1) * 512], in0=m1_, in1=m2_, op=ALU.add,
                reads=[B_m1, B_m2], writes=B_mix[bo_ * 4:(bo_ + 1) * 4])
        proj_fm(s_ud[0], s_ud[1], 8, dn_rhs, lambda bo_: list(B_dno), range(NBO), cons_ud)
    wsr["ring"] = WS
    A.release(m_mix)
    H.release(0)
    P.mark("mix")

    rT, B_r = H.alloc("rT", [128, NTO, D], F32, nbufs=NTO)
    fnw, B_fnw = A.alloc("fnw", [128, D], F32)
    P.d("sp", fnw, fnw_in.to_broadcast([128, D]),
        writes=[B_fnw])
    WO = Ring([A.alloc(f"wo{i}", [128, KT, 512], BF16) for i in range(2)])
    XO = Ring([A.alloc(f"xo{i}", [128, 512], F32) for i in range(3)])
    st2, B_st2 = A.alloc("st2", [128, NTO, 2], F32, nbufs=NTO)
    fjunk, B_fjunk = A.alloc("fjunk", [128, D], BF16)
    for cb in range(4):
        wo, B_wo = WO.next()
        P.d("pool", wo, wout_t[cb], writes=[B_wo])
        for tt_ in range(NTO):
            xo, B_xo = XO.next()
            P.d("sp", xo, x_in[T0 + tt_ * 128:T0 + (tt_ + 1) * 128, cb * 512:(cb + 1) * 512], writes=[B_xo])
            ps, B_ps = cur["big"].next()
            for kt in range(KT):
                P.x("pe", "matmul", ps, lhsT=mixT[:, kt, tt_ * 128:(tt_ + 1) * 128], rhs=wo[:, kt, :],
                    start=(kt == 0), stop=(kt == KT - 1), reads=[B_mix[tt_], B_wo], writes=[B_ps])
            P.x("dve", "tensor_tensor", out=rT[:, tt_, cb * 512:(cb + 1) * 512], in0=ps, in1=xo, op=ALU.add,
                reads=[B_ps, B_xo], writes=[B_r[tt_]])
    for tt_ in range(NTO):
        P.x("act", "activation", out=fjunk, in_=rT[:, tt_, :], func=AF.Square, accum_out=st2[:, tt_, 0:1],
            reads=[B_r[tt_]], writes=[B_fjunk, B_st2[tt_]])
        P.x("act", "activation", out=st2[:, tt_, 1:2], in_=st2[:, tt_, 0:1], func=AF.Sqrt, bias=epsc[:, 0:1], scale=1.0 / D,
            reads=[B_st2[tt_], B_eps], writes=[B_st2[tt_]])
        P.x("dve", "reciprocal", out=st2[:, tt_, 1:2], in_=st2[:, tt_, 1:2], reads=[B_st2[tt_]], writes=[B_st2[tt_]])
        P.x("dve", "scalar_tensor_tensor", out=rT[:, tt_, :], in0=rT[:, tt_, :], scalar=st2[:, tt_, 1:2], in1=fnw,
            op0=ALU.mult, op1=ALU.mult, reads=[B_r[tt_], B_st2[tt_], B_fnw], writes=[B_r[tt_]])
        P.d("sp", out_ap[tt_ * 128:(tt_ + 1) * 128, :], rT[:, tt_, :], reads=[B_r[tt_]], is_output=True)
    P.mark("out")
    P.emit()
    build.last_marks = P.marks
    return nc, dbg


_NC_CACHE = {}


def kernel(**inputs):
    x = np.asarray(inputs["x"], dtype=np.float32)
    Bsz, L, Dm = x.shape
    LO = L // 2
    key = (L,)
    if key not in _NC_CACHE:
        _NC_CACHE[key] = build(LT=L, LO=LO)[0]
    nc = _NC_CACHE[key]
    in_maps = []
    wmap = make_in_map(inputs, np.zeros((1, 1), np.float32))
    wmap.pop("x")
    for c in range(8):
        b, p = c // 2, c % 2
        if p == 0:
            xloc = np.concatenate([np.zeros((L - LO, Dm), np.float32), x[b, :LO]], axis=0)
        else:
            xloc = x[b]
        in_maps.append(make_in_map(inputs, xloc, wmap))
    res = run_bass_kernel_spmd(nc, in_maps, core_ids=list(range(8)))
    out = np.empty((Bsz, L, Dm), np.float32)
    for c in range(8):
        b, p = c // 2, c % 2
        out[b, p * LO:(p + 1) * LO] = np.asarray(res.results[c]["out"], dtype=np.float32)
    return out


def make_in_map(inp, xloc, wmap=None):
    if wmap is not None:
        m = dict(wmap)
        m["x"] = np.ascontiguousarray(xloc, dtype=np.float32)
        return m
    f = np.float32
    g = lambda k: np.asarray(inp[k], dtype=f)
    m = {}
    m["x"] = np.ascontiguousarray(xloc, dtype=f)
    w_in = g("w_in")[0]

    def tile_of(w, col0, ncols=128, kt=16):
        return w[:, col0:col0 + ncols].reshape(kt, 128, ncols).transpose(1, 0, 2)
    col0s = ([0 + 128 * j for j in range(8)] + [1024 + 128 * j for j in range(8)] + [2048 + 128 * h for h in range(8)]
             + [3072 + 128 * h for h in range(8)] + [4096 + 128 * h for h in range(8)] + [5120 + 128 * h for h in range(8)]
             + [None] + [6160 + 128 * j for j in range(16)] + [8208 + 128 * j for j in range(16)])
    wt = np.zeros((81, 128, 16, 128), f)
    for ti, c0 in enumerate(col0s):
        if c0 is None:
            wt[ti, :, :, 0:8] = tile_of(w_in, 6144, 8)
            for r0 in (32, 64, 96):
                wt[ti, :, :, r0:r0 + 8] = tile_of(w_in, 6152, 8)
        else:
            wt[ti] = tile_of(w_in, c0)
    m["w_in_t"] = wt
    m["lnw"] = np.ascontiguousarray(g("ln_w")[0].reshape(16, 128).T)
    m["fnw"] = np.ascontiguousarray(g("final_norm_w")[None, :])
    m["convw"] = np.ascontiguousarray(g("dn_conv_w")[0].reshape(4, 24, 128).transpose(2, 1, 0))
    dnp = np.zeros((128, 2), f)
    for r0 in (32, 64, 96):
        dnp[r0:r0 + 8, 0] = g("dn_a_log")[0]
        dnp[r0:r0 + 8, 1] = g("dn_dt_bias")[0]
    m["dnp"] = dnp
    m["dnnw"] = np.ascontiguousarray(g("dn_norm_w")[0][:, None])
    lam = np.zeros((128, 3, 32), f)
    lre, lim, lst = g("s5_lam_re")[0], g("s5_lam_im")[0], g("s5_log_step")[0]
    for half in range(2):
        gs = slice(half * 32, half * 32 + 32)
        lam[half * 64:(half + 1) * 64, 0, :] = lre[gs].T
        lam[half * 64:(half + 1) * 64, 1, :] = lim[gs].T
        lam[half * 64:(half + 1) * 64, 2, :] = np.broadcast_to(lst[gs][None, :], (64, 32))
    m["lam"] = lam
    sb = np.zeros((128, 2, 32, 16), f)
    sc = np.zeros((128, 2, 32, 16), f)
    for half in range(2):
        gs = slice(half * 32, half * 32 + 32)
        ps = slice(half * 64, half * 64 + 64)
        sb[ps, 0] = g("s5_b_re")[0][gs].transpose(1, 0, 2)
        sb[ps, 1] = g("s5_b_im")[0][gs].transpose(1, 0, 2)
        sc[ps, 0] = g("s5_c_re")[0][gs].transpose(2, 0, 1)
        sc[ps, 1] = g("s5_c_im")[0][gs].transpose(2, 0, 1)
    m["s5b"] = sb
    m["s5c"] = sc
    d = g("s5_d")[0].reshape(64, 16)
    m["s5d"] = np.ascontiguousarray(np.tile(d.T, (8, 1)))
    m["w_glu_t"] = np.ascontiguousarray(np.stack([tile_of(g("s5_w_glu")[0], 128 * j, 128, 8) for j in range(8)]))
    m["w_ups_t"] = np.ascontiguousarray(np.stack([tile_of(g("s5_w_up")[0], 128 * j, 128, 8) for j in range(16)]))
    m["w_upd_t"] = np.ascontiguousarray(np.stack([tile_of(g("dn_w_up")[0], 128 * j, 128, 8) for j in range(16)]))
    m["w_out_t"] = np.ascontiguousarray(np.stack([tile_of(g("w_out")[0], 512 * cb, 512, 16) for cb in range(4)]))
    return m
```

```python
import contextlib
import numpy as np
import concourse.bass as bass
import concourse.mybir as mybir
from concourse.bass_utils import run_bass_kernel_spmd

F32 = mybir.dt.float32
BF16 = mybir.dt.bfloat16
AF = mybir.ActivationFunctionType
ALU = mybir.AluOpType

ENGS = ("pe", "act", "dve", "pool", "sp")
N_DMA_SEMS = 32


class Buf:
    __slots__ = ("name", "writers", "readers")

    def __init__(self, name):
        self.name = name
        self.writers = []
        self.readers = []


class Ins:
    __slots__ = ("eng", "fn", "deps", "is_dma", "sem", "val", "tag")

    def __init__(self, eng, fn, is_dma, tag=None):
        self.eng = eng
        self.fn = fn
        self.deps = []
        self.is_dma = is_dma
        self.sem = None
        self.val = None
        self.tag = tag


def _mk(method, *args, **kw):
    def fn(e):
        return getattr(e, method)(*args, **kw)
    return fn


class Prog:
    def __init__(self, nc):
        self.nc = nc
        self.q = {e: [] for e in ENGS}
        self.cnt = {e: 0 for e in ENGS}
        self.dma_rr = 0
        self.dma_rr_pool = 0
        self.dma_cnt = [0] * N_DMA_SEMS
        self.dma_last = [None] * N_DMA_SEMS
        self.out_dmas = []

    def mark(self, name):
        if not hasattr(self, 'marks'):
            self.marks = []
        self.marks.append((name, dict(self.cnt)))

    def _add(self, ins, reads, writes):
        deps = []
        for b in reads:
            deps.extend(b.writers)
        for b in writes:
            deps.extend(b.writers)
            deps.extend(b.readers)
        seen = set()
        for d in deps:
            if d is ins or id(d) in seen:
                continue
            seen.add(id(d))
            if d.eng == "pe" and ins.eng == "pe" and not d.is_dma and not ins.is_dma:
                continue
            ins.deps.append(d)
        for b in reads:
            b.readers.append(ins)
        for b in writes:
            b.writers = [ins]
            b.readers = []
        self.q[ins.eng].append(ins)
        return ins

    def x(self, eng, method, *args, reads=(), writes=(), **kw):
        ins = Ins(eng, _mk(method, *args, **kw), False, method)
        self.cnt[eng] += 1
        ins.val = self.cnt[eng]
        return self._add(ins, list(reads), list(writes))

    def d(self, eng, out, in_, reads=(), writes=(), is_output=False, **kw):
        ins = Ins(eng, _mk("dma_start", out=out, in_=in_, **kw), True, "dma")
        half = N_DMA_SEMS // 2
        if eng == "pool":
            k = half + (self.dma_rr_pool % half)
            self.dma_rr_pool += 1
        else:
            k = self.dma_rr % half
            self.dma_rr += 1
        ins.sem = k
        self.dma_cnt[k] += 16
        ins.val = self.dma_cnt[k]
        prev = self.dma_last[k]
        self.dma_last[k] = ins
        self._add(ins, list(reads), list(writes))
        if prev is not None and all(dd is not prev for dd in ins.deps):
            ins.deps.append(prev)
        if is_output:
            self.out_dmas.append(ins)
        return ins

    def emit(self):
        nc = self.nc
        with contextlib.ExitStack() as st:
            csem = {e: st.enter_context(nc.semaphore(f"c_{e}")) for e in ("pe", "act", "dve", "pool")}
            dsem = [st.enter_context(nc.semaphore(f"d_{i}")) for i in range(N_DMA_SEMS)]
            block = st.enter_context(nc.Block())
            final_waits = list(self.out_dmas)

            def run(eng_name, handle):
                seen_c = {}
                seen_d = {}
                for ins in self.q[eng_name]:
                    for d in ins.deps:
                        if d.is_dma:
                            if seen_d.get(d.sem, 0) >= d.val:
                                continue
                            seen_d[d.sem] = d.val
                            handle.wait_ge(dsem[d.sem], d.val)
                        else:
                            if seen_c.get(d.eng, 0) >= d.val:
                                continue
                            seen_c[d.eng] = d.val
                            handle.wait_ge(csem[d.eng], d.val)
                    bi = ins.fn(handle)
                    if ins.is_dma:
                        bi.then_inc(dsem[ins.sem], 16)
                    else:
                        bi.then_inc(csem[ins.eng], 1)
                if eng_name == "sp":
                    for d in final_waits:
                        if seen_d.get(d.sem, 0) >= d.val:
                            continue
                        seen_d[d.sem] = d.val
                        handle.wait_ge(dsem[d.sem], d.val)

            @block.tensor
            def _(e):
                run("pe", e)

            @block.scalar
            def _(e):
                run("act", e)

            @block.vector
            def _(e):
                run("dve", e)

            @block.gpsimd
            def _(e):
                run("pool", e)

            @block.sync
            def _(e):
                run("sp", e)


class Arena:
    def __init__(self, nc, nbytes, name="arena"):
        self.t = nc.alloc_sbuf_tensor(name, [128, nbytes // 2], BF16).ap()
        self.nbytes = nbytes
        self.top = 0
        self.top2 = nbytes
        self.hist = []
        self.peak = 0

    def _mkbuf(self, name, s, e):
        b = Buf(name)
        keep = []
        for (s0, e0, b0) in self.hist:
            if s0 < e and s < e0:
                b.readers.extend(b0.readers)
                b.readers.extend(b0.writers)
            keep.append((s0, e0, b0))
        self.hist = keep
        self.hist.append((s, e, b))
        return b

    def alloc(self, name, shape, dt, nbufs=1, top=False):
        esz = 4 if dt == F32 else 2
        free = int(np.prod(shape[1:]))
        nb = (free * esz + 31) // 32 * 32
        if top:
            e = self.top2
            s = e - nb
            assert s >= self.top, f"arena overflow (top) allocating {name}"
            self.top2 = s
        else:
            s = self.top
            e = s + nb
            assert e <= self.top2, f"arena overflow allocating {name}: {e} > {self.top2}"
            self.top = e
        self.peak = max(self.peak, self.top + (self.nbytes - self.top2))
        ap = self.t[0:shape[0], s // 2:(s + free * esz) // 2]
        if dt == F32:
            ap = ap.bitcast(F32)
        if len(shape) > 2:
            names = "abcdefg"[:len(shape) - 1]
            pat = "p (" + " ".join(names) + ") -> p " + " ".join(names)
            ap = ap.rearrange(pat, **{n: int(v) for n, v in zip(names, shape[1:])})
        if nbufs == 1:
            return ap, self._mkbuf(name, s, e)
        return ap, [self._mkbuf(f"{name}{i}", s, e) for i in range(nbufs)]

    def release_top(self):
        self.top2 = self.nbytes

    def mark(self):
        return self.top

    def release(self, mark):
        self.top = mark


class Ring:
    def __init__(self, items):
        self.items = items
        self.i = 0

    def next(self):
        it = self.items[self.i % len(self.items)]
        self.i += 1
        return it


D = 2048
DIN = 10256
KT = 16
NH = 8
C_US, C_ZS, C_Q, C_K, C_V, C_ZD, C_BETA, C_A, C_GS, C_GD = 0, 1024, 2048, 3072, 4096, 5120, 6144, 6152, 6160, 8208
EPS = 1e-6
PI = float(np.pi)


def build(LT=2048, LO=1024, debug=(), stop_after=None, nheads=NH):
    assert LT % 512 == 0 and LO % 512 == 0 and LO <= LT
    NT = LT // 128
    NTO = LO // 128
    T0 = LT - LO
    NB = LT // 512
    NBO = LO // 512
    NCH = LT // 64
    NP = LT // 128
    NC8 = LT // 8
    NC8O = LO // 8

    nc = bass.Bass("TRN2", target_bir_lowering=False)
    P = Prog(nc)

    def din(name, shape, dt=F32):
        return nc.dram_tensor(name, list(shape), dt, kind="ExternalInput").ap()

    x_in = din("x", [LT, D])
    w_in_t = din("w_in_t", [81, 128, KT, 128])
    lnw_in = din("lnw", [128, KT])
    fnw_in = din("fnw", [1, D])
    convw_in = din("convw", [128, 24, 4])
    dnp_in = din("dnp", [128, 2])
    dnnw_in = din("dnnw", [128, 1])
    lam_in = din("lam", [128, 3, 32])
    sb_in = din("s5b", [128, 2, 32, 16])
    sc_in = din("s5c", [128, 2, 32, 16])
    sd_in = din("s5d", [128, 64])
    wglu_t = din("w_glu_t", [8, 128, 8, 128])
    wups_t = din("w_ups_t", [16, 128, 8, 128])
    wupd_t = din("w_upd_t", [16, 128, 8, 128])
    wout_t = din("w_out_t", [4, 128, KT, 512])
    out_ap = nc.dram_tensor("out", [LO, D], F32, kind="ExternalOutput").ap()
    U_scr = nc.dram_tensor("U_scr", [1024, 8, NC8], BF16).ap()
    Y_scr = nc.dram_tensor("Y_scr", [1024, 8, NC8O], BF16).ap()
    dbg = {}

    def dump(name, ap, buf, shape, dt=F32):
        if name not in debug:
            return
        o = nc.dram_tensor("dbg_" + name, list(shape), dt, kind="ExternalOutput").ap()
        bufs = buf if isinstance(buf, list) else [buf]
        P.d("sp", o, ap, reads=bufs, is_output=True)
        dbg[name] = o

    A = Arena(nc, 143 * 1024, "arena")
    H = Arena(nc, 64 * 1024, "harena")

    def psum_ring(prefix, nbanks, per_bank, dt=F32):
        items = []
        for i in range(nbanks):
            width = 512 if dt == F32 else 1024
            t = nc.alloc_psum_tensor(f"{prefix}{i}", [128, width], dt).ap()
            w = width // per_bank
            for j in range(per_bank):
                items.append((t[:, j * w:(j + 1) * w], Buf(f"{prefix}{i}_{j}")))
        return Ring(items)

    BIG = psum_ring("psb", 2, 1)
    _SM = psum_ring("pss", 6, 1)

    BIG8 = Ring(list(BIG.items) + list(_SM.items))
    cur = {"big": BIG8}

    cur["small"] = _SM

    class _SmallF:
        @staticmethod
        def next():
            t, b = cur["small"].next()
            return t[:, 0:128], b

    class _SmallB:
        @staticmethod
        def next():
            t, b = cur["small"].next()
            return t.bitcast(BF16)[:, 0:128], b
    SMALL = _SmallF
    TRB = _SmallB

    ident_f, B_identf = A.alloc("ident_f", [128, 128], F32)
    ident_b, B_identb = A.alloc("ident_b", [128, 128], BF16)
    ones_b, B_onesb = A.alloc("ones_b", [128, 128], BF16)
    ones_f, B_onesf = A.alloc("ones_f", [128, 128], F32)
    SLm, B_SL = A.alloc("SL", [128, 128], F32)
    CUm, B_CU = A.alloc("CU", [64, 64], F32)
    lnw, B_lnw = A.alloc("lnw", [128, KT], F32)
    convw, B_convw = A.alloc("convw", [128, 24, 4], F32)
    dnnw, B_dnnw = A.alloc("dnnw", [128, 1], F32)
    dnp, B_dnp = A.alloc("dnp", [128, 2], F32)
    epsc, B_eps = A.alloc("epsc", [128, 1], F32)
    onec, B_one = A.alloc("onec", [128, 1], F32)

    P.x("pool", "memset", ident_f, 0.0, writes=[B_identf])
    P.x("pool", "affine_select", out=ident_f, in_=ident_f, pattern=[[-1, 128]], compare_op=ALU.not_equal,
        fill=1.0, base=0, channel_multiplier=1, reads=[B_identf], writes=[B_identf])
    P.x("pool", "tensor_copy", out=ident_b, in_=ident_f, reads=[B_identf], writes=[B_identb])
    P.x("pool", "memset", ones_b, 1.0, writes=[B_onesb])
    P.x("pool", "memset", ones_f, 1.0, writes=[B_onesf])
    P.x("pool", "memset", epsc, EPS, writes=[B_eps])
    P.x("pool", "memset", onec, 1.0, writes=[B_one])
    P.x("pool", "memset", SLm, 1.0, writes=[B_SL])
    P.x("pool", "affine_select", out=SLm, in_=SLm, pattern=[[-1, 128]], compare_op=ALU.is_gt,
        fill=0.0, base=0, channel_multiplier=1, reads=[B_SL], writes=[B_SL])
    P.x("pool", "memset", SLm[64:128, 0:64], 0.0, reads=[B_SL], writes=[B_SL])
    P.x("pool", "memset", CUm, 1.0, writes=[B_CU])
    P.x("pool", "affine_select", out=CUm, in_=CUm, pattern=[[1, 64]], compare_op=ALU.is_ge,
        fill=0.0, base=0, channel_multiplier=-1, reads=[B_CU], writes=[B_CU])
    P.d("sp", lnw, lnw_in, writes=[B_lnw])
    P.d("sp", convw, convw_in, writes=[B_convw])
    P.d("sp", dnnw, dnnw_in, writes=[B_dnnw])
    P.d("sp", dnp, dnp_in, writes=[B_dnp])

    hTo, B_hTo = H.alloc("hTo", [128, KT, LO], BF16, nbufs=NTO)
    mH = H.mark()
    if T0 > 0:
        hTp, B_hTp = H.alloc("hTp", [128, KT, T0], BF16, nbufs=NT - NTO)
    else:
        hTp, B_hTp = None, []
    B_hT = list(B_hTp) + list(B_hTo)

    def hcols(kt, t0, n):
        if t0 >= T0:
            return hTo[:, kt, t0 - T0:t0 - T0 + n]
        assert t0 + n <= T0
        return hTp[:, kt, t0:t0 + n]

    def hcols8(k0, t0, n):
        if t0 >= T0:
            return hTo[:, k0:k0 + 8, t0 - T0:t0 - T0 + n]
        return hTp[:, k0:k0 + 8, t0:t0 + n]

    wslots = [A.alloc(f"wslot{i}", [128, KT, 128], BF16) for i in range(3)]
    WS = Ring(wslots)

    wsr = {"ring": WS}

    def load_w(src_tile_ap, kt=KT):
        slot, bslot = wsr["ring"].next()
        P.d("pool", slot[:, 0:kt, :], src_tile_ap, writes=[bslot])
        return slot, bslot

    T_US, T_ZS, T_Q, T_K, T_V, T_ZD, T_BA, T_GS, T_GD = 0, 8, 16, 24, 32, 40, 48, 49, 65

    class WPipe:
        def __init__(self, tiles, depth=1):
            self.tiles = list(tiles)
            self.depth = depth
            self.q = []
            self.p = 0
            self.started = False

        def start(self):
            if not self.started:
                self.started = True
                self._fill(self.depth)

        def _fill(self, n):
            while len(self.q) < n and self.p < len(self.tiles):
                ap_, kt_ = self.tiles[self.p]
                self.q.append(load_w(ap_, kt=kt_))
                self.p += 1

        def take(self):
            self.start()
            self._fill(self.depth + 1)
            return self.q.pop(0)


    def proj_fm_g(slot, bslot, kts, rhs_fn, rhs_bufs_fn, blocks, consume, m=128, ring=None, G=None):
        blocks = list(blocks)
        rg = ring if ring is not None else cur["big"]
        if G is None:
            G = 4 if rg is BIG8 else 2
        for g0 in range(0, len(blocks), G):
            grp = blocks[g0:g0 + G]
            pss = [rg.next() for _ in grp]
            for kt in range(kts):
                for b, (ps, B_ps) in zip(grp, pss):
                    P.x("pe", "matmul", ps[0:m, :], lhsT=slot[:, kt, 0:m], rhs=rhs_fn(kt, b), start=(kt == 0), stop=(kt == kts - 1),
                        reads=[bslot] + rhs_bufs_fn(b), writes=[B_ps])
            for b, (ps, B_ps) in zip(grp, pss):
                consume(b, ps, B_ps)
            yield "P"

    def proj_fm(*a, **kw):
        for _ in proj_fm_g(*a, **kw):
            pass

    def h_rhs(kt, b):
        return hcols(kt, b * 512, 512)

    def h_bufs(b):
        return B_hT[b * 4:(b + 1) * 4]

    G_scr = nc.dram_tensor("G_scr", [128, 64 * 2 * 128], BF16).ap()
    Mi_scr = nc.dram_tensor("Mi_scr", [128, 64 * 128], BF16).ap()
    Mr_scr = nc.dram_tensor("Mr_scr", [128, 32 * 2 * 128], BF16).ap()
    A8_scr = nc.dram_tensor("A8_scr", [128, 128], F32).ap()
    B_Gscr, B_Miscr, B_Mrscr, B_A8scr = Buf("Gscr"), Buf("Miscr"), Buf("Mrscr"), Buf("A8scr")

    def s5_tables_gen():
        Gpad, B_Gpad = A.alloc("Gpad", [128, 64, 2, 128], BF16, top=True)
        Mintra, B_Mintra = A.alloc("Mintra", [128, 64, 128], BF16, top=True)
        Minter, B_Minter = A.alloc("Minter", [128, 32, 2, 128], BF16, top=True)
        A8, B_A8 = A.alloc("A8", [128, 2, 2, 32], F32, top=True)
        A8P, B_A8P = A.alloc("A8P", [128, 8, 2, 32], F32, top=True)
        A64, B_A64 = A.alloc("A64", [128, 2, 2, 32], F32, top=True)
        _wmark[0] = A.top2
        lam, B_lam = A.alloc("lam", [128, 3, 32], F32, top=True)
        sbt, B_sbt = A.alloc("sbt", [128, 2, 32, 16], F32, top=True)
        sct, B_sct = A.alloc("sct", [128, 2, 32, 16], F32, top=True)
        sdt, B_sdt = A.alloc("sdt", [128, 64], F32, top=True)
        MK, B_MK = A.alloc("MK", [128, 128], F32, top=True)
        POW, B_POW = A.alloc("POW", [128, 9, 2, 32], F32, top=True)
        NEG, B_NEG = A.alloc("NEG", [128, 8, 2, 32], F32, top=True)
        bbt, B_bbt = A.alloc("bbt", [128, 2, 32, 16], F32, top=True)
        sm = {}
        for nm in ("step", "x1", "mag", "th", "kk", "ths", "s", "c", "ar", "ai", "den", "xr", "fre", "fim", "t1", "t2", "t3", "t4", "n1", "n2", "n3", "n4", "ivr", "ivi"):
            sm[nm] = A.alloc("sm_" + nm, [128, 32], F32, top=True)
        P.d("sp", lam, lam_in, writes=[B_lam])
        P.d("sp", sbt, sb_in, writes=[B_sbt])
        P.d("sp", sct, sc_in, writes=[B_sct])
        P.d("sp", sdt, sd_in, writes=[B_sdt])
        P.x("pool", "memset", MK, 1.0, writes=[B_MK])
        P.x("pool", "affine_select", out=MK.rearrange("p (t c) -> p t c", c=16), in_=MK.rearrange("p (t c) -> p t c", c=16),
            pattern=[[16, 8], [0, 16]], compare_op=ALU.is_ge, fill=0.0, base=15, channel_multiplier=-1,
            reads=[B_MK], writes=[B_MK])
        P.x("pool", "memset", Gpad, 0.0, writes=[B_Gpad])

        def V(nm):
            return sm[nm][0]

        def BV(nm):
            return sm[nm][1]

        def tt(out, ob, a, ab, b, bb_, op, eng="dve"):
            P.x(eng, "tensor_tensor", out=out, in0=a, in1=b, op=op, reads=list(ab) + list(bb_), writes=list(ob))

        lre, lim, lst = lam[:, 0, :], lam[:, 1, :], lam[:, 2, :]
        P.x("act", "activation", out=V("step"), in_=lst, func=AF.Exp, reads=[B_lam], writes=[BV("step")])
        tt(V("x1"), [BV("x1")], lre, [B_lam], V("step"), [BV("step")], ALU.mult)
        P.x("act", "activation", out=V("mag"), in_=V("x1"), func=AF.Exp, reads=[BV("x1")], writes=[BV("mag")])
        tt(V("th"), [BV("th")], lim, [B_lam], V("step"), [BV("step")], ALU.mult)

        def sin_of(dst, shift):
            P.x("dve", "tensor_scalar", out=V("ths"), in0=V("th"), scalar1=shift, scalar2=None, op0=ALU.add,
                reads=[BV("th")], writes=[BV("ths")])
            P.x("dve", "tensor_scalar", out=V("kk"), in0=V("ths"), scalar1=PI, scalar2=None, op0=ALU.is_ge,
                reads=[BV("ths")], writes=[BV("kk")])
            for mth in range(1, 6):
                P.x("dve", "scalar_tensor_tensor", out=V("kk"), in0=V("ths"), scalar=(2 * mth + 1) * PI, in1=V("kk"),
                    op0=ALU.is_ge, op1=ALU.add, reads=[BV("ths"), BV("kk")], writes=[BV("kk")])
            P.x("dve", "scalar_tensor_tensor", out=V("ths"), in0=V("kk"), scalar=-2.0 * PI, in1=V("ths"),
                op0=ALU.mult, op1=ALU.add, reads=[BV("ths"), BV("kk")], writes=[BV("ths")])
            P.x("act", "activation", out=V(dst), in_=V("ths"), func=AF.Sin, reads=[BV("ths")], writes=[BV(dst)])
        sin_of("s", 0.0)
        yield
        sin_of("c", PI / 2)
        yield
        tt(V("ar"), [BV("ar")], V("mag"), [BV("mag")], V("c"), [BV("c")], ALU.mult)
        tt(V("ai"), [BV("ai")], V("mag"), [BV("mag")], V("s"), [BV("s")], ALU.mult)
        tt(V("t1"), [BV("t1")], lre, [B_lam], lre, [B_lam], ALU.mult)
        tt(V("t2"), [BV("t2")], lim, [B_lam], lim, [B_lam], ALU.mult)
        tt(V("den"), [BV("den")], V("t1"), [BV("t1")], V("t2"), [BV("t2")], ALU.add)
        P.x("dve", "reciprocal", out=V("den"), in_=V("den"), reads=[BV("den")], writes=[BV("den")])
        P.x("dve", "tensor_scalar", out=V("xr"), in0=V("ar"), scalar1=-1.0, scalar2=None, op0=ALU.add, reads=[BV("ar")], writes=[BV("xr")])
        tt(V("t1"), [BV("t1")], V("xr"), [BV("xr")], lre, [B_lam], ALU.mult)
        tt(V("t2"), [BV("t2")], V("ai"), [BV("ai")], lim, [B_lam], ALU.mult)
        tt(V("t1"), [BV("t1")], V("t1"), [BV("t1")], V("t2"), [BV("t2")], ALU.add)
        tt(V("fre"), [BV("fre")], V("t1"), [BV("t1")], V("den"), [BV("den")], ALU.mult)
        tt(V("t3"), [BV("t3")], V("ai"), [BV("ai")], lre, [B_lam], ALU.mult)
        tt(V("t4"), [BV("t4")], V("xr"), [BV("xr")], lim, [B_lam], ALU.mult)
        tt(V("t3"), [BV("t3")], V("t3"), [BV("t3")], V("t4"), [BV("t4")], ALU.subtract)
        tt(V("fim"), [BV("fim")], V("t3"), [BV("t3")], V("den"), [BV("den")], ALU.mult)
        tt(V("t1"), [BV("t1")], V("mag"), [BV("mag")], V("mag"), [BV("mag")], ALU.mult)
        P.x("dve", "reciprocal", out=V("t1"), in_=V("t1"), reads=[BV("t1")], writes=[BV("t1")])
        tt(V("ivr"), [BV("ivr")], V("ar"), [BV("ar")], V("t1"), [BV("t1")], ALU.mult)
        tt(V("t2"), [BV("t2")], V("ai"), [BV("ai")], V("t1"), [BV("t1")], ALU.mult)
        P.x("dve", "tensor_scalar", out=V("ivi"), in0=V("t2"), scalar1=-1.0, scalar2=None, op0=ALU.mult, reads=[BV("t2")], writes=[BV("ivi")])

        def cmul_small(dst, dbuf, j, src, sbuf, jm, mr, mrb, mi, mib, eng="dve", tp="t"):
            pr, pi = src[:, jm, 0, :], src[:, jm, 1, :]
            n1, n2, n3, n4 = tp + "1", tp + "2", tp + "3", tp + "4"
            tt(V(n1), [BV(n1)], pr, [sbuf], mr, [mrb], ALU.mult, eng=eng)
            tt(V(n2), [BV(n2)], pi, [sbuf], mi, [mib], ALU.mult, eng=eng)
            tt(dst[:, j, 0, :], [dbuf], V(n1), [BV(n1), dbuf], V(n2), [BV(n2)], ALU.subtract, eng=eng)
            tt(V(n3), [BV(n3)], pr, [sbuf], mi, [mib], ALU.mult, eng=eng)
            tt(V(n4), [BV(n4)], pi, [sbuf], mr, [mrb], ALU.mult, eng=eng)
            tt(dst[:, j, 1, :], [dbuf], V(n3), [BV(n3), dbuf], V(n4), [BV(n4)], ALU.add, eng=eng)

        P.x("pool", "memset", POW[:, 0, 0, :], 1.0, writes=[B_POW])
        P.x("pool", "memset", POW[:, 0, 1, :], 0.0, reads=[B_POW], writes=[B_POW])
        P.x("pool", "memset", NEG[:, 0, 0, :], 1.0, writes=[B_NEG])
        P.x("pool", "memset", NEG[:, 0, 1, :], 0.0, reads=[B_NEG], writes=[B_NEG])
        for j in range(1, 9):
            cmul_small(POW, B_POW, j, POW, B_POW, j - 1, V("ar"), BV("ar"), V("ai"), BV("ai"))
            yield
        for j in range(1, 8):
            cmul_small(NEG, B_NEG, j, NEG, B_NEG, j - 1, V("ivr"), BV("ivr"), V("ivi"), BV("ivi"), eng="pool", tp="n")
            yield
        for r in range(2):
            P.x("pool", "tensor_copy", out=A8[:, 0, r, :], in_=POW[:, 8, 0, :], reads=[B_POW, B_A8], writes=[B_A8])
            P.x("pool", "tensor_copy", out=A8[:, 1, r, :], in_=POW[:, 8, 1, :], reads=[B_POW, B_A8], writes=[B_A8])
        for r in range(2):
            P.x("pool", "tensor_copy", out=A8P[:, 0, r, :], in_=POW[:, 8, r, :], reads=[B_POW, B_A8P], writes=[B_A8P])
        for k in range(1, 8):
            cmul_small(A8P, B_A8P, k, A8P, B_A8P, k - 1, POW[:, 8, 0, :], B_POW, POW[:, 8, 1, :], B_POW)
        for r in range(2):
            P.x("pool", "tensor_copy", out=A64[:, 0, r, :], in_=A8P[:, 7, 0, :], reads=[B_A8P, B_A64], writes=[B_A64])
            P.x("pool", "tensor_copy", out=A64[:, 1, r, :], in_=A8P[:, 7, 1, :], reads=[B_A8P, B_A64], writes=[B_A64])
        yield
        fre_b = V("fre").unsqueeze(2).to_broadcast([128, 32, 16])
        fim_b = V("fim").unsqueeze(2).to_broadcast([128, 32, 16])
        big1, B_big1 = A.alloc("big1", [128, 32, 16], F32, top=True)
        big2, B_big2 = A.alloc("big2", [128, 32, 16], F32, top=True)
        tt(big1, [B_big1], sbt[:, 0], [B_sbt], fre_b, [BV("fre")], ALU.mult)
        tt(big2, [B_big2], sbt[:, 1], [B_sbt], fim_b, [BV("fim")], ALU.mult)
        tt(bbt[:, 0], [B_bbt], big1, [B_big1], big2, [B_big2], ALU.subtract)
        tt(big1, [B_big1], sbt[:, 1], [B_sbt], fre_b, [BV("fre")], ALU.mult)
        tt(big2, [B_big2], sbt[:, 0], [B_sbt], fim_b, [BV("fim")], ALU.mult)
        tt(bbt[:, 1], [B_bbt], big1, [B_big1, B_bbt], big2, [B_big2], ALU.add)

        GB = 2
        Pt, B_Pt = A.alloc("Pt", [128, 2, GB, 8, 16], F32, top=True)
        GTt, B_GTt = A.alloc("GTt", [128, 2, GB, 8, 16], F32, top=True)
        Qt, B_Qt = A.alloc("Qt", [128, 2, GB, 9, 16], F32, top=True)
        w1, B_w1 = A.alloc("w1", [128, GB, 9, 16], F32, top=True)
        w2, B_w2 = A.alloc("w2", [128, GB, 9, 16], F32, top=True)
        mtmp, B_mtmp = A.alloc("mtmp", [128, 128], F32, top=True)
        q1, B_q1 = A.alloc("q1", [128, GB, 9, 16], F32, top=True)
        q2, B_q2 = A.alloc("q2", [128, GB, 9, 16], F32, top=True)
        qz, B_qz = A.alloc("qz", [128, GB, 9, 16], F32, top=True)
        P.x("pool", "memset", qz, 0.0, writes=[B_qz])
        for gb in range(32 // GB):
            g0 = gb * GB
            gs = slice(g0, g0 + GB)
            sh = [128, GB, 8, 16]
            nr = NEG[:, :, 0, gs].rearrange("p t g -> p g t").unsqueeze(3).to_broadcast(sh)
            ni = NEG[:, :, 1, gs].rearrange("p t g -> p g t").unsqueeze(3).to_broadcast(sh)
            br = bbt[:, 0, gs, :].unsqueeze(2).to_broadcast(sh)
            bi = bbt[:, 1, gs, :].unsqueeze(2).to_broadcast(sh)
            a1, a2 = w1[:, :, 0:8, :], w2[:, :, 0:8, :]
            tt(a1, [B_w1], nr, [B_NEG], br, [B_bbt], ALU.mult)
            tt(a2, [B_w2], ni, [B_NEG], bi, [B_bbt], ALU.mult)
            tt(Pt[:, 0], [B_Pt], a1, [B_w1], a2, [B_w2], ALU.subtract)
            tt(a1, [B_w1], nr, [B_NEG], bi, [B_bbt], ALU.mult)
            tt(a2, [B_w2], ni, [B_NEG], br, [B_bbt], ALU.mult)
            tt(Pt[:, 1], [B_Pt], a1, [B_w1, B_Pt], a2, [B_w2], ALU.add)
            yield
            p7r = POW[:, 7, 0, gs].unsqueeze(2).unsqueeze(3).to_broadcast(sh)
            p7i = POW[:, 7, 1, gs].unsqueeze(2).unsqueeze(3).to_broadcast(sh)
            tt(a1, [B_w1], Pt[:, 0], [B_Pt], p7r, [B_POW], ALU.mult)
            tt(a2, [B_w2], Pt[:, 1], [B_Pt], p7i, [B_POW], ALU.mult)
            tt(GTt[:, 0], [B_GTt], a1, [B_w1], a2, [B_w2], ALU.subtract)
            tt(a1, [B_w1], Pt[:, 0], [B_Pt], p7i, [B_POW], ALU.mult)
            tt(a2, [B_w2], Pt[:, 1], [B_Pt], p7r, [B_POW], ALU.mult)
            tt(GTt[:, 1], [B_GTt], a1, [B_w1, B_GTt], a2, [B_w2], ALU.add)
            yield
            shq = [128, GB, 9, 16]
            pr = POW[:, :, 0, gs].rearrange("p j g -> p g j").unsqueeze(3).to_broadcast(shq)
            pi_ = POW[:, :, 1, gs].rearrange("p j g -> p g j").unsqueeze(3).to_broadcast(shq)
            cr = sct[:, 0, gs, :].unsqueeze(2).to_broadcast(shq)
            ci = sct[:, 1, gs, :].unsqueeze(2).to_broadcast(shq)
            tt(q1, [B_q1], cr, [B_sct], pr, [B_POW], ALU.mult, eng="pool")
            tt(q2, [B_q2], ci, [B_sct], pi_, [B_POW], ALU.mult, eng="pool")
            tt(Qt[:, 0], [B_Qt], q1, [B_q1], q2, [B_q2], ALU.subtract, eng="pool")
            tt(q1, [B_q1], cr, [B_sct], pi_, [B_POW], ALU.mult, eng="pool")
            tt(q2, [B_q2], ci, [B_sct], pr, [B_POW], ALU.mult, eng="pool")
            tt(q1, [B_q1], q1, [B_q1], q2, [B_q2], ALU.add, eng="pool")
            tt(Qt[:, 1], [B_Qt], qz, [B_qz], q1, [B_q1, B_Qt], ALU.subtract, eng="pool")
            for ri in range(2):
                P.x("pool", "tensor_copy", out=Minter[:, gs, ri, :].rearrange("p g (t c) -> p g t c", c=16),
                    in_=Qt[:, ri, :, 1:9, :], reads=[B_Qt, B_Minter], writes=[B_Minter])
            yield
            for gl in range(GB):
                for half in range(2):
                    yield
                    g = half * 32 + g0 + gl
                    base = half * 64
                    rows = slice(base, base + 64)
                    ps, B_ps = SMALL.next()
                    P.x("pe", "matmul", ps, lhsT=Pt[rows, 0, gl].rearrange("p t c -> p (t c)"),
                        rhs=Qt[rows, 0, gl, 0:8, :].rearrange("p t c -> p (t c)"), start=True, stop=False,
                        reads=[B_Pt, B_Qt], writes=[B_ps])
                    P.x("pe", "matmul", ps, lhsT=Pt[rows, 1, gl].rearrange("p t c -> p (t c)"),
                        rhs=Qt[rows, 1, gl, 0:8, :].rearrange("p t c -> p (t c)"), start=False, stop=True,
                        reads=[B_Pt, B_Qt], writes=[B_ps])
                    P.x("dve", "tensor_tensor", out=mtmp, in0=ps, in1=MK, op=ALU.mult, reads=[B_ps, B_MK], writes=[B_mtmp])
                    P.x("dve", "scalar_tensor_tensor", out=Mintra[:, g, :], in0=ident_f, scalar=sdt[:, g:g + 1], in1=mtmp,
                        op0=ALU.mult, op1=ALU.add, reads=[B_identf, B_sdt, B_mtmp, B_Mintra], writes=[B_Mintra])
                    for ri in range(2):
                        ps2, B_ps2 = SMALL.next()
                        P.x("pe", "transpose", ps2[:, 0:64], GTt[rows, ri, gl].rearrange("p t c -> p (t c)"),
                            ident_f[rows, base:base + 64], reads=[B_GTt, B_identf], writes=[B_ps2])
                        P.x("act", "activation", out=Gpad[:, g, ri, base:base + 64], in_=ps2[:, 0:64], func=AF.Copy,
                            reads=[B_ps2, B_Gpad], writes=[B_Gpad])

        A.top2 = _wmark[0]
        _tres.update(Gpad=(Gpad, B_Gpad), Mintra=(Mintra, B_Mintra), Minter=(Minter, B_Minter), A8=(A8, B_A8), A8P=(A8P, B_A8P), A64=(A64, B_A64))

    _wmark = [None]
    _tres = {}
    _tg = [s5_tables_gen()]

    def tick(n=1):
        for _ in range(n):
            if _tg[0] is None:
                return
            try:
                next(_tg[0])
            except StopIteration:
                _tg[0] = None
                return

    def drain_tables():
        while _tg[0] is not None:
            tick()

    m1 = A.mark()
    xts = [A.alloc(f"xt{i}", [128, D], F32) for i in range(2)]
    xss = [A.alloc(f"xs{i}", [128, D], BF16) for i in range(2)]
    ss, B_ss = A.alloc("ss", [128, NT], F32, nbufs=NT)
    rstd, B_rstd = A.alloc("rstd", [128, NT], F32, nbufs=NT)
    for tt in range(NT):
        xt, B_xt = xts[tt % 2]
        xs, B_xs = xss[tt % 2]
        P.d("sp", xt, x_in[tt * 128:(tt + 1) * 128, :], writes=[B_xt])
        P.x("act", "activation", out=xs, in_=xt, func=AF.Square, accum_out=ss[:, tt:tt + 1],
            reads=[B_xt], writes=[B_xs, B_ss[tt]])
        P.x("act", "activation", out=rstd[:, tt:tt + 1], in_=ss[:, tt:tt + 1], func=AF.Sqrt,
            bias=epsc[:, 0:1], scale=1.0 / D, reads=[B_ss[tt], B_eps], writes=[B_rstd[tt]])
        P.x("dve", "reciprocal", out=rstd[:, tt:tt + 1], in_=rstd[:, tt:tt + 1], reads=[B_rstd[tt]], writes=[B_rstd[tt]])
        P.x("act", "activation", out=xs, in_=xt, func=AF.Copy, scale=rstd[:, tt:tt + 1],
            reads=[B_xt, B_rstd[tt]], writes=[B_xs])
        for half in range(2):
            psf, B_psf = cur["big"].next()
            psb = psf.bitcast(BF16)
            for k in range(8):
                kt = half * 8 + k
                P.x("pe", "transpose", psb[:, k * 128:(k + 1) * 128], xs[:, kt * 128:(kt + 1) * 128], ident_b,
                    reads=[B_xs, B_identb], writes=[B_psf])
            P.x("dve", "tensor_tensor", out=hcols8(half * 8, tt * 128, 128),
                in0=psb.rearrange("p (k t) -> p k t", k=8),
                in1=lnw[:, half * 8:half * 8 + 8].unsqueeze(2).to_broadcast([128, 8, 128]), op=ALU.mult,
                reads=[B_psf, B_lnw], writes=[B_hT[tt]])
    A.release(m1)
    P.mark("stage1")
    dump("hTo", hTo, B_hTo, [128, KT, LO], BF16)
    if stop_after == "stage1":
        P.emit()
        return nc, dbg

    m1 = A.mark()
    usts = [A.alloc(f"ust{i}", [128, 8, NC8], BF16) for i in range(2)]
    u_pipe = WPipe([(w_in_t[T_US + j], KT) for j in range(8)], depth=1)
    B_U = [Buf(f"U{j}") for j in range(8)]
    for j in range(8):
        slot, bslot = u_pipe.take()
        ust, B_ust = usts[j % 2]

        def cons_u(b, ps, B_ps, ust=ust, B_ust=B_ust):
            P.x("act", "activation", out=ust[:, :, b * 64:(b + 1) * 64].rearrange("p t c -> p c t"),
                in_=ps.rearrange("p (c t) -> p c t", t=8), func=AF.Copy, reads=[B_ps], writes=[B_ust])
        proj_fm(slot, bslot, KT, h_rhs, h_bufs, range(NB), cons_u)
        P.d("sp", U_scr[j * 128:(j + 1) * 128], ust, reads=[B_ust], writes=[B_U[j]])
    A.release(m1)
    P.mark("u")
    if stop_after == "u":
        P.emit()
        return nc, dbg

    dnoT, B_dno = A.alloc("dnoT", [128, NH, LO], BF16, nbufs=NH)
    m_dn0 = A.mark()
    GC, B_GC = A.alloc("GC", [128, LT], F32)
    PT, B_PT = A.alloc("PT", [128, NP, 4, 8], F32)
    CT, B_CT = A.alloc("CT", [64, NCH, 3, 8], F32)
    GLB, B_GLB = A.alloc("GLB", [128, NCH * 8], F32)
    SEL, B_SEL = A.alloc("SEL", [128, 8, 128], F32)
    m0 = A.mark()
    G1, B_G1 = A.alloc("G1", [128, LT], F32)
    RM, B_RM = A.alloc("RM", [128, LT], F32)
    PTin, B_PTin = A.alloc("PTin", [128, LT], F32)
    CTin, B_CTin = A.alloc("CTin", [128, LT], F32)
    Dg, B_Dg = A.alloc("Dg", [128, NCH, 8], F32)
    negA, B_negA = A.alloc("negA", [128, 1], F32)

    P.x("pool", "memset", PTin, 0.0, writes=[B_PTin])
    P.x("pool", "memset", CTin, 0.0, writes=[B_CTin])
    P.x("pool", "memset", G1, 0.0, writes=[B_G1])
    P.x("pool", "memset", RM, 1.0, writes=[B_RM])
    P.x("pool", "memset", RM.rearrange("p (c t) -> p c t", t=64)[:, :, 0:1], 0.0, reads=[B_RM], writes=[B_RM])
    P.x("pool", "tensor_copy", out=SEL[32:40], in_=ident_f[32:40, 32:40].unsqueeze(2).to_broadcast([8, 8, 128]),
        reads=[B_identf], writes=[B_SEL])
    P.x("act", "activation", out=negA, in_=dnp[:, 0:1], func=AF.Exp, reads=[B_dnp], writes=[B_negA])
    P.x("dve", "tensor_scalar", out=negA, in0=negA, scalar1=-1.0, scalar2=None, op0=ALU.mult, reads=[B_negA], writes=[B_negA])

    wba, B_wba = load_w(w_in_t[T_BA])

    def cons_ba(b, ps, B_ps):
        sl = slice(b * 512, (b + 1) * 512)
        P.x("act", "activation", out=PTin[0:8, sl], in_=ps[0:8, :], func=AF.Sigmoid, reads=[B_ps], writes=[B_PTin])
        for (ra, rb) in ((32, 40), (64, 104)):
            P.x("act", "activation", out=G1[ra:rb, sl], in_=ps[ra:rb, :], func=AF.Exp, bias=dnp[ra:rb, 1:2], scale=1.0,
                reads=[B_ps, B_dnp], writes=[B_G1])
    proj_fm(wba, B_wba, KT, h_rhs, h_bufs, range(NB), cons_ba, m=104)
    for (ra, rb) in ((32, 40), (64, 104)):
        P.x("act", "activation", out=G1[ra:rb, :], in_=G1[ra:rb, :], func=AF.Ln, bias=onec[ra:rb, 0:1], scale=1.0,
            reads=[B_G1, B_one], writes=[B_G1])
        P.x("dve", "tensor_scalar", out=G1[ra:rb, :], in0=G1[ra:rb, :], scalar1=negA[ra:rb, 0:1], scalar2=None, op0=ALU.mult,
            reads=[B_G1, B_negA], writes=[B_G1])
        P.x("dve", "tensor_tensor_scan", out=GC[ra:rb, :], data0=RM[ra:rb, :], data1=G1[ra:rb, :], initial=0.0,
            op0=ALU.mult, op1=ALU.add, reads=[B_RM, B_G1], writes=[B_GC])
    P.x("pool", "tensor_copy", out=PTin[32:40, :], in_=GC[32:40, :], reads=[B_GC, B_PTin], writes=[B_PTin])
    P.x("act", "activation", out=PTin[64:72, :], in_=GC[64:72, :], func=AF.Exp, reads=[B_GC, B_PTin], writes=[B_PTin])
    P.x("dve", "tensor_scalar", out=CTin[32:40, :], in0=GC[32:40, :], scalar1=-1.0, scalar2=None, op0=ALU.mult,
        reads=[B_GC, B_CTin], writes=[B_CTin])
    P.x("act", "activation", out=CTin[64:72, :], in_=GC[64:72, :], func=AF.Exp, reads=[B_GC, B_CTin], writes=[B_CTin])
    gc3 = GC[96:104, :].rearrange("p (c t) -> p c t", t=64)
    P.x("dve", "tensor_tensor", out=CTin[96:104, :].rearrange("p (c t) -> p c t", t=64),
        in0=gc3[:, :, 63:64].to_broadcast([8, NCH, 64]), in1=gc3, op=ALU.subtract,
        reads=[B_GC, B_CTin], writes=[B_CTin])
    P.x("act", "activation", out=CTin[96:104, :], in_=CTin[96:104, :], func=AF.Exp, reads=[B_CTin], writes=[B_CTin])
    P.x("pool", "tensor_copy", out=PTin[96:104, :], in_=CTin[96:104, :], reads=[B_CTin, B_PTin], writes=[B_PTin])
    eg3 = PTin[64:72, :].rearrange("p (c t) -> p c t", t=64)
    P.x("dve", "tensor_tensor", out=Dg[64:72], in0=eg3[:, :, 63:64].to_broadcast([8, NCH, 8]),
        in1=ident_f[64:72, 64:72].unsqueeze(1).to_broadcast([8, NCH, 8]), op=ALU.mult,
        reads=[B_PTin, B_identf], writes=[B_Dg])
    psg, B_psg = BIG.next()
    P.x("pe", "matmul", psg[:, 0:NCH * 8], lhsT=ones_f[64:72, :], rhs=Dg[64:72].rearrange("p c h -> p (c h)"),
        start=True, stop=True, reads=[B_onesf, B_Dg], writes=[B_psg])
    P.x("act", "activation", out=GLB, in_=psg[:, 0:NCH * 8], func=AF.Copy, reads=[B_psg], writes=[B_GLB])
    for i in range(NP):
        ps, B_ps = SMALL.next()
        P.x("pe", "transpose", ps[:, 0:104], PTin[0:104, i * 128:(i + 1) * 128], ident_f[0:104, 0:104],
            reads=[B_PTin, B_identf], writes=[B_ps])
        pv = ps[:, 0:128].rearrange("p (k c) -> p k c", c=32)[:, :, 0:8]
        if i % 2:
            P.x("dve", "tensor_copy", out=PT[:, i], in_=pv, reads=[B_ps], writes=[B_PT])
        else:
            P.x("act", "activation", out=PT[:, i], in_=pv, func=AF.Copy, reads=[B_ps], writes=[B_PT])
    for ch in range(NCH):
        ps, B_ps = SMALL.next()
        P.x("pe", "transpose", ps[0:64, 0:104], CTin[0:104, ch * 64:(ch + 1) * 64], ident_f[0:104, 0:104],
            reads=[B_CTin, B_identf], writes=[B_ps])
        cv = ps[0:64, 32:128].rearrange("p (k c) -> p k c", c=32)[:, :, 0:8]
        if ch % 2:
            P.x("dve", "tensor_copy", out=CT[:, ch], in_=cv, reads=[B_ps], writes=[B_CT])
        else:
            P.x("act", "activation", out=CT[:, ch], in_=cv, func=AF.Copy, reads=[B_ps], writes=[B_CT])
    A.release(m0)
    P.mark("dn0")
    dump("PT", PT, B_PT, [128, NP, 4, 8])
    dump("CT", CT, B_CT, [64, NCH, 3, 8])
    dump("GLB", GLB, B_GLB, [128, NCH * 8])
    if stop_after == "dn0":
        P.emit()
        return nc, dbg

    HG = 2
    cur["big"] = BIG
    m_dn = A.mark()
    pre, B_pre = A.alloc("pre", [128, 3 + LT], F32)
    P.x("pool", "memset", pre[:, 0:3], 0.0, writes=[B_pre])
    acc, B_acc = A.alloc("acc", [128, LT], F32)
    MnSL, B_MnSL = A.alloc("MnSL", [128, 128], F32)
    MnCU, B_MnCU = A.alloc("MnCU", [64, 64], F32)
    P.x("dve", "tensor_scalar", out=MnSL, in0=SLm, scalar1=-1.0, scalar2=30000.0, op0=ALU.add, op1=ALU.mult,
        reads=[B_SL], writes=[B_MnSL])
    P.x("dve", "tensor_scalar", out=MnCU, in0=CUm, scalar1=-1.0, scalar2=30000.0, op0=ALU.add, op1=ALU.mult,
        reads=[B_CU], writes=[B_MnCU])
    MnCUp, B_MnCUp = A.alloc("MnCUp", [128, 128], F32)
    P.x("pool", "memset", MnCUp, 1.0, writes=[B_MnCUp])
    P.x("pool", "affine_select", out=MnCUp, in_=MnCUp, pattern=[[1, 128]], compare_op=ALU.is_ge,
        fill=0.0, base=0, channel_multiplier=-1, reads=[B_MnCUp], writes=[B_MnCUp])
    P.x("pool", "memset", MnCUp[0:64, 64:128], 0.0, reads=[B_MnCUp], writes=[B_MnCUp])
    P.x("dve", "tensor_scalar", out=MnCUp, in0=MnCUp, scalar1=-1.0, scalar2=30000.0, op0=ALU.add, op1=ALU.mult,
        reads=[B_MnCUp], writes=[B_MnCUp])
    sq, B_sq = A.alloc("sq", [128, LT], BF16)
    RN = Ring([A.alloc(f"rn{i}", [128, 512], F32) for i in range(1)])

    def ring(name, n, shape, dt):
        return Ring([A.alloc(f"{name}{i}", shape, dt) for i in range(n)])

    slots = []
    for s_ in range(HG):
        d_ = {}
        for nm in ("qT", "kT", "vT"):
            d_[nm] = A.alloc(f"{nm}{s_}", [128, LT], BF16)
        d_["szd"] = A.alloc(f"szd{s_}", [128, LO], BF16)
        d_["S_f"] = A.alloc(f"S_f{s_}", [128, 128], F32)
        d_["S_b"] = A.alloc(f"S_b{s_}", [128, 128], BF16)
        d_["E"] = ring(f"E{s_}_", 2, [128, 128], F32)
        d_["A"] = ring(f"Am{s_}_", 23, [128, 128], BF16)
        d_["P"] = ring(f"Pm{s_}_", 10, [128, 128], BF16)
        d_["bv"] = ring(f"bv{s_}_", 3, [128, 128], BF16)
        d_["kbg"] = ring(f"kbg{s_}_", 3, [128, 128], BF16)
        d_["wT"] = ring(f"wT{s_}_", 3, [128, 128], BF16)
        d_["TT"] = ring(f"TT{s_}_", 3, [128, 128], BF16)
        d_["u"] = ring(f"u{s_}_", 3, [128, 128], F32)
        d_["kd"] = ring(f"kd{s_}_", 3, [128, 128], BF16)
        d_["ET"] = ring(f"ET{s_}_", 2, [128, 128], F32)
        d_["qk"] = ring(f"qk{s_}_", 3, [128, 128], BF16)
        d_["vn"] = ring(f"vn{s_}_", 2, [128, 128], BF16)
        d_["wTz"] = ring(f"wTz{s_}_", 3, [128, 128], BF16)
        for (wz_, B_wz_) in d_["wTz"].items:
            P.x("pool", "memset", wz_, 0.0, writes=[B_wz_])
        d_["o1"] = ring(f"o1{s_}_", 2, [64, 128], F32)
        d_["o"] = ring(f"o{s_}_", 2, [64, 128], F32)
        d_["on"] = ring(f"on{s_}_", 2, [64, 128], BF16)
        d_["st"] = ring(f"st{s_}_", 4, [64, 2], F32)
        d_["ojunk"] = A.alloc(f"ojunk{s_}", [64, 128], BF16)
        slots.append(d_)

    QSCALE = 128.0 ** -0.5

    def head_proj(h, sl_):
        qT, B_qT = sl_["qT"]
        kT, B_kT = sl_["kT"]
        vT, B_vT = sl_["vT"]
        szd, B_szd = sl_["szd"]
        deferred = []

        def l2norm(idx, dst, B_dst):
            for b in range(NB):
                sl = slice(b * 512, (b + 1) * 512)
                ps, B_ps = BIG.next()
                P.x("pe", "matmul", ps, lhsT=ones_b, rhs=sq[:, sl], start=True, stop=True,
                    reads=[B_onesb, B_sq], writes=[B_ps])
                rn, B_rn = RN.next()
                P.x("act", "activation", out=rn, in_=ps, func=AF.Sqrt, bias=epsc[:, 0:1], scale=1.0,
                    reads=[B_ps, B_eps], writes=[B_rn])
                P.x("dve", "reciprocal", out=rn, in_=rn, reads=[B_rn], writes=[B_rn])
                P.x("dve", "scalar_tensor_tensor", out=dst[:, sl], in0=acc[:, sl], scalar=(QSCALE if idx == 0 else 1.0),
                    in1=rn, op0=ALU.mult, op1=ALU.mult, reads=[B_acc, B_rn], writes=[B_dst])

        for idx, (cbase, dst, B_dst) in enumerate(((T_Q, qT, B_qT), (T_K, kT, B_kT), (T_V, vT, B_vT))):
            slot, bslot = dn_pipe.take()

            def cons_pre(b, ps, B_ps):
                P.x("act", "activation", out=pre[:, 3 + b * 512:3 + (b + 1) * 512], in_=ps, func=AF.Copy,
                    reads=[B_ps], writes=[B_pre])
            yield from proj_fm_g(slot, bslot, KT, h_rhs, h_bufs, range(NB), cons_pre, ring=BIG8, G=4)
            while deferred:
                deferred.pop(0)()
            tile = idx * 8 + h
            P.x("dve", "tensor_scalar", out=acc, in0=pre[:, 0:LT], scalar1=convw[:, tile, 0:1], scalar2=None, op0=ALU.mult,
                reads=[B_pre, B_convw], writes=[B_acc])
            for j in range(1, 4):
                P.x("dve", "scalar_tensor_tensor", out=acc, in0=pre[:, j:j + LT], scalar=convw[:, tile, j:j + 1], in1=acc,
                    op0=ALU.mult, op1=ALU.add, reads=[B_pre, B_convw, B_acc], writes=[B_acc])
            yield "P"
            if idx == 2:
                P.x("act", "activation", out=vT, in_=acc, func=AF.Silu, reads=[B_acc], writes=[B_vT])
            else:
                P.x("act", "activation", out=acc, in_=acc, func=AF.Silu, reads=[B_acc], writes=[B_acc])
                P.x("act", "activation", out=sq, in_=acc, func=AF.Square, reads=[B_acc], writes=[B_sq])
                deferred.append(lambda idx=idx, dst=dst, B_dst=B_dst: l2norm(idx, dst, B_dst))
        slot, bslot = dn_pipe.take()

        def cons_zd(b, ps, B_ps):
            bo = b - (NB - NBO)
            P.x("act", "activation", out=szd[:, bo * 512:(bo + 1) * 512], in_=ps, func=AF.Silu, reads=[B_ps], writes=[B_szd])
        yield from proj_fm_g(slot, bslot, KT, h_rhs, h_bufs, range(NB - NBO, NB), cons_zd, ring=BIG8, G=4)
        while deferred:
            deferred.pop(0)()

    def intra_gen(h, i, sl_):
        qT, B_qT = sl_["qT"]
        kT, B_kT = sl_["kT"]
        vT, B_vT = sl_["vT"]
        tok = slice(i * 128, (i + 1) * 128)
        psD, B_psD = SMALL.next()
        P.x("pe", "matmul", psD, lhsT=SEL[32:40, h, :], rhs=GC[32:40, tok], start=True, stop=True,
            reads=[B_SEL, B_GC], writes=[B_psD])
        E, B_E = sl_["E"].next()
        gcp = PT[:, i, 1, h:h + 1]
        P.x("dve", "scalar_tensor_tensor", out=E, in0=psD, scalar=gcp, in1=MnSL, op0=ALU.subtract, op1=ALU.subtract,
            reads=[B_psD, B_PT, B_MnSL], writes=[B_E])
        ETp, B_ETp = sl_["ET"].next()
        P.x("dve", "scalar_tensor_tensor", out=ETp, in0=psD, scalar=gcp, in1=MnCUp, op0=ALU.subtract, op1=ALU.add,
            reads=[B_psD, B_PT, B_MnCUp], writes=[B_ETp])
        P.x("act", "activation", out=E, in_=E, func=AF.Exp, scale=-1.0, reads=[B_E], writes=[B_E])
        P.x("act", "activation", out=ETp, in_=ETp, func=AF.Exp, reads=[B_ETp], writes=[B_ETp])
        yield
        pskk, B_pskk = SMALL.next()
        P.x("pe", "matmul", pskk, lhsT=kT[:, tok], rhs=kT[:, tok], start=True, stop=True, reads=[B_kT], writes=[B_pskk])
        Am, B_Am = sl_["A"].next()
        P.x("dve", "scalar_tensor_tensor", out=Am, in0=pskk, scalar=PT[:, i, 0, h:h + 1], in1=E, op0=ALU.mult, op1=ALU.mult,
            reads=[B_pskk, B_PT, B_E], writes=[B_Am])
        yield
        psB, B_psB = TRB.next()
        P.x("pe", "transpose", psB, Am, ident_b, reads=[B_Am, B_identb], writes=[B_psB])
        Bm, B_Bm = sl_["A"].next()
        P.x("act", "activation", out=Bm, in_=psB, func=AF.Copy, reads=[B_psB], writes=[B_Bm])
        P0, B_P0 = sl_["P"].next()
        P.x("pool", "tensor_tensor", out=P0, in0=ident_b, in1=Bm, op=ALU.subtract, reads=[B_identb, B_Bm], writes=[B_P0])
        yield
        psv, B_psv = TRB.next()
        P.x("pe", "transpose", psv, vT[:, tok], ident_b, reads=[B_vT, B_identb], writes=[B_psv])
        bv, B_bv = sl_["bv"].next()
        P.x("act", "activation", out=bv, in_=psv, func=AF.Copy, scale=PT[:, i, 0, h:h + 1], reads=[B_psv, B_PT], writes=[B_bv])
        psk, B_psk = TRB.next()
        P.x("pe", "transpose", psk, kT[:, tok], ident_b, reads=[B_kT, B_identb], writes=[B_psk])
        kbg, B_kbg = sl_["kbg"].next()
        P.x("dve", "tensor_scalar", out=kbg, in0=psk, scalar1=PT[:, i, 0, h:h + 1], scalar2=PT[:, i, 2, h:h + 1],
            op0=ALU.mult, op1=ALU.mult, reads=[B_psk, B_PT], writes=[B_kbg])
        kdp, B_kdp = sl_["kd"].next()
        P.x("dve", "tensor_scalar", out=kdp, in0=psk, scalar1=PT[:, i, 3, h:h + 1], scalar2=None, op0=ALU.mult,
            reads=[B_psk, B_PT], writes=[B_kdp])
        yield
        pskq, B_pskq = SMALL.next()
        P.x("pe", "matmul", pskq, lhsT=kT[:, tok], rhs=qT[:, tok], start=True, stop=True, reads=[B_kT, B_qT], writes=[B_pskq])
        qkp, B_qkp = sl_["qk"].next()
        P.x("dve", "tensor_tensor", out=qkp, in0=pskq, in1=ETp, op=ALU.mult, reads=[B_pskq, B_ETp], writes=[B_qkp])
        yield
        chunks = []
        for xh in range(2):
            chunks.append(dict(ch=2 * i + xh, xh=xh, R=slice(64 * xh, 64 * xh + 64),
                               ctok=slice(i * 128 + 64 * xh, i * 128 + 64 * xh + 64)))
        Ac, B_Ac, Bc, B_Bc, Pc, B_Pc = Am, B_Am, Bm, B_Bm, P0, B_P0
        for lvl in range(5):
            psA, B_psA = SMALL.next()
            P.x("pe", "matmul", psA, lhsT=Bc, rhs=Ac, start=True, stop=True, reads=[B_Bc, B_Ac], writes=[B_psA])
            A2, B_A2 = sl_["A"].next()
            P.x("dve", "tensor_copy", out=A2, in_=psA, reads=[B_psA], writes=[B_A2])
            if lvl < 4:
                psB2, B_psB2 = SMALL.next()
                P.x("pe", "matmul", psB2, lhsT=Ac, rhs=Bc, start=True, stop=True, reads=[B_Bc, B_Ac], writes=[B_psB2])
                B2, B_B2 = sl_["A"].next()
                P.x("act", "activation", out=B2, in_=psB2, func=AF.Copy, reads=[B_psB2], writes=[B_B2])
            else:
                B2, B_B2 = None, None
            yield
            psP, B_psP = SMALL.next()
            P.x("pe", "matmul", psP, lhsT=A2, rhs=Pc, start=True, stop=True, reads=[B_A2, B_Pc], writes=[B_psP])
            if lvl < 4:
                Pn, B_Pn = sl_["P"].next()
            else:
                Pn, B_Pn = sl_["TT"].next()
            P.x("dve", "tensor_tensor", out=Pn, in0=Pc, in1=psP, op=ALU.add, reads=[B_Pc, B_psP], writes=[B_Pn])
            Ac, B_Ac, Bc, B_Bc, Pc, B_Pc = A2, B_A2, B2, B_B2, Pn, B_Pn
            yield
        TT, B_TT = Pc, B_Pc
        psw, B_psw = SMALL.next()
        P.x("pe", "matmul", psw, lhsT=kbg, rhs=TT, start=True, stop=True, reads=[B_kbg, B_TT], writes=[B_psw])
        wT, B_wT = sl_["wT"].next()
        P.x("act", "activation", out=wT, in_=psw, func=AF.Copy, reads=[B_psw], writes=[B_wT])
        wTz, B_wTz = sl_["wTz"].next()
        P.x("act", "activation", out=wTz[:, 64:128], in_=psw[:, 64:128], func=AF.Copy, reads=[B_psw, B_wTz], writes=[B_wTz])
        yield
        psu, B_psu = SMALL.next()
        P.x("pe", "matmul", psu, lhsT=TT, rhs=bv, start=True, stop=True, reads=[B_TT, B_bv], writes=[B_psu])
        u_sb, B_u = sl_["u"].next()
        P.x("act", "activation", out=u_sb, in_=psu, func=AF.Copy, reads=[B_psu], writes=[B_u])
        yield
        return dict(wT=wT, B_wT=B_wT, wTz=wTz, B_wTz=B_wTz, u=u_sb, B_u=B_u, kd=kdp, B_kd=B_kdp, qk=qkp, B_qk=B_qkp, chunks=chunks)

    def recur_gen(h, i, r, sl_):
        qT, B_qT = sl_["qT"]
        szd, B_szd = sl_["szd"]
        S_f, B_Sf = sl_["S_f"]
        S_b, B_Sb = sl_["S_b"]
        ojunk, B_ojunk = sl_["ojunk"]
        vn, B_vn = sl_["vn"].next()
        for c in r["chunks"]:
            ch, R, ctok, xh = c["ch"], c["R"], c["ctok"], c["xh"]
            own = ctok.start >= T0
            psws, B_psws = SMALL.next()
            if xh == 0:
                P.x("pe", "matmul", psws[0:64, :], lhsT=r["wT"][:, 0:64], rhs=S_b, start=True, stop=True,
                    reads=[r["B_wT"], B_Sb], writes=[B_psws])
            else:
                P.x("pe", "matmul", psws, lhsT=r["wTz"], rhs=S_b, start=True, stop=True,
                    reads=[r["B_wTz"], B_Sb], writes=[B_psws])
            P.x("dve", "tensor_tensor", out=vn[R, :], in0=r["u"][R, :], in1=psws[R, :], op=ALU.subtract,
                reads=[r["B_u"], B_psws, B_vn], writes=[B_vn])
            if own:
                pso1, B_pso1 = SMALL.next()
                P.x("pe", "matmul", pso1[0:64, :], lhsT=qT[:, ctok], rhs=S_b, start=True, stop=True,
                    reads=[B_qT, B_Sb], writes=[B_pso1])
                o1, B_o1 = sl_["o1"].next()
                P.x("act", "activation", out=o1, in_=pso1[0:64, :], func=AF.Copy, scale=CT[:, ch, 1, h:h + 1],
                    reads=[B_pso1, B_CT], writes=[B_o1])
            yield
            psdS, B_psdS = SMALL.next()
            P.x("pe", "matmul", psdS, lhsT=r["kd"][R, :], rhs=vn[R, :], start=True, stop=True, reads=[r["B_kd"], B_vn], writes=[B_psdS])
            gl = GLB[:, ch * 8 + h:ch * 8 + h + 1]
            P.x("dve", "scalar_tensor_tensor", out=S_b, in0=S_f, scalar=gl, in1=psdS,
                op0=ALU.mult, op1=ALU.add, reads=[B_Sf, B_GLB, B_psdS, B_Sb], writes=[B_Sb])
            P.x("dve", "scalar_tensor_tensor", out=S_f, in0=S_f, scalar=gl, in1=psdS,
                op0=ALU.mult, op1=ALU.add, reads=[B_Sf, B_GLB, B_psdS], writes=[B_Sf])
            yield
            if own:
                pso2, B_pso2 = SMALL.next()
                P.x("pe", "matmul", pso2[0:64, :], lhsT=r["qk"][R, R], rhs=vn[R, :], start=True, stop=True,
                    reads=[r["B_qk"], B_vn], writes=[B_pso2])
                o, B_o = sl_["o"].next()
                P.x("dve", "tensor_tensor", out=o, in0=o1, in1=pso2[0:64, :], op=ALU.add, reads=[B_o1, B_pso2], writes=[B_o])
                st_, B_st = sl_["st"].next()
                P.x("act", "activation", out=ojunk, in_=o, func=AF.Square, accum_out=st_[:, 0:1],
                    reads=[B_o], writes=[B_ojunk, B_st])
                P.x("act", "activation", out=st_[:, 1:2], in_=st_[:, 0:1], func=AF.Sqrt, bias=epsc[0:64, 0:1], scale=1.0 / 128,
                    reads=[B_st, B_eps], writes=[B_st])
                P.x("dve", "reciprocal", out=st_[:, 1:2], in_=st_[:, 1:2], reads=[B_st], writes=[B_st])
                on, B_on = sl_["on"].next()
                P.x("act", "activation", out=on, in_=o, func=AF.Copy, scale=st_[:, 1:2], reads=[B_o, B_st], writes=[B_on])
                yield
                psoT, B_psoT = TRB.next()
                P.x("pe", "transpose", psoT[:, 0:64], on, ident_b[0:64, 0:64], reads=[B_on, B_identb], writes=[B_psoT])
                t0o = ctok.start - T0
                P.x("dve", "scalar_tensor_tensor", out=dnoT[:, h, t0o:t0o + 64], in0=psoT[:, 0:64], scalar=dnnw[:, 0:1],
                    in1=szd[:, t0o:t0o + 64], op0=ALU.mult, op1=ALU.mult,
                    reads=[B_psoT, B_dnnw, B_szd], writes=[B_dno[h]])
                yield

    def interleave(g1, g2):
        res = None
        act = [g for g in (g1, g2) if g is not None]
        while act:
            for g in list(act):
                try:
                    next(g)
                except StopIteration as e:
                    if g is g1:
                        res = e.value
                    act.remove(g)
            yield
        return res

    def head_gen(h, sl_):
        S_f, B_Sf = sl_["S_f"]
        S_b, B_Sb = sl_["S_b"]
        P.x("pool", "memset", S_f, 0.0, writes=[B_Sf])
        P.x("pool", "memset", S_b, 0.0, writes=[B_Sb])
        results = {}
        intras = {}
        next_intra = 0
        next_recur = 0
        recur_g = None
        recur_done = 0
        while recur_done < NP:
            while len(intras) < 2 and next_intra < NP and next_intra <= recur_done + 2:
                intras[next_intra] = intra_gen(h, next_intra, sl_)
                next_intra += 1
            if recur_g is None and next_recur in results:
                recur_g = recur_gen(h, next_recur, results.pop(next_recur), sl_)
                next_recur += 1
            for j in list(intras):
                try:
                    next(intras[j])
                except StopIteration as e:
                    results[j] = e.value
                    del intras[j]
            if recur_g is not None:
                try:
                    next(recur_g)
                except StopIteration:
                    recur_g = None
                    recur_done += 1
            yield

    dn_tiles = []
    for hg in range(0, nheads, HG):
        for h in range(hg, min(hg + HG, nheads)):
            dn_tiles += [(w_in_t[T_Q + h], KT), (w_in_t[T_K + h], KT), (w_in_t[T_V + h], KT), (w_in_t[T_ZD + h], KT)]
    dn_pipe = WPipe(dn_tiles, depth=1)
    dn_pipe.start()
    for hg in range(0, nheads, HG):
        hs = list(range(hg, min(hg + HG, nheads)))
        for k, h in enumerate(hs):
            for _ in head_proj(h, slots[k]):
                pass
        if hg == 0 and hs:
            dump("qT", slots[len(hs) - 1]["qT"][0], slots[len(hs) - 1]["qT"][1], [128, LT], BF16)
            dump("kT", slots[len(hs) - 1]["kT"][0], slots[len(hs) - 1]["kT"][1], [128, LT], BF16)
            dump("vT", slots[len(hs) - 1]["vT"][0], slots[len(hs) - 1]["vT"][1], [128, LT], BF16)
        P.mark(f"dn_g{hg}_proj")
        gens = [head_gen(h, slots[k]) for k, h in enumerate(hs)]
        cur["small"] = BIG8
        while gens:
            for g in list(gens):
                try:
                    next(g)
                except StopIteration:
                    gens.remove(g)
        cur["small"] = _SM
    A.release(m_dn0)
    H.release(mH)
    cur["big"] = BIG8
    P.mark("dn")
    dump("dnoT", dnoT[:, 0:nheads], B_dno, [128, nheads, LO], BF16)
    if stop_after == "dn":
        P.emit()
        return nc, dbg

    drain_tables()
    m_s5w = A.mark()
    Gpad, B_Gpad = _tres["Gpad"]
    Mintra, B_Mintra = _tres["Mintra"]
    Minter, B_Minter = _tres["Minter"]
    A8, B_A8 = _tres["A8"]
    A8P, B_A8P = _tres["A8P"]
    A64, B_A64 = _tres["A64"]
    Sst, B_Sst = A.alloc("Sst", [128, 2, 32], F32)
    y2T, B_y2T = H.alloc("y2T", [128, 8, LO], BF16, nbufs=8)
    P.x("pool", "memset", Sst, 0.0, writes=[B_Sst])
    P.mark("s5tab")
    m_blk = A.mark()
    UC = Ring([A.alloc(f"ucol{i}", [128, 64, 64], BF16) for i in range(1)])
    YC = Ring([A.alloc(f"ycol{i}", [128, 64, 64], BF16) for i in range(1)])
    _uy = [UC.items[0], YC.items[0]]
    Lt, B_Lt = A.alloc("Lt", [128, 2, 32, 64], F32)
    B_Lre, B_Lim = Buf("Lre"), Buf("Lim")
    ct1, B_ct1 = A.alloc("ct1", [128, 32, 4, 8], F32)
    ct2, B_ct2 = A.alloc("ct2", [128, 32, 4, 8], F32)
    mH2 = H.mark()
    hist, B_hist = H.alloc("hist", [128, 2, 32, 64], BF16)
    sc1, B_sc1 = H.alloc("sc1", [128, 2, 32], F32)
    sc2, B_sc2 = H.alloc("sc2", [128, 2, 32], F32)
    xm1, B_xm1 = H.alloc("xm1", [128, 2, 32, 8], F32)
    xm2, B_xm2 = H.alloc("xm2", [128, 2, 32, 8], F32)
    Cs, B_Cs = H.alloc("Cs", [128, 2, 32, 9], F32)
    Uv = U_scr.rearrange("(g ci) t c -> t ci g c", ci=16)
    Yv = Y_scr.rearrange("(g co) t c -> t co g c", co=16)
    B_Y = Buf("Yscr")
    AR2 = A8[:, 0]
    AI2 = A8[:, 1]
    for b in range(NB):
        own = b >= NB - NBO
        bo = b - (NB - NBO)
        ucol, B_uc = _uy[b % 2]
        for tau in range(8):
            P.d("sp", ucol[16 * tau:16 * tau + 16, :, :], Uv[tau][:, :, b * 64:(b + 1) * 64], reads=B_U, writes=[B_uc])
        for q4 in range(8):
            ps, B_ps = _SM.next()
            for k in range(4):
                gp = q4 * 4 + k
                for ri in range(2):
                    o_ = ps[:, (k * 2 + ri) * 64:(k * 2 + ri + 1) * 64]
                    P.x("pe", "matmul", o_, lhsT=Gpad[:, gp, ri, :], rhs=ucol[:, gp, :], start=True, stop=False,
                        reads=[B_Gpad, B_uc], writes=[B_ps])
                    P.x("pe", "matmul", o_, lhsT=Gpad[:, 32 + gp, ri, :], rhs=ucol[:, 32 + gp, :], start=False, stop=True,
                        reads=[B_Gpad, B_uc], writes=[B_ps])
            o_l = Lt[:, :, q4 * 4:q4 * 4 + 4, :].rearrange("p r g c -> p g r c")
            i_l = ps.rearrange("p (g r c) -> p g r c", g=4, r=2)
            if q4 % 2:
                P.x("act", "activation", out=o_l, in_=i_l, func=AF.Copy, reads=[B_ps, B_Lt, B_Lre, B_Lim], writes=[B_Lt, B_Lre, B_Lim])
            else:
                P.x("dve", "tensor_copy", out=o_l, in_=i_l, reads=[B_ps, B_Lt, B_Lre, B_Lim], writes=[B_Lt, B_Lre, B_Lim])
        L5 = Lt.rearrange("p r g (s k) -> p r g s k", k=8)
        LB = [B_Lt, B_Lre, B_Lim]
        ARb = A8[:, 0].unsqueeze(3).to_broadcast([128, 2, 32, 8])
        AIb = A8[:, 1].unsqueeze(3).to_broadcast([128, 2, 32, 8])
        for k in range(1, 8):
            xp = L5[:, :, :, :, k - 1]
            P.x("dve", "tensor_tensor", out=xm1, in0=ARb, in1=xp, op=ALU.mult, reads=[B_A8] + LB, writes=[B_xm1])
            P.x("dve", "tensor_tensor", out=xm2, in0=AIb, in1=xp, op=ALU.mult, reads=[B_A8] + LB, writes=[B_xm2])
            P.x("dve", "tensor_tensor", out=xm1, in0=xm1, in1=L5[:, :, :, :, k], op=ALU.add, reads=[B_xm1] + LB, writes=[B_xm1])
            P.x("dve", "tensor_tensor", out=L5[:, 0, :, :, k], in0=xm1[:, 0], in1=xm2[:, 1], op=ALU.subtract,
                reads=[B_xm1, B_xm2] + LB, writes=LB)
            P.x("dve", "tensor_tensor", out=L5[:, 1, :, :, k], in0=xm1[:, 1], in1=xm2[:, 0], op=ALU.add,
                reads=[B_xm1, B_xm2] + LB, writes=LB)
        P.x("dve", "tensor_copy", out=Cs[:, :, :, 0], in_=Sst, reads=[B_Sst, B_Cs], writes=[B_Cs])
        for sg_ in range(8):
            cp = Cs[:, :, :, sg_]
            P.x("dve", "tensor_tensor", out=sc1, in0=A64[:, 0], in1=cp, op=ALU.mult, reads=[B_A64, B_Cs], writes=[B_sc1])
            P.x("dve", "tensor_tensor", out=sc2, in0=A64[:, 1], in1=cp, op=ALU.mult, reads=[B_A64, B_Cs], writes=[B_sc2])
            P.x("dve", "tensor_tensor", out=sc1, in0=sc1, in1=L5[:, :, :, sg_, 7], op=ALU.add, reads=[B_sc1] + LB, writes=[B_sc1])
            P.x("dve", "tensor_tensor", out=Cs[:, 0, :, sg_ + 1], in0=sc1[:, 0, :], in1=sc2[:, 1, :], op=ALU.subtract,
                reads=[B_sc1, B_sc2, B_Cs], writes=[B_Cs])
            P.x("dve", "tensor_tensor", out=Cs[:, 1, :, sg_ + 1], in0=sc1[:, 1, :], in1=sc2[:, 0, :], op=ALU.add,
                reads=[B_sc1, B_sc2, B_Cs], writes=[B_Cs])
        if own:
            sh4 = [128, 32, 4, 8]
            Wr = A8P[:, :, 0, :].rearrange("p k g -> p g k").unsqueeze(2).to_broadcast(sh4)
            Wi = A8P[:, :, 1, :].rearrange("p k g -> p g k").unsqueeze(2).to_broadcast(sh4)
            for s0 in (0, 4):
                Cr = Cs[:, 0, :, s0:s0 + 4].unsqueeze(3).to_broadcast(sh4)
                Ci = Cs[:, 1, :, s0:s0 + 4].unsqueeze(3).to_broadcast(sh4)
                Lre, Lim = L5[:, 0, :, s0:s0 + 4, :], L5[:, 1, :, s0:s0 + 4, :]
                P.x("dve", "tensor_tensor", out=ct1, in0=Wr, in1=Cr, op=ALU.mult, reads=[B_A8P, B_Cs, B_ct1], writes=[B_ct1])
                P.x("dve", "tensor_tensor", out=Lre, in0=Lre, in1=ct1, op=ALU.add, reads=[B_ct1, B_Lre, B_Lt], writes=[B_Lre])
                P.x("dve", "tensor_tensor", out=ct1, in0=Wi, in1=Ci, op=ALU.mult, reads=[B_A8P, B_Cs, B_ct1], writes=[B_ct1])
                P.x("dve", "tensor_tensor", out=Lre, in0=Lre, in1=ct1, op=ALU.subtract, reads=[B_ct1, B_Lre], writes=[B_Lre])
                P.x("pool", "tensor_tensor", out=ct2, in0=Wr, in1=Ci, op=ALU.mult, reads=[B_A8P, B_Cs, B_ct2], writes=[B_ct2])
                P.x("pool", "tensor_tensor", out=Lim, in0=Lim, in1=ct2, op=ALU.add, reads=[B_ct2, B_Lim, B_Lt], writes=[B_Lim])
                P.x("pool", "tensor_tensor", out=ct2, in0=Wi, in1=Cr, op=ALU.mult, reads=[B_A8P, B_Cs, B_ct2], writes=[B_ct2])
                P.x("pool", "tensor_tensor", out=Lim, in0=Lim, in1=ct2, op=ALU.add, reads=[B_ct2, B_Lim], writes=[B_Lim])
            P.x("act", "activation", out=hist[:, :, :, 1:64], in_=Lt[:, :, :, 0:63], func=AF.Copy,
                reads=[B_Lre, B_Lim, B_Lt, B_hist], writes=[B_hist])
            P.x("pool", "tensor_copy", out=hist[:, :, :, 0], in_=Sst, reads=[B_Sst, B_hist], writes=[B_hist])
        P.x("dve", "tensor_copy", out=Sst, in_=Cs[:, :, :, 8], reads=[B_Cs, B_Sst, B_hist], writes=[B_Sst])
        if not own:
            continue
        ycol, B_yc = _uy[(b + 1) % 2]
        for q8 in range(8):
            ps, B_ps = _SM.next()
            for k in range(8):
                g = q8 * 8 + k
                half, gp = g // 32, g % 32
                rows = slice(half * 64, half * 64 + 64)
                o_ = ps[:, k * 64:(k + 1) * 64]
                P.x("pe", "matmul", o_, lhsT=Mintra[:, g, :], rhs=ucol[:, g, :], start=True, stop=False,
                    reads=[B_Mintra, B_uc], writes=[B_ps])
                P.x("pe", "matmul", o_, lhsT=Minter[rows, gp, 0, :], rhs=hist[rows, 0, gp, :], start=False, stop=False,
                    reads=[B_Minter, B_hist], writes=[B_ps])
                P.x("pe", "matmul", o_, lhsT=Minter[rows, gp, 1, :], rhs=hist[rows, 1, gp, :], start=False, stop=True,
                    reads=[B_Minter, B_hist], writes=[B_ps])
            P.x("act", "activation", out=ycol[:, q8 * 8:q8 * 8 + 8, :], in_=ps.rearrange("p (g c) -> p g c", g=8),
                func=AF.Gelu_apprx_tanh, reads=[B_ps, B_yc], writes=[B_yc])
        for t in range(8):
            P.d("sp", Yv[t][:, :, bo * 64:(bo + 1) * 64], ycol[16 * t:16 * t + 16, :, :], reads=[B_yc], writes=[B_Y])
    A.release(m_s5w)
    A.release_top()
    H.release(mH2)
    for j in range(8):
        P.d("sp", y2T[:, j, :], Y_scr[j * 128:(j + 1) * 128].rearrange("p t c -> p (t c)"), reads=[B_Y], writes=[B_y2T[j]])
    P.mark("s5blk")
    dump("y2T", y2T, B_y2T, [128, 8, LO], BF16)
    if stop_after == "s5":
        P.emit()
        return nc, dbg

    y4T, B_y4 = A.alloc("y4T", [128, 8, LO], BF16, nbufs=8)
    m_glu = A.mark()
    y3s = [A.alloc(f"y3_{i}", [128, LO], BF16) for i in range(2)]
    SG = Ring([A.alloc(f"sg{i}", [128, 512], BF16) for i in range(2)])
    SZ = Ring([A.alloc(f"sz{i}", [128, 512], BF16) for i in range(2)])

    def y2_rhs(kt, pb):
        return y2T[:, kt, pb * 512:(pb + 1) * 512]

    def y2_bufs(pb):
        return list(B_y2T)

    glu_tiles = []
    for j in range(8):
        glu_tiles += [(wglu_t[j], 8), (w_in_t[T_ZS + j], KT)]
    glu_pipe = WPipe(glu_tiles, depth=1)
    for j in range(8):
        y3, B_y3 = y3s[j % 2]
        slot, bslot = glu_pipe.take()

        def cons_glu(pb, ps, B_ps, j=j, y3=y3, B_y3=B_y3):
            sg, B_sg = SG.next()
            P.x("act", "activation", out=sg, in_=ps, func=AF.Sigmoid, reads=[B_ps], writes=[B_sg])
            P.x("dve", "tensor_tensor", out=y3[:, pb * 512:(pb + 1) * 512], in0=y2T[:, j, pb * 512:(pb + 1) * 512], in1=sg,
                op=ALU.mult, reads=[B_y2T[j], B_sg], writes=[B_y3])
        proj_fm(slot, bslot, 8, y2_rhs, y2_bufs, range(NBO), cons_glu)
        slot2, bslot2 = glu_pipe.take()

        def cons_zs(b, ps, B_ps, j=j, y3=y3, B_y3=B_y3):
            bo = b - (NB - NBO)
            sz, B_sz = SZ.next()
            P.x("act", "activation", out=sz, in_=ps, func=AF.Silu, reads=[B_ps], writes=[B_sz])
            P.x("dve", "tensor_tensor", out=y4T[:, j, bo * 512:(bo + 1) * 512].rearrange("p (c t) -> p c t", t=8),
                in0=y3.rearrange("p (t c) -> p c t", t=8)[:, bo * 64:(bo + 1) * 64, :],
                in1=sz.rearrange("p (c t) -> p c t", t=8), op=ALU.mult,
                reads=[B_y3, B_sz], writes=[B_y4[j]])
        proj_fm(slot2, bslot2, KT, h_rhs, h_bufs, range(NB - NBO, NB), cons_zs)
    A.release(m_glu)
    P.mark("glu")
    dump("y4T", y4T, B_y4, [128, 8, LO], BF16)
    if stop_after == "glu":
        P.emit()
        return nc, dbg

    mixT, B_mix = A.alloc("mixT", [128, KT, LO], BF16, nbufs=NTO)
    m_mix = A.mark()
    SGS = Ring([A.alloc(f"sgs{i}", [128, 512], BF16) for i in range(4)])
    SGD = Ring([A.alloc(f"sgd{i}", [128, 512], BF16) for i in range(4)])
    M1 = Ring([A.alloc(f"m1_{i}", [128, 512], F32) for i in range(4)])
    M2 = Ring([A.alloc(f"m2_{i}", [128, 512], F32) for i in range(3)])

    def y4_rhs(kt, bo):
        return y4T[:, kt, bo * 512:(bo + 1) * 512]

    def dn_rhs(kt, bo):
        return dnoT[:, kt, bo * 512:(bo + 1) * 512]

    mix_tiles = []
    for j in range(KT):
        mix_tiles += [(w_in_t[T_GS + j], KT), (w_in_t[T_GD + j], KT), (wups_t[j], 8), (wupd_t[j], 8)]
    wsr["ring"] = Ring(list(wslots) + [A.alloc(f"wslotx{i}", [128, KT, 128], BF16) for i in range(2)])
    mix_pipe = WPipe(mix_tiles, depth=2)
    for j in range(KT):
        s_gs = mix_pipe.take()
        st_ = {bo: {} for bo in range(NBO)}

        def cons_gs(b_, ps, B_ps, st_=st_):
            sg, B_sg = SGS.next()
            P.x("act", "activation", out=sg, in_=ps, func=AF.Sigmoid, reads=[B_ps], writes=[B_sg])
            st_[b_ - (NB - NBO)]["gs"] = (sg, B_sg)
        proj_fm(s_gs[0], s_gs[1], KT, h_rhs, h_bufs, range(NB - NBO, NB), cons_gs)

        s_gd = mix_pipe.take()

        def cons_gd(b_, ps, B_ps, st_=st_):
            sg, B_sg = SGD.next()
            P.x("act", "activation", out=sg, in_=ps, func=AF.Sigmoid, reads=[B_ps], writes=[B_sg])
            st_[b_ - (NB - NBO)]["gd"] = (sg, B_sg)
        proj_fm(s_gd[0], s_gd[1], KT, h_rhs, h_bufs, range(NB - NBO, NB), cons_gd)

        s_us = mix_pipe.take()

        def cons_us(bo_, ps, B_ps, st_=st_):
            m1_, B_m1 = M1.next()
            sg, B_sg = st_[bo_]["gs"]
            P.x("dve", "tensor_tensor", out=m1_, in0=ps, in1=sg, op=ALU.mult, reads=[B_ps, B_sg], writes=[B_m1])
            st_[bo_]["m1"] = (m1_, B_m1)
        proj_fm(s_us[0], s_us[1], 8, y4_rhs, lambda bo_: list(B_y4), range(NBO), cons_us)

        s_ud = mix_pipe.take()

        def cons_ud(bo_, ps, B_ps, j=j, st_=st_):
            m2_, B_m2 = M2.next()
            sg, B_sg = st_[bo_]["gd"]
            P.x("dve", "tensor_tensor", out=m2_, in0=ps, in1=sg, op=ALU.mult, reads=[B_ps, B_sg], writes=[B_m2])
            m1_, B_m1 = st_[bo_]["m1"]
            P.x("pool", "tensor_tensor", out=mixT[:, j, bo_ * 512:(bo_ + 1) * 512], in0=m1_, in1=m2_, op=ALU.add,
                reads=[B_m1, B_m2], writes=B_mix[bo_ * 4:(bo_ + 1) * 4])
        proj_fm(s_ud[0], s_ud[1], 8, dn_rhs, lambda bo_: list(B_dno), range(NBO), cons_ud)
    wsr["ring"] = WS
    A.release(m_mix)
    H.release(0)
    P.mark("mix")

    rT, B_r = H.alloc("rT", [128, NTO, D], F32, nbufs=NTO)
    fnw, B_fnw = A.alloc("fnw", [128, D], F32)
    P.d("sp", fnw, fnw_in.to_broadcast([128, D]),
        writes=[B_fnw])
    WO = Ring([A.alloc(f"wo{i}", [128, KT, 512], BF16) for i in range(2)])
    XO = Ring([A.alloc(f"xo{i}", [128, 512], F32) for i in range(3)])
    st2, B_st2 = A.alloc("st2", [128, NTO, 2], F32, nbufs=NTO)
    fjunk, B_fjunk = A.alloc("fjunk", [128, D], BF16)
    for cb in range(4):
        wo, B_wo = WO.next()
        P.d("pool", wo, wout_t[cb], writes=[B_wo])
        for tt_ in range(NTO):
            xo, B_xo = XO.next()
            P.d("sp", xo, x_in[T0 + tt_ * 128:T0 + (tt_ + 1) * 128, cb * 512:(cb + 1) * 512], writes=[B_xo])
            ps, B_ps = cur["big"].next()
            for kt in range(KT):
                P.x("pe", "matmul", ps, lhsT=mixT[:, kt, tt_ * 128:(tt_ + 1) * 128], rhs=wo[:, kt, :],
                    start=(kt == 0), stop=(kt == KT - 1), reads=[B_mix[tt_], B_wo], writes=[B_ps])
            P.x("dve", "tensor_tensor", out=rT[:, tt_, cb * 512:(cb + 1) * 512], in0=ps, in1=xo, op=ALU.add,
                reads=[B_ps, B_xo], writes=[B_r[tt_]])
    for tt_ in range(NTO):
        P.x("act", "activation", out=fjunk, in_=rT[:, tt_, :], func=AF.Square, accum_out=st2[:, tt_, 0:1],
            reads=[B_r[tt_]], writes=[B_fjunk, B_st2[tt_]])
        P.x("act", "activation", out=st2[:, tt_, 1:2], in_=st2[:, tt_, 0:1], func=AF.Sqrt, bias=epsc[:, 0:1], scale=1.0 / D,
            reads=[B_st2[tt_], B_eps], writes=[B_st2[tt_]])
        P.x("dve", "reciprocal", out=st2[:, tt_, 1:2], in_=st2[:, tt_, 1:2], reads=[B_st2[tt_]], writes=[B_st2[tt_]])
        P.x("dve", "scalar_tensor_tensor", out=rT[:, tt_, :], in0=rT[:, tt_, :], scalar=st2[:, tt_, 1:2], in1=fnw,
            op0=ALU.mult, op1=ALU.mult, reads=[B_r[tt_], B_st2[tt_], B_fnw], writes=[B_r[tt_]])
        P.d("sp", out_ap[tt_ * 128:(tt_ + 1) * 128, :], rT[:, tt_, :], reads=[B_r[tt_]], is_output=True)
    P.mark("out")
    P.emit()
    build.last_marks = P.marks
    return nc, dbg


_NC_CACHE = {}


def kernel(**inputs):
    x = np.asarray(inputs["x"], dtype=np.float32)
    Bsz, L, Dm = x.shape
    LO = L // 2
    key = (L,)
    if key not in _NC_CACHE:
        _NC_CACHE[key] = build(LT=L, LO=LO)[0]
    nc = _NC_CACHE[key]
    in_maps = []
    wmap = make_in_map(inputs, np.zeros((1, 1), np.float32))
    wmap.pop("x")
    for c in range(8):
        b, p = c // 2, c % 2
        if p == 0:
            xloc = np.concatenate([np.zeros((L - LO, Dm), np.float32), x[b, :LO]], axis=0)
        else:
            xloc = x[b]
        in_maps.append(make_in_map(inputs, xloc, wmap))
    res = run_bass_kernel_spmd(nc, in_maps, core_ids=list(range(8)))
    out = np.empty((Bsz, L, Dm), np.float32)
    for c in range(8):
        b, p = c // 2, c % 2
        out[b, p * LO:(p + 1) * LO] = np.asarray(res.results[c]["out"], dtype=np.float32)
    return out


def make_in_map(inp, xloc, wmap=None):
    if wmap is not None:
        m = dict(wmap)
        m["x"] = np.ascontiguousarray(xloc, dtype=np.float32)
        return m
    f = np.float32
    g = lambda k: np.asarray(inp[k], dtype=f)
    m = {}
    m["x"] = np.ascontiguousarray(xloc, dtype=f)
    w_in = g("w_in")[0]

    def tile_of(w, col0, ncols=128, kt=16):
        return w[:, col0:col0 + ncols].reshape(kt, 128, ncols).transpose(1, 0, 2)
    col0s = ([0 + 128 * j for j in range(8)] + [1024 + 128 * j for j in range(8)] + [2048 + 128 * h for h in range(8)]
             + [3072 + 128 * h for h in range(8)] + [4096 + 128 * h for h in range(8)] + [5120 + 128 * h for h in range(8)]
             + [None] + [6160 + 128 * j for j in range(16)] + [8208 + 128 * j for j in range(16)])
    wt = np.zeros((81, 128, 16, 128), f)
    for ti, c0 in enumerate(col0s):
        if c0 is None:
            wt[ti, :, :, 0:8] = tile_of(w_in, 6144, 8)
            for r0 in (32, 64, 96):
                wt[ti, :, :, r0:r0 + 8] = tile_of(w_in, 6152, 8)
        else:
            wt[ti] = tile_of(w_in, c0)
    m["w_in_t"] = wt
    m["lnw"] = np.ascontiguousarray(g("ln_w")[0].reshape(16, 128).T)
    m["fnw"] = np.ascontiguousarray(g("final_norm_w")[None, :])
    m["convw"] = np.ascontiguousarray(g("dn_conv_w")[0].reshape(4, 24, 128).transpose(2, 1, 0))
    dnp = np.zeros((128, 2), f)
    for r0 in (32, 64, 96):
        dnp[r0:r0 + 8, 0] = g("dn_a_log")[0]
        dnp[r0:r0 + 8, 1] = g("dn_dt_bias")[0]
    m["dnp"] = dnp
    m["dnnw"] = np.ascontiguousarray(g("dn_norm_w")[0][:, None])
    lam = np.zeros((128, 3, 32), f)
    lre, lim, lst = g("s5_lam_re")[0], g("s5_lam_im")[0], g("s5_log_step")[0]
    for half in range(2):
        gs = slice(half * 32, half * 32 + 32)
        lam[half * 64:(half + 1) * 64, 0, :] = lre[gs].T
        lam[half * 64:(half + 1) * 64, 1, :] = lim[gs].T
        lam[half * 64:(half + 1) * 64, 2, :] = np.broadcast_to(lst[gs][None, :], (64, 32))
    m["lam"] = lam
    sb = np.zeros((128, 2, 32, 16), f)
    sc = np.zeros((128, 2, 32, 16), f)
    for half in range(2):
        gs = slice(half * 32, half * 32 + 32)
        ps = slice(half * 64, half * 64 + 64)
        sb[ps, 0] = g("s5_b_re")[0][gs].transpose(1, 0, 2)
        sb[ps, 1] = g("s5_b_im")[0][gs].transpose(1, 0, 2)
        sc[ps, 0] = g("s5_c_re")[0][gs].transpose(2, 0, 1)
        sc[ps, 1] = g("s5_c_im")[0][gs].transpose(2, 0, 1)
    m["s5b"] = sb
    m["s5c"] = sc
    d = g("s5_d")[0].reshape(64, 16)
    m["s5d"] = np.ascontiguousarray(np.tile(d.T, (8, 1)))
    m["w_glu_t"] = np.ascontiguousarray(np.stack([tile_of(g("s5_w_glu")[0], 128 * j, 128, 8) for j in range(8)]))
    m["w_ups_t"] = np.ascontiguousarray(np.stack([tile_of(g("s5_w_up")[0], 128 * j, 128, 8) for j in range(16)]))
    m["w_upd_t"] = np.ascontiguousarray(np.stack([tile_of(g("dn_w_up")[0], 128 * j, 128, 8) for j in range(16)]))
    m["w_out_t"] = np.ascontiguousarray(np.stack([tile_of(g("w_out")[0], 512 * cb, 512, 16) for cb in range(4)]))
    return m
```

```python
import contextlib
import numpy as np
import concourse.bass as bass
import concourse.mybir as mybir
from concourse.bass_utils import run_bass_kernel_spmd

F32 = mybir.dt.float32
BF16 = mybir.dt.bfloat16
AF = mybir.ActivationFunctionType
ALU = mybir.AluOpType

ENGS = ("pe", "act", "dve", "pool", "sp")
N_DMA_SEMS = 32


class Buf:
    __slots__ = ("name", "writers", "readers")

    def __init__(self, name):
        self.name = name
        self.writers = []
        self.readers = []


class Ins:
    __slots__ = ("eng", "fn", "deps", "is_dma", "sem", "val", "tag")

    def __init__(self, eng, fn, is_dma, tag=None):
        self.eng = eng
        self.fn = fn
        self.deps = []
        self.is_dma = is_dma
        self.sem = None
        self.val = None
        self.tag = tag


def _mk(method, *args, **kw):
    def fn(e):
        return getattr(e, method)(*args, **kw)
    return fn


class Prog:
    def __init__(self, nc):
        self.nc = nc
        self.q = {e: [] for e in ENGS}
        self.cnt = {e: 0 for e in ENGS}
        self.dma_rr = 0
        self.dma_rr_pool = 0
        self.dma_cnt = [0] * N_DMA_SEMS
        self.dma_last = [None] * N_DMA_SEMS
        self.out_dmas = []

    def mark(self, name):
        if not hasattr(self, 'marks'):
            self.marks = []
        self.marks.append((name, dict(self.cnt)))

    def _add(self, ins, reads, writes):
        deps = []
        for b in reads:
            deps.extend(b.writers)
        for b in writes:
            deps.extend(b.writers)
            deps.extend(b.readers)
        seen = set()
        for d in deps:
            if d is ins or id(d) in seen:
                continue
            seen.add(id(d))
            if d.eng == "pe" and ins.eng == "pe" and not d.is_dma and not ins.is_dma:
                continue
            ins.deps.append(d)
        for b in reads:
            b.readers.append(ins)
        for b in writes:
            b.writers = [ins]
            b.readers = []
        self.q[ins.eng].append(ins)
        return ins

    def x(self, eng, method, *args, reads=(), writes=(), **kw):
        ins = Ins(eng, _mk(method, *args, **kw), False, method)
        self.cnt[eng] += 1
        ins.val = self.cnt[eng]
        return self._add(ins, list(reads), list(writes))

    def d(self, eng, out, in_, reads=(), writes=(), is_output=False, **kw):
        ins = Ins(eng, _mk("dma_start", out=out, in_=in_, **kw), True, "dma")
        half = N_DMA_SEMS // 2
        if eng == "pool":
            k = half + (self.dma_rr_pool % half)
            self.dma_rr_pool += 1
        else:
            k = self.dma_rr % half
            self.dma_rr += 1
        ins.sem = k
        self.dma_cnt[k] += 16
        ins.val = self.dma_cnt[k]
        prev = self.dma_last[k]
        self.dma_last[k] = ins
        self._add(ins, list(reads), list(writes))
        if prev is not None and all(dd is not prev for dd in ins.deps):
            ins.deps.append(prev)
        if is_output:
            self.out_dmas.append(ins)
        return ins

    def emit(self):
        nc = self.nc
        with contextlib.ExitStack() as st:
            csem = {e: st.enter_context(nc.semaphore(f"c_{e}")) for e in ("pe", "act", "dve", "pool")}
            dsem = [st.enter_context(nc.semaphore(f"d_{i}")) for i in range(N_DMA_SEMS)]
            block = st.enter_context(nc.Block())
            final_waits = list(self.out_dmas)

            def run(eng_name, handle):
                seen_c = {}
                seen_d = {}
                for ins in self.q[eng_name]:
                    for d in ins.deps:
                        if d.is_dma:
                            if seen_d.get(d.sem, 0) >= d.val:
                                continue
                            seen_d[d.sem] = d.val
                            handle.wait_ge(dsem[d.sem], d.val)
                        else:
                            if seen_c.get(d.eng, 0) >= d.val:
                                continue
                            seen_c[d.eng] = d.val
                            handle.wait_ge(csem[d.eng], d.val)
                    bi = ins.fn(handle)
                    if ins.is_dma:
                        bi.then_inc(dsem[ins.sem], 16)
                    else:
                        bi.then_inc(csem[ins.eng], 1)
                if eng_name == "sp":
                    for d in final_waits:
                        if seen_d.get(d.sem, 0) >= d.val:
                            continue
                        seen_d[d.sem] = d.val
                        handle.wait_ge(dsem[d.sem], d.val)

            @block.tensor
            def _(e):
                run("pe", e)

            @block.scalar
            def _(e):
                run("act", e)

            @block.vector
            def _(e):
                run("dve", e)

            @block.gpsimd
            def _(e):
                run("pool", e)

            @block.sync
            def _(e):
                run("sp", e)


class Arena:
    def __init__(self, nc, nbytes, name="arena"):
        self.t = nc.alloc_sbuf_tensor(name, [128, nbytes // 2], BF16).ap()
        self.nbytes = nbytes
        self.top = 0
        self.top2 = nbytes
        self.hist = []
        self.peak = 0

    def _mkbuf(self, name, s, e):
        b = Buf(name)
        keep = []
        for (s0, e0, b0) in self.hist:
            if s0 < e and s < e0:
                b.readers.extend(b0.readers)
                b.readers.extend(b0.writers)
            keep.append((s0, e0, b0))
        self.hist = keep
        self.hist.append((s, e, b))
        return b

    def alloc(self, name, shape, dt, nbufs=1, top=False):
        esz = 4 if dt == F32 else 2
        free = int(np.prod(shape[1:]))
        nb = (free * esz + 31) // 32 * 32
        if top:
            e = self.top2
            s = e - nb
            assert s >= self.top, f"arena overflow (top) allocating {name}"
            self.top2 = s
        else:
            s = self.top
            e = s + nb
            assert e <= self.top2, f"arena overflow allocating {name}: {e} > {self.top2}"
            self.top = e
        self.peak = max(self.peak, self.top + (self.nbytes - self.top2))
        ap = self.t[0:shape[0], s // 2:(s + free * esz) // 2]
        if dt == F32:
            ap = ap.bitcast(F32)
        if len(shape) > 2:
            names = "abcdefg"[:len(shape) - 1]
            pat = "p (" + " ".join(names) + ") -> p " + " ".join(names)
            ap = ap.rearrange(pat, **{n: int(v) for n, v in zip(names, shape[1:])})
        if nbufs == 1:
            return ap, self._mkbuf(name, s, e)
        return ap, [self._mkbuf(f"{name}{i}", s, e) for i in range(nbufs)]

    def release_top(self):
        self.top2 = self.nbytes

    def mark(self):
        return self.top

    def release(self, mark):
        self.top = mark


class Ring:
    def __init__(self, items):
        self.items = items
        self.i = 0

    def next(self):
        it = self.items[self.i % len(self.items)]
        self.i += 1
        return it


D = 2048
DIN = 10256
KT = 16
NH = 8
C_US, C_ZS, C_Q, C_K, C_V, C_ZD, C_BETA, C_A, C_GS, C_GD = 0, 1024, 2048, 3072, 4096, 5120, 6144, 6152, 6160, 8208
EPS = 1e-6
PI = float(np.pi)


def build(LT=2048, LO=1024, debug=(), stop_after=None, nheads=NH):
    assert LT % 512 == 0 and LO % 512 == 0 and LO <= LT
    NT = LT // 128
    NTO = LO // 128
    T0 = LT - LO
    NB = LT // 512
    NBO = LO // 512
    NCH = LT // 64
    NP = LT // 128
    NC8 = LT // 8
    NC8O = LO // 8

    nc = bass.Bass("TRN2", target_bir_lowering=False)
    P = Prog(nc)

    def din(name, shape, dt=F32):
        return nc.dram_tensor(name, list(shape), dt, kind="ExternalInput").ap()

    x_in = din("x", [LT, D])
    w_in_t = din("w_in_t", [81, 128, KT, 128])
    lnw_in = din("lnw", [128, KT])
    fnw_in = din("fnw", [1, D])
    convw_in = din("convw", [128, 24, 4])
    dnp_in = din("dnp", [128, 2])
    dnnw_in = din("dnnw", [128, 1])
    lam_in = din("lam", [128, 3, 32])
    sb_in = din("s5b", [128, 2, 32, 16])
    sc_in = din("s5c", [128, 2, 32, 16])
    sd_in = din("s5d", [128, 64])
    wglu_t = din("w_glu_t", [8, 128, 8, 128])
    wups_t = din("w_ups_t", [16, 128, 8, 128])
    wupd_t = din("w_upd_t", [16, 128, 8, 128])
    wout_t = din("w_out_t", [4, 128, KT, 512])
    out_ap = nc.dram_tensor("out", [LO, D], F32, kind="ExternalOutput").ap()
    U_scr = nc.dram_tensor("U_scr", [1024, 8, NC8], BF16).ap()
    Y_scr = nc.dram_tensor("Y_scr", [1024, 8, NC8O], BF16).ap()
    dbg = {}

    def dump(name, ap, buf, shape, dt=F32):
        if name not in debug:
            return
        o = nc.dram_tensor("dbg_" + name, list(shape), dt, kind="ExternalOutput").ap()
        bufs = buf if isinstance(buf, list) else [buf]
        P.d("sp", o, ap, reads=bufs, is_output=True)
        dbg[name] = o

    A = Arena(nc, 143 * 1024, "arena")
    H = Arena(nc, 64 * 1024, "harena")

    def psum_ring(prefix, nbanks, per_bank, dt=F32):
        items = []
        for i in range(nbanks):
            width = 512 if dt == F32 else 1024
            t = nc.alloc_psum_tensor(f"{prefix}{i}", [128, width], dt).ap()
            w = width // per_bank
            for j in range(per_bank):
                items.append((t[:, j * w:(j + 1) * w], Buf(f"{prefix}{i}_{j}")))
        return Ring(items)

    BIG = psum_ring("psb", 2, 1)
    _SM = psum_ring("pss", 6, 1)

    BIG8 = Ring(list(BIG.items) + list(_SM.items))
    cur = {"big": BIG8}

    cur["small"] = _SM

    class _SmallF:
        @staticmethod
        def next():
            t, b = cur["small"].next()
            return t[:, 0:128], b

    class _SmallB:
        @staticmethod
        def next():
            t, b = cur["small"].next()
            return t.bitcast(BF16)[:, 0:128], b
    SMALL = _SmallF
    TRB = _SmallB

    ident_f, B_identf = A.alloc("ident_f", [128, 128], F32)
    ident_b, B_identb = A.alloc("ident_b", [128, 128], BF16)
    ones_b, B_onesb = A.alloc("ones_b", [128, 128], BF16)
    ones_f, B_onesf = A.alloc("ones_f", [128, 128], F32)
    SLm, B_SL = A.alloc("SL", [128, 128], F32)
    CUm, B_CU = A.alloc("CU", [64, 64], F32)
    lnw, B_lnw = A.alloc("lnw", [128, KT], F32)
    convw, B_convw = A.alloc("convw", [128, 24, 4], F32)
    dnnw, B_dnnw = A.alloc("dnnw", [128, 1], F32)
    dnp, B_dnp = A.alloc("dnp", [128, 2], F32)
    epsc, B_eps = A.alloc("epsc", [128, 1], F32)
    onec, B_one = A.alloc("onec", [128, 1], F32)

    P.x("pool", "memset", ident_f, 0.0, writes=[B_identf])
    P.x("pool", "affine_select", out=ident_f, in_=ident_f, pattern=[[-1, 128]], compare_op=ALU.not_equal,
        fill=1.0, base=0, channel_multiplier=1, reads=[B_identf], writes=[B_identf])
    P.x("pool", "tensor_copy", out=ident_b, in_=ident_f, reads=[B_identf], writes=[B_identb])
    P.x("pool", "memset", ones_b, 1.0, writes=[B_onesb])
    P.x("pool", "memset", ones_f, 1.0, writes=[B_onesf])
    P.x("pool", "memset", epsc, EPS, writes=[B_eps])
    P.x("pool", "memset", onec, 1.0, writes=[B_one])
    P.x("pool", "memset", SLm, 1.0, writes=[B_SL])
    P.x("pool", "affine_select", out=SLm, in_=SLm, pattern=[[-1, 128]], compare_op=ALU.is_gt,
        fill=0.0, base=0, channel_multiplier=1, reads=[B_SL], writes=[B_SL])
    P.x("pool", "memset", SLm[64:128, 0:64], 0.0, reads=[B_SL], writes=[B_SL])
    P.x("pool", "memset", CUm, 1.0, writes=[B_CU])
    P.x("pool", "affine_select", out=CUm, in_=CUm, pattern=[[1, 64]], compare_op=ALU.is_ge,
        fill=0.0, base=0, channel_multiplier=-1, reads=[B_CU], writes=[B_CU])
    P.d("sp", lnw, lnw_in, writes=[B_lnw])
    P.d("sp", convw, convw_in, writes=[B_convw])
    P.d("sp", dnnw, dnnw_in, writes=[B_dnnw])
    P.d("sp", dnp, dnp_in, writes=[B_dnp])

    hTo, B_hTo = H.alloc("hTo", [128, KT, LO], BF16, nbufs=NTO)
    mH = H.mark()
    if T0 > 0:
        hTp, B_hTp = H.alloc("hTp", [128, KT, T0], BF16, nbufs=NT - NTO)
    else:
        hTp, B_hTp = None, []
    B_hT = list(B_hTp) + list(B_hTo)

    def hcols(kt, t0, n):
        if t0 >= T0:
            return hTo[:, kt, t0 - T0:t0 - T0 + n]
        assert t0 + n <= T0
        return hTp[:, kt, t0:t0 + n]

    def hcols8(k0, t0, n):
        if t0 >= T0:
            return hTo[:, k0:k0 + 8, t0 - T0:t0 - T0 + n]
        return hTp[:, k0:k0 + 8, t0:t0 + n]

    wslots = [A.alloc(f"wslot{i}", [128, KT, 128], BF16) for i in range(3)]
    WS = Ring(wslots)

    wsr = {"ring": WS}

    def load_w(src_tile_ap, kt=KT):
        slot, bslot = wsr["ring"].next()
        P.d("pool", slot[:, 0:kt, :], src_tile_ap, writes=[bslot])
        return slot, bslot

    T_US, T_ZS, T_Q, T_K, T_V, T_ZD, T_BA, T_GS, T_GD = 0, 8, 16, 24, 32, 40, 48, 49, 65

    class WPipe:
        def __init__(self, tiles, depth=1):
            self.tiles = list(tiles)
            self.depth = depth
            self.q = []
            self.p = 0
            self.started = False

        def start(self):
            if not self.started:
                self.started = True
                self._fill(self.depth)

        def _fill(self, n):
            while len(self.q) < n and self.p < len(self.tiles):
                ap_, kt_ = self.tiles[self.p]
                self.q.append(load_w(ap_, kt=kt_))
                self.p += 1

        def take(self):
            self.start()
            self._fill(self.depth + 1)
            return self.q.pop(0)


    def proj_fm_g(slot, bslot, kts, rhs_fn, rhs_bufs_fn, blocks, consume, m=128, ring=None, G=None):
        blocks = list(blocks)
        rg = ring if ring is not None else cur["big"]
        if G is None:
            G = 4 if rg is BIG8 else 2
        for g0 in range(0, len(blocks), G):
            grp = blocks[g0:g0 + G]
            pss = [rg.next() for _ in grp]
            for kt in range(kts):
                for b, (ps, B_ps) in zip(grp, pss):
                    P.x("pe", "matmul", ps[0:m, :], lhsT=slot[:, kt, 0:m], rhs=rhs_fn(kt, b), start=(kt == 0), stop=(kt == kts - 1),
                        reads=[bslot] + rhs_bufs_fn(b), writes=[B_ps])
            for b, (ps, B_ps) in zip(grp, pss):
                consume(b, ps, B_ps)
            yield "P"

    def proj_fm(*a, **kw):
        for _ in proj_fm_g(*a, **kw):
            pass

    def h_rhs(kt, b):
        return hcols(kt, b * 512, 512)

    def h_bufs(b):
        return B_hT[b * 4:(b + 1) * 4]

    G_scr = nc.dram_tensor("G_scr", [128, 64 * 2 * 128], BF16).ap()
    Mi_scr = nc.dram_tensor("Mi_scr", [128, 64 * 128], BF16).ap()
    Mr_scr = nc.dram_tensor("Mr_scr", [128, 32 * 2 * 128], BF16).ap()
    A8_scr = nc.dram_tensor("A8_scr", [128, 128], F32).ap()
    B_Gscr, B_Miscr, B_Mrscr, B_A8scr = Buf("Gscr"), Buf("Miscr"), Buf("Mrscr"), Buf("A8scr")

    def s5_tables_gen():
        Gpad, B_Gpad = A.alloc("Gpad", [128, 64, 2, 128], BF16, top=True)
        Mintra, B_Mintra = A.alloc("Mintra", [128, 64, 128], BF16, top=True)
        Minter, B_Minter = A.alloc("Minter", [128, 32, 2, 128], BF16, top=True)
        A8, B_A8 = A.alloc("A8", [128, 2, 2, 32], F32, top=True)
        A8P, B_A8P = A.alloc("A8P", [128, 8, 2, 32], F32, top=True)
        A64, B_A64 = A.alloc("A64", [128, 2, 2, 32], F32, top=True)
        _wmark[0] = A.top2
        lam, B_lam = A.alloc("lam", [128, 3, 32], F32, top=True)
        sbt, B_sbt = A.alloc("sbt", [128, 2, 32, 16], F32, top=True)
        sct, B_sct = A.alloc("sct", [128, 2, 32, 16], F32, top=True)
        sdt, B_sdt = A.alloc("sdt", [128, 64], F32, top=True)
        MK, B_MK = A.alloc("MK", [128, 128], F32, top=True)
        POW, B_POW = A.alloc("POW", [128, 9, 2, 32], F32, top=True)
        NEG, B_NEG = A.alloc("NEG", [128, 8, 2, 32], F32, top=True)
        bbt, B_bbt = A.alloc("bbt", [128, 2, 32, 16], F32, top=True)
        sm = {}
        for nm in ("step", "x1", "mag", "th", "kk", "ths", "s", "c", "ar", "ai", "den", "xr", "fre", "fim", "t1", "t2", "t3", "t4", "n1", "n2", "n3", "n4", "ivr", "ivi"):
            sm[nm] = A.alloc("sm_" + nm, [128, 32], F32, top=True)
        P.d("sp", lam, lam_in, writes=[B_lam])
        P.d("sp", sbt, sb_in, writes=[B_sbt])
        P.d("sp", sct, sc_in, writes=[B_sct])
        P.d("sp", sdt, sd_in, writes=[B_sdt])
        P.x("pool", "memset", MK, 1.0, writes=[B_MK])
        P.x("pool", "affine_select", out=MK.rearrange("p (t c) -> p t c", c=16), in_=MK.rearrange("p (t c) -> p t c", c=16),
            pattern=[[16, 8], [0, 16]], compare_op=ALU.is_ge, fill=0.0, base=15, channel_multiplier=-1,
            reads=[B_MK], writes=[B_MK])
        P.x("pool", "memset", Gpad, 0.0, writes=[B_Gpad])

        def V(nm):
            return sm[nm][0]

        def BV(nm):
            return sm[nm][1]

        def tt(out, ob, a, ab, b, bb_, op, eng="dve"):
            P.x(eng, "tensor_tensor", out=out, in0=a, in1=b, op=op, reads=list(ab) + list(bb_), writes=list(ob))

        lre, lim, lst = lam[:, 0, :], lam[:, 1, :], lam[:, 2, :]
        P.x("act", "activation", out=V("step"), in_=lst, func=AF.Exp, reads=[B_lam], writes=[BV("step")])
        tt(V("x1"), [BV("x1")], lre, [B_lam], V("step"), [BV("step")], ALU.mult)
        P.x("act", "activation", out=V("mag"), in_=V("x1"), func=AF.Exp, reads=[BV("x1")], writes=[BV("mag")])
        tt(V("th"), [BV("th")], lim, [B_lam], V("step"), [BV("step")], ALU.mult)

        def sin_of(dst, shift):
            P.x("dve", "tensor_scalar", out=V("ths"), in0=V("th"), scalar1=shift, scalar2=None, op0=ALU.add,
                reads=[BV("th")], writes=[BV("ths")])
            P.x("dve", "tensor_scalar", out=V("kk"), in0=V("ths"), scalar1=PI, scalar2=None, op0=ALU.is_ge,
                reads=[BV("ths")], writes=[BV("kk")])
            for mth in range(1, 6):
                P.x("dve", "scalar_tensor_tensor", out=V("kk"), in0=V("ths"), scalar=(2 * mth + 1) * PI, in1=V("kk"),
                    op0=ALU.is_ge, op1=ALU.add, reads=[BV("ths"), BV("kk")], writes=[BV("kk")])
            P.x("dve", "scalar_tensor_tensor", out=V("ths"), in0=V("kk"), scalar=-2.0 * PI, in1=V("ths"),
                op0=ALU.mult, op1=ALU.add, reads=[BV("ths"), BV("kk")], writes=[BV("ths")])
            P.x("act", "activation", out=V(dst), in_=V("ths"), func=AF.Sin, reads=[BV("ths")], writes=[BV(dst)])
        sin_of("s", 0.0)
        yield
        sin_of("c", PI / 2)
        yield
        tt(V("ar"), [BV("ar")], V("mag"), [BV("mag")], V("c"), [BV("c")], ALU.mult)
        tt(V("ai"), [BV("ai")], V("mag"), [BV("mag")], V("s"), [BV("s")], ALU.mult)
        tt(V("t1"), [BV("t1")], lre, [B_lam], lre, [B_lam], ALU.mult)
        tt(V("t2"), [BV("t2")], lim, [B_lam], lim, [B_lam], ALU.mult)
        tt(V("den"), [BV("den")], V("t1"), [BV("t1")], V("t2"), [BV("t2")], ALU.add)
        P.x("dve", "reciprocal", out=V("den"), in_=V("den"), reads=[BV("den")], writes=[BV("den")])
        P.x("dve", "tensor_scalar", out=V("xr"), in0=V("ar"), scalar1=-1.0, scalar2=None, op0=ALU.add, reads=[BV("ar")], writes=[BV("xr")])
        tt(V("t1"), [BV("t1")], V("xr"), [BV("xr")], lre, [B_lam], ALU.mult)
        tt(V("t2"), [BV("t2")], V("ai"), [BV("ai")], lim, [B_lam], ALU.mult)
        tt(V("t1"), [BV("t1")], V("t1"), [BV("t1")], V("t2"), [BV("t2")], ALU.add)
        tt(V("fre"), [BV("fre")], V("t1"), [BV("t1")], V("den"), [BV("den")], ALU.mult)
        tt(V("t3"), [BV("t3")], V("ai"), [BV("ai")], lre, [B_lam], ALU.mult)
        tt(V("t4"), [BV("t4")], V("xr"), [BV("xr")], lim, [B_lam], ALU.mult)
        tt(V("t3"), [BV("t3")], V("t3"), [BV("t3")], V("t4"), [BV("t4")], ALU.subtract)
        tt(V("fim"), [BV("fim")], V("t3"), [BV("t3")], V("den"), [BV("den")], ALU.mult)
        tt(V("t1"), [BV("t1")], V("mag"), [BV("mag")], V("mag"), [BV("mag")], ALU.mult)
        P.x("dve", "reciprocal", out=V("t1"), in_=V("t1"), reads=[BV("t1")], writes=[BV("t1")])
        tt(V("ivr"), [BV("ivr")], V("ar"), [BV("ar")], V("t1"), [BV("t1")], ALU.mult)
        tt(V("t2"), [BV("t2")], V("ai"), [BV("ai")], V("t1"), [BV("t1")], ALU.mult)
        P.x("dve", "tensor_scalar", out=V("ivi"), in0=V("t2"), scalar1=-1.0, scalar2=None, op0=ALU.mult, reads=[BV("t2")], writes=[BV("ivi")])

        def cmul_small(dst, dbuf, j, src, sbuf, jm, mr, mrb, mi, mib, eng="dve", tp="t"):
            pr, pi = src[:, jm, 0, :], src[:, jm, 1, :]
            n1, n2, n3, n4 = tp + "1", tp + "2", tp + "3", tp + "4"
            tt(V(n1), [BV(n1)], pr, [sbuf], mr, [mrb], ALU.mult, eng=eng)
            tt(V(n2), [BV(n2)], pi, [sbuf], mi, [mib], ALU.mult, eng=eng)
            tt(dst[:, j, 0, :], [dbuf], V(n1), [BV(n1), dbuf], V(n2), [BV(n2)], ALU.subtract, eng=eng)
            tt(V(n3), [BV(n3)], pr, [sbuf], mi, [mib], ALU.mult, eng=eng)
            tt(V(n4), [BV(n4)], pi, [sbuf], mr, [mrb], ALU.mult, eng=eng)
            tt(dst[:, j, 1, :], [dbuf], V(n3), [BV(n3), dbuf], V(n4), [BV(n4)], ALU.add, eng=eng)

        P.x("pool", "memset", POW[:, 0, 0, :], 1.0, writes=[B_POW])
        P.x("pool", "memset", POW[:, 0, 1, :], 0.0, reads=[B_POW], writes=[B_POW])
        P.x("pool", "memset", NEG[:, 0, 0, :], 1.0, writes=[B_NEG])
        P.x("pool", "memset", NEG[:, 0, 1, :], 0.0, reads=[B_NEG], writes=[B_NEG])
        for j in range(1, 9):
            cmul_small(POW, B_POW, j, POW, B_POW, j - 1, V("ar"), BV("ar"), V("ai"), BV("ai"))
            yield
        for j in range(1, 8):
            cmul_small(NEG, B_NEG, j, NEG, B_NEG, j - 1, V("ivr"), BV("ivr"), V("ivi"), BV("ivi"), eng="pool", tp="n")
            yield
        for r in range(2):
            P.x("pool", "tensor_copy", out=A8[:, 0, r, :], in_=POW[:, 8, 0, :], reads=[B_POW, B_A8], writes=[B_A8])
            P.x("pool", "tensor_copy", out=A8[:, 1, r, :], in_=POW[:, 8, 1, :], reads=[B_POW, B_A8], writes=[B_A8])
        for r in range(2):
            P.x("pool", "tensor_copy", out=A8P[:, 0, r, :], in_=POW[:, 8, r, :], reads=[B_POW, B_A8P], writes=[B_A8P])
        for k in range(1, 8):
            cmul_small(A8P, B_A8P, k, A8P, B_A8P, k - 1, POW[:, 8, 0, :], B_POW, POW[:, 8, 1, :], B_POW)
        for r in range(2):
            P.x("pool", "tensor_copy", out=A64[:, 0, r, :], in_=A8P[:, 7, 0, :], reads=[B_A8P, B_A64], writes=[B_A64])
            P.x("pool", "tensor_copy", out=A64[:, 1, r, :], in_=A8P[:, 7, 1, :], reads=[B_A8P, B_A64], writes=[B_A64])
        yield
        fre_b = V("fre").unsqueeze(2).to_broadcast([128, 32, 16])
        fim_b = V("fim").unsqueeze(2).to_broadcast([128, 32, 16])
        big1, B_big1 = A.alloc("big1", [128, 32, 16], F32, top=True)
        big2, B_big2 = A.alloc("big2", [128, 32, 16], F32, top=True)
        tt(big1, [B_big1], sbt[:, 0], [B_sbt], fre_b, [BV("fre")], ALU.mult)
        tt(big2, [B_big2], sbt[:, 1], [B_sbt], fim_b, [BV("fim")], ALU.mult)
        tt(bbt[:, 0], [B_bbt], big1, [B_big1], big2, [B_big2], ALU.subtract)
        tt(big1, [B_big1], sbt[:, 1], [B_sbt], fre_b, [BV("fre")], ALU.mult)
        tt(big2, [B_big2], sbt[:, 0], [B_sbt], fim_b, [BV("fim")], ALU.mult)
        tt(bbt[:, 1], [B_bbt], big1, [B_big1, B_bbt], big2, [B_big2], ALU.add)

        GB = 2
        Pt, B_Pt = A.alloc("Pt", [128, 2, GB, 8, 16], F32, top=True)
        GTt, B_GTt = A.alloc("GTt", [128, 2, GB, 8, 16], F32, top=True)
        Qt, B_Qt = A.alloc("Qt", [128, 2, GB, 9, 16], F32, top=True)
        w1, B_w1 = A.alloc("w1", [128, GB, 9, 16], F32, top=True)
        w2, B_w2 = A.alloc("w2", [128, GB, 9, 16], F32, top=True)
        mtmp, B_mtmp = A.alloc("mtmp", [128, 128], F32, top=True)
        q1, B_q1 = A.alloc("q1", [128, GB, 9, 16], F32, top=True)
        q2, B_q2 = A.alloc("q2", [128, GB, 9, 16], F32, top=True)
        qz, B_qz = A.alloc("qz", [128, GB, 9, 16], F32, top=True)
        P.x("pool", "memset", qz, 0.0, writes=[B_qz])
        for gb in range(32 // GB):
            g0 = gb * GB
            gs = slice(g0, g0 + GB)
            sh = [128, GB, 8, 16]
            nr = NEG[:, :, 0, gs].rearrange("p t g -> p g t").unsqueeze(3).to_broadcast(sh)
            ni = NEG[:, :, 1, gs].rearrange("p t g -> p g t").unsqueeze(3).to_broadcast(sh)
            br = bbt[:, 0, gs, :].unsqueeze(2).to_broadcast(sh)
            bi = bbt[:, 1, gs, :].unsqueeze(2).to_broadcast(sh)
            a1, a2 = w1[:, :, 0:8, :], w2[:, :, 0:8, :]
            tt(a1, [B_w1], nr, [B_NEG], br, [B_bbt], ALU.mult)
            tt(a2, [B_w2], ni, [B_NEG], bi, [B_bbt], ALU.mult)
            tt(Pt[:, 0], [B_Pt], a1, [B_w1], a2, [B_w2], ALU.subtract)
            tt(a1, [B_w1], nr, [B_NEG], bi, [B_bbt], ALU.mult)
            tt(a2, [B_w2], ni, [B_NEG], br, [B_bbt], ALU.mult)
            tt(Pt[:, 1], [B_Pt], a1, [B_w1, B_Pt], a2, [B_w2], ALU.add)
            yield
            p7r = POW[:, 7, 0, gs].unsqueeze(2).unsqueeze(3).to_broadcast(sh)
            p7i = POW[:, 7, 1, gs].unsqueeze(2).unsqueeze(3).to_broadcast(sh)
            tt(a1, [B_w1], Pt[:, 0], [B_Pt], p7r, [B_POW], ALU.mult)
            tt(a2, [B_w2], Pt[:, 1], [B_Pt], p7i, [B_POW], ALU.mult)
            tt(GTt[:, 0], [B_GTt], a1, [B_w1], a2, [B_w2], ALU.subtract)
            tt(a1, [B_w1], Pt[:, 0], [B_Pt], p7i, [B_POW], ALU.mult)
            tt(a2, [B_w2], Pt[:, 1], [B_Pt], p7r, [B_POW], ALU.mult)
            tt(GTt[:, 1], [B_GTt], a1, [B_w1, B_GTt], a2, [B_w2], ALU.add)
            yield
            shq = [128, GB, 9, 16]
            pr = POW[:, :, 0, gs].rearrange("p j g -> p g j").unsqueeze(3).to_broadcast(shq)
            pi_ = POW[:, :, 1, gs].rearrange("p j g -> p g j").unsqueeze(3).to_broadcast(shq)
            cr = sct[:, 0, gs, :].unsqueeze(2).to_broadcast(shq)
            ci = sct[:, 1, gs, :].unsqueeze(2).to_broadcast(shq)
            tt(q1, [B_q1], cr, [B_sct], pr, [B_POW], ALU.mult, eng="pool")
            tt(q2, [B_q2], ci, [B_sct], pi_, [B_POW], ALU.mult, eng="pool")
            tt(Qt[:, 0], [B_Qt], q1, [B_q1], q2, [B_q2], ALU.subtract, eng="pool")
            tt(q1, [B_q1], cr, [B_sct], pi_, [B_POW], ALU.mult, eng="pool")
            tt(q2, [B_q2], ci, [B_sct], pr, [B_POW], ALU.mult, eng="pool")
            tt(q1, [B_q1], q1, [B_q1], q2, [B_q2], ALU.add, eng="pool")
            tt(Qt[:, 1], [B_Qt], qz, [B_qz], q1, [B_q1, B_Qt], ALU.subtract, eng="pool")
            for ri in range(2):
                P.x("pool", "tensor_copy", out=Minter[:, gs, ri, :].rearrange("p g (t c) -> p g t c", c=16),
                    in_=Qt[:, ri, :, 1:9, :], reads=[B_Qt, B_Minter], writes=[B_Minter])
            yield
            for gl in range(GB):
                for half in range(2):
                    yield
                    g = half * 32 + g0 + gl
                    base = half * 64
                    rows = slice(base, base + 64)
                    ps, B_ps = SMALL.next()
                    P.x("pe", "matmul", ps, lhsT=Pt[rows, 0, gl].rearrange("p t c -> p (t c)"),
                        rhs=Qt[rows, 0, gl, 0:8, :].rearrange("p t c -> p (t c)"), start=True, stop=False,
                        reads=[B_Pt, B_Qt], writes=[B_ps])
                    P.x("pe", "matmul", ps, lhsT=Pt[rows, 1, gl].rearrange("p t c -> p (t c)"),
                        rhs=Qt[rows, 1, gl, 0:8, :].rearrange("p t c -> p (t c)"), start=False, stop=True,
                        reads=[B_Pt, B_Qt], writes=[B_ps])
                    P.x("dve", "tensor_tensor", out=mtmp, in0=ps, in1=MK, op=ALU.mult, reads=[B_ps, B_MK], writes=[B_mtmp])
                    P.x("dve", "scalar_tensor_tensor", out=Mintra[:, g, :], in0=ident_f, scalar=sdt[:, g:g + 1], in1=mtmp,
                        op0=ALU.mult, op1=ALU.add, reads=[B_identf, B_sdt, B_mtmp, B_Mintra], writes=[B_Mintra])
                    for ri in range(2):
                        ps2, B_ps2 = SMALL.next()
                        P.x("pe", "transpose", ps2[:, 0:64], GTt[rows, ri, gl].rearrange("p t c -> p (t c)"),
                            ident_f[rows, base:base + 64], reads=[B_GTt, B_identf], writes=[B_ps2])
                        P.x("act", "activation", out=Gpad[:, g, ri, base:base + 64], in_=ps2[:, 0:64], func=AF.Copy,
                            reads=[B_ps2, B_Gpad], writes=[B_Gpad])

        A.top2 = _wmark[0]
        _tres.update(Gpad=(Gpad, B_Gpad), Mintra=(Mintra, B_Mintra), Minter=(Minter, B_Minter), A8=(A8, B_A8), A8P=(A8P, B_A8P), A64=(A64, B_A64))

    _wmark = [None]
    _tres = {}
    _tg = [s5_tables_gen()]

    def tick(n=1):
        for _ in range(n):
            if _tg[0] is None:
                return
            try:
                next(_tg[0])
            except StopIteration:
                _tg[0] = None
                return

    def drain_tables():
        while _tg[0] is not None:
            tick()

    m1 = A.mark()
    xts = [A.alloc(f"xt{i}", [128, D], F32) for i in range(2)]
    xss = [A.alloc(f"xs{i}", [128, D], BF16) for i in range(2)]
    ss, B_ss = A.alloc("ss", [128, NT], F32, nbufs=NT)
    rstd, B_rstd = A.alloc("rstd", [128, NT], F32, nbufs=NT)
    for tt in range(NT):
        xt, B_xt = xts[tt % 2]
        xs, B_xs = xss[tt % 2]
        P.d("sp", xt, x_in[tt * 128:(tt + 1) * 128, :], writes=[B_xt])
        P.x("act", "activation", out=xs, in_=xt, func=AF.Square, accum_out=ss[:, tt:tt + 1],
            reads=[B_xt], writes=[B_xs, B_ss[tt]])
        P.x("act", "activation", out=rstd[:, tt:tt + 1], in_=ss[:, tt:tt + 1], func=AF.Sqrt,
            bias=epsc[:, 0:1], scale=1.0 / D, reads=[B_ss[tt], B_eps], writes=[B_rstd[tt]])
        P.x("dve", "reciprocal", out=rstd[:, tt:tt + 1], in_=rstd[:, tt:tt + 1], reads=[B_rstd[tt]], writes=[B_rstd[tt]])
        P.x("act", "activation", out=xs, in_=xt, func=AF.Copy, scale=rstd[:, tt:tt + 1],
            reads=[B_xt, B_rstd[tt]], writes=[B_xs])
        for half in range(2):
            psf, B_psf = cur["big"].next()
            psb = psf.bitcast(BF16)
            for k in range(8):
                kt = half * 8 + k
                P.x("pe", "transpose", psb[:, k * 128:(k + 1) * 128], xs[:, kt * 128:(kt + 1) * 128], ident_b,
                    reads=[B_xs, B_identb], writes=[B_psf])
            P.x("dve", "tensor_tensor", out=hcols8(half * 8, tt * 128, 128),
                in0=psb.rearrange("p (k t) -> p k t", k=8),
                in1=lnw[:, half * 8:half * 8 + 8].unsqueeze(2).to_broadcast([128, 8, 128]), op=ALU.mult,
                reads=[B_psf, B_lnw], writes=[B_hT[tt]])
    A.release(m1)
    P.mark("stage1")
    dump("hTo", hTo, B_hTo, [128, KT, LO], BF16)
    if stop_after == "stage1":
        P.emit()
        return nc, dbg

    m1 = A.mark()
    usts = [A.alloc(f"ust{i}", [128, 8, NC8], BF16) for i in range(2)]
    u_pipe = WPipe([(w_in_t[T_US + j], KT) for j in range(8)], depth=1)
    B_U = [Buf(f"U{j}") for j in range(8)]
    for j in range(8):
        slot, bslot = u_pipe.take()
        ust, B_ust = usts[j % 2]

        def cons_u(b, ps, B_ps, ust=ust, B_ust=B_ust):
            P.x("act", "activation", out=ust[:, :, b * 64:(b + 1) * 64].rearrange("p t c -> p c t"),
                in_=ps.rearrange("p (c t) -> p c t", t=8), func=AF.Copy, reads=[B_ps], writes=[B_ust])
        proj_fm(slot, bslot, KT, h_rhs, h_bufs, range(NB), cons_u)
        P.d("sp", U_scr[j * 128:(j + 1) * 128], ust, reads=[B_ust], writes=[B_U[j]])
    A.release(m1)
    P.mark("u")
    if stop_after == "u":
        P.emit()
        return nc, dbg

    dnoT, B_dno = A.alloc("dnoT", [128, NH, LO], BF16, nbufs=NH)
    m_dn0 = A.mark()
    GC, B_GC = A.alloc("GC", [128, LT], F32)
    PT, B_PT = A.alloc("PT", [128, NP, 4, 8], F32)
    CT, B_CT = A.alloc("CT", [64, NCH, 3, 8], F32)
    GLB, B_GLB = A.alloc("GLB", [128, NCH * 8], F32)
    SEL, B_SEL = A.alloc("SEL", [128, 8, 128], F32)
    m0 = A.mark()
    G1, B_G1 = A.alloc("G1", [128, LT], F32)
    RM, B_RM = A.alloc("RM", [128, LT], F32)
    PTin, B_PTin = A.alloc("PTin", [128, LT], F32)
    CTin, B_CTin = A.alloc("CTin", [128, LT], F32)
    Dg, B_Dg = A.alloc("Dg", [128, NCH, 8], F32)
    negA, B_negA = A.alloc("negA", [128, 1], F32)

    P.x("pool", "memset", PTin, 0.0, writes=[B_PTin])
    P.x("pool", "memset", CTin, 0.0, writes=[B_CTin])
    P.x("pool", "memset", G1, 0.0, writes=[B_G1])
    P.x("pool", "memset", RM, 1.0, writes=[B_RM])
    P.x("pool", "memset", RM.rearrange("p (c t) -> p c t", t=64)[:, :, 0:1], 0.0, reads=[B_RM], writes=[B_RM])
    P.x("pool", "tensor_copy", out=SEL[32:40], in_=ident_f[32:40, 32:40].unsqueeze(2).to_broadcast([8, 8, 128]),
        reads=[B_identf], writes=[B_SEL])
    P.x("act", "activation", out=negA, in_=dnp[:, 0:1], func=AF.Exp, reads=[B_dnp], writes=[B_negA])
    P.x("dve", "tensor_scalar", out=negA, in0=negA, scalar1=-1.0, scalar2=None, op0=ALU.mult, reads=[B_negA], writes=[B_negA])

    wba, B_wba = load_w(w_in_t[T_BA])

    def cons_ba(b, ps, B_ps):
        sl = slice(b * 512, (b + 1) * 512)
        P.x("act", "activation", out=PTin[0:8, sl], in_=ps[0:8, :], func=AF.Sigmoid, reads=[B_ps], writes=[B_PTin])
        for (ra, rb) in ((32, 40), (64, 104)):
            P.x("act", "activation", out=G1[ra:rb, sl], in_=ps[ra:rb, :], func=AF.Exp, bias=dnp[ra:rb, 1:2], scale=1.0,
                reads=[B_ps, B_dnp], writes=[B_G1])
    proj_fm(wba, B_wba, KT, h_rhs, h_bufs, range(NB), cons_ba, m=104)
    for (ra, rb) in ((32, 40), (64, 104)):
        P.x("act", "activation", out=G1[ra:rb, :], in_=G1[ra:rb, :], func=AF.Ln, bias=onec[ra:rb, 0:1], scale=1.0,
            reads=[B_G1, B_one], writes=[B_G1])
        P.x("dve", "tensor_scalar", out=G1[ra:rb, :], in0=G1[ra:rb, :], scalar1=negA[ra:rb, 0:1], scalar2=None, op0=ALU.mult,
            reads=[B_G1, B_negA], writes=[B_G1])
        P.x("dve", "tensor_tensor_scan", out=GC[ra:rb, :], data0=RM[ra:rb, :], data1=G1[ra:rb, :], initial=0.0,
            op0=ALU.mult, op1=ALU.add, reads=[B_RM, B_G1], writes=[B_GC])
    P.x("pool", "tensor_copy", out=PTin[32:40, :], in_=GC[32:40, :], reads=[B_GC, B_PTin], writes=[B_PTin])
    P.x("act", "activation", out=PTin[64:72, :], in_=GC[64:72, :], func=AF.Exp, reads=[B_GC, B_PTin], writes=[B_PTin])
    P.x("dve", "tensor_scalar", out=CTin[32:40, :], in0=GC[32:40, :], scalar1=-1.0, scalar2=None, op0=ALU.mult,
        reads=[B_GC, B_CTin], writes=[B_CTin])
    P.x("act", "activation", out=CTin[64:72, :], in_=GC[64:72, :], func=AF.Exp, reads=[B_GC, B_CTin], writes=[B_CTin])
    gc3 = GC[96:104, :].rearrange("p (c t) -> p c t", t=64)
    P.x("dve", "tensor_tensor", out=CTin[96:104, :].rearrange("p (c t) -> p c t", t=64),
        in0=gc3[:, :, 63:64].to_broadcast([8, NCH, 64]), in1=gc3, op=ALU.subtract,
        reads=[B_GC, B_CTin], writes=[B_CTin])
    P.x("act", "activation", out=CTin[96:104, :], in_=CTin[96:104, :], func=AF.Exp, reads=[B_CTin], writes=[B_CTin])
    P.x("pool", "tensor_copy", out=PTin[96:104, :], in_=CTin[96:104, :], reads=[B_CTin, B_PTin], writes=[B_PTin])
    eg3 = PTin[64:72, :].rearrange("p (c t) -> p c t", t=64)
    P.x("dve", "tensor_tensor", out=Dg[64:72], in0=eg3[:, :, 63:64].to_broadcast([8, NCH, 8]),
        in1=ident_f[64:72, 64:72].unsqueeze(1).to_broadcast([8, NCH, 8]), op=ALU.mult,
        reads=[B_PTin, B_identf], writes=[B_Dg])
    psg, B_psg = BIG.next()
    P.x("pe", "matmul", psg[:, 0:NCH * 8], lhsT=ones_f[64:72, :], rhs=Dg[64:72].rearrange("p c h -> p (c h)"),
        start=True, stop=True, reads=[B_onesf, B_Dg], writes=[B_psg])
    P.x("act", "activation", out=GLB, in_=psg[:, 0:NCH * 8], func=AF.Copy, reads=[B_psg], writes=[B_GLB])
    for i in range(NP):
        ps, B_ps = SMALL.next()
        P.x("pe", "transpose", ps[:, 0:104], PTin[0:104, i * 128:(i + 1) * 128], ident_f[0:104, 0:104],
            reads=[B_PTin, B_identf], writes=[B_ps])
        pv = ps[:, 0:128].rearrange("p (k c) -> p k c", c=32)[:, :, 0:8]
        if i % 2:
            P.x("dve", "tensor_copy", out=PT[:, i], in_=pv, reads=[B_ps], writes=[B_PT])
        else:
            P.x("act", "activation", out=PT[:, i], in_=pv, func=AF.Copy, reads=[B_ps], writes=[B_PT])
    for ch in range(NCH):
        ps, B_ps = SMALL.next()
        P.x("pe", "transpose", ps[0:64, 0:104], CTin[0:104, ch * 64:(ch + 1) * 64], ident_f[0:104, 0:104],
            reads=[B_CTin, B_identf], writes=[B_ps])
        cv = ps[0:64, 32:128].rearrange("p (k c) -> p k c", c=32)[:, :, 0:8]
        if ch % 2:
            P.x("dve", "tensor_copy", out=CT[:, ch], in_=cv, reads=[B_ps], writes=[B_CT])
        else:
            P.x("act", "activation", out=CT[:, ch], in_=cv, func=AF.Copy, reads=[B_ps], writes=[B_CT])
    A.release(m0)
    P.mark("dn0")
    dump("PT", PT, B_PT, [128, NP, 4, 8])
    dump("CT", CT, B_CT, [64, NCH, 3, 8])
    dump("GLB", GLB, B_GLB, [128, NCH * 8])
    if stop_after == "dn0":
        P.emit()
        return nc, dbg

    HG = 2
    cur["big"] = BIG
    m_dn = A.mark()
    pre, B_pre = A.alloc("pre", [128, 3 + LT], F32)
    P.x("pool", "memset", pre[:, 0:3], 0.0, writes=[B_pre])
    acc, B_acc = A.alloc("acc", [128, LT], F32)
    MnSL, B_MnSL = A.alloc("MnSL", [128, 128], F32)
    MnCU, B_MnCU = A.alloc("MnCU", [64, 64], F32)
    P.x("dve", "tensor_scalar", out=MnSL, in0=SLm, scalar1=-1.0, scalar2=30000.0, op0=ALU.add, op1=ALU.mult,
        reads=[B_SL], writes=[B_MnSL])
    P.x("dve", "tensor_scalar", out=MnCU, in0=CUm, scalar1=-1.0, scalar2=30000.0, op0=ALU.add, op1=ALU.mult,
        reads=[B_CU], writes=[B_MnCU])
    MnCUp, B_MnCUp = A.alloc("MnCUp", [128, 128], F32)
    P.x("pool", "memset", MnCUp, 1.0, writes=[B_MnCUp])
    P.x("pool", "affine_select", out=MnCUp, in_=MnCUp, pattern=[[1, 128]], compare_op=ALU.is_ge,
        fill=0.0, base=0, channel_multiplier=-1, reads=[B_MnCUp], writes=[B_MnCUp])
    P.x("pool", "memset", MnCUp[0:64, 64:128], 0.0, reads=[B_MnCUp], writes=[B_MnCUp])
    P.x("dve", "tensor_scalar", out=MnCUp, in0=MnCUp, scalar1=-1.0, scalar2=30000.0, op0=ALU.add, op1=ALU.mult,
        reads=[B_MnCUp], writes=[B_MnCUp])
    sq, B_sq = A.alloc("sq", [128, LT], BF16)
    RN = Ring([A.alloc(f"rn{i}", [128, 512], F32) for i in range(1)])

    def ring(name, n, shape, dt):
        return Ring([A.alloc(f"{name}{i}", shape, dt) for i in range(n)])

    slots = []
    for s_ in range(HG):
        d_ = {}
        for nm in ("qT", "kT", "vT"):
            d_[nm] = A.alloc(f"{nm}{s_}", [128, LT], BF16)
        d_["szd"] = A.alloc(f"szd{s_}", [128, LO], BF16)
        d_["S_f"] = A.alloc(f"S_f{s_}", [128, 128], F32)
        d_["S_b"] = A.alloc(f"S_b{s_}", [128, 128], BF16)
        d_["E"] = ring(f"E{s_}_", 2, [128, 128], F32)
        d_["A"] = ring(f"Am{s_}_", 23, [128, 128], BF16)
        d_["P"] = ring(f"Pm{s_}_", 10, [128, 128], BF16)
        d_["bv"] = ring(f"bv{s_}_", 3, [128, 128], BF16)
        d_["kbg"] = ring(f"kbg{s_}_", 3, [128, 128], BF16)
        d_["wT"] = ring(f"wT{s_}_", 3, [128, 128], BF16)
        d_["TT"] = ring(f"TT{s_}_", 3, [128, 128], BF16)
        d_["u"] = ring(f"u{s_}_", 3, [128, 128], F32)
        d_["kd"] = ring(f"kd{s_}_", 3, [128, 128], BF16)
        d_["ET"] = ring(f"ET{s_}_", 2, [128, 128], F32)
        d_["qk"] = ring(f"qk{s_}_", 3, [128, 128], BF16)
        d_["vn"] = ring(f"vn{s_}_", 2, [128, 128], BF16)
        d_["wTz"] = ring(f"wTz{s_}_", 3, [128, 128], BF16)
        for (wz_, B_wz_) in d_["wTz"].items:
            P.x("pool", "memset", wz_, 0.0, writes=[B_wz_])
        d_["o1"] = ring(f"o1{s_}_", 2, [64, 128], F32)
        d_["o"] = ring(f"o{s_}_", 2, [64, 128], F32)
        d_["on"] = ring(f"on{s_}_", 2, [64, 128], BF16)
        d_["st"] = ring(f"st{s_}_", 4, [64, 2], F32)
        d_["ojunk"] = A.alloc(f"ojunk{s_}", [64, 128], BF16)
        slots.append(d_)

    QSCALE = 128.0 ** -0.5

    def head_proj(h, sl_):
        qT, B_qT = sl_["qT"]
        kT, B_kT = sl_["kT"]
        vT, B_vT = sl_["vT"]
        szd, B_szd = sl_["szd"]
        deferred = []

        def l2norm(idx, dst, B_dst):
            for b in range(NB):
                sl = slice(b * 512, (b + 1) * 512)
                ps, B_ps = BIG.next()
                P.x("pe", "matmul", ps, lhsT=ones_b, rhs=sq[:, sl], start=True, stop=True,
                    reads=[B_onesb, B_sq], writes=[B_ps])
                rn, B_rn = RN.next()
                P.x("act", "activation", out=rn, in_=ps, func=AF.Sqrt, bias=epsc[:, 0:1], scale=1.0,
                    reads=[B_ps, B_eps], writes=[B_rn])
                P.x("dve", "reciprocal", out=rn, in_=rn, reads=[B_rn], writes=[B_rn])
                P.x("dve", "scalar_tensor_tensor", out=dst[:, sl], in0=acc[:, sl], scalar=(QSCALE if idx == 0 else 1.0),
                    in1=rn, op0=ALU.mult, op1=ALU.mult, reads=[B_acc, B_rn], writes=[B_dst])

        for idx, (cbase, dst, B_dst) in enumerate(((T_Q, qT, B_qT), (T_K, kT, B_kT), (T_V, vT, B_vT))):
            slot, bslot = dn_pipe.take()

            def cons_pre(b, ps, B_ps):
                P.x("act", "activation", out=pre[:, 3 + b * 512:3 + (b + 1) * 512], in_=ps, func=AF.Copy,
                    reads=[B_ps], writes=[B_pre])
            yield from proj_fm_g(slot, bslot, KT, h_rhs, h_bufs, range(NB), cons_pre, ring=BIG8, G=4)
            while deferred:
                deferred.pop(0)()
            tile = idx * 8 + h
            P.x("dve", "tensor_scalar", out=acc, in0=pre[:, 0:LT], scalar1=convw[:, tile, 0:1], scalar2=None, op0=ALU.mult,
                reads=[B_pre, B_convw], writes=[B_acc])
            for j in range(1, 4):
                P.x("dve", "scalar_tensor_tensor", out=acc, in0=pre[:, j:j + LT], scalar=convw[:, tile, j:j + 1], in1=acc,
                    op0=ALU.mult, op1=ALU.add, reads=[B_pre, B_convw, B_acc], writes=[B_acc])
            yield "P"
            if idx == 2:
                P.x("act", "activation", out=vT, in_=acc, func=AF.Silu, reads=[B_acc], writes=[B_vT])
            else:
                P.x("act", "activation", out=acc, in_=acc, func=AF.Silu, reads=[B_acc], writes=[B_acc])
                P.x("act", "activation", out=sq, in_=acc, func=AF.Square, reads=[B_acc], writes=[B_sq])
                deferred.append(lambda idx=idx, dst=dst, B_dst=B_dst: l2norm(idx, dst, B_dst))
        slot, bslot = dn_pipe.take()

        def cons_zd(b, ps, B_ps):
            bo = b - (NB - NBO)
            P.x("act", "activation", out=szd[:, bo * 512:(bo + 1) * 512], in_=ps, func=AF.Silu, reads=[B_ps], writes=[B_szd])
        yield from proj_fm_g(slot, bslot, KT, h_rhs, h_bufs, range(NB - NBO, NB), cons_zd, ring=BIG8, G=4)
        while deferred:
            deferred.pop(0)()

    def intra_gen(h, i, sl_):
        qT, B_qT = sl_["qT"]
        kT, B_kT = sl_["kT"]
        vT, B_vT = sl_["vT"]
        tok = slice(i * 128, (i + 1) * 128)
        psD, B_psD = SMALL.next()
        P.x("pe", "matmul", psD, lhsT=SEL[32:40, h, :], rhs=GC[32:40, tok], start=True, stop=True,
            reads=[B_SEL, B_GC], writes=[B_psD])
        E, B_E = sl_["E"].next()
        gcp = PT[:, i, 1, h:h + 1]
        P.x("dve", "scalar_tensor_tensor", out=E, in0=psD, scalar=gcp, in1=MnSL, op0=ALU.subtract, op1=ALU.subtract,
            reads=[B_psD, B_PT, B_MnSL], writes=[B_E])
        ETp, B_ETp = sl_["ET"].next()
        P.x("dve", "scalar_tensor_tensor", out=ETp, in0=psD, scalar=gcp, in1=MnCUp, op0=ALU.subtract, op1=ALU.add,
            reads=[B_psD, B_PT, B_MnCUp], writes=[B_ETp])
        P.x("act", "activation", out=E, in_=E, func=AF.Exp, scale=-1.0, reads=[B_E], writes=[B_E])
        P.x("act", "activation", out=ETp, in_=ETp, func=AF.Exp, reads=[B_ETp], writes=[B_ETp])
        yield
        pskk, B_pskk = SMALL.next()
        P.x("pe", "matmul", pskk, lhsT=kT[:, tok], rhs=kT[:, tok], start=True, stop=True, reads=[B_kT], writes=[B_pskk])
        Am, B_Am = sl_["A"].next()
        P.x("dve", "scalar_tensor_tensor", out=Am, in0=pskk, scalar=PT[:, i, 0, h:h + 1], in1=E, op0=ALU.mult, op1=ALU.mult,
            reads=[B_pskk, B_PT, B_E], writes=[B_Am])
        yield
        psB, B_psB = TRB.next()
        P.x("pe", "transpose", psB, Am, ident_b, reads=[B_Am, B_identb], writes=[B_psB])
        Bm, B_Bm = sl_["A"].next()
        P.x("act", "activation", out=Bm, in_=psB, func=AF.Copy, reads=[B_psB], writes=[B_Bm])
        P0, B_P0 = sl_["P"].next()
        P.x("pool", "tensor_tensor", out=P0, in0=ident_b, in1=Bm, op=ALU.subtract, reads=[B_identb, B_Bm], writes=[B_P0])
        yield
        psv, B_psv = TRB.next()
        P.x("pe", "transpose", psv, vT[:, tok], ident_b, reads=[B_vT, B_identb], writes=[B_psv])
        bv, B_bv = sl_["bv"].next()
        P.x("act", "activation", out=bv, in_=psv, func=AF.Copy, scale=PT[:, i, 0, h:h + 1], reads=[B_psv, B_PT], writes=[B_bv])
        psk, B_psk = TRB.next()
        P.x("pe", "transpose", psk, kT[:, tok], ident_b, reads=[B_kT, B_identb], writes=[B_psk])
        kbg, B_kbg = sl_["kbg"].next()
        P.x("dve", "tensor_scalar", out=kbg, in0=psk, scalar1=PT[:, i, 0, h:h + 1], scalar2=PT[:, i, 2, h:h + 1],
            op0=ALU.mult, op1=ALU.mult, reads=[B_psk, B_PT], writes=[B_kbg])
        kdp, B_kdp = sl_["kd"].next()
        P.x("dve", "tensor_scalar", out=kdp, in0=psk, scalar1=PT[:, i, 3, h:h + 1], scalar2=None, op0=ALU.mult,
            reads=[B_psk, B_PT], writes=[B_kdp])
        yield
        pskq, B_pskq = SMALL.next()
        P.x("pe", "matmul", pskq, lhsT=kT[:, tok], rhs=qT[:, tok], start=True, stop=True, reads=[B_kT, B_qT], writes=[B_pskq])
        qkp, B_qkp = sl_["qk"].next()
        P.x("dve", "tensor_tensor", out=qkp, in0=pskq, in1=ETp, op=ALU.mult, reads=[B_pskq, B_ETp], writes=[B_qkp])
        yield
        chunks = []
        for xh in range(2):
            chunks.append(dict(ch=2 * i + xh, xh=xh, R=slice(64 * xh, 64 * xh + 64),
                               ctok=slice(i * 128 + 64 * xh, i * 128 + 64 * xh + 64)))
        Ac, B_Ac, Bc, B_Bc, Pc, B_Pc = Am, B_Am, Bm, B_Bm, P0, B_P0
        for lvl in range(5):
            psA, B_psA = SMALL.next()
            P.x("pe", "matmul", psA, lhsT=Bc, rhs=Ac, start=True, stop=True, reads=[B_Bc, B_Ac], writes=[B_psA])
            A2, B_A2 = sl_["A"].next()
            P.x("dve", "tensor_copy", out=A2, in_=psA, reads=[B_psA], writes=[B_A2])
            if lvl < 4:
                psB2, B_psB2 = SMALL.next()
                P.x("pe", "matmul", psB2, lhsT=Ac, rhs=Bc, start=True, stop=True, reads=[B_Bc, B_Ac], writes=[B_psB2])
                B2, B_B2 = sl_["A"].next()
                P.x("act", "activation", out=B2, in_=psB2, func=AF.Copy, reads=[B_psB2], writes=[B_B2])
            else:
                B2, B_B2 = None, None
            yield
            psP, B_psP = SMALL.next()
            P.x("pe", "matmul", psP, lhsT=A2, rhs=Pc, start=True, stop=True, reads=[B_A2, B_Pc], writes=[B_psP])
            if lvl < 4:
                Pn, B_Pn = sl_["P"].next()
            else:
                Pn, B_Pn = sl_["TT"].next()
            P.x("dve", "tensor_tensor", out=Pn, in0=Pc, in1=psP, op=ALU.add, reads=[B_Pc, B_psP], writes=[B_Pn])
            Ac, B_Ac, Bc, B_Bc, Pc, B_Pc = A2, B_A2, B2, B_B2, Pn, B_Pn
            yield
        TT, B_TT = Pc, B_Pc
        psw, B_psw = SMALL.next()
        P.x("pe", "matmul", psw, lhsT=kbg, rhs=TT, start=True, stop=True, reads=[B_kbg, B_TT], writes=[B_psw])
        wT, B_wT = sl_["wT"].next()
        P.x("act", "activation", out=wT, in_=psw, func=AF.Copy, reads=[B_psw], writes=[B_wT])
        wTz, B_wTz = sl_["wTz"].next()
        P.x("act", "activation", out=wTz[:, 64:128], in_=psw[:, 64:128], func=AF.Copy, reads=[B_psw, B_wTz], writes=[B_wTz])
        yield
        psu, B_psu = SMALL.next()
        P.x("pe", "matmul", psu, lhsT=TT, rhs=bv, start=True, stop=True, reads=[B_TT, B_bv], writes=[B_psu])
        u_sb, B_u = sl_["u"].next()
        P.x("act", "activation", out=u_sb, in_=psu, func=AF.Copy, reads=[B_psu], writes=[B_u])
        yield
        return dict(wT=wT, B_wT=B_wT, wTz=wTz, B_wTz=B_wTz, u=u_sb, B_u=B_u, kd=kdp, B_kd=B_kdp, qk=qkp, B_qk=B_qkp, chunks=chunks)

    def recur_gen(h, i, r, sl_):
        qT, B_qT = sl_["qT"]
        szd, B_szd = sl_["szd"]
        S_f, B_Sf = sl_["S_f"]
        S_b, B_Sb = sl_["S_b"]
        ojunk, B_ojunk = sl_["ojunk"]
        vn, B_vn = sl_["vn"].next()
        for c in r["chunks"]:
            ch, R, ctok, xh = c["ch"], c["R"], c["ctok"], c["xh"]
            own = ctok.start >= T0
            psws, B_psws = SMALL.next()
            if xh == 0:
                P.x("pe", "matmul", psws[0:64, :], lhsT=r["wT"][:, 0:64], rhs=S_b, start=True, stop=True,
                    reads=[r["B_wT"], B_Sb], writes=[B_psws])
            else:
                P.x("pe", "matmul", psws, lhsT=r["wTz"], rhs=S_b, start=True, stop=True,
                    reads=[r["B_wTz"], B_Sb], writes=[B_psws])
            P.x("dve", "tensor_tensor", out=vn[R, :], in0=r["u"][R, :], in1=psws[R, :], op=ALU.subtract,
                reads=[r["B_u"], B_psws, B_vn], writes=[B_vn])
            if own:
                pso1, B_pso1 = SMALL.next()
                P.x("pe", "matmul", pso1[0:64, :], lhsT=qT[:, ctok], rhs=S_b, start=True, stop=True,
                    reads=[B_qT, B_Sb], writes=[B_pso1])
                o1, B_o1 = sl_["o1"].next()
                P.x("act", "activation", out=o1, in_=pso1[0:64, :], func=AF.Copy, scale=CT[:, ch, 1, h:h + 1],
                    reads=[B_pso1, B_CT], writes=[B_o1])
            yield
            psdS, B_psdS = SMALL.next()
            P.x("pe", "matmul", psdS, lhsT=r["kd"][R, :], rhs=vn[R, :], start=True, stop=True, reads=[r["B_kd"], B_vn], writes=[B_psdS])
            gl = GLB[:, ch * 8 + h:ch * 8 + h + 1]
            P.x("dve", "scalar_tensor_tensor", out=S_b, in0=S_f, scalar=gl, in1=psdS,
                op0=ALU.mult, op1=ALU.add, reads=[B_Sf, B_GLB, B_psdS, B_Sb], writes=[B_Sb])
            P.x("dve", "scalar_tensor_tensor", out=S_f, in0=S_f, scalar=gl, in1=psdS,
                op0=ALU.mult, op1=ALU.add, reads=[B_Sf, B_GLB, B_psdS], writes=[B_Sf])
            yield
            if own:
                pso2, B_pso2 = SMALL.next()
                P.x("pe", "matmul", pso2[0:64, :], lhsT=r["qk"][R, R], rhs=vn[R, :], start=True, stop=True,
                    reads=[r["B_qk"], B_vn], writes=[B_pso2])
                o, B_o = sl_["o"].next()
                P.x("dve", "tensor_tensor", out=o, in0=o1, in1=pso2[0:64, :], op=ALU.add, reads=[B_o1, B_pso2], writes=[B_o])
                st_, B_st = sl_["st"].next()
                P.x("act", "activation", out=ojunk, in_=o, func=AF.Square, accum_out=st_[:, 0:1],
                    reads=[B_o], writes=[B_ojunk, B_st])
                P.x("act", "activation", out=st_[:, 1:2], in_=st_[:, 0:1], func=AF.Sqrt, bias=epsc[0:64, 0:1], scale=1.0 / 128,
                    reads=[B_st, B_eps], writes=[B_st])
                P.x("dve", "reciprocal", out=st_[:, 1:2], in_=st_[:, 1:2], reads=[B_st], writes=[B_st])
                on, B_on = sl_["on"].next()
                P.x("act", "activation", out=on, in_=o, func=AF.Copy, scale=st_[:, 1:2], reads=[B_o, B_st], writes=[B_on])
                yield
                psoT, B_psoT = TRB.next()
                P.x("pe", "transpose", psoT[:, 0:64], on, ident_b[0:64, 0:64], reads=[B_on, B_identb], writes=[B_psoT])
                t0o = ctok.start - T0
                P.x("dve", "scalar_tensor_tensor", out=dnoT[:, h, t0o:t0o + 64], in0=psoT[:, 0:64], scalar=dnnw[:, 0:1],
                    in1=szd[:, t0o:t0o + 64], op0=ALU.mult, op1=ALU.mult,
                    reads=[B_psoT, B_dnnw, B_szd], writes=[B_dno[h]])
                yield

    def interleave(g1, g2):
        res = None
        act = [g for g in (g1, g2) if g is not None]
        while act:
            for g in list(act):
                try:
                    next(g)
                except StopIteration as e:
                    if g is g1:
                        res = e.value
                    act.remove(g)
            yield
        return res

    def head_gen(h, sl_):
        S_f, B_Sf = sl_["S_f"]
        S_b, B_Sb = sl_["S_b"]
        P.x("pool", "memset", S_f, 0.0, writes=[B_Sf])
        P.x("pool", "memset", S_b, 0.0, writes=[B_Sb])
        results = {}
        intras = {}
        next_intra = 0
        next_recur = 0
        recur_g = None
        recur_done = 0
        while recur_done < NP:
            while len(intras) < 2 and next_intra < NP and next_intra <= recur_done + 2:
                intras[next_intra] = intra_gen(h, next_intra, sl_)
                next_intra += 1
            if recur_g is None and next_recur in results:
                recur_g = recur_gen(h, next_recur, results.pop(next_recur), sl_)
                next_recur += 1
            for j in list(intras):
                try:
                    next(intras[j])
                except StopIteration as e:
                    results[j] = e.value
                    del intras[j]
            if recur_g is not None:
                try:
                    next(recur_g)
                except StopIteration:
                    recur_g = None
                    recur_done += 1
            yield

    dn_tiles = []
    for hg in range(0, nheads, HG):
        for h in range(hg, min(hg + HG, nheads)):
            dn_tiles += [(w_in_t[T_Q + h], KT), (w_in_t[T_K + h], KT), (w_in_t[T_V + h], KT), (w_in_t[T_ZD + h], KT)]
    dn_pipe = WPipe(dn_tiles, depth=1)
    dn_pipe.start()
    for hg in range(0, nheads, HG):
        hs = list(range(hg, min(hg + HG, nheads)))
        for k, h in enumerate(hs):
            for _ in head_proj(h, slots[k]):
                pass
        if hg == 0 and hs:
            dump("qT", slots[len(hs) - 1]["qT"][0], slots[len(hs) - 1]["qT"][1], [128, LT], BF16)
            dump("kT", slots[len(hs) - 1]["kT"][0], slots[len(hs) - 1]["kT"][1], [128, LT], BF16)
            dump("vT", slots[len(hs) - 1]["vT"][0], slots[len(hs) - 1]["vT"][1], [128, LT], BF16)
        P.mark(f"dn_g{hg}_proj")
        gens = [head_gen(h, slots[k]) for k, h in enumerate(hs)]
        cur["small"] = BIG8
        while gens:
            for g in list(gens):
                try:
                    next(g)
                except StopIteration:
                    gens.remove(g)
        cur["small"] = _SM
    A.release(m_dn0)
    H.release(mH)
    cur["big"] = BIG8
    P.mark("dn")
    dump("dnoT", dnoT[:, 0:nheads], B_dno, [128, nheads, LO], BF16)
    if stop_after == "dn":
        P.emit()
        return nc, dbg

    drain_tables()
    m_s5w = A.mark()
    Gpad, B_Gpad = _tres["Gpad"]
    Mintra, B_Mintra = _tres["Mintra"]
    Minter, B_Minter = _tres["Minter"]
    A8, B_A8 = _tres["A8"]
    A8P, B_A8P = _tres["A8P"]
    A64, B_A64 = _tres["A64"]
    Sst, B_Sst = A.alloc("Sst", [128, 2, 32], F32)
    y2T, B_y2T = H.alloc("y2T", [128, 8, LO], BF16, nbufs=8)
    P.x("pool", "memset", Sst, 0.0, writes=[B_Sst])
    P.mark("s5tab")
    m_blk = A.mark()
    UC = Ring([A.alloc(f"ucol{i}", [128, 64, 64], BF16) for i in range(1)])
    YC = Ring([A.alloc(f"ycol{i}", [128, 64, 64], BF16) for i in range(1)])
    _uy = [UC.items[0], YC.items[0]]
    Lt, B_Lt = A.alloc("Lt", [128, 2, 32, 64], F32)
    B_Lre, B_Lim = Buf("Lre"), Buf("Lim")
    ct1, B_ct1 = A.alloc("ct1", [128, 32, 4, 8], F32)
    ct2, B_ct2 = A.alloc("ct2", [128, 32, 4, 8], F32)
    mH2 = H.mark()
    hist, B_hist = H.alloc("hist", [128, 2, 32, 64], BF16)
    sc1, B_sc1 = H.alloc("sc1", [128, 2, 32], F32)
    sc2, B_sc2 = H.alloc("sc2", [128, 2, 32], F32)
    xm1, B_xm1 = H.alloc("xm1", [128, 2, 32, 8], F32)
    xm2, B_xm2 = H.alloc("xm2", [128, 2, 32, 8], F32)
    Cs, B_Cs = H.alloc("Cs", [128, 2, 32, 9], F32)
    Uv = U_scr.rearrange("(g ci) t c -> t ci g c", ci=16)
    Yv = Y_scr.rearrange("(g co) t c -> t co g c", co=16)
    B_Y = Buf("Yscr")
    AR2 = A8[:, 0]
    AI2 = A8[:, 1]
    for b in range(NB):
        own = b >= NB - NBO
        bo = b - (NB - NBO)
        ucol, B_uc = _uy[b % 2]
        for tau in range(8):
            P.d("sp", ucol[16 * tau:16 * tau + 16, :, :], Uv[tau][:, :, b * 64:(b + 1) * 64], reads=B_U, writes=[B_uc])
        for q4 in range(8):
            ps, B_ps = _SM.next()
            for k in range(4):
                gp = q4 * 4 + k
                for ri in range(2):
                    o_ = ps[:, (k * 2 + ri) * 64:(k * 2 + ri + 1) * 64]
                    P.x("pe", "matmul", o_, lhsT=Gpad[:, gp, ri, :], rhs=ucol[:, gp, :], start=True, stop=False,
                        reads=[B_Gpad, B_uc], writes=[B_ps])
                    P.x("pe", "matmul", o_, lhsT=Gpad[:, 32 + gp, ri, :], rhs=ucol[:, 32 + gp, :], start=False, stop=True,
                        reads=[B_Gpad, B_uc], writes=[B_ps])
            o_l = Lt[:, :, q4 * 4:q4 * 4 + 4, :].rearrange("p r g c -> p g r c")
            i_l = ps.rearrange("p (g r c) -> p g r c", g=4, r=2)
            if q4 % 2:
                P.x("act", "activation", out=o_l, in_=i_l, func=AF.Copy, reads=[B_ps, B_Lt, B_Lre, B_Lim], writes=[B_Lt, B_Lre, B_Lim])
            else:
                P.x("dve", "tensor_copy", out=o_l, in_=i_l, reads=[B_ps, B_Lt, B_Lre, B_Lim], writes=[B_Lt, B_Lre, B_Lim])
        L5 = Lt.rearrange("p r g (s k) -> p r g s k", k=8)
        LB = [B_Lt, B_Lre, B_Lim]
        ARb = A8[:, 0].unsqueeze(3).to_broadcast([128, 2, 32, 8])
        AIb = A8[:, 1].unsqueeze(3).to_broadcast([128, 2, 32, 8])
        for k in range(1, 8):
            xp = L5[:, :, :, :, k - 1]
            P.x("dve", "tensor_tensor", out=xm1, in0=ARb, in1=xp, op=ALU.mult, reads=[B_A8] + LB, writes=[B_xm1])
            P.x("dve", "tensor_tensor", out=xm2, in0=AIb, in1=xp, op=ALU.mult, reads=[B_A8] + LB, writes=[B_xm2])
            P.x("dve", "tensor_tensor", out=xm1, in0=xm1, in1=L5[:, :, :, :, k], op=ALU.add, reads=[B_xm1] + LB, writes=[B_xm1])
            P.x("dve", "tensor_tensor", out=L5[:, 0, :, :, k], in0=xm1[:, 0], in1=xm2[:, 1], op=ALU.subtract,
                reads=[B_xm1, B_xm2] + LB, writes=LB)
            P.x("dve", "tensor_tensor", out=L5[:, 1, :, :, k], in0=xm1[:, 1], in1=xm2[:, 0], op=ALU.add,
                reads=[B_xm1, B_xm2] + LB, writes=LB)
        P.x("dve", "tensor_copy", out=Cs[:, :, :, 0], in_=Sst, reads=[B_Sst, B_Cs], writes=[B_Cs])
        for sg_ in range(8):
            cp = Cs[:, :, :, sg_]
            P.x("dve", "tensor_tensor", out=sc1, in0=A64[:, 0], in1=cp, op=ALU.mult, reads=[B_A64, B_Cs], writes=[B_sc1])
            P.x("dve", "tensor_tensor", out=sc2, in0=A64[:, 1], in1=cp, op=ALU.mult, reads=[B_A64, B_Cs], writes=[B_sc2])
            P.x("dve", "tensor_tensor", out=sc1, in0=sc1, in1=L5[:, :, :, sg_, 7], op=ALU.add, reads=[B_sc1] + LB, writes=[B_sc1])
            P.x("dve", "tensor_tensor", out=Cs[:, 0, :, sg_ + 1], in0=sc1[:, 0, :], in1=sc2[:, 1, :], op=ALU.subtract,
                reads=[B_sc1, B_sc2, B_Cs], writes=[B_Cs])
            P.x("dve", "tensor_tensor", out=Cs[:, 1, :, sg_ + 1], in0=sc1[:, 1, :], in1=sc2[:, 0, :], op=ALU.add,
                reads=[B_sc1, B_sc2, B_Cs], writes=[B_Cs])
        if own:
            sh4 = [128, 32, 4, 8]
            Wr = A8P[:, :, 0, :].rearrange("p k g -> p g k").unsqueeze(2).to_broadcast(sh4)
            Wi = A8P[:, :, 1, :].rearrange("p k g -> p g k").unsqueeze(2).to_broadcast(sh4)
            for s0 in (0, 4):
                Cr = Cs[:, 0, :, s0:s0 + 4].unsqueeze(3).to_broadcast(sh4)
                Ci = Cs[:, 1, :, s0:s0 + 4].unsqueeze(3).to_broadcast(sh4)
                Lre, Lim = L5[:, 0, :, s0:s0 + 4, :], L5[:, 1, :, s0:s0 + 4, :]
                P.x("dve", "tensor_tensor", out=ct1, in0=Wr, in1=Cr, op=ALU.mult, reads=[B_A8P, B_Cs, B_ct1], writes=[B_ct1])
                P.x("dve", "tensor_tensor", out=Lre, in0=Lre, in1=ct1, op=ALU.add, reads=[B_ct1, B_Lre, B_Lt], writes=[B_Lre])
                P.x("dve", "tensor_tensor", out=ct1, in0=Wi, in1=Ci, op=ALU.mult, reads=[B_A8P, B_Cs, B_ct1], writes=[B_ct1])
                P.x("dve", "tensor_tensor", out=Lre, in0=Lre, in1=ct1, op=ALU.subtract, reads=[B_ct1, B_Lre], writes=[B_Lre])
                P.x("pool", "tensor_tensor", out=ct2, in0=Wr, in1=Ci, op=ALU.mult, reads=[B_A8P, B_Cs, B_ct2], writes=[B_ct2])
                P.x("pool", "tensor_tensor", out=Lim, in0=Lim, in1=ct2, op=ALU.add, reads=[B_ct2, B_Lim, B_Lt], writes=[B_Lim])
                P.x("pool", "tensor_tensor", out=ct2, in0=Wi, in1=Cr, op=ALU.mult, reads=[B_A8P, B_Cs, B_ct2], writes=[B_ct2])
                P.x("pool", "tensor_tensor", out=Lim, in0=Lim, in1=ct2, op=ALU.add, reads=[B_ct2, B_Lim], writes=[B_Lim])
            P.x("act", "activation", out=hist[:, :, :, 1:64], in_=Lt[:, :, :, 0:63], func=AF.Copy,
                reads=[B_Lre, B_Lim, B_Lt, B_hist], writes=[B_hist])
            P.x("pool", "tensor_copy", out=hist[:, :, :, 0], in_=Sst, reads=[B_Sst, B_hist], writes=[B_hist])
        P.x("dve", "tensor_copy", out=Sst, in_=Cs[:, :, :, 8], reads=[B_Cs, B_Sst, B_hist], writes=[B_Sst])
        if not own:
            continue
        ycol, B_yc = _uy[(b + 1) % 2]
        for q8 in range(8):
            ps, B_ps = _SM.next()
            for k in range(8):
                g = q8 * 8 + k
                half, gp = g // 32, g % 32
                rows = slice(half * 64, half * 64 + 64)
                o_ = ps[:, k * 64:(k + 1) * 64]
                P.x("pe", "matmul", o_, lhsT=Mintra[:, g, :], rhs=ucol[:, g, :], start=True, stop=False,
                    reads=[B_Mintra, B_uc], writes=[B_ps])
                P.x("pe", "matmul", o_, lhsT=Minter[rows, gp, 0, :], rhs=hist[rows, 0, gp, :], start=False, stop=False,
                    reads=[B_Minter, B_hist], writes=[B_ps])
                P.x("pe", "matmul", o_, lhsT=Minter[rows, gp, 1, :], rhs=hist[rows, 1, gp, :], start=False, stop=True,
                    reads=[B_Minter, B_hist], writes=[B_ps])
            P.x("act", "activation", out=ycol[:, q8 * 8:q8 * 8 + 8, :], in_=ps.rearrange("p (g c) -> p g c", g=8),
                func=AF.Gelu_apprx_tanh, reads=[B_ps, B_yc], writes=[B_yc])
        for t in range(8):
            P.d("sp", Yv[t][:, :, bo * 64:(bo + 1) * 64], ycol[16 * t:16 * t + 16, :, :], reads=[B_yc], writes=[B_Y])
    A.release(m_s5w)
    A.release_top()
    H.release(mH2)
    for j in range(8):
        P.d("sp", y2T[:, j, :], Y_scr[j * 128:(j + 1) * 128].rearrange("p t c -> p (t c)"), reads=[B_Y], writes=[B_y2T[j]])
    P.mark("s5blk")
    dump("y2T", y2T, B_y2T, [128, 8, LO], BF16)
    if stop_after == "s5":
        P.emit()
        return nc, dbg

    y4T, B_y4 = A.alloc("y4T", [128, 8, LO], BF16, nbufs=8)
    m_glu = A.mark()
    y3s = [A.alloc(f"y3_{i}", [128, LO], BF16) for i in range(2)]
    SG = Ring([A.alloc(f"sg{i}", [128, 512], BF16) for i in range(2)])
    SZ = Ring([A.alloc(f"sz{i}", [128, 512], BF16) for i in range(2)])

    def y2_rhs(kt, pb):
        return y2T[:, kt, pb * 512:(pb + 1) * 512]

    def y2_bufs(pb):
        return list(B_y2T)

    glu_tiles = []
    for j in range(8):
        glu_tiles += [(wglu_t[j], 8), (w_in_t[T_ZS + j], KT)]
    glu_pipe = WPipe(glu_tiles, depth=1)
    for j in range(8):
        y3, B_y3 = y3s[j % 2]
        slot, bslot = glu_pipe.take()

        def cons_glu(pb, ps, B_ps, j=j, y3=y3, B_y3=B_y3):
            sg, B_sg = SG.next()
            P.x("act", "activation", out=sg, in_=ps, func=AF.Sigmoid, reads=[B_ps], writes=[B_sg])
            P.x("dve", "tensor_tensor", out=y3[:, pb * 512:(pb + 1) * 512], in0=y2T[:, j, pb * 512:(pb + 1) * 512], in1=sg,
                op=ALU.mult, reads=[B_y2T[j], B_sg], writes=[B_y3])
        proj_fm(slot, bslot, 8, y2_rhs, y2_bufs, range(NBO), cons_glu)
        slot2, bslot2 = glu_pipe.take()

        def cons_zs(b, ps, B_ps, j=j, y3=y3, B_y3=B_y3):
            bo = b - (NB - NBO)
            sz, B_sz = SZ.next()
            P.x("act", "activation", out=sz, in_=ps, func=AF.Silu, reads=[B_ps], writes=[B_sz])
            P.x("dve", "tensor_tensor", out=y4T[:, j, bo * 512:(bo + 1) * 512].rearrange("p (c t) -> p c t", t=8),
                in0=y3.rearrange("p (t c) -> p c t", t=8)[:, bo * 64:(bo + 1) * 64, :],
                in1=sz.rearrange("p (c t) -> p c t", t=8), op=ALU.mult,
                reads=[B_y3, B_sz], writes=[B_y4[j]])
        proj_fm(slot2, bslot2, KT, h_rhs, h_bufs, range(NB - NBO, NB), cons_zs)
    A.release(m_glu)
    P.mark("glu")
    dump("y4T", y4T, B_y4, [128, 8, LO], BF16)
    if stop_after == "glu":
        P.emit()
        return nc, dbg

    mixT, B_mix = A.alloc("mixT", [128, KT, LO], BF16, nbufs=NTO)
    fnw, B_fnw = A.alloc("fnw", [128, D], F32)
    P.d("sp", fnw, fnw_in.to_broadcast([128, D]), writes=[B_fnw])
    wo0, B_wo0 = A.alloc("wo0", [128, KT, 512], BF16)
    P.d("pool", wo0, wout_t[0], writes=[B_wo0])
    m_mix = A.mark()
    SGS = Ring([A.alloc(f"sgs{i}", [128, 512], BF16) for i in range(4)])
    SGD = Ring([A.alloc(f"sgd{i}", [128, 512], BF16) for i in range(4)])
    M1 = Ring([A.alloc(f"m1_{i}", [128, 512], F32) for i in range(4)])
    M2 = Ring([A.alloc(f"m2_{i}", [128, 512], F32) for i in range(3)])

    def y4_rhs(kt, bo):
        return y4T[:, kt, bo * 512:(bo + 1) * 512]

    def dn_rhs(kt, bo):
        return dnoT[:, kt, bo * 512:(bo + 1) * 512]

    mix_tiles = []
    for j in range(KT):
        mix_tiles += [(w_in_t[T_GS + j], KT), (w_in_t[T_GD + j], KT), (wups_t[j], 8), (wupd_t[j], 8)]
    wsr["ring"] = Ring(list(wslots) + [A.alloc(f"wslotx{i}", [128, KT, 128], BF16) for i in range(2)])
    mix_pipe = WPipe(mix_tiles, depth=2)
    for j in range(KT):
        s_gs = mix_pipe.take()
        st_ = {bo: {} for bo in range(NBO)}

        def cons_gs(b_, ps, B_ps, st_=st_):
            sg, B_sg = SGS.next()
            P.x("act", "activation", out=sg, in_=ps, func=AF.Sigmoid, reads=[B_ps], writes=[B_sg])
            st_[b_ - (NB - NBO)]["gs"] = (sg, B_sg)
        proj_fm(s_gs[0], s_gs[1], KT, h_rhs, h_bufs, range(NB - NBO, NB), cons_gs)

        s_gd = mix_pipe.take()

        def cons_gd(b_, ps, B_ps, st_=st_):
            sg, B_sg = SGD.next()
            P.x("act", "activation", out=sg, in_=ps, func=AF.Sigmoid, reads=[B_ps], writes=[B_sg])
            st_[b_ - (NB - NBO)]["gd"] = (sg, B_sg)
        proj_fm(s_gd[0], s_gd[1], KT, h_rhs, h_bufs, range(NB - NBO, NB), cons_gd)

        s_us = mix_pipe.take()

        def cons_us(bo_, ps, B_ps, st_=st_):
            m1_, B_m1 = M1.next()
            sg, B_sg = st_[bo_]["gs"]
            P.x("dve", "tensor_tensor", out=m1_, in0=ps, in1=sg, op=ALU.mult, reads=[B_ps, B_sg], writes=[B_m1])
            st_[bo_]["m1"] = (m1_, B_m1)
        proj_fm(s_us[0], s_us[1], 8, y4_rhs, lambda bo_: list(B_y4), range(NBO), cons_us)

        s_ud = mix_pipe.take()

        def cons_ud(bo_, ps, B_ps, j=j, st_=st_):
            m2_, B_m2 = M2.next()
            sg, B_sg = st_[bo_]["gd"]
            P.x("dve", "tensor_tensor", out=m2_, in0=ps, in1=sg, op=ALU.mult, reads=[B_ps, B_sg], writes=[B_m2])
            m1_, B_m1 = st_[bo_]["m1"]
            P.x("pool", "tensor_tensor", out=mixT[:, j, bo_ * 512:(bo_ + 1) * 512], in0=m1_, in1=m2_, op=ALU.add,
                reads=[B_m1, B_m2], writes=B_mix[bo_ * 4:(bo_ + 1) * 4])
        proj_fm(s_ud[0], s_ud[1], 8, dn_rhs, lambda bo_: list(B_dno), range(NBO), cons_ud)
    wsr["ring"] = WS
    A.release(m_mix)
    H.release(0)
    P.mark("mix")

    rT, B_r = H.alloc("rT", [128, NTO, D], F32, nbufs=NTO)
    wo1, B_wo1 = A.alloc("wo1", [128, KT, 512], BF16)
    WO = Ring([(wo0, B_wo0), (wo1, B_wo1)])
    XO = Ring([A.alloc(f"xo{i}", [128, 512], F32) for i in range(3)])
    st2, B_st2 = A.alloc("st2", [128, NTO, 2], F32, nbufs=NTO)
    fjunk, B_fjunk = A.alloc("fjunk", [128, D], BF16)
    for cb in range(4):
        wo, B_wo = WO.next()
        if cb > 0:
            P.d("pool", wo, wout_t[cb], writes=[B_wo])
        for tt_ in range(NTO):
            xo, B_xo = XO.next()
            P.d("sp", xo, x_in[T0 + tt_ * 128:T0 + (tt_ + 1) * 128, cb * 512:(cb + 1) * 512], writes=[B_xo])
            ps, B_ps = cur["big"].next()
            for kt in range(KT):
                P.x("pe", "matmul", ps, lhsT=mixT[:, kt, tt_ * 128:(tt_ + 1) * 128], rhs=wo[:, kt, :],
                    start=(kt == 0), stop=(kt == KT - 1), reads=[B_mix[tt_], B_wo], writes=[B_ps])
            P.x("dve", "tensor_tensor", out=rT[:, tt_, cb * 512:(cb + 1) * 512], in0=ps, in1=xo, op=ALU.add,
                reads=[B_ps, B_xo], writes=[B_r[tt_]])
    for tt_ in range(NTO):
        P.x("act", "activation", out=fjunk, in_=rT[:, tt_, :], func=AF.Square, accum_out=st2[:, tt_, 0:1],
            reads=[B_r[tt_]], writes=[B_fjunk, B_st2[tt_]])
        P.x("act", "activation", out=st2[:, tt_, 1:2], in_=st2[:, tt_, 0:1], func=AF.Sqrt, bias=epsc[:, 0:1], scale=1.0 / D,
            reads=[B_st2[tt_], B_eps], writes=[B_st2[tt_]])
        P.x("dve", "reciprocal", out=st2[:, tt_, 1:2], in_=st2[:, tt_, 1:2], reads=[B_st2[tt_]], writes=[B_st2[tt_]])
        P.x("dve", "scalar_tensor_tensor", out=rT[:, tt_, :], in0=rT[:, tt_, :], scalar=st2[:, tt_, 1:2], in1=fnw,
            op0=ALU.mult, op1=ALU.mult, reads=[B_r[tt_], B_st2[tt_], B_fnw], writes=[B_r[tt_]])
        P.d("sp", out_ap[tt_ * 128:(tt_ + 1) * 128, :], rT[:, tt_, :], reads=[B_r[tt_]], is_output=True)
    P.mark("out")
    P.emit()
    build.last_marks = P.marks
    return nc, dbg


_NC_CACHE = {}


def kernel(**inputs):
    x = np.asarray(inputs["x"], dtype=np.float32)
    Bsz, L, Dm = x.shape
    LO = L // 2
    key = (L,)
    if key not in _NC_CACHE:
        _NC_CACHE[key] = build(LT=L, LO=LO)[0]
    nc = _NC_CACHE[key]
    in_maps = []
    wmap = make_in_map(inputs, np.zeros((1, 1), np.float32))
    wmap.pop("x")
    for c in range(8):
        b, p = c // 2, c % 2
        if p == 0:
            xloc = np.concatenate([np.zeros((L - LO, Dm), np.float32), x[b, :LO]], axis=0)
        else:
            xloc = x[b]
        in_maps.append(make_in_map(inputs, xloc, wmap))
    res = run_bass_kernel_spmd(nc, in_maps, core_ids=list(range(8)))
    out = np.empty((Bsz, L, Dm), np.float32)
    for c in range(8):
        b, p = c // 2, c % 2
        out[b, p * LO:(p + 1) * LO] = np.asarray(res.results[c]["out"], dtype=np.float32)
    return out


def make_in_map(inp, xloc, wmap=None):
    if wmap is not None:
        m = dict(wmap)
        m["x"] = np.ascontiguousarray(xloc, dtype=np.float32)
        return m
    f = np.float32
    g = lambda k: np.asarray(inp[k], dtype=f)
    m = {}
    m["x"] = np.ascontiguousarray(xloc, dtype=f)
    w_in = g("w_in")[0]

    def tile_of(w, col0, ncols=128, kt=16):
        return w[:, col0:col0 + ncols].reshape(kt, 128, ncols).transpose(1, 0, 2)
    col0s = ([0 + 128 * j for j in range(8)] + [1024 + 128 * j for j in range(8)] + [2048 + 128 * h for h in range(8)]
             + [3072 + 128 * h for h in range(8)] + [4096 + 128 * h for h in range(8)] + [5120 + 128 * h for h in range(8)]
             + [None] + [6160 + 128 * j for j in range(16)] + [8208 + 128 * j for j in range(16)])
    wt = np.zeros((81, 128, 16, 128), f)
    for ti, c0 in enumerate(col0s):
        if c0 is None:
            wt[ti, :, :, 0:8] = tile_of(w_in, 6144, 8)
            for r0 in (32, 64, 96):
                wt[ti, :, :, r0:r0 + 8] = tile_of(w_in, 6152, 8)
        else:
            wt[ti] = tile_of(w_in, c0)
    m["w_in_t"] = wt
    m["lnw"] = np.ascontiguousarray(g("ln_w")[0].reshape(16, 128).T)
    m["fnw"] = np.ascontiguousarray(g("final_norm_w")[None, :])
    m["convw"] = np.ascontiguousarray(g("dn_conv_w")[0].reshape(4, 24, 128).transpose(2, 1, 0))
    dnp = np.zeros((128, 2), f)
    for r0 in (32, 64, 96):
        dnp[r0:r0 + 8, 0] = g("dn_a_log")[0]
        dnp[r0:r0 + 8, 1] = g("dn_dt_bias")[0]
    m["dnp"] = dnp
    m["dnnw"] = np.ascontiguousarray(g("dn_norm_w")[0][:, None])
    lam = np.zeros((128, 3, 32), f)
    lre, lim, lst = g("s5_lam_re")[0], g("s5_lam_im")[0], g("s5_log_step")[0]
    for half in range(2):
        gs = slice(half * 32, half * 32 + 32)
        lam[half * 64:(half + 1) * 64, 0, :] = lre[gs].T
        lam[half * 64:(half + 1) * 64, 1, :] = lim[gs].T
        lam[half * 64:(half + 1) * 64, 2, :] = np.broadcast_to(lst[gs][None, :], (64, 32))
    m["lam"] = lam
    sb = np.zeros((128, 2, 32, 16), f)
    sc = np.zeros((128, 2, 32, 16), f)
    for half in range(2):
        gs = slice(half * 32, half * 32 + 32)
        ps = slice(half * 64, half * 64 + 64)
        sb[ps, 0] = g("s5_b_re")[0][gs].transpose(1, 0, 2)
        sb[ps, 1] = g("s5_b_im")[0][gs].transpose(1, 0, 2)
        sc[ps, 0] = g("s5_c_re")[0][gs].transpose(2, 0, 1)
        sc[ps, 1] = g("s5_c_im")[0][gs].transpose(2, 0, 1)
    m["s5b"] = sb
    m["s5c"] = sc
    d = g("s5_d")[0].reshape(64, 16)
    m["s5d"] = np.ascontiguousarray(np.tile(d.T, (8, 1)))
    m["w_glu_t"] = np.ascontiguousarray(np.stack([tile_of(g("s5_w_glu")[0], 128 * j, 128, 8) for j in range(8)]))
    m["w_ups_t"] = np.ascontiguousarray(np.stack([tile_of(g("s5_w_up")[0], 128 * j, 128, 8) for j in range(16)]))
    m["w_upd_t"] = np.ascontiguousarray(np.stack([tile_of(g("dn_w_up")[0], 128 * j, 128, 8) for j in range(16)]))
    m["w_out_t"] = np.ascontiguousarray(np.stack([tile_of(g("w_out")[0], 512 * cb, 512, 16) for cb in range(4)]))
    return m
```
